# Optimizing a Trainium2 kernel written in Bass

```python
import functools
import jax, jax.numpy as jnp
from jax import lax
import numpy as np

D_MODEL = 1024
BATCH = 4
SEQ = 4096
DEPTH = 4

GRID_W = 64
CTX_LEN = 256
H_RET = 4
DK_RET = 64
DV_RET = 128
H_ATT = 8
KV_ATT = 2
HD_ATT = 64
H_M = 4
DK_M = 64
DV_M = 128
CHUNK = 128
Q_BLOCK = 128
D_FF = 2816
CONV_W = 3
ROPE_THETA = 10000.0
NORM_EPS = 1e-6
SPLIT_SIZES = (H_RET * DK_RET, H_RET * DK_RET, H_RET * DV_RET, H_RET * DV_RET,
               H_ATT * HD_ATT, KV_ATT * HD_ATT, KV_ATT * HD_ATT,
               H_M * DK_M, H_M * DK_M, H_M * DV_M, H_M * DV_M, 2 * H_M, 2 * H_M,
               3 * D_MODEL)
N_IN = sum(SPLIT_SIZES)

kernel_name = "hybrid_ret_gqa_mlstm_convffn_dit"


def split_proj(p):
    idx, acc = [], 0
    for s in SPLIT_SIZES[:-1]:
        acc += s
        idx.append(acc)
    return jnp.split(p, idx, axis=-1)


def layer_norm(x, gain=None, bias=None):
    xf = x.astype(jnp.float32)
    mu = jnp.mean(xf, axis=-1, keepdims=True)
    var = jnp.mean(jnp.square(xf - mu), axis=-1, keepdims=True)
    y = (xf - mu) * lax.rsqrt(var + NORM_EPS)
    if gain is not None:
        y = y * gain.astype(jnp.float32) + bias.astype(jnp.float32)
    return y.astype(x.dtype)


def rms_norm(x, gain):
    xf = x.astype(jnp.float32)
    y = xf * lax.rsqrt(jnp.mean(jnp.square(xf), axis=-1, keepdims=True) + NORM_EPS)
    return (y * gain.astype(jnp.float32)).astype(x.dtype)


def head_norm(y, gain):
    b, s, h, d = y.shape
    yf = y.astype(jnp.float32)
    mu = jnp.mean(yf, axis=-1, keepdims=True)
    var = jnp.mean(jnp.square(yf - mu), axis=-1, keepdims=True)
    yn = ((yf - mu) * lax.rsqrt(var + NORM_EPS)).reshape(b, s, h * d)
    return (yn * gain.astype(jnp.float32)).astype(y.dtype)


def modulate(x, shift, scale):
    return layer_norm(x) * (1.0 + scale) + shift


def to_chunks(a):
    b, s = a.shape[:2]
    a = a.reshape((b, s // CHUNK, CHUNK) + a.shape[2:])
    return jnp.moveaxis(jnp.moveaxis(a, 1, 0), 2, 3)


def from_chunks(o):
    nc, b, h, l, d = o.shape
    return o.transpose(1, 0, 3, 2, 4).reshape(b, nc * l, h, d)


def retention_scan(log_gamma, args, state):
    q, k, v = (a.astype(jnp.float32) for a in args)
    pos = jnp.arange(CHUNK, dtype=jnp.float32)
    diff = pos[:, None] - pos[None, :]
    lower = diff >= 0
    intra = jnp.where(lower, jnp.exp(jnp.where(lower, diff, 0.0)[None] * log_gamma[:, None, None]), 0.0)
    q_decay = jnp.exp((pos + 1.0)[None, :] * log_gamma[:, None])
    k_decay = jnp.exp((CHUNK - 1.0 - pos)[None, :] * log_gamma[:, None])
    chunk_decay = jnp.exp(CHUNK * log_gamma)

    def step(s_state, blk):
        qc, kc, vc = blk
        scores = jnp.einsum('bhld,bhmd->bhlm', qc, kc) * intra
        out = (jnp.einsum('bhlm,bhme->bhle', scores, vc)
               + jnp.einsum('bhld,bhde->bhle', qc, s_state) * q_decay[None, :, :, None])
        s_state = (s_state * chunk_decay[None, :, None, None]
                   + jnp.einsum('bhld,bhle->bhde', kc * k_decay[None, :, :, None], vc))
        return s_state, out

    state, out = lax.scan(step, state, (to_chunks(q), to_chunks(k), to_chunks(v)))
    return from_chunks(out).astype(args[2].dtype), state


def mlstm_scan(args, state):
    q, k, v, li, lf = (a.astype(jnp.float32) for a in args)
    lower = jnp.tril(jnp.ones((CHUNK, CHUNK), dtype=bool))

    def step(carry, blk):
        c_mat, n_vec, m = carry
        qc, kc, vc, ic, fc = blk
        b = jnp.cumsum(fc, axis=-1)
        d_mat = jnp.where(lower, b[..., :, None] - b[..., None, :] + ic[..., None, :], -jnp.inf)
        inter = b + m[..., None]
        m_t = jnp.maximum(inter, jnp.max(d_mat, axis=-1))
        w = jnp.exp(d_mat - m_t[..., None])
        s_qk = jnp.einsum('bhld,bhmd->bhlm', qc, kc) * w
        a_inter = jnp.exp(inter - m_t)
        num = (jnp.einsum('bhlm,bhme->bhle', s_qk, vc)
               + a_inter[..., None] * jnp.einsum('bhld,bhde->bhle', qc, c_mat))
        den = jnp.sum(s_qk, axis=-1) + a_inter * jnp.einsum('bhld,bhd->bhl', qc, n_vec)
        h = num / jnp.maximum(jnp.abs(den), jnp.exp(-m_t))[..., None]
        b_end = b[..., -1]
        g = b_end[..., None] - b + ic
        m_new = jnp.maximum(b_end + m, jnp.max(g, axis=-1))
        wk = jnp.exp(g - m_new[..., None])
        carry_scale = jnp.exp(b_end + m - m_new)
        c_mat = carry_scale[..., None, None] * c_mat + jnp.einsum('bhld,bhle->bhde', kc * wk[..., None], vc)
        n_vec = carry_scale[..., None] * n_vec + jnp.einsum('bhl,bhld->bhd', wk, kc)
        return (c_mat, n_vec, m_new), h

    state, h = lax.scan(step, state, tuple(to_chunks(a) for a in (q, k, v, li, lf)))
    return from_chunks(h).astype(args[2].dtype), state


def bidirectional(scan_f, scan_b, ctx_f, lat_f, ctx_b, lat_b, init_state):
    flip = lambda args: tuple(jnp.flip(a, axis=1) for a in args)
    c_f, s_f = scan_f(ctx_f, init_state)
    l_f, _ = scan_f(lat_f, s_f)
    c_b, s_b = scan_b(flip(ctx_b), init_state)
    l_b, _ = scan_b(flip(lat_b), s_b)
    return c_f + jnp.flip(c_b, axis=1), l_f + jnp.flip(l_b, axis=1)


def retention_branch(pc, pl, decay_logit, gn_g, with_ctx):
    log_gamma = jax.nn.log_sigmoid(decay_logit.astype(jnp.float32))

    def heads(p):
        b, s = p[0].shape[:2]
        q = p[0].reshape(b, s, H_RET, DK_RET)
        k = p[1].reshape(b, s, H_RET, DK_RET) * (DK_RET ** -0.5)
        v = p[2].reshape(b, s, H_RET, DV_RET)
        return (q, k, v)

    args_c, args_l = heads(pc), heads(pl)
    init = jnp.zeros((pl[0].shape[0], H_RET, DK_RET, DV_RET), jnp.float32)
    o_c, o_l = bidirectional(functools.partial(retention_scan, log_gamma[0]),
                             functools.partial(retention_scan, log_gamma[1]),
                             args_c, args_l, args_c, args_l, init)
    readout = lambda o, p: head_norm(o, gn_g) * jax.nn.silu(p[3])
    return (readout(o_c, pc) if with_ctx else None), readout(o_l, pl)


def rope_half(x, ang):
    n = x.shape[-1] // 2
    cos = jnp.cos(ang)[None, :, None, :]
    sin = jnp.sin(ang)[None, :, None, :]
    x1, x2 = x[..., :n], x[..., n:]
    return jnp.concatenate([x1 * cos - x2 * sin, x2 * cos + x1 * sin], axis=-1).astype(x.dtype)


def axial_rope(x, ang_row, ang_col):
    half = x.shape[-1] // 2
    return jnp.concatenate([rope_half(x[..., :half], ang_row), rope_half(x[..., half:], ang_col)], axis=-1)


def attention_branch(pc, pl, qn_g, kn_g, ang_row, ang_col, with_ctx):
    scale = HD_ATT ** -0.5
    rep = H_ATT // KV_ATT

    def heads(p):
        b, s = p[4].shape[:2]
        q = rms_norm(p[4].reshape(b, s, H_ATT, HD_ATT), qn_g)
        k = rms_norm(p[5].reshape(b, s, KV_ATT, HD_ATT), kn_g)
        v = p[6].reshape(b, s, KV_ATT, HD_ATT)
        return q, k, v

    q_c, k_c, v_c = heads(pc)
    q_l, k_l, v_l = heads(pl)
    q_l = axial_rope(q_l, ang_row, ang_col)
    k_l = axial_rope(k_l, ang_row, ang_col)
    k_all = jnp.concatenate([k_c, k_l], axis=1)
    v_all = jnp.concatenate([v_c, v_l], axis=1)

    def attend(q, k, v):
        b, nq = q.shape[:2]
        qg = q.reshape(b, nq, KV_ATT, rep, HD_ATT)
        s = jnp.einsum('bqgrd,bkgd->bgrqk', qg, k).astype(jnp.float32) * scale
        p = jax.nn.softmax(s, axis=-1).astype(v.dtype)
        return jnp.einsum('bgrqk,bkgd->bqgrd', p, v).reshape(b, nq, H_ATT * HD_ATT)

    b, s = q_l.shape[:2]
    q_blocks = jnp.moveaxis(q_l.reshape(b, s // Q_BLOCK, Q_BLOCK, H_ATT, HD_ATT), 1, 0)
    y_l = jnp.moveaxis(lax.map(lambda qb: attend(qb, k_all, v_all), q_blocks), 0, 1)
    y_l = y_l.reshape(b, s, H_ATT * HD_ATT)
    y_c = attend(q_c, k_c, v_c) if with_ctx else None
    return y_c, y_l


def mlstm_branch(pc, pl, gn_g, with_ctx):
    def heads(p, d):
        b, s = p[7].shape[:2]
        q = p[7].reshape(b, s, H_M, DK_M)
        k = p[8].reshape(b, s, H_M, DK_M) * (DK_M ** -0.5)
        v = p[9].reshape(b, s, H_M, DV_M)
        li = p[11][..., d * H_M:(d + 1) * H_M].astype(jnp.float32)
        lf = jax.nn.log_sigmoid(p[12][..., d * H_M:(d + 1) * H_M].astype(jnp.float32))
        return (q, k, v, li, lf)

    bsz = pl[7].shape[0]
    init = (jnp.zeros((bsz, H_M, DK_M, DV_M), jnp.float32),
            jnp.zeros((bsz, H_M, DK_M), jnp.float32),
            jnp.zeros((bsz, H_M), jnp.float32))
    o_c, o_l = bidirectional(mlstm_scan, mlstm_scan, heads(pc, 0), heads(pl, 0), heads(pc, 1), heads(pl, 1), init)
    readout = lambda o, p: jax.nn.sigmoid(p[10]) * head_norm(o, gn_g)
    return (readout(o_c, pc) if with_ctx else None), readout(o_l, pl)


def gated_merge(p_gate, y_ret, y_att, y_m, w_r, w_a, w_m, w_o):
    g_r, g_a, g_m = jnp.split(p_gate, 3, axis=-1)
    z = (jax.nn.sigmoid(g_r) * (y_ret @ w_r) + jax.nn.sigmoid(g_a) * (y_att @ w_a)
         + jax.nn.sigmoid(g_m) * (y_m @ w_m))
    return z @ w_o


def conv_ffn(h, w_up, conv_w, conv_b, w_down):
    u = h @ w_up
    u = lax.conv_general_dilated(u, conv_w[:, None, :].astype(u.dtype), window_strides=(1,),
                                 padding=((CONV_W // 2, CONV_W // 2),),
                                 dimension_numbers=('NWC', 'WIO', 'NWC'),
                                 feature_group_count=u.shape[-1]) + conv_b
    a, g = jnp.split(u, 2, axis=-1)
    return (jax.nn.silu(g) * a) @ w_down


def setup_inputs(seed: int = 0) -> dict:
    key = jax.random.key(seed)
    ks = jax.random.split(key, 32)
    f32 = jnp.float32
    beta = (8.0 * DEPTH) ** -0.25
    nrm = lambda k, shape, sc: jax.random.normal(k, shape, f32) * sc
    f_off = sum(SPLIT_SIZES[:12])
    b_in = nrm(ks[7], (DEPTH, N_IN), 0.02).at[:, f_off:f_off + 2 * H_M].add(
        jnp.tile(jnp.linspace(3.0, 6.0, H_M, dtype=f32), 2))
    gamma = 1.0 - 2.0 ** (-5.0 - jnp.arange(H_RET, dtype=f32))
    ret_decay_logit = jnp.log(gamma / (1.0 - gamma))[None, None, :] + nrm(ks[8], (DEPTH, 2, H_RET), 0.1)
    return {
        "x": nrm(ks[0], (BATCH, SEQ, D_MODEL), 1.0),
        "c": nrm(ks[1], (BATCH, D_MODEL), 1.0),
        "ctx": nrm(ks[2], (BATCH, CTX_LEN, D_MODEL), 1.0),
        "c_ctx": nrm(ks[3], (D_MODEL,), 1.0),
        "w_mod": nrm(ks[4], (DEPTH, D_MODEL, 6 * D_MODEL), D_MODEL ** -0.5),
        "b_mod": nrm(ks[5], (DEPTH, 6 * D_MODEL), 0.01),
        "w_in": nrm(ks[6], (DEPTH, D_MODEL, N_IN), D_MODEL ** -0.5),
        "b_in": b_in,
        "ret_decay_logit": ret_decay_logit,
        "ret_gn_g": 1.0 + nrm(ks[9], (DEPTH, H_RET * DV_RET), 0.02),
        "attn_qn_g": 1.0 + nrm(ks[10], (DEPTH, HD_ATT), 0.02),
        "attn_kn_g": 1.0 + nrm(ks[11], (DEPTH, HD_ATT), 0.02),
        "mlstm_gn_g": 1.0 + nrm(ks[12], (DEPTH, H_M * DV_M), 0.02),
        "w_br_ret": nrm(ks[13], (DEPTH, H_RET * DV_RET, D_MODEL), (H_RET * DV_RET) ** -0.5 * beta),
        "w_br_att": nrm(ks[14], (DEPTH, H_ATT * HD_ATT, D_MODEL), (H_ATT * HD_ATT) ** -0.5 * beta),
        "w_br_mlstm": nrm(ks[15], (DEPTH, H_M * DV_M, D_MODEL), (H_M * DV_M) ** -0.5 * beta),
        "w_out": nrm(ks[16], (DEPTH, D_MODEL, D_MODEL), D_MODEL ** -0.5 * beta),
        "ln1_g": 1.0 + nrm(ks[17], (DEPTH, D_MODEL), 0.02),
        "ln1_b": nrm(ks[18], (DEPTH, D_MODEL), 0.02),
        "w_up": nrm(ks[19], (DEPTH, D_MODEL, 2 * D_FF), D_MODEL ** -0.5),
        "conv_w": nrm(ks[20], (DEPTH, CONV_W, 2 * D_FF), CONV_W ** -0.5),
        "conv_b": nrm(ks[21], (DEPTH, 2 * D_FF), 0.02),
        "w_down": nrm(ks[22], (DEPTH, D_FF, D_MODEL), D_FF ** -0.5 * beta),
        "ln2_g": 1.0 + nrm(ks[23], (DEPTH, D_MODEL), 0.02),
        "ln2_b": nrm(ks[24], (DEPTH, D_MODEL), 0.02),
    }


def reference(x, c, ctx, c_ctx, w_mod, b_mod, w_in, b_in, ret_decay_logit, ret_gn_g, attn_qn_g, attn_kn_g,
              mlstm_gn_g, w_br_ret, w_br_att, w_br_mlstm, w_out, ln1_g, ln1_b, w_up, conv_w, conv_b, w_down,
              ln2_g, ln2_b):
    alpha = (2.0 * DEPTH) ** 0.25
    seq_lat = x.shape[1]
    rows = seq_lat // GRID_W
    row = jnp.repeat(jnp.arange(rows), GRID_W).astype(jnp.float32)
    col = jnp.tile(jnp.arange(GRID_W), rows).astype(jnp.float32)
    n_freq = HD_ATT // 4
    freqs = ROPE_THETA ** (-jnp.arange(n_freq, dtype=jnp.float32) / n_freq)
    ang_row = row[:, None] * freqs[None, :]
    ang_col = col[:, None] * freqs[None, :]
    silu_c = jax.nn.silu(c)
    silu_cc = jax.nn.silu(c_ctx)
    xc = ctx
    for l in range(DEPTH):
        with_ctx = l < DEPTH - 1
        mod_l = jnp.split((silu_c @ w_mod[l] + b_mod[l])[:, None, :], 6, axis=-1)
        mod_c = jnp.split(silu_cc @ w_mod[l] + b_mod[l], 6, axis=-1)
        pl = split_proj(modulate(x, mod_l[0], mod_l[1]) @ w_in[l] + b_in[l])
        pc = split_proj(modulate(xc, mod_c[0], mod_c[1]) @ w_in[l] + b_in[l])
        yr_c, yr_l = retention_branch(pc, pl, ret_decay_logit[l], ret_gn_g[l], with_ctx)
        ya_c, ya_l = attention_branch(pc, pl, attn_qn_g[l], attn_kn_g[l], ang_row, ang_col, with_ctx)
        ym_c, ym_l = mlstm_branch(pc, pl, mlstm_gn_g[l], with_ctx)
        mix_l = gated_merge(pl[-1], yr_l, ya_l, ym_l, w_br_ret[l], w_br_att[l], w_br_mlstm[l], w_out[l])
        x = layer_norm(alpha * x + mod_l[2] * mix_l, ln1_g[l], ln1_b[l])
        ffn_l = conv_ffn(modulate(x, mod_l[3], mod_l[4]), w_up[l], conv_w[l], conv_b[l], w_down[l])
        x = layer_norm(alpha * x + mod_l[5] * ffn_l, ln2_g[l], ln2_b[l])
        if with_ctx:
            mix_c = gated_merge(pc[-1], yr_c, ya_c, ym_c, w_br_ret[l], w_br_att[l], w_br_mlstm[l], w_out[l])
            xc = layer_norm(alpha * xc + mod_c[2] * mix_c, ln1_g[l], ln1_b[l])
            ffn_c = conv_ffn(modulate(xc, mod_c[3], mod_c[4]), w_up[l], conv_w[l], conv_b[l], w_down[l])
            xc = layer_norm(alpha * xc + mod_c[5] * ffn_c, ln2_g[l], ln2_b[l])
    return x
```

```python
import math
import numpy as np
from contextlib import ExitStack
import concourse.bass as bass
import concourse.mybir as mybir
from concourse.bass_utils import run_bass_kernel_spmd

F32 = mybir.dt.float32
BF16 = mybir.dt.bfloat16
I32 = mybir.dt.int32
AF = mybir.ActivationFunctionType
ALU = mybir.AluOpType
AX = mybir.AxisListType

D = 1024
DEPTH = 4
NT = 34
NCT = 2
T = NT * 128
DFF = 2816
NF = 22
EPS = 1e-6
ALPHA = (2.0 * DEPTH) ** 0.25
NCORES = 4

O_RQ, O_RK, O_RV, O_RG = 0, 256, 512, 1024
O_AQ, O_AK, O_AV = 1536, 2048, 2176
O_MQ, O_MK, O_MV, O_MO, O_MI, O_MF, O_GATE = 2304, 2560, 2816, 3328, 3840, 3848, 3856
N_IN = 6928

FM_PIECES = [(0, O_RQ, 256), (256, O_RK, 256), (512, O_MQ, 256), (768, O_MK, 256)]
TMW0 = 1024
TM_GROUPS = [
    ("MG", [(O_MI, 16)]),
    ("RV", [(O_RV, 512)]),
    ("RG", [(O_RG, 512)]),
    ("AQ", [(O_AQ, 512)]),
    ("AKV", [(O_AK, 128), (O_AV, 128)]),
    ("MV", [(O_MV, 512)]),
    ("MO", [(O_MO, 512)]),
] + [("G%d" % i, [(O_GATE + 512 * i, 512)]) for i in range(6)]
TM_OFF = {}
_o = 0
for _n, _p in TM_GROUPS:
    TM_OFF[_n] = _o
    _o += sum(n for _, n in _p)
TM_COLS = _o
W_COLS = TMW0 + TM_COLS

R_RV, R_RG, R_AV, R_RK, R_MK, R_MV, R_MO, R_MG = 0, 1024, 1536, 1666, 1922, 2178, 3210, 3722
R_COLS = 6794
SM_COLS = 16
R_SPLIT = R_MG


class Sem:
    def __init__(self, h, name):
        self.h = h
        self.name = name
        self.owner = None
        self.total = 0


class Buf:
    __slots__ = ("w", "r", "name")

    def __init__(self, name=""):
        self.w = {}
        self.r = {}
        self.name = name


class Tl:
    def __init__(self, fw, ap, name, buf=None):
        self.fw = fw
        self.ap = ap
        self.name = name
        self.buf = buf if buf is not None else Buf(name)
        self._ds = None

    @property
    def ds(self):
        if self._ds is None:
            self._ds = self.fw.pool_sem()
        return self._ds

    def __getitem__(self, idx):
        return self.ap[idx]


class Eng:
    def __init__(self, fw, name, eng):
        self.fw = fw
        self.name = name
        self.eng = eng
        self.sem = fw.new_sem("c_" + name)
        self.sem.owner = self
        self.cnt = 0
        self.waited = {}

    def wait(self, sem, val):
        if val <= 0:
            return
        if self.waited.get(sem, 0) >= val:
            return
        if sem.owner is not None:
            assert val <= sem.owner.cnt, ("wait on unissued instr", self.name, sem.name, val, sem.owner.cnt)
        else:
            assert val <= sem.total
        self.eng.wait_ge(sem.h, val)
        self.waited[sem] = val


class FW:
    def __init__(self, nc):
        self.nc = nc
        self.es = ExitStack()
        self.nsem = 0
        self.pe = Eng(self, "pe", nc.tensor)
        self.act = Eng(self, "act", nc.scalar)
        self.dve = Eng(self, "dve", nc.vector)
        self.pool = Eng(self, "pool", nc.gpsimd)
        self.sp = Eng(self, "sp", nc.sync)
        self.engs = [self.pe, self.act, self.dve, self.pool, self.sp]
        self.dsems = []
        self.swq = []
        self.ndram = 0
        self.sem_pool = []
        self.sem_idx = 0
        self.pool_base = 0
        self.nsb = 0

    def pool_sem(self):
        if self.sem_idx == len(self.sem_pool):
            self.sem_pool.append(self.new_sem("dp%d" % self.sem_idx))
        s = self.sem_pool[self.sem_idx]
        self.sem_idx += 1
        return s

    def reset_pool(self):
        self.sem_idx = self.pool_base

    def new_sem(self, name):
        h = self.es.enter_context(self.nc.semaphore(name + "_%d" % self.nsem))
        self.nsem += 1
        s = Sem(h, name)
        return s

    def sb(self, es, name, shape, dtype):
        self.nsb += 1
        t = es.enter_context(self.nc.sbuf_tensor("%s_%d" % (name, self.nsb), list(shape), dtype))
        return Tl(self, t[:], name)

    def ring(self, es, name, shape, dtype, n):
        return [self.sb(es, "%s%d" % (name, i), shape, dtype) for i in range(n)]

    def dram(self, name, shape, dtype, kind="Internal"):
        t = self.nc.dram_tensor(name, list(shape), dtype, kind=kind)
        return t.ap()

    def _deps(self, E, reads, writes):
        for b in reads:
            b = b.buf if isinstance(b, Tl) else b
            for sem, v in b.w.items():
                if sem is E.sem and E is self.pe:
                    continue
                E.wait(sem, v)
        for b in writes:
            b = b.buf if isinstance(b, Tl) else b
            for sem, v in list(b.w.items()) + list(b.r.items()):
                if sem is E.sem:
                    continue
                E.wait(sem, v)

    def _mark(self, sem, tok, reads, writes):
        for b in reads:
            b = b.buf if isinstance(b, Tl) else b
            if b.r.get(sem, 0) < tok:
                b.r[sem] = tok
        for b in writes:
            b = b.buf if isinstance(b, Tl) else b
            b.w = {sem: tok}
            b.r = {}

    def op(self, E, fn, reads=(), writes=(), inc=True):
        self._deps(E, reads, writes)
        ins = fn(E.eng)
        if inc:
            E.cnt += 1
            ins.then_inc(E.sem.h, 1)
            tok = E.cnt
        else:
            tok = E.cnt + 1
        self._mark(E.sem, tok, reads, writes)
        return ins

    def dma(self, Q, out, in_, reads=(), writes=(), ds=None, serialize=True, **kw):
        self._deps(Q, reads, writes)
        if serialize and ds.total > 0:
            Q.wait(ds, ds.total)
        if Q is self.pool:
            while len(self.swq) >= 2:
                s_, v_ = self.swq.pop(0)
                Q.wait(s_, v_)
        ins = Q.eng.dma_start(out=out, in_=in_, **kw)
        ds.total += 16
        if Q is self.pool:
            self.swq.append((ds, ds.total))
        ins.then_inc(ds.h, 16)
        if ds not in self.dsems:
            self.dsems.append(ds)
        self._mark(ds, ds.total, reads, writes)
        return ins

    def barrier(self):
        for E in self.engs:
            for P in self.engs:
                if P is not E:
                    E.wait(P.sem, P.cnt)
            for ds in self.dsems:
                E.wait(ds, ds.total)


def tt(fw, E, out, in0, in1, op, reads, writes):
    return fw.op(E, lambda e: e.tensor_tensor(out=out, in0=in0, in1=in1, op=op), reads, writes)


def ts(fw, E, out, in0, s1, s2, op0, op1, reads, writes):
    if op1 is None:
        return fw.op(E, lambda e: e.tensor_scalar(out=out, in0=in0, scalar1=s1, scalar2=None, op0=op0), reads, writes)
    return fw.op(E, lambda e: e.tensor_scalar(out=out, in0=in0, scalar1=s1, scalar2=s2, op0=op0, op1=op1), reads, writes)


def stt(fw, E, out, in0, scalar, in1, op0, op1, reads, writes):
    return fw.op(E, lambda e: e.scalar_tensor_tensor(out=out, in0=in0, scalar=scalar, in1=in1, op0=op0, op1=op1), reads, writes)


def act(fw, out, in_, func, reads, writes, bias=None, scale=None):
    kw = {}
    if bias is not None:
        kw["bias"] = bias
    if scale is not None:
        kw["scale"] = scale
    return fw.op(fw.act, lambda e: e.activation(out=out, in_=in_, func=func, **kw), reads, writes)


def cp(fw, E, out, in_, reads, writes):
    if E is fw.act:
        return fw.op(E, lambda e: e.copy(out=out, in_=in_), reads, writes)
    return fw.op(E, lambda e: e.tensor_copy(out=out, in_=in_), reads, writes)


def mm(fw, out, lhsT, rhs, start, stop, reads, writes, inc=None, **kw):
    if inc is None:
        inc = stop
    return fw.op(fw.pe, lambda e: e.matmul(out, lhsT=lhsT, rhs=rhs, start=start, stop=stop, **kw), reads, writes, inc=inc)


def tr(fw, out, in_, ident, reads, writes, inc=True):
    return fw.op(fw.pe, lambda e: e.transpose(out, in_, ident), reads, writes, inc=inc)


def bcast_rows(ap2d, nparts):
    return ap2d.to_broadcast([nparts, ap2d.shape[-1]])


def rstd_from(fw, out, in_, reads, writes, scale=1.0):
    act(fw, out, in_, AF.Ln, reads, writes, bias=fw.eps_t[:, 0:1] if in_.shape[0] == 128 else fw.eps_t[0:in_.shape[0], 0:1], scale=scale)
    act(fw, out, out, AF.Exp, writes, writes, scale=-0.5)


class Prog:
    def __init__(self, n_layers=DEPTH, debug=None, stop_after=None, a_lim=None, skip="", d_lim=None):
        self.a_lim = a_lim
        self.skip = skip
        self.d_lim = d_lim
        self.n_layers = n_layers
        self.debug = debug or []
        self.stop_after = stop_after
        self.nc = bass.Bass("TRN2", target_bir_lowering=False)
        self.fw = FW(self.nc)
        self.inputs = {}
        self.build()

    def din(self, name, shape, dtype=F32):
        ap = self.fw.dram(name, shape, dtype, kind="ExternalInput")
        self.inputs[name] = ap
        return ap

    def dscr(self, name, shape, dtype):
        kind = "ExternalOutput" if name in self.debug else "Internal"
        return self.fw.dram(name, shape, dtype, kind=kind)

    def build(self):
        nc, fw = self.nc, self.fw
        L = self.n_layers
        self.xin = self.din("xin", [T, D])
        self.ccT = self.din("ccT", [128, 8, 2])
        self.w_mod = self.din("w_mod", [L, D, 6 * D])
        self.b_mod = self.din("b_mod", [L, 6 * D])
        self.w_in = self.din("w_in", [L, D, N_IN])
        self.b_in = self.din("b_in", [L, N_IN])
        self.b_in_fm = self.din("b_in_fm", [L, 128, 8])
        self.decay = self.din("decay", [L, 8])
        self.ret_gn = self.din("ret_gn", [L, 512])
        self.qn_g = self.din("qn_g", [L, 64])
        self.kn_g = self.din("kn_g", [L, 64])
        self.m_gn = self.din("m_gn", [L, 512])
        self.w_br = self.din("w_br", [L, 3, 512, D])
        self.w_out = self.din("w_out", [L, D, D])
        self.ln1 = self.din("ln1", [L, 2, D])
        self.w_up = self.din("w_up", [L, D, 2 * DFF])
        self.convp = self.din("convp", [L, 128, 4, 44])
        self.w_down = self.din("w_down", [L, DFF, D])
        self.ln2 = self.din("ln2", [L, 2, D])
        self.consts = self.din("consts", [128, 1024])
        self.out = self.fw.dram("out", [32 * 128, D], F32, kind="ExternalOutput")
        self.X = self.dscr("X", [T, D], F32)
        self.X1 = self.dscr("X1", [T, D], F32)
        self.MOD = self.dscr("MOD", [L, 2, 6 * D], F32)
        self.TMd = self.dscr("TMd", [NT, 128, R_COLS], BF16)
        self.SMd = self.dscr("SMd", [NT, 128, SM_COLS], F32)
        self.FMd = self.dscr("FMd", [NT, 128, 1024], BF16)
        self.AQd = self.dscr("AQd", [NT, 64, 1024], BF16)
        self.AKd = self.dscr("AKd", [64, 2, T], BF16)
        self.ROPEd = self.dscr("ROPEd", [64, 32], F32)
        self.RETCd = self.dscr("RETCd", [L, 128, 16], F32)
        self.SFd = self.dscr("SFd", [NT, 128, 4 * 258], BF16)
        self.SBd = self.dscr("SBd", [NT, 128, 4 * 258], BF16)
        self.YDd = self.dscr("YDd", [NT, 128, 1536], BF16)

        with ExitStack() as es0:
            self.setup_globals(es0)
            fw.barrier()
            fw.pool_base = fw.sem_idx
            if self.stop_after == "S":
                return self.finish()
            for l in range(self.n_layers):
                if "A" not in self.skip:
                    self.phase_a(l)
                fw.barrier()
                fw.reset_pool()
                if self.stop_after == "A%d" % l:
                    return self.finish()
                if "B" not in self.skip:
                    self.phase_b(l)
                else:
                    self.x1buf = Buf("X1")
                fw.barrier()
                fw.reset_pool()
                if self.stop_after == "B%d" % l:
                    return self.finish()
                self.phase_d(l)
                fw.barrier()
                fw.reset_pool()
                if self.stop_after == "D%d" % l:
                    return self.finish()
            self.finish()

    def finish(self):
        fw = self.fw
        fw.barrier()
        for ds in fw.dsems:
            fw.sp.wait(ds, ds.total)

    def setup_globals(self, es):
        nc, fw = self.nc, self.fw
        self.cst = fw.sb(es, "cst", [128, 1024], F32)
        fw.dma(fw.sp, self.cst.ap, self.consts, writes=[self.cst], ds=self.cst.ds)
        self.ident_f = self.cst[:, 0:128]
        self.triF = self.cst[:, 128:256]
        self.triB = self.cst[:, 256:384]
        self.ones_f = self.cst[:, 384:512]
        self.identb = fw.sb(es, "identb", [128, 128], BF16)
        cp(fw, fw.dve, self.identb.ap, self.ident_f, [self.cst], [self.identb])
        self.eps_t = fw.sb(es, "eps_t", [128, 1], F32)
        fw.eps_t = self.eps_t
        fw.op(fw.dve, lambda e: e.memset(self.eps_t.ap, EPS), [], [self.eps_t])
        self.one_t = fw.sb(es, "one_t", [128, 1], F32)
        fw.op(fw.dve, lambda e: e.memset(self.one_t.ap, 1.0), [], [self.one_t])
        self.maskF = fw.sb(es, "maskF", [128, 4, 128], BF16)
        self.maskB = fw.sb(es, "maskB", [128, 4, 128], BF16)
        for h in range(4):
            cp(fw, fw.dve, self.maskF[:, h, :], self.triF, [self.cst], [self.maskF])
            cp(fw, fw.dve, self.maskB[:, h, :], self.triB, [self.cst], [self.maskB])
        self.ps = []
        for i in range(8):
            t = es.enter_context(nc.psum_tensor("psb%d" % i, [128, 512], F32))
            self.ps.append(Tl(fw, t[:], "psb%d" % i))
        xds = fw.new_sem("xcopy")
        fw.dma(fw.sp, self.X, self.xin, ds=xds)
        self.xbuf = Buf("Xall")
        self.xbuf.w = {xds: xds.total}
        self.compute_mod(es)
        self.compute_rope(es)

    def compute_mod(self, es0):
        nc, fw = self.nc, self.fw
        with ExitStack() as es:
            cc = fw.sb(es, "cc", [128, 8, 2], F32)
            sc = fw.sb(es, "sc", [128, 8, 2], F32)
            fw.dma(fw.sp, cc.ap, self.ccT, writes=[cc], ds=cc.ds)
            act(fw, sc.ap, cc.ap, AF.Silu, [cc], [sc])
            wring = fw.ring(es, "wm", [128, 8, 512], F32, 3)
            bm = fw.sb(es, "bm", [2, 6 * D], F32)
            orow = fw.ring(es, "orow", [2, 6 * D], F32, 2)
            k = 0
            for l in range(self.n_layers):
                fw.dma(fw.sp, bm.ap, self.b_mod[l:l + 1, :].to_broadcast([2, 6 * D]), writes=[bm], ds=bm.ds)
                orw = orow[l % 2]
                for g in range(12):
                    wt = wring[k % 3]
                    src = self.w_mod[l].rearrange("(kc p) c -> p kc c", p=128)[:, :, g * 512:(g + 1) * 512]
                    fw.dma(fw.sp, wt.ap, src, writes=[wt], ds=wt.ds)
                    pst = self.ps[k % 2]
                    for kc in range(8):
                        mm(fw, pst[0:2, :], sc[:, kc, :], wt[:, kc, :], kc == 0, kc == 7, [sc, wt], [pst])
                    tt(fw, fw.dve, orw[:, g * 512:(g + 1) * 512], pst[0:2, :], bm[:, g * 512:(g + 1) * 512], ALU.add, [pst, bm], [orw])
                    k += 1
                for ch in (1, 4):
                    ts(fw, fw.dve, orw[:, ch * D:(ch + 1) * D], orw[:, ch * D:(ch + 1) * D], 1.0, None, ALU.add, None, [orw], [orw])
                fw.dma(fw.sp, self.MOD[l], orw.ap, reads=[orw], ds=orw.ds)
            self.modbuf = Buf("MOD")
            for o in orow:
                self.modbuf.w[o.ds] = o.ds.total
            fw.barrier()

    def compute_rope(self, es0):
        nc, fw = self.nc, self.fw
        with ExitStack() as es:
            tl = fw.sb(es, "rp", [128, 8, 32], F32)
            itl = fw.sb(es, "rpi", [128, 32], I32)
            c = self.cst
            fr, u, r, fx, ang = (tl[:, i, :] for i in range(5))
            nidx = c[:, 516:517]
            act(fw, fr[:, 0:16], c[:, 517:533], AF.Exp, [c], [tl], scale=-math.log(10000.0) / 16.0)
            ts(fw, fw.dve, ang[:, 0:16], fr[:, 0:16], nidx, 1.0 / (2 * math.pi), ALU.mult, ALU.mult, [tl, c], [tl])
            ts(fw, fw.dve, u[:, 0:16], ang[:, 0:16], 0.25, None, ALU.add, None, [tl], [tl])
            cp(fw, fw.dve, u[:, 16:32], ang[:, 0:16], [tl], [tl])
            cp(fw, fw.dve, itl.ap, u, [tl], [itl])
            cp(fw, fw.dve, r, itl.ap, [itl], [tl])
            tt(fw, fw.dve, r, u, r, ALU.subtract, [tl], [tl])
            ts(fw, fw.dve, fx, r, 0.5, None, ALU.is_gt, None, [tl], [tl])
            tt(fw, fw.dve, r, r, fx, ALU.subtract, [tl], [tl])
            ts(fw, fw.dve, fx, r, -0.5, None, ALU.is_lt, None, [tl], [tl])
            tt(fw, fw.dve, r, r, fx, ALU.add, [tl], [tl])
            res = tl[:, 5, :]
            act(fw, res, r, AF.Sin, [tl], [tl], scale=2 * math.pi)
            fw.dma(fw.sp, self.ROPEd, tl[0:64, 5, :], reads=[tl], ds=tl.ds)
            self.ropebuf = Buf("rope")
            self.ropebuf.w = {tl.ds: tl.ds.total}
            fw.barrier()

    def load_bcast(self, dst_tl, dst_ap, src_row_ap, q=None, reads=()):
        fw = self.fw
        q = q or fw.sp
        n = src_row_ap.shape[-1]
        fw.dma(q, dst_ap, src_row_ap.to_broadcast([dst_ap.shape[0], n]), reads=list(reads), writes=[dst_tl], ds=dst_tl.ds, serialize=False)

    def phase_a(self, l):
        nc, fw = self.nc, self.fw
        ps = self.ps
        with ExitStack() as es:
            W = fw.sb(es, "Wa", [128, 8, W_COLS], BF16)
            wsrc = self.w_in[l].rearrange("(kc p) c -> p kc c", p=128)
            pieces = list(FM_PIECES)
            for name, pl_ in TM_GROUPS:
                o = TMW0 + TM_OFF[name]
                for (src, n) in pl_:
                    pieces.append((o, src, n))
                    o += n
            for (dst, src, n) in pieces:
                fw.dma(fw.pool, W[:, :, dst:dst + n], wsrc[:, :, src:src + n], writes=[W], ds=W.ds, serialize=False)
            BB = fw.sb(es, "BBa", [128, TM_COLS - 16], BF16)
            BG = fw.sb(es, "BGa", [128, 16], F32)
            for name, pl_ in TM_GROUPS:
                o = TM_OFF[name]
                for (src, n) in pl_:
                    row = self.b_in[l:l + 1, src:src + n]
                    if name == "MG":
                        self.load_bcast(BG, BG[:, o:o + n], row)
                    else:
                        self.load_bcast(BB, BB[:, o - 16:o - 16 + n], row, q=fw.pool)
                    o += n
            bfm = fw.sb(es, "bfm", [128, 8], F32)
            fw.dma(fw.sp, bfm.ap, self.b_in_fm[l], writes=[bfm], ds=bfm.ds)
            modt = fw.sb(es, "moda", [128, 2, D], F32)

            def load_mod(j):
                for ch in range(2):
                    self.load_bcast(modt, modt[:, ch, :], self.MOD[l, j:j + 1, ch * D:(ch + 1) * D], reads=[self.modbuf])
            load_mod(1)
            gn = fw.sb(es, "gna", [128, 2, 512], F32)
            self.load_bcast(gn, gn[:, 0, :], self.ret_gn[l:l + 1, :])
            self.load_bcast(gn, gn[:, 1, :], self.m_gn[l:l + 1, :])
            qk = fw.sb(es, "qka", [128, 2, 64], F32)
            self.load_bcast(qk, qk[:, 0, :], self.qn_g[l:l + 1, :])
            self.load_bcast(qk, qk[:, 1, :], self.kn_g[l:l + 1, :])
            ts(fw, fw.dve, qk[:, 0, :], qk[:, 0, :], 0.125, None, ALU.mult, None, [qk], [qk])
            colT = fw.sb(es, "colT", [128, 32], F32)
            rowT = fw.sb(es, "rowT", [128, 32, 32], F32)
            for hf in range(2):
                fw.dma(fw.sp, colT[hf * 64:(hf + 1) * 64, :], self.ROPEd, reads=[self.ropebuf], writes=[colT], ds=colT.ds, serialize=False)
                src = self.ROPEd.rearrange("(j two) c -> two j c", two=2)[hf:hf + 1]
                fw.dma(fw.sp, rowT[hf * 64:(hf + 1) * 64, :, :], src.to_broadcast([64, 32, 32]), reads=[self.ropebuf], writes=[rowT], ds=rowT.ds, serialize=False)
            dk = fw.sb(es, "dka", [128, 8, 8], F32)
            c = self.cst
            self.load_bcast(dk, dk[:, 0, :], self.decay[l:l + 1, :])
            act(fw, dk[:, 1, :], dk[:, 0, :], AF.Exp, [dk], [dk], scale=-1.0)
            act(fw, dk[:, 2, :], dk[:, 1, :], AF.Ln, [dk], [dk], bias=self.one_t[:, 0:1])
            rEA = dk[:, 3, :]
            rEB = dk[:, 4, :]
            rEE = dk[:, 5, :]
            act(fw, rEA[:, 0:4], dk[:, 2, 0:4], AF.Exp, [dk, c], [dk], scale=c[:, 512:513])
            act(fw, rEA[:, 4:8], dk[:, 2, 4:8], AF.Exp, [dk, c], [dk], scale=c[:, 513:514])
            act(fw, rEB[:, 0:4], dk[:, 2, 0:4], AF.Exp, [dk, c], [dk], scale=c[:, 514:515])
            act(fw, rEB[:, 4:8], dk[:, 2, 4:8], AF.Exp, [dk, c], [dk], scale=c[:, 515:516])
            act(fw, rEE, dk[:, 2, :], AF.Exp, [dk], [dk], scale=-128.0)
            retc = fw.sb(es, "retc", [128, 16], F32)
            cp(fw, fw.dve, retc[:, 0:8], rEB, [dk], [retc])
            for hf in range(2):
                cp(fw, fw.dve, retc[hf * 64:(hf + 1) * 64, 8:12].rearrange("p (d j) -> p d j", d=2),
                   rEE[hf * 64:(hf + 1) * 64, :].rearrange("p (d j two) -> p d j two", d=2, j=2)[:, :, :, hf], [dk], [retc])
            fw.dma(fw.sp, self.RETCd[l], retc.ap, reads=[retc], ds=retc.ds)

            xt = fw.sb(es, "xta", [128, D], F32)
            st6 = fw.sb(es, "st6a", [128, 2, 6], F32)
            mv = fw.sb(es, "mva", [128, 4], F32)
            xn = fw.sb(es, "xna", [128, D], F32)
            xm = fw.sb(es, "xma", [128, D], BF16)
            xmT = fw.ring(es, "xmTa", [128, 8, 128], BF16, 2)
            tmA = fw.sb(es, "tmAa", [128, R_SPLIT], BF16)
            tmB = fw.sb(es, "tmBa", [128, R_COLS - R_SPLIT], BF16)
            sm_ = fw.sb(es, "smra", [128, SM_COLS], F32)
            fm_ = fw.sb(es, "fmra", [128, 8, 128], BF16)
            aq_ = fw.sb(es, "aqra", [64, 8, 128], BF16)
            ak_ = fw.sb(es, "akra", [64, 2, 128], BF16)
            tmpA = fw.ring(es, "tmpAa", [128, 512], F32, 3)
            tmpB = fw.ring(es, "tmpBa", [128, 512], F32, 2)
            qb_ = fw.sb(es, "qba", [128, 640], BF16)
            g_ = fw.sb(es, "gtsa", [128, 64], F32)
            sl_ = fw.sb(es, "smla", [128, 32], F32)
            fw.op(fw.dve, lambda e: e.memset(tmA[:, R_AV:R_AV + 130].rearrange("p (g c) -> p g c", g=2)[:, :, 64:65], 1.0), [], [tmA])

            ps_tr, ps_fm, ps_sm, ps_aq = ps[0], ps[1], ps[2], ps[3]
            ps_tm = ps[4:8]
            self._tmk = 0

            def s1(t):
                if t == NCT:
                    load_mod(0)
                fw.dma(fw.sp, xt.ap, self.X[t * 128:(t + 1) * 128, :], reads=[self.xbuf], writes=[xt], ds=xt.ds)
                for hh in range(2):
                    fw.op(fw.dve, lambda e, hh=hh: e.bn_stats(out=st6[:, hh, :], in_=xt[:, hh * 512:(hh + 1) * 512]), [xt], [st6])
                fw.op(fw.dve, lambda e: e.bn_aggr(out=mv[:, 0:2], in_=st6.ap.rearrange("p a b -> p (a b)")), [st6], [mv])
                rstd_from(fw, mv[:, 2:3], mv[:, 1:2], [mv], [mv])
                ts(fw, fw.dve, xn.ap, xt.ap, mv[:, 0:1], mv[:, 2:3], ALU.subtract, ALU.mult, [xt, mv], [xn])
                tt(fw, fw.pool, xn.ap, xn.ap, modt[:, 1, :], ALU.mult, [xn, modt], [xn])
                tt(fw, fw.dve, xm.ap, xn.ap, modt[:, 0, :], ALU.add, [xn, modt], [xm])

            def s2(t):
                xT_ = xmT[t % 2]
                pb = ps_tr.ap.bitcast(BF16).rearrange("p (a b) -> p a b", a=8)
                for kc in range(8):
                    tr(fw, pb[:, kc, :], xm[:, kc * 128:(kc + 1) * 128], self.identb.ap, [xm, self.identb], [ps_tr], inc=(kc == 7))
                cp(fw, fw.act, xT_.ap.rearrange("p a b -> p (a b)"), ps_tr.ap.bitcast(BF16), [ps_tr], [xT_])

            def tm_matmul(t, name, n):
                xT_ = xmT[t % 2]
                pst = ps_tm[self._tmk % len(ps_tm)]
                self._tmk += 1
                o = TMW0 + TM_OFF[name]
                for kc in range(8):
                    mm(fw, pst[:, 0:n], xT_[:, kc, :], W[:, kc, o:o + n], kc == 0, kc == 7, [xT_, W], [pst])
                return pst

            def bias_of(name, n, off=0):
                o = TM_OFF[name] - 16 + off
                return BB[:, o:o + n]

            def s3(t):
                xT_ = xmT[t % 2]
                for half in range(2):
                    for i4 in range(4):
                        i = half * 4 + i4
                        for kc in range(8):
                            mm(fw, ps_fm[:, i4 * 128:(i4 + 1) * 128], W[:, kc, i * 128:(i + 1) * 128], xT_[:, kc, :], kc == 0, kc == 7, [xT_, W], [ps_fm],
                               inc=(kc == 7 and i4 == 3))
                    for i4 in range(4):
                        i = half * 4 + i4
                        sc_ = 0.125 if i in (2, 3, 6, 7) else 1.0
                        ts(fw, fw.dve, fm_[:, i, :], ps_fm[:, i4 * 128:(i4 + 1) * 128], bfm[:, i:i + 1], sc_, ALU.add, ALU.mult, [ps_fm, bfm], [fm_])
                fw.dma(fw.sp, self.FMd[t], fm_.ap.rearrange("p a b -> p (a b)"), reads=[fm_], ds=fm_.ds)
                if lim is not None and len(lim) > 2 and lim[2] <= 1:
                    return
                pbk = ps_sm.ap.bitcast(BF16)
                for n_, i in enumerate((2, 3, 6, 7)):
                    tr(fw, pbk[:, 512 + n_ * 128:512 + (n_ + 1) * 128], fm_[:, i, :], self.identb.ap, [fm_, self.identb], [ps_sm], inc=(n_ == 3))
                cp(fw, fw.act, tmA[:, R_RK:R_RK + 512], pbk[:, 512:1024], [ps_sm], [tmA])
                if lim is not None and len(lim) > 2 and lim[2] <= 2:
                    return
                pst = tm_matmul(t, "MG", 16)
                tt(fw, fw.dve, g_[:, 0:16], pst[:, 0:16], BG.ap, ALU.add, [pst, BG], [g_])
                e_ = g_[:, 16:24]
                sp_ = g_[:, 24:32]
                act(fw, e_, g_[:, 8:16], AF.Exp, [g_], [g_], scale=-1.0)
                act(fw, sp_, e_, AF.Ln, [g_], [g_], bias=self.one_t[:, 0:1])
                mm(fw, ps_sm[:, 0:4], self.triF, sp_[:, 0:4], True, True, [g_, self.cst], [ps_sm], inc=False)
                mm(fw, ps_sm[:, 4:8], self.triB, sp_[:, 4:8], True, True, [g_, self.cst], [ps_sm], inc=False)
                mm(fw, ps_sm[:, 8:16], self.ones_f, sp_, True, True, [g_, self.cst], [ps_sm], inc=True)
                ta = g_[:, 32:40]
                EA = g_[:, 40:48]
                tt(fw, fw.dve, ta, g_[:, 0:8], ps_sm[:, 0:8], ALU.add, [g_, ps_sm], [g_])
                act(fw, EA, ta, AF.Exp, [g_], [g_])
                act(fw, sm_[:, 0:8], ps_sm[:, 0:8], AF.Exp, [ps_sm], [sm_], scale=-1.0)
                ebe = g_[:, 48:56]
                act(fw, ebe, ps_sm[:, 8:16], AF.Exp, [ps_sm], [g_], scale=-1.0)
                for hf in range(2):
                    cp(fw, fw.dve, sm_[hf * 64:(hf + 1) * 64, 8:12].rearrange("p (d j) -> p d j", d=2),
                       ebe[hf * 64:(hf + 1) * 64, :].rearrange("p (d j two) -> p d j two", d=2, j=2)[:, :, :, hf], [g_], [sm_])
                fw.dma(fw.sp, self.SMd[t], sm_.ap, reads=[sm_], ds=sm_.ds)
                if lim is not None and len(lim) > 2 and lim[2] <= 3:
                    return
                pst = tm_matmul(t, "RV", 512)
                v_ = tmpA[0]
                tt(fw, fw.dve, v_.ap, pst.ap, bias_of("RV", 512), ALU.add, [pst, BB], [v_])
                for d in range(2):
                    eng = fw.dve if d == 0 else fw.pool
                    tt(fw, eng, tmA[:, R_RV + d * 512:R_RV + (d + 1) * 512].rearrange("p (h e) -> p h e", h=4),
                       v_.ap.rearrange("p (h e) -> p h e", h=4), rEA[:, d * 4:(d + 1) * 4].unsqueeze(2).to_broadcast([128, 4, 128]), ALU.mult, [v_, dk], [tmA])
                if lim is not None and len(lim) > 2 and lim[2] <= 4:
                    return
                pst = tm_matmul(t, "RG", 512)
                a_, b_ = tmpA[1], tmpB[0]
                tt(fw, fw.dve, a_.ap, pst.ap, bias_of("RG", 512), ALU.add, [pst, BB], [a_])
                act(fw, b_.ap, a_.ap, AF.Silu, [a_], [b_])
                tt(fw, fw.pool, tmA[:, R_RG:R_RG + 512], b_.ap, gn[:, 0, :], ALU.mult, [b_, gn], [tmA])
                if lim is not None and len(lim) > 2 and lim[2] <= 5:
                    return
                q_ = tmpA[2]
                pst = tm_matmul(t, "AQ", 512)
                tt(fw, fw.dve, q_.ap, pst.ap, bias_of("AQ", 512), ALU.add, [pst, BB], [q_])
                self.norm_rope(t, q_, q_.ap, 8, qk[:, 0, :], qb_, qb_[:, 0:512], sl_, sl_[:, 0:8], tmpB[1], colT, rowT, qk)
                if lim is not None and len(lim) > 2 and lim[2] <= 6:
                    return
                pst = tm_matmul(t, "AKV", 256)
                k_ = tmpA[0]
                tt(fw, fw.dve, k_[:, 0:128], pst[:, 0:128], bias_of("AKV", 128), ALU.add, [pst, BB], [k_])
                self.norm_rope(t, k_, k_[:, 0:128], 2, qk[:, 1, :], qb_, qb_[:, 512:640], sl_, sl_[:, 8:10], tmpB[1], colT, rowT, qk)
                tt(fw, fw.dve, tmA[:, R_AV:R_AV + 130].rearrange("p (g c) -> p g c", g=2)[:, :, 0:64],
                   pst[:, 128:256].rearrange("p (g c) -> p g c", g=2), bias_of("AKV", 128, 128).rearrange("p (g c) -> p g c", g=2), ALU.add, [pst, BB], [tmA])
                if lim is not None and len(lim) > 2 and lim[2] <= 7:
                    return
                pbq = ps_aq.ap.bitcast(BF16)
                for h in range(8):
                    tr(fw, pbq[0:64, h * 128:(h + 1) * 128], qb_[:, h * 64:(h + 1) * 64], self.identb.ap, [qb_, self.identb], [ps_aq], inc=(h == 7))
                for h in range(2):
                    tr(fw, pbk[0:64, 256 + h * 128:256 + (h + 1) * 128], qb_[:, 512 + h * 64:512 + (h + 1) * 64], self.identb.ap, [qb_, self.identb], [ps_sm], inc=(h == 1))
                cp(fw, fw.act, aq_.ap.rearrange("p a b -> p (a b)"), pbq[0:64, :], [ps_aq], [aq_])
                cp(fw, fw.act, ak_.ap.rearrange("p a b -> p (a b)"), pbk[0:64, 256:512], [ps_sm], [ak_])
                fw.dma(fw.sp, self.AQd[t], aq_.ap.rearrange("p a b -> p (a b)"), reads=[aq_], ds=aq_.ds)
                fw.dma(fw.sp, self.AKd[:, :, t * 128:(t + 1) * 128], ak_.ap, reads=[ak_], ds=ak_.ds)
                if lim is not None and len(lim) > 2 and lim[2] <= 8:
                    return
                pst = tm_matmul(t, "MV", 512)
                v_ = tmpA[1]
                tt(fw, fw.dve, v_.ap, pst.ap, bias_of("MV", 512), ALU.add, [pst, BB], [v_])
                for d in range(2):
                    eng = fw.dve if d == 0 else fw.pool
                    dst = tmA[:, R_MV + d * 516:R_MV + (d + 1) * 516].rearrange("p (h e) -> p h e", h=4)
                    tt(fw, eng, dst[:, :, 0:128], v_.ap.rearrange("p (h e) -> p h e", h=4),
                       EA[:, d * 4:(d + 1) * 4].unsqueeze(2).to_broadcast([128, 4, 128]), ALU.mult, [v_, g_], [tmA])
                    cp(fw, eng, dst[:, :, 128:129], EA[:, d * 4:(d + 1) * 4].unsqueeze(2), [g_], [tmA])
                if lim is not None and len(lim) > 2 and lim[2] <= 9:
                    return
                pst = tm_matmul(t, "MO", 512)
                a_, b_ = tmpA[2], tmpB[0]
                tt(fw, fw.dve, a_.ap, pst.ap, bias_of("MO", 512), ALU.add, [pst, BB], [a_])
                act(fw, b_.ap, a_.ap, AF.Sigmoid, [a_], [b_])
                tt(fw, fw.pool, tmA[:, R_MO:R_MO + 512], b_.ap, gn[:, 1, :], ALU.mult, [b_, gn], [tmA])
                fw.dma(fw.sp, self.TMd[t][:, 0:R_SPLIT], tmA.ap, reads=[tmA], ds=tmA.ds)
                if lim is not None and len(lim) > 2 and lim[2] <= 10:
                    return
                for i in range(6):
                    pst = tm_matmul(t, "G%d" % i, 512)
                    a_ = tmpA[i % 3]
                    tt(fw, fw.dve, a_.ap, pst.ap, bias_of("G%d" % i, 512), ALU.add, [pst, BB], [a_])
                    act(fw, tmB[:, i * 512:(i + 1) * 512], a_.ap, AF.Sigmoid, [a_], [tmB])
                fw.dma(fw.sp, self.TMd[t][:, R_SPLIT:R_COLS], tmB.ap, reads=[tmB], ds=tmB.ds)

            lim = getattr(self, "a_lim", None)
            if lim == "pre":
                fw.barrier()
                return
            nt = NT if lim is None else lim[0]
            s1(0)
            s2(0)
            for t in range(nt):
                if t + 1 < nt:
                    s1(t + 1)
                    s2(t + 1)
                if lim is None or lim[1] >= 3:
                    s3(t)
            fw.barrier()

    def phase_b(self, l):
        nc, fw = self.nc, self.fw
        ps = self.ps
        last = (l == DEPTH - 1)
        with ExitStack() as es:
            WB = fw.sb(es, "WBb", [128, 3, 4, D], BF16)
            for b in range(3):
                fw.dma(fw.pool, WB[:, b, :, :], self.w_br[l, b].rearrange("(kc p) c -> p kc c", p=128), writes=[WB], ds=WB.ds, serialize=False)
            WO = fw.sb(es, "WOb", [128, 8, D], BF16)
            wo_src = self.w_out[l].rearrange("(kc p) c -> p kc c", p=128)
            for hh in range(2):
                fw.dma(fw.pool, WO[:, hh * 4:(hh + 1) * 4, :], wo_src[:, hh * 4:(hh + 1) * 4, :], writes=[WO], ds=WO.ds, serialize=False)
            AKT = fw.sb(es, "AKTb", [64, 2, T], BF16)
            fw.dma(fw.sp, AKT.ap, self.AKd, writes=[AKT], ds=AKT.ds)
            AVa = fw.sb(es, "AVab", [128, NT, 130], BF16)
            for q in range(0, NT, 8):
                q1 = min(NT, q + 8)
                fw.dma(fw.sp, AVa[:, q:q1, :], self.TMd[q:q1, :, R_AV:R_AV + 130].rearrange("t p c -> p t c"), writes=[AVa], ds=AVa.ds, serialize=False)
            retc = fw.sb(es, "retcb", [128, 16], F32)
            fw.dma(fw.sp, retc.ap, self.RETCd[l], writes=[retc], ds=retc.ds)
            gms = fw.sb(es, "gmsb", [128, D], F32)
            ln1t = fw.sb(es, "ln1tb", [128, 2, D], F32)
            for i in range(2):
                self.load_bcast(ln1t, ln1t[:, i, :], self.ln1[l, i:i + 1, :])

            def load_gms(j):
                self.load_bcast(gms, gms.ap, self.MOD[l, j:j + 1, 2 * D:3 * D], reads=[self.modbuf])
            load_gms(1)
            orders = {0: list(range(NT)), 1: [1, 0] + list(range(NT - 1, 1, -1))}
            SXd = (self.SFd, self.SBd)

            with ExitStack() as es2:
                S = [fw.sb(es2, "Sst%d" % d, [128, 4, 258], F32) for d in range(2)]
                for d in range(2):
                    fw.op(fw.dve, lambda e, d=d: e.memset(S[d].ap, 0.0), [], [S[d]])
                Sbf = [fw.ring(es2, "Sbf%d" % d, [128, 4, 258], BF16, 2) for d in range(2)]
                ldr = [fw.ring(es2, "ldr%d" % d, [128, 1540], BF16, 3) for d in range(2)]
                smr = [fw.ring(es2, "smr%d" % d, [128, SM_COLS], F32, 3) for d in range(2)]
                for i in range(NT):
                    for d in range(2):
                        t = orders[d][i]
                        L_ = ldr[d][i % 3]
                        sm_ = smr[d][i % 3]
                        fw.dma(fw.sp, L_[:, 0:512], self.TMd[t][:, R_RK:R_RK + 512], writes=[L_], ds=L_.ds)
                        fw.dma(fw.sp, L_[:, 512:1024], self.TMd[t][:, R_RV + d * 512:R_RV + (d + 1) * 512], writes=[L_], ds=L_.ds, serialize=False)
                        fw.dma(fw.sp, L_[:, 1024:1540], self.TMd[t][:, R_MV + d * 516:R_MV + (d + 1) * 516], writes=[L_], ds=L_.ds, serialize=False)
                        fw.dma(fw.sp, sm_.ap, self.SMd[t], writes=[sm_], ds=sm_.ds)
                        sb_ = Sbf[d][i % 2]
                        cp(fw, fw.act, sb_.ap, S[d].ap, [S[d]], [sb_])
                        fw.dma(fw.sp, SXd[d][t], sb_.ap.rearrange("p a b -> p (a b)"), reads=[sb_], ds=sb_.ds)
                        for mxj in range(4):
                            mx, j = mxj // 2, mxj % 2
                            W_ = 128 if mx == 0 else 129
                            pst = ps[d * 4 + mxj]
                            K_ = L_[:, mx * 256 + j * 128:mx * 256 + (j + 1) * 128]
                            for blk in range(2):
                                h = 2 * j + blk
                                V_ = L_[:, 512 + h * 128:512 + (h + 1) * 128] if mx == 0 else L_[:, 1024 + h * 129:1024 + (h + 1) * 129]
                                mm(fw, pst[:, blk * 129:blk * 129 + W_], K_, V_, True, True, [L_], [pst], inc=(blk == 1))
                            e_ = retc[:, 8 + d * 2 + j:9 + d * 2 + j] if mx == 0 else sm_[:, 8 + d * 2 + j:9 + d * 2 + j]
                            e_src = retc if mx == 0 else sm_
                            Sv = S[d][:, mxj, :].rearrange("p (b w) -> p b w", b=2)[:, :, 0:W_]
                            Pv = pst[:, 0:258].rearrange("p (b w) -> p b w", b=2)[:, :, 0:W_]
                            act(fw, Sv, Sv, AF.Identity, [S[d], e_src], [S[d]], scale=e_)
                            stt(fw, fw.dve, Sv, Pv, e_, Sv, ALU.mult, ALU.add, [pst, e_src, S[d]], [S[d]])
                fw.barrier()
            self.sxbuf = Buf("SX")

            tmA = fw.ring(es, "tmAb", [128, R_SPLIT], BF16, 2)
            tmG = fw.ring(es, "tmGb", [128, R_COLS - R_SPLIT], BF16, 3)
            smr = fw.ring(es, "smrb", [128, SM_COLS], F32, 2)
            fmr = fw.ring(es, "fmrb", [128, 8, 128], BF16, 2)
            aqr = fw.ring(es, "aqrb", [64, 8, 128], BF16, 2)
            sfr = [fw.ring(es, "sxr%d" % d, [128, 4, 258], BF16, 2) for d in range(2)]
            xr = fw.ring(es, "xrb", [128, D], F32, 2)
            PT = fw.ring(es, "PTb", [128, 2, 512], BF16, 2)
            pTr = fw.ring(es, "pTb", [128, 512], BF16, 3)
            yf = fw.ring(es, "yfb", [128, 512], F32, 4)
            sml = fw.ring(es, "smlb", [128, 64], F32, 2)
            st4 = fw.sb(es, "st4b", [128, 4, 6], F32)
            ymix = fw.ring(es, "ymixb", [128, 3, 512], BF16, 2)
            yT = fw.ring(es, "yTb", [128, 12, 128], BF16, 2)
            zt = fw.ring(es, "ztb", [128, D], F32, 2)
            zb = fw.sb(es, "zbb", [128, D], BF16)
            zT = fw.sb(es, "zTb", [128, 8, 128], BF16)
            rr = fw.ring(es, "rrb", [128, D], F32, 2)
            st6 = fw.sb(es, "st6b", [128, 2, 6], F32)
            mv = fw.sb(es, "mvb", [128, 4], F32)
            ps_s, ps_of, ps_ob, ps_sc, ps_acc = ps[0], (ps[1], ps[2]), (ps[3], ps[4]), (ps[5], ps[6]), ps[7]
            self._yk = 0

            def nexty():
                self._yk += 1
                return yf[self._yk % 4]

            def loads(t):
                fw.dma(fw.sp, tmA[t % 2].ap, self.TMd[t][:, 0:R_SPLIT], writes=[tmA[t % 2]], ds=tmA[t % 2].ds)
                fw.dma(fw.sp, tmG[t % 3].ap, self.TMd[t][:, R_SPLIT:R_COLS], writes=[tmG[t % 3]], ds=tmG[t % 3].ds)
                fw.dma(fw.sp, smr[t % 2].ap, self.SMd[t], writes=[smr[t % 2]], ds=smr[t % 2].ds)
                fw.dma(fw.sp, fmr[t % 2].ap.rearrange("p a b -> p (a b)"), self.FMd[t], writes=[fmr[t % 2]], ds=fmr[t % 2].ds)
                fw.dma(fw.sp, aqr[t % 2].ap.rearrange("p a b -> p (a b)"), self.AQd[t], writes=[aqr[t % 2]], ds=aqr[t % 2].ds)
                for d in range(2):
                    fw.dma(fw.sp, sfr[d][t % 2].ap.rearrange("p a b -> p (a b)"), SXd[d][t], writes=[sfr[d][t % 2]], ds=sfr[d][t % 2].ds)

            def par(ap2, hp, n=2):
                return ap2.rearrange("p (j two k) -> p j two k", j=2, two=2)[:, :, hp, :]

            def linattn(t, mx):
                ta_, sm_, fm_ = tmA[t % 2], smr[t % 2], fmr[t % 2]
                W_ = 128 if mx == 0 else 129
                qi, ki = (0, 2) if mx == 0 else (4, 6)
                pt_ = PT[(2 * t + mx) % 2]
                for h in range(4):
                    j, hf = h // 2, h % 2
                    mm(fw, ps_sc[hf][:, j * 128:(j + 1) * 128], fm_[hf * 64:(hf + 1) * 64, ki + j, :], fm_[hf * 64:(hf + 1) * 64, qi + j, :], True, True, [fm_], [ps_sc[hf]], inc=(h >= 2))
                for d, msk in ((0, self.maskF), (1, self.maskB)):
                    for hp in range(2):
                        tt(fw, fw.dve, par(pt_[:, d, :], hp), ps_sc[hp][:, 0:256].rearrange("p (j k) -> p j k", j=2), msk[:, 0:2, :], ALU.mult, [ps_sc[hp], msk], [pt_])
                for d in range(2):
                    banks = ps_of if d == 0 else ps_ob
                    S_ = sfr[d][t % 2]
                    for h in (0, 2, 1, 3):
                        j, hf = h // 2, h % 2
                        bank = banks[hf]
                        if mx == 0:
                            V_ = ta_[:, R_RV + d * 512 + h * 128:R_RV + d * 512 + (h + 1) * 128]
                        else:
                            V_ = ta_[:, R_MV + d * 516 + h * 129:R_MV + d * 516 + (h + 1) * 129]
                        out = bank[:, j * 129:j * 129 + W_]
                        mm(fw, out, pt_[:, d, h * 128:(h + 1) * 128], V_, (j == 0), False, [pt_, ta_], [bank], inc=False, skip_group_check=True)
                        Sv = S_[hf * 64:(hf + 1) * 64, mx * 2 + j, hf * 129:hf * 129 + W_]
                        mm(fw, out, fm_[hf * 64:(hf + 1) * 64, qi + j, :], Sv, False, True, [fm_, S_], [bank], inc=(j == 1), skip_group_check=True)
                y = nexty()
                sl = sml[t % 2]
                if mx == 0:
                    t1 = nexty()
                    for hp in range(2):
                        o_f = ps_of[hp][:, 0:258].rearrange("p (j w) -> p j w", j=2)[:, :, 0:128]
                        o_b = ps_ob[hp][:, 0:258].rearrange("p (j w) -> p j w", j=2)[:, :, 0:128]
                        ebf = par(retc[:, 0:4], hp).to_broadcast([128, 2, 128])
                        ebb = par(retc[:, 4:8], hp).to_broadcast([128, 2, 128])
                        tt(fw, fw.dve, par(t1.ap, hp), o_f, ebf, ALU.mult, [ps_of[hp], retc], [t1])
                        tt(fw, fw.dve, par(y.ap, hp), o_b, ebb, ALU.mult, [ps_ob[hp], retc], [y])
                    tt(fw, fw.pool, y.ap, y.ap, t1.ap, ALU.add, [y, t1], [y])
                else:
                    hd = []
                    for d in range(2):
                        banks = ps_of if d == 0 else ps_ob
                        q1 = sl[:, d * 16:d * 16 + 4]
                        q2 = sl[:, d * 16 + 4:d * 16 + 8]
                        r_ = sl[:, d * 16 + 8:d * 16 + 12]
                        eb = sm_[:, d * 4:(d + 1) * 4]
                        for hp in range(2):
                            den = banks[hp][:, 0:258].rearrange("p (j w) -> p j w", j=2)[:, :, 128:129]
                            tt(fw, fw.dve, par(q1, hp), den, par(eb, hp), ALU.mult, [banks[hp], sm_], [sl])
                        stt(fw, fw.dve, q2, q1, -1.0, q1, ALU.mult, ALU.max, [sl], [sl])
                        ts(fw, fw.dve, q2, q2, 1.0, None, ALU.max, None, [sl], [sl])
                        fw.op(fw.dve, lambda e, q2=q2: e.reciprocal(out=q2, in_=q2), [sl], [sl])
                        tt(fw, fw.dve, r_, q2, eb, ALU.mult, [sl, sm_], [sl])
                        hdt = y if d == 0 else nexty()
                        for hp in range(2):
                            num = banks[hp][:, 0:258].rearrange("p (j w) -> p j w", j=2)[:, :, 0:128]
                            tt(fw, fw.dve, par(hdt.ap, hp), num, par(r_, hp).to_broadcast([128, 2, 128]), ALU.mult, [banks[hp], sl], [hdt])
                        hd.append(hdt)
                    tt(fw, fw.pool, y.ap, hd[0].ap, hd[1].ap, ALU.add, [hd[0], hd[1]], [y])
                y3 = y.ap.rearrange("p (h e) -> p h e", h=4)
                for h in range(4):
                    fw.op(fw.dve, lambda e, h=h: e.bn_stats(out=st4[:, h, :], in_=y3[:, h, :]), [y], [st4])
                mvh = sl[:, 32:40].rearrange("p (h two) -> p h two", h=4)
                for h in range(4):
                    fw.op(fw.dve, lambda e, h=h: e.bn_aggr(out=mvh[:, h, :], in_=st4[:, h, :]), [st4], [sl])
                rs = sl[:, 40:44]
                act(fw, rs, mvh[:, :, 1], AF.Ln, [sl], [sl], bias=self.eps_t[:, 0:1])
                act(fw, rs, rs, AF.Exp, [sl], [sl], scale=-0.5)
                for h in range(4):
                    ts(fw, fw.dve, y3[:, h, :], y3[:, h, :], mvh[:, h, 0:1], rs[:, h:h + 1], ALU.subtract, ALU.mult, [y, sl], [y])
                gcol = R_RG if mx == 0 else R_MO
                ym = ymix[t % 2]
                tt(fw, fw.pool, ym[:, 0 if mx == 0 else 2, :], y.ap, ta_[:, gcol:gcol + 512], ALU.mult, [y, ta_], [ym])

            def attention(t):
                aq_ = aqr[t % 2]
                ym = ymix[t % 2]
                kts = list(range(NCT)) if t < NCT else list(range(NT))
                k = 0
                for g in range(2):
                    for n_, kt in enumerate(kts):
                        psc = ps_sc[k % 2]
                        p_ = pTr[k % 3]
                        k += 1
                        mm(fw, psc.ap, AKT[:, g, kt * 128:(kt + 1) * 128], aq_[:, g * 4:(g + 1) * 4, :].rearrange("p a b -> p (a b)"), True, True, [AKT, aq_], [psc])
                        act(fw, p_.ap, psc.ap, AF.Exp, [psc], [p_])
                        for r in range(4):
                            mm(fw, ps_acc[:, r * 65:(r + 1) * 65], p_[:, r * 128:(r + 1) * 128], AVa[:, kt, g * 65:(g + 1) * 65],
                               (n_ == 0 and r == 0), (n_ == len(kts) - 1), [p_, AVa], [ps_acc], inc=(n_ == len(kts) - 1 and r == 3), skip_group_check=True)
                    sl = sml[t % 2]
                    rd = sl[:, 48 + g * 4:52 + g * 4]
                    acc3 = ps_acc[:, 0:260].rearrange("p (r c) -> p r c", r=4)
                    fw.op(fw.dve, lambda e, rd=rd, acc3=acc3: e.reciprocal(out=rd, in_=acc3[:, :, 64]), [ps_acc], [sl])
                    tt(fw, fw.dve, ym[:, 1, g * 256:(g + 1) * 256].rearrange("p (r c) -> p r c", r=4), acc3[:, :, 0:64],
                       rd.unsqueeze(2).to_broadcast([128, 4, 64]), ALU.mult, [ps_acc, sl], [ym])

            def merge(t):
                if t == NCT:
                    load_gms(0)
                ym, yT_, tg_, x_ = ymix[t % 2], yT[t % 2], tmG[t % 3], xr[t % 2]
                if "YDd" in self.debug:
                    fw.dma(fw.sp, self.YDd[t], ym.ap.rearrange("p a b -> p (a b)"), reads=[ym], ds=ym.ds)
                fw.dma(fw.sp, x_.ap, self.X[t * 128:(t + 1) * 128, :], reads=[self.xbuf], writes=[x_], ds=x_.ds)
                pb = ps_s.ap.bitcast(BF16)
                for grp in range(2):
                    n = 8 if grp == 0 else 4
                    for i in range(n):
                        ii = grp * 8 + i
                        tr(fw, pb[:, i * 128:(i + 1) * 128], ym[:, ii // 4, (ii % 4) * 128:(ii % 4 + 1) * 128], self.identb.ap, [ym, self.identb], [ps_s], inc=(i == n - 1))
                    cp(fw, fw.act, yT_[:, grp * 8:grp * 8 + n, :].rearrange("p a b -> p (a b)"), pb[:, 0:n * 128], [ps_s], [yT_])
                zsum = zt[0]
                for b in range(3):
                    banks = ps_of if b % 2 == 0 else ps_ob
                    for n in range(2):
                        for kc in range(4):
                            mm(fw, banks[n].ap, yT_[:, b * 4 + kc, :], WB[:, b, kc, n * 512:(n + 1) * 512], kc == 0, kc == 3, [yT_, WB], [banks[n]])
                    dst = zsum if b == 0 else zt[1]
                    for n in range(2):
                        tt(fw, fw.dve, dst[:, n * 512:(n + 1) * 512], banks[n].ap, tg_[:, b * D + n * 512:b * D + (n + 1) * 512], ALU.mult, [banks[n], tg_], [dst])
                    if b == 1:
                        tt(fw, fw.pool, zsum.ap, zsum.ap, dst.ap, ALU.add, [zsum, dst], [zsum])
                    if b == 2:
                        tt(fw, fw.pool, zb.ap, zsum.ap, dst.ap, ALU.add, [zsum, dst], [zb])
                pb8 = pb.rearrange("p (a b) -> p a b", a=8)
                for kc in range(8):
                    tr(fw, pb8[:, kc, :], zb[:, kc * 128:(kc + 1) * 128], self.identb.ap, [zb, self.identb], [ps_s], inc=(kc == 7))
                cp(fw, fw.act, zT.ap.rearrange("p a b -> p (a b)"), pb, [ps_s], [zT])
                for n in range(2):
                    for kc in range(8):
                        mm(fw, ps_ob[n].ap, zT[:, kc, :], WO[:, kc, n * 512:(n + 1) * 512], kc == 0, kc == 7, [zT, WO], [ps_ob[n]])
                r_ = rr[0]
                for n in range(2):
                    tt(fw, fw.dve, r_[:, n * 512:(n + 1) * 512], ps_ob[n].ap, gms[:, n * 512:(n + 1) * 512], ALU.mult, [ps_ob[n], gms], [r_])
                stt(fw, fw.dve, r_.ap, x_.ap, ALPHA, r_.ap, ALU.mult, ALU.add, [x_, r_], [r_])
                self.ln_affine(r_, rr[1], st6, mv, ln1t)
                fw.dma(fw.sp, self.X1[t * 128:(t + 1) * 128, :], rr[1].ap, reads=[rr[1]], ds=rr[1].ds)

            t0 = NCT if last else 0
            loads(t0)
            for t in range(t0, NT):
                if t + 1 < NT:
                    loads(t + 1)
                linattn(t, 0)
                linattn(t, 1)
                attention(t)
                if t - 1 >= t0:
                    merge(t - 1)
            merge(NT - 1)
            self.x1buf = Buf("X1")
            fw.barrier()

    def phase_d(self, l):
        nc, fw = self.nc, self.fw
        ps = self.ps
        last = (l == DEPTH - 1)
        with ExitStack() as es:
            import os
            dsk = os.environ.get("D_SKIP", "").split(",")
            WU = fw.sb(es, "WUd", [128, 8, 2 * DFF], BF16)
            wu_src = self.w_up[l].rearrange("(kc p) c -> p kc c", p=128)
            for c0 in range(0, 2 * DFF if "wu" not in dsk else 0, 512):
                fw.dma(fw.pool, WU[:, :, c0:c0 + 512], wu_src[:, :, c0:c0 + 512], writes=[WU], ds=WU.ds, serialize=False)
            WD = fw.sb(es, "WDd", [128, NF, D], BF16)
            wd_src = self.w_down[l].rearrange("(f p) c -> p f c", p=128)
            for f0 in range(0, NF if "wd" not in dsk else 0, 4):
                f1 = min(NF, f0 + 4)
                fw.dma(fw.pool, WD[:, f0:f1, :], wd_src[:, f0:f1, :], writes=[WD], ds=WD.ds, serialize=False)
            cvp = fw.sb(es, "cvpd", [128, 4, 44], F32)
            if "cvp" not in dsk:
                fw.dma(fw.sp, cvp.ap, self.convp[l], writes=[cvp], ds=cvp.ds)
            modt = fw.sb(es, "modd", [128, 2, D], F32)
            ln2t = fw.sb(es, "ln2td", [128, 2, D], F32)
            for i in range(2):
                self.load_bcast(ln2t, ln2t[:, i, :], self.ln2[l, i:i + 1, :])

            def load_mod(j):
                for i in range(2):
                    self.load_bcast(modt, modt[:, i, :], self.MOD[l, j:j + 1, (3 + i) * D:(4 + i) * D], reads=[self.modbuf])

            def load_gate(j):
                self.load_bcast(gmlp, gmlp.ap, self.MOD[l, j:j + 1, 5 * D:6 * D], reads=[self.modbuf])
            gmlp = fw.sb(es, "gmlpd", [128, D], F32)
            load_mod(1)
            load_gate(1)
            x1r = fw.ring(es, "x1rd", [128, D], F32, 2)
            xn = fw.sb(es, "xnd", [128, D], F32)
            hb = fw.sb(es, "hbd", [128, D], BF16)
            HTB = fw.ring(es, "HTBd", [128, 8, 132], BF16, 3)
            cvt = fw.ring(es, "cvtd", [128, 2, 128], F32, 3)
            sg = fw.ring(es, "sgd", [128, 128], F32, 2)
            actT = fw.ring(es, "actTd", [128, NF, 128], BF16, 2)
            rr = fw.ring(es, "rrd", [128, D], F32, 2)
            st6 = fw.sb(es, "st6d", [128, 2, 6], F32)
            mv = fw.sb(es, "mvd", [128, 4], F32)
            ps_tr = ps[0]
            ps_up = ps[1:5]
            ps_dn = (ps[5], ps[6])
            t0 = NCT if last else 0

            def seq_first(t):
                return t == 0 or t == NCT

            def seq_last(t):
                return t == NCT - 1 or t == NT - 1

            def f1(t):
                if t == NCT:
                    load_mod(0)
                x_ = x1r[t % 2]
                fw.dma(fw.sp, x_.ap, self.X1[t * 128:(t + 1) * 128, :], reads=[self.x1buf], writes=[x_], ds=x_.ds)
                for hh in range(2):
                    fw.op(fw.dve, lambda e, hh=hh: e.bn_stats(out=st6[:, hh, :], in_=x_[:, hh * 512:(hh + 1) * 512]), [x_], [st6])
                fw.op(fw.dve, lambda e: e.bn_aggr(out=mv[:, 0:2], in_=st6.ap.rearrange("p a b -> p (a b)")), [st6], [mv])
                rstd_from(fw, mv[:, 2:3], mv[:, 1:2], [mv], [mv])
                ts(fw, fw.dve, xn.ap, x_.ap, mv[:, 0:1], mv[:, 2:3], ALU.subtract, ALU.mult, [x_, mv], [xn])
                tt(fw, fw.pool, xn.ap, xn.ap, modt[:, 1, :], ALU.mult, [xn, modt], [xn])
                tt(fw, fw.dve, hb.ap, xn.ap, modt[:, 0, :], ALU.add, [xn, modt], [hb])
                pb = ps_tr.ap.bitcast(BF16).rearrange("p (a b) -> p a b", a=8)
                for kc in range(8):
                    tr(fw, pb[:, kc, :], hb[:, kc * 128:(kc + 1) * 128], self.identb.ap, [hb, self.identb], [ps_tr], inc=(kc == 7))
                H_ = HTB[t % 3]
                if "cpH" in dsk:
                    return
                cp(fw, fw.act, H_[:, :, 2:130], pb, [ps_tr], [H_])
                if "halo" in dsk:
                    return
                if seq_first(t):
                    fw.op(fw.dve, lambda e: e.memset(H_[:, :, 1:2], 0.0), [], [H_])
                else:
                    Hp = HTB[(t - 1) % 3]
                    cp(fw, fw.dve, Hp[:, :, 130:131], H_[:, :, 2:3], [H_], [Hp])
                if seq_last(t):
                    fw.op(fw.dve, lambda e: e.memset(H_[:, :, 130:131], 0.0), [], [H_])
                elif t + 1 < NT:
                    Hn = HTB[(t + 1) % 3]
                    cp(fw, fw.dve, Hn[:, :, 1:2], H_[:, :, 129:130], [H_], [Hn])

            def f2(t):
                if t == NCT:
                    load_gate(0)
                H_ = HTB[t % 3]
                x_ = x1r[t % 2]
                aT = actT[t % 2]

                def down(i):
                    for n in range(2):
                        mm(fw, ps_dn[n].ap, aT[:, i, :], WD[:, i, n * 512:(n + 1) * 512], i == 0, i == NF - 1, [aT, WD], [ps_dn[n]], inc=True)

                npair = NF if self.d_lim is None else self.d_lim[1]
                for i in range(npair):
                    pu = ps_up[i % 4]
                    c_ = cvt[i % 3]
                    for n_, ch in enumerate((NF + i, i)):
                        for kc in range(8):
                            mm(fw, pu[:, n_ * 130:(n_ + 1) * 130], WU[:, kc, ch * 128:(ch + 1) * 128], H_[:, kc, 1:131], kc == 0, kc == 7, [WU, H_], [pu],
                               inc=(kc == 7 and n_ == 1))
                    for n_, ch in enumerate((NF + i, i)):
                        u = pu[:, n_ * 130:(n_ + 1) * 130]
                        act(fw, c_[:, n_, :], u[:, 1:129], AF.Identity, [pu, cvp], [c_], bias=cvp[:, 3, ch:ch + 1], scale=cvp[:, 1, ch:ch + 1])
                        stt(fw, fw.dve, c_[:, n_, :], u[:, 0:128], cvp[:, 0, ch:ch + 1], c_[:, n_, :], ALU.mult, ALU.add, [pu, cvp, c_], [c_])
                        stt(fw, fw.dve, c_[:, n_, :], u[:, 2:130], cvp[:, 2, ch:ch + 1], c_[:, n_, :], ALU.mult, ALU.add, [pu, cvp, c_], [c_])
                    s_ = sg[i % 2]
                    act(fw, s_.ap, c_[:, 0, :], AF.Silu, [c_], [s_])
                    tt(fw, fw.pool, aT[:, i, :], c_[:, 1, :], s_.ap, ALU.mult, [c_, s_], [aT])
                    if i >= 2 and npair == NF:
                        down(i - 2)
                if npair < NF:
                    return
                down(NF - 2)
                down(NF - 1)
                r_ = rr[0]
                for n in range(2):
                    tt(fw, fw.dve, r_[:, n * 512:(n + 1) * 512], ps_dn[n].ap, gmlp[:, n * 512:(n + 1) * 512], ALU.mult, [ps_dn[n], gmlp], [r_])
                stt(fw, fw.dve, r_.ap, x_.ap, ALPHA, r_.ap, ALU.mult, ALU.add, [x_, r_], [r_])
                self.ln_affine(r_, rr[1], st6, mv, ln2t)
                if last:
                    fw.dma(fw.sp, self.out[(t - NCT) * 128:(t - NCT + 1) * 128, :], rr[1].ap, reads=[rr[1]], ds=rr[1].ds)
                else:
                    fw.dma(fw.sp, self.X[t * 128:(t + 1) * 128, :], rr[1].ap, reads=[rr[1]], writes=[self.xbuf], ds=rr[1].ds)

            dl = self.d_lim
            nt_ = NT if dl is None else t0 + dl[0]
            if "f1" in dsk:
                fw.barrier()
                return
            f1(t0)
            for t in range(t0, nt_):
                if t + 1 < NT:
                    f1(t + 1)
                if dl is None or dl[1] > 0:
                    f2(t)
            fw.barrier()

    def ln_affine(self, src, dst, st6, mv, gbt):
        fw = self.fw
        for hh in range(2):
            fw.op(fw.dve, lambda e, hh=hh: e.bn_stats(out=st6[:, hh, :], in_=src[:, hh * 512:(hh + 1) * 512]), [src], [st6])
        fw.op(fw.dve, lambda e: e.bn_aggr(out=mv[:, 0:2], in_=st6.ap.rearrange("p a b -> p (a b)")), [st6], [mv])
        rstd_from(fw, mv[:, 2:3], mv[:, 1:2], [mv], [mv])
        ts(fw, fw.dve, dst.ap, src.ap, mv[:, 0:1], mv[:, 2:3], ALU.subtract, ALU.mult, [src, mv], [dst])
        tt(fw, fw.pool, dst.ap, dst.ap, gbt[:, 0, :], ALU.mult, [dst, gbt], [dst])
        tt(fw, fw.pool, dst.ap, dst.ap, gbt[:, 1, :], ALU.add, [dst, gbt], [dst])

    def norm_rope(self, t, src_tl, src, nh, gain, dst_tl, dst, sl_tl, ss, tmp_tl, colT, rowT, qk):
        fw = self.fw
        n = nh * 64
        sq = tmp_tl[:, 0:n]
        s3 = src.rearrange("p (h d) -> p h d", h=nh)
        tt(fw, fw.pool, sq, src, src, ALU.mult, [src_tl], [tmp_tl])
        fw.op(fw.dve, lambda e: e.tensor_reduce(out=ss, in_=sq.rearrange("p (h d) -> p h d", h=nh), axis=AX.X, op=ALU.add), [tmp_tl], [sl_tl])
        rstd_from(fw, ss, ss, [sl_tl], [sl_tl], scale=1.0 / 64.0)
        tt(fw, fw.dve, s3, s3, ss.unsqueeze(2).to_broadcast([128, nh, 64]), ALU.mult, [src_tl, sl_tl], [src_tl])
        is_lat = t >= NCT
        gb = gain.unsqueeze(1).to_broadcast([128, nh, 64])
        if not is_lat:
            tt(fw, fw.dve, dst.rearrange("p (h d) -> p h d", h=nh), s3, gb, ALU.mult, [src_tl, qk], [dst_tl])
            return
        tt(fw, fw.pool, s3, s3, gb, ALU.mult, [src_tl, qk], [src_tl])
        j = t - NCT
        s5 = src.rearrange("p (h a two i) -> p h a two i", h=nh, a=2, two=2)
        o5 = tmp_tl[:, 0:n].rearrange("p (h a two i) -> p h a two i", h=nh, a=2, two=2)
        d5 = dst.rearrange("p (h a two i) -> p h a two i", h=nh, a=2, two=2)
        for a, tab in ((0, rowT[:, j, :]), (1, colT.ap)):
            cos_b = tab[:, 0:16].unsqueeze(1).to_broadcast([128, nh, 16])
            sin_b = tab[:, 16:32].unsqueeze(1).to_broadcast([128, nh, 16])
            tabt = rowT if a == 0 else colT
            x1 = s5[:, :, a, 0, :]
            x2 = s5[:, :, a, 1, :]
            tt(fw, fw.dve, o5[:, :, a, 0, :], x2, sin_b, ALU.mult, [src_tl, tabt], [tmp_tl])
            tt(fw, fw.pool, o5[:, :, a, 1, :], x1, sin_b, ALU.mult, [src_tl, tabt], [tmp_tl])
            tt(fw, fw.dve, x1, x1, cos_b, ALU.mult, [src_tl, tabt], [src_tl])
            tt(fw, fw.pool, x2, x2, cos_b, ALU.mult, [src_tl, tabt], [src_tl])
            tt(fw, fw.dve, d5[:, :, a, 0, :], x1, o5[:, :, a, 0, :], ALU.subtract, [src_tl, tmp_tl], [dst_tl])
            tt(fw, fw.dve, d5[:, :, a, 1, :], x2, o5[:, :, a, 1, :], ALU.add, [src_tl, tmp_tl], [dst_tl])


def make_consts():
    c = np.zeros((128, 1024), np.float32)
    p = np.arange(128)
    c[:, 0:128] = np.eye(128, dtype=np.float32)
    c[:, 128:256] = (p[:, None] <= p[None, :])
    c[:, 256:384] = (p[:, None] >= p[None, :])
    c[:, 384:512] = 1.0
    c[:, 512] = p + 1
    c[:, 513] = 128 - p
    c[:, 514] = -(p + 1)
    c[:, 515] = -(128 - p)
    c[:, 516] = p % 64
    c[:, 517:533] = np.arange(16)[None, :]
    return c


def host_inputs(inputs, b, L=DEPTH):
    f = lambda a: np.ascontiguousarray(np.asarray(a, dtype=np.float32))
    inputs = {k: (np.asarray(v)[:L] if k not in ('x', 'c', 'ctx', 'c_ctx') else v) for k, v in inputs.items()}
    m = {}
    m["xin"] = f(np.concatenate([inputs["ctx"][b], inputs["x"][b]], axis=0))
    cc = np.stack([np.asarray(inputs["c"][b]), np.asarray(inputs["c_ctx"])], axis=-1)
    m["ccT"] = f(cc.reshape(8, 128, 2).transpose(1, 0, 2))
    m["w_mod"] = f(inputs["w_mod"])
    m["b_mod"] = f(inputs["b_mod"])
    m["w_in"] = f(inputs["w_in"])
    m["b_in"] = f(inputs["b_in"])
    fmcols = np.concatenate([np.arange(O_RQ, O_RQ + 256), np.arange(O_RK, O_RK + 256), np.arange(O_MQ, O_MQ + 256), np.arange(O_MK, O_MK + 256)])
    m["b_in_fm"] = f(np.asarray(inputs["b_in"])[:, fmcols].reshape(L, 8, 128).transpose(0, 2, 1))
    m["decay"] = f(np.asarray(inputs["ret_decay_logit"]).reshape(L, 8))
    m["ret_gn"] = f(inputs["ret_gn_g"])
    m["qn_g"] = f(inputs["attn_qn_g"])
    m["kn_g"] = f(inputs["attn_kn_g"])
    m["m_gn"] = f(inputs["mlstm_gn_g"])
    m["w_br"] = f(np.stack([inputs["w_br_ret"], inputs["w_br_att"], inputs["w_br_mlstm"]], axis=1))
    m["w_out"] = f(inputs["w_out"])
    m["ln1"] = f(np.stack([inputs["ln1_g"], inputs["ln1_b"]], axis=1))
    m["w_up"] = f(inputs["w_up"])
    cw = np.asarray(inputs["conv_w"])
    cb = np.asarray(inputs["conv_b"])
    cpk = np.concatenate([cw, cb[:, None, :]], axis=1)
    m["convp"] = f(cpk.reshape(L, 4, 44, 128).transpose(0, 3, 1, 2))
    m["w_down"] = f(inputs["w_down"])
    m["ln2"] = f(np.stack([inputs["ln2_g"], inputs["ln2_b"]], axis=1))
    m["consts"] = make_consts()
    return m


_PROG = {}


def kernel(**inputs):
    if "p" not in _PROG:
        _PROG["p"] = Prog()
    prog = _PROG["p"]
    in_maps = [host_inputs(inputs, b) for b in range(NCORES)]
    res = run_bass_kernel_spmd(prog.nc, in_maps, core_ids=list(range(NCORES)))
    out = np.stack([np.asarray(r["out"]).reshape(32 * 128, D) for r in res.results], axis=0)
    return out.astype(np.float32)
```

```python
import math
import numpy as np
from contextlib import ExitStack
import concourse.bass as bass
import concourse.mybir as mybir
from concourse.bass_utils import run_bass_kernel_spmd

F32 = mybir.dt.float32
BF16 = mybir.dt.bfloat16
I32 = mybir.dt.int32
AF = mybir.ActivationFunctionType
ALU = mybir.AluOpType
AX = mybir.AxisListType

D = 1024
DEPTH = 4
NT = 34
NCT = 2
T = NT * 128
DFF = 2816
NF = 22
EPS = 1e-6
ALPHA = (2.0 * DEPTH) ** 0.25
NCORES = 4

O_RQ, O_RK, O_RV, O_RG = 0, 256, 512, 1024
O_AQ, O_AK, O_AV = 1536, 2048, 2176
O_MQ, O_MK, O_MV, O_MO, O_MI, O_MF, O_GATE = 2304, 2560, 2816, 3328, 3840, 3848, 3856
N_IN = 6928

FM_PIECES = [(0, O_RQ, 256), (256, O_RK, 256), (512, O_MQ, 256), (768, O_MK, 256)]
TMW0 = 1024
TM_GROUPS = [
    ("MG", [(O_MI, 16)]),
    ("RV", [(O_RV, 512)]),
    ("RG", [(O_RG, 512)]),
    ("AQ", [(O_AQ, 512)]),
    ("AKV", [(O_AK, 128), (O_AV, 128)]),
    ("MV", [(O_MV, 512)]),
    ("MO", [(O_MO, 512)]),
] + [("G%d" % i, [(O_GATE + 512 * i, 512)]) for i in range(6)]
TM_OFF = {}
_o = 0
for _n, _p in TM_GROUPS:
    TM_OFF[_n] = _o
    _o += sum(n for _, n in _p)
TM_COLS = _o
W_COLS = TMW0 + TM_COLS

R_RV, R_RG, R_AV, R_RK, R_MK, R_MV, R_MO, R_MG = 0, 1024, 1536, 1666, 1922, 2178, 3210, 3722
R_COLS = 6794
SM_COLS = 16
R_SPLIT = R_MG


class Sem:
    def __init__(self, h, name):
        self.h = h
        self.name = name
        self.owner = None
        self.total = 0


class Buf:
    __slots__ = ("w", "r", "name")

    def __init__(self, name=""):
        self.w = {}
        self.r = {}
        self.name = name


class Tl:
    def __init__(self, fw, ap, name, buf=None):
        self.fw = fw
        self.ap = ap
        self.name = name
        self.buf = buf if buf is not None else Buf(name)
        self._ds = None

    @property
    def ds(self):
        if self._ds is None:
            self._ds = self.fw.pool_sem()
        return self._ds

    def __getitem__(self, idx):
        return self.ap[idx]


class Eng:
    def __init__(self, fw, name, eng):
        self.fw = fw
        self.name = name
        self.eng = eng
        self.sem = fw.new_sem("c_" + name)
        self.sem.owner = self
        self.cnt = 0
        self.waited = {}

    def wait(self, sem, val):
        if val <= 0:
            return
        if self.waited.get(sem, 0) >= val:
            return
        if sem.owner is not None:
            assert val <= sem.owner.cnt, ("wait on unissued instr", self.name, sem.name, val, sem.owner.cnt)
        else:
            assert val <= sem.total
        self.eng.wait_ge(sem.h, val)
        self.waited[sem] = val


class FW:
    def __init__(self, nc):
        self.nc = nc
        self.es = ExitStack()
        self.nsem = 0
        self.pe = Eng(self, "pe", nc.tensor)
        self.act = Eng(self, "act", nc.scalar)
        self.dve = Eng(self, "dve", nc.vector)
        self.pool = Eng(self, "pool", nc.gpsimd)
        self.sp = Eng(self, "sp", nc.sync)
        self.engs = [self.pe, self.act, self.dve, self.pool, self.sp]
        self.dsems = []
        self.swq = []
        self.ndram = 0
        self.sem_pool = []
        self.sem_idx = 0
        self.pool_base = 0
        self.nsb = 0

    def pool_sem(self):
        if self.sem_idx == len(self.sem_pool):
            self.sem_pool.append(self.new_sem("dp%d" % self.sem_idx))
        s = self.sem_pool[self.sem_idx]
        self.sem_idx += 1
        return s

    def reset_pool(self):
        self.sem_idx = self.pool_base

    def new_sem(self, name):
        h = self.es.enter_context(self.nc.semaphore(name + "_%d" % self.nsem))
        self.nsem += 1
        s = Sem(h, name)
        return s

    def sb(self, es, name, shape, dtype):
        self.nsb += 1
        t = es.enter_context(self.nc.sbuf_tensor("%s_%d" % (name, self.nsb), list(shape), dtype))
        return Tl(self, t[:], name)

    def ring(self, es, name, shape, dtype, n):
        return [self.sb(es, "%s%d" % (name, i), shape, dtype) for i in range(n)]

    def dram(self, name, shape, dtype, kind="Internal"):
        t = self.nc.dram_tensor(name, list(shape), dtype, kind=kind)
        return t.ap()

    def _deps(self, E, reads, writes):
        for b in reads:
            b = b.buf if isinstance(b, Tl) else b
            for sem, v in b.w.items():
                if sem is E.sem and E is self.pe:
                    continue
                E.wait(sem, v)
        for b in writes:
            b = b.buf if isinstance(b, Tl) else b
            for sem, v in list(b.w.items()) + list(b.r.items()):
                if sem is E.sem:
                    continue
                E.wait(sem, v)

    def _mark(self, sem, tok, reads, writes):
        for b in reads:
            b = b.buf if isinstance(b, Tl) else b
            if b.r.get(sem, 0) < tok:
                b.r[sem] = tok
        for b in writes:
            b = b.buf if isinstance(b, Tl) else b
            b.w = {sem: tok}
            b.r = {}

    def op(self, E, fn, reads=(), writes=(), inc=True):
        self._deps(E, reads, writes)
        ins = fn(E.eng)
        if inc:
            E.cnt += 1
            ins.then_inc(E.sem.h, 1)
            tok = E.cnt
        else:
            tok = E.cnt + 1
        self._mark(E.sem, tok, reads, writes)
        return ins

    def dma(self, Q, out, in_, reads=(), writes=(), ds=None, serialize=True, **kw):
        self._deps(Q, reads, writes)
        if serialize and ds.total > 0:
            Q.wait(ds, ds.total)
        if Q is self.pool:
            while len(self.swq) >= 2:
                s_, v_ = self.swq.pop(0)
                Q.wait(s_, v_)
        ins = Q.eng.dma_start(out=out, in_=in_, **kw)
        ds.total += 16
        if Q is self.pool:
            self.swq.append((ds, ds.total))
        ins.then_inc(ds.h, 16)
        if ds not in self.dsems:
            self.dsems.append(ds)
        self._mark(ds, ds.total, reads, writes)
        return ins

    def barrier(self):
        for E in self.engs:
            for P in self.engs:
                if P is not E:
                    E.wait(P.sem, P.cnt)
            for ds in self.dsems:
                E.wait(ds, ds.total)


def tt(fw, E, out, in0, in1, op, reads, writes):
    return fw.op(E, lambda e: e.tensor_tensor(out=out, in0=in0, in1=in1, op=op), reads, writes)


def ts(fw, E, out, in0, s1, s2, op0, op1, reads, writes):
    if op1 is None:
        return fw.op(E, lambda e: e.tensor_scalar(out=out, in0=in0, scalar1=s1, scalar2=None, op0=op0), reads, writes)
    return fw.op(E, lambda e: e.tensor_scalar(out=out, in0=in0, scalar1=s1, scalar2=s2, op0=op0, op1=op1), reads, writes)


def stt(fw, E, out, in0, scalar, in1, op0, op1, reads, writes):
    return fw.op(E, lambda e: e.scalar_tensor_tensor(out=out, in0=in0, scalar=scalar, in1=in1, op0=op0, op1=op1), reads, writes)


def act(fw, out, in_, func, reads, writes, bias=None, scale=None):
    kw = {}
    if bias is not None:
        kw["bias"] = bias
    if scale is not None:
        kw["scale"] = scale
    return fw.op(fw.act, lambda e: e.activation(out=out, in_=in_, func=func, **kw), reads, writes)


def cp(fw, E, out, in_, reads, writes):
    if E is fw.act:
        return fw.op(E, lambda e: e.copy(out=out, in_=in_), reads, writes)
    return fw.op(E, lambda e: e.tensor_copy(out=out, in_=in_), reads, writes)


def mm(fw, out, lhsT, rhs, start, stop, reads, writes, inc=None, **kw):
    if inc is None:
        inc = stop
    return fw.op(fw.pe, lambda e: e.matmul(out, lhsT=lhsT, rhs=rhs, start=start, stop=stop, **kw), reads, writes, inc=inc)


def tr(fw, out, in_, ident, reads, writes, inc=True):
    return fw.op(fw.pe, lambda e: e.transpose(out, in_, ident), reads, writes, inc=inc)


def bcast_rows(ap2d, nparts):
    return ap2d.to_broadcast([nparts, ap2d.shape[-1]])


def rstd_from(fw, out, in_, reads, writes, scale=1.0):
    act(fw, out, in_, AF.Ln, reads, writes, bias=fw.eps_t[:, 0:1] if in_.shape[0] == 128 else fw.eps_t[0:in_.shape[0], 0:1], scale=scale)
    act(fw, out, out, AF.Exp, writes, writes, scale=-0.5)


class Prog:
    def __init__(self, n_layers=DEPTH, debug=None, stop_after=None, a_lim=None, skip="", d_lim=None):
        self.a_lim = a_lim
        self.skip = skip
        self.d_lim = d_lim
        self.n_layers = n_layers
        self.debug = debug or []
        self.stop_after = stop_after
        self.nc = bass.Bass("TRN2", target_bir_lowering=False)
        self.fw = FW(self.nc)
        self.inputs = {}
        self.build()

    def din(self, name, shape, dtype=F32):
        ap = self.fw.dram(name, shape, dtype, kind="ExternalInput")
        self.inputs[name] = ap
        return ap

    def dscr(self, name, shape, dtype):
        kind = "ExternalOutput" if name in self.debug else "Internal"
        return self.fw.dram(name, shape, dtype, kind=kind)

    def build(self):
        nc, fw = self.nc, self.fw
        L = self.n_layers
        self.xin = self.din("xin", [T, D])
        self.ccT = self.din("ccT", [128, 8, 2])
        self.w_mod = self.din("w_mod", [L, D, 6 * D])
        self.b_mod = self.din("b_mod", [L, 6 * D])
        self.w_in = self.din("w_in", [L, D, N_IN])
        self.b_in = self.din("b_in", [L, N_IN])
        self.b_in_fm = self.din("b_in_fm", [L, 128, 8])
        self.decay = self.din("decay", [L, 8])
        self.ret_gn = self.din("ret_gn", [L, 512])
        self.qn_g = self.din("qn_g", [L, 64])
        self.kn_g = self.din("kn_g", [L, 64])
        self.m_gn = self.din("m_gn", [L, 512])
        self.w_br = self.din("w_br", [L, 3, 512, D])
        self.w_out = self.din("w_out", [L, D, D])
        self.ln1 = self.din("ln1", [L, 2, D])
        self.w_up = self.din("w_up", [L, D, 2 * DFF])
        self.convp = self.din("convp", [L, 128, 4, 44])
        self.w_down = self.din("w_down", [L, DFF, D])
        self.ln2 = self.din("ln2", [L, 2, D])
        self.consts = self.din("consts", [128, 1024])
        self.out = self.fw.dram("out", [32 * 128, D], F32, kind="ExternalOutput")
        self.X = self.dscr("X", [T, D], F32)
        self.X1 = self.dscr("X1", [T, D], F32)
        self.MOD = self.dscr("MOD", [L, 2, 6 * D], F32)
        self.TMd = self.dscr("TMd", [NT, 128, R_COLS], BF16)
        self.SMd = self.dscr("SMd", [NT, 128, SM_COLS], F32)
        self.FMd = self.dscr("FMd", [NT, 128, 1024], BF16)
        self.AQd = self.dscr("AQd", [NT, 64, 1024], BF16)
        self.AKd = self.dscr("AKd", [64, 2, T], BF16)
        self.ROPEd = self.dscr("ROPEd", [64, 32], F32)
        self.RETCd = self.dscr("RETCd", [L, 128, 16], F32)
        self.SFd = self.dscr("SFd", [NT, 128, 4 * 258], BF16)
        self.SBd = self.dscr("SBd", [NT, 128, 4 * 258], BF16)
        self.YDd = self.dscr("YDd", [NT, 128, 1536], BF16)

        with ExitStack() as es0:
            self.setup_globals(es0)
            fw.barrier()
            fw.pool_base = fw.sem_idx
            if self.stop_after == "S":
                return self.finish()
            for l in range(self.n_layers):
                if "A" not in self.skip:
                    self.phase_a(l)
                fw.barrier()
                fw.reset_pool()
                if self.stop_after == "A%d" % l:
                    return self.finish()
                if "B" not in self.skip:
                    self.phase_b(l)
                else:
                    self.x1buf = Buf("X1")
                fw.barrier()
                fw.reset_pool()
                if self.stop_after == "B%d" % l:
                    return self.finish()
                self.phase_d(l)
                fw.barrier()
                fw.reset_pool()
                if self.stop_after == "D%d" % l:
                    return self.finish()
            self.finish()

    def finish(self):
        fw = self.fw
        fw.barrier()
        for ds in fw.dsems:
            fw.sp.wait(ds, ds.total)

    def setup_globals(self, es):
        nc, fw = self.nc, self.fw
        self.cst = fw.sb(es, "cst", [128, 1024], F32)
        fw.dma(fw.sp, self.cst.ap, self.consts, writes=[self.cst], ds=self.cst.ds)
        self.ident_f = self.cst[:, 0:128]
        self.triF = self.cst[:, 128:256]
        self.triB = self.cst[:, 256:384]
        self.ones_f = self.cst[:, 384:512]
        self.identb = fw.sb(es, "identb", [128, 128], BF16)
        cp(fw, fw.dve, self.identb.ap, self.ident_f, [self.cst], [self.identb])
        self.eps_t = fw.sb(es, "eps_t", [128, 1], F32)
        fw.eps_t = self.eps_t
        fw.op(fw.dve, lambda e: e.memset(self.eps_t.ap, EPS), [], [self.eps_t])
        self.one_t = fw.sb(es, "one_t", [128, 1], F32)
        fw.op(fw.dve, lambda e: e.memset(self.one_t.ap, 1.0), [], [self.one_t])
        self.maskF = fw.sb(es, "maskF", [128, 4, 128], BF16)
        self.maskB = fw.sb(es, "maskB", [128, 4, 128], BF16)
        for h in range(4):
            cp(fw, fw.dve, self.maskF[:, h, :], self.triF, [self.cst], [self.maskF])
            cp(fw, fw.dve, self.maskB[:, h, :], self.triB, [self.cst], [self.maskB])
        self.ps = []
        for i in range(8):
            t = es.enter_context(nc.psum_tensor("psb%d" % i, [128, 512], F32))
            self.ps.append(Tl(fw, t[:], "psb%d" % i))
        xds = fw.new_sem("xcopy")
        fw.dma(fw.sp, self.X, self.xin, ds=xds)
        self.xbuf = Buf("Xall")
        self.xbuf.w = {xds: xds.total}
        self.compute_mod(es)
        self.compute_rope(es)

    def compute_mod(self, es0):
        nc, fw = self.nc, self.fw
        with ExitStack() as es:
            cc = fw.sb(es, "cc", [128, 8, 2], F32)
            sc = fw.sb(es, "sc", [128, 8, 2], F32)
            fw.dma(fw.sp, cc.ap, self.ccT, writes=[cc], ds=cc.ds)
            act(fw, sc.ap, cc.ap, AF.Silu, [cc], [sc])
            wring = fw.ring(es, "wm", [128, 8, 512], F32, 3)
            bm = fw.sb(es, "bm", [2, 6 * D], F32)
            orow = fw.ring(es, "orow", [2, 6 * D], F32, 2)
            k = 0
            for l in range(self.n_layers):
                fw.dma(fw.sp, bm.ap, self.b_mod[l:l + 1, :].to_broadcast([2, 6 * D]), writes=[bm], ds=bm.ds)
                orw = orow[l % 2]
                for g in range(12):
                    wt = wring[k % 3]
                    src = self.w_mod[l].rearrange("(kc p) c -> p kc c", p=128)[:, :, g * 512:(g + 1) * 512]
                    fw.dma(fw.sp, wt.ap, src, writes=[wt], ds=wt.ds)
                    pst = self.ps[k % 2]
                    for kc in range(8):
                        mm(fw, pst[0:2, :], sc[:, kc, :], wt[:, kc, :], kc == 0, kc == 7, [sc, wt], [pst])
                    tt(fw, fw.dve, orw[:, g * 512:(g + 1) * 512], pst[0:2, :], bm[:, g * 512:(g + 1) * 512], ALU.add, [pst, bm], [orw])
                    k += 1
                for ch in (1, 4):
                    ts(fw, fw.dve, orw[:, ch * D:(ch + 1) * D], orw[:, ch * D:(ch + 1) * D], 1.0, None, ALU.add, None, [orw], [orw])
                fw.dma(fw.sp, self.MOD[l], orw.ap, reads=[orw], ds=orw.ds)
            self.modbuf = Buf("MOD")
            for o in orow:
                self.modbuf.w[o.ds] = o.ds.total
            fw.barrier()

    def compute_rope(self, es0):
        nc, fw = self.nc, self.fw
        with ExitStack() as es:
            tl = fw.sb(es, "rp", [128, 8, 32], F32)
            itl = fw.sb(es, "rpi", [128, 32], I32)
            c = self.cst
            fr, u, r, fx, ang = (tl[:, i, :] for i in range(5))
            nidx = c[:, 516:517]
            act(fw, fr[:, 0:16], c[:, 517:533], AF.Exp, [c], [tl], scale=-math.log(10000.0) / 16.0)
            ts(fw, fw.dve, ang[:, 0:16], fr[:, 0:16], nidx, 1.0 / (2 * math.pi), ALU.mult, ALU.mult, [tl, c], [tl])
            ts(fw, fw.dve, u[:, 0:16], ang[:, 0:16], 0.25, None, ALU.add, None, [tl], [tl])
            cp(fw, fw.dve, u[:, 16:32], ang[:, 0:16], [tl], [tl])
            cp(fw, fw.dve, itl.ap, u, [tl], [itl])
            cp(fw, fw.dve, r, itl.ap, [itl], [tl])
            tt(fw, fw.dve, r, u, r, ALU.subtract, [tl], [tl])
            ts(fw, fw.dve, fx, r, 0.5, None, ALU.is_gt, None, [tl], [tl])
            tt(fw, fw.dve, r, r, fx, ALU.subtract, [tl], [tl])
            ts(fw, fw.dve, fx, r, -0.5, None, ALU.is_lt, None, [tl], [tl])
            tt(fw, fw.dve, r, r, fx, ALU.add, [tl], [tl])
            res = tl[:, 5, :]
            act(fw, res, r, AF.Sin, [tl], [tl], scale=2 * math.pi)
            fw.dma(fw.sp, self.ROPEd, tl[0:64, 5, :], reads=[tl], ds=tl.ds)
            self.ropebuf = Buf("rope")
            self.ropebuf.w = {tl.ds: tl.ds.total}
            fw.barrier()

    def load_bcast(self, dst_tl, dst_ap, src_row_ap, q=None, reads=()):
        fw = self.fw
        q = q or fw.sp
        n = src_row_ap.shape[-1]
        fw.dma(q, dst_ap, src_row_ap.to_broadcast([dst_ap.shape[0], n]), reads=list(reads), writes=[dst_tl], ds=dst_tl.ds, serialize=False)

    def phase_a(self, l):
        nc, fw = self.nc, self.fw
        ps = self.ps
        with ExitStack() as es:
            W = fw.sb(es, "Wa", [128, 8, W_COLS], BF16)
            wsrc = self.w_in[l].rearrange("(kc p) c -> p kc c", p=128)
            pieces = list(FM_PIECES)
            for name, pl_ in TM_GROUPS:
                o = TMW0 + TM_OFF[name]
                for (src, n) in pl_:
                    pieces.append((o, src, n))
                    o += n
            for (dst, src, n) in pieces:
                fw.dma(fw.pool, W[:, :, dst:dst + n], wsrc[:, :, src:src + n], writes=[W], ds=W.ds, serialize=False)
            BB = fw.sb(es, "BBa", [128, TM_COLS - 16], BF16)
            BG = fw.sb(es, "BGa", [128, 16], F32)
            for name, pl_ in TM_GROUPS:
                o = TM_OFF[name]
                for (src, n) in pl_:
                    row = self.b_in[l:l + 1, src:src + n]
                    if name == "MG":
                        self.load_bcast(BG, BG[:, o:o + n], row)
                    else:
                        self.load_bcast(BB, BB[:, o - 16:o - 16 + n], row, q=fw.pool)
                    o += n
            bfm = fw.sb(es, "bfm", [128, 8], F32)
            fw.dma(fw.sp, bfm.ap, self.b_in_fm[l], writes=[bfm], ds=bfm.ds)
            modt = fw.sb(es, "moda", [128, 2, D], F32)

            def load_mod(j):
                for ch in range(2):
                    self.load_bcast(modt, modt[:, ch, :], self.MOD[l, j:j + 1, ch * D:(ch + 1) * D], reads=[self.modbuf])
            load_mod(1)
            gn = fw.sb(es, "gna", [128, 2, 512], F32)
            self.load_bcast(gn, gn[:, 0, :], self.ret_gn[l:l + 1, :])
            self.load_bcast(gn, gn[:, 1, :], self.m_gn[l:l + 1, :])
            qk = fw.sb(es, "qka", [128, 2, 64], F32)
            self.load_bcast(qk, qk[:, 0, :], self.qn_g[l:l + 1, :])
            self.load_bcast(qk, qk[:, 1, :], self.kn_g[l:l + 1, :])
            ts(fw, fw.dve, qk[:, 0, :], qk[:, 0, :], 0.125, None, ALU.mult, None, [qk], [qk])
            colT = fw.sb(es, "colT", [128, 32], F32)
            rowT = fw.sb(es, "rowT", [128, 32, 32], F32)
            for hf in range(2):
                fw.dma(fw.sp, colT[hf * 64:(hf + 1) * 64, :], self.ROPEd, reads=[self.ropebuf], writes=[colT], ds=colT.ds, serialize=False)
                src = self.ROPEd.rearrange("(j two) c -> two j c", two=2)[hf:hf + 1]
                fw.dma(fw.sp, rowT[hf * 64:(hf + 1) * 64, :, :], src.to_broadcast([64, 32, 32]), reads=[self.ropebuf], writes=[rowT], ds=rowT.ds, serialize=False)
            dk = fw.sb(es, "dka", [128, 8, 8], F32)
            c = self.cst
            self.load_bcast(dk, dk[:, 0, :], self.decay[l:l + 1, :])
            act(fw, dk[:, 1, :], dk[:, 0, :], AF.Exp, [dk], [dk], scale=-1.0)
            act(fw, dk[:, 2, :], dk[:, 1, :], AF.Ln, [dk], [dk], bias=self.one_t[:, 0:1])
            rEA = dk[:, 3, :]
            rEB = dk[:, 4, :]
            rEE = dk[:, 5, :]
            act(fw, rEA[:, 0:4], dk[:, 2, 0:4], AF.Exp, [dk, c], [dk], scale=c[:, 512:513])
            act(fw, rEA[:, 4:8], dk[:, 2, 4:8], AF.Exp, [dk, c], [dk], scale=c[:, 513:514])
            act(fw, rEB[:, 0:4], dk[:, 2, 0:4], AF.Exp, [dk, c], [dk], scale=c[:, 514:515])
            act(fw, rEB[:, 4:8], dk[:, 2, 4:8], AF.Exp, [dk, c], [dk], scale=c[:, 515:516])
            act(fw, rEE, dk[:, 2, :], AF.Exp, [dk], [dk], scale=-128.0)
            retc = fw.sb(es, "retc", [128, 16], F32)
            cp(fw, fw.dve, retc[:, 0:8], rEB, [dk], [retc])
            for hf in range(2):
                cp(fw, fw.dve, retc[hf * 64:(hf + 1) * 64, 8:12].rearrange("p (d j) -> p d j", d=2),
                   rEE[hf * 64:(hf + 1) * 64, :].rearrange("p (d j two) -> p d j two", d=2, j=2)[:, :, :, hf], [dk], [retc])
            fw.dma(fw.sp, self.RETCd[l], retc.ap, reads=[retc], ds=retc.ds)

            xt = fw.sb(es, "xta", [128, D], F32)
            st6 = fw.sb(es, "st6a", [128, 2, 6], F32)
            mv = fw.sb(es, "mva", [128, 4], F32)
            xn = fw.sb(es, "xna", [128, D], F32)
            xm = fw.sb(es, "xma", [128, D], BF16)
            xmT = fw.ring(es, "xmTa", [128, 8, 128], BF16, 2)
            tmA = fw.sb(es, "tmAa", [128, R_SPLIT], BF16)
            tmB = fw.sb(es, "tmBa", [128, R_COLS - R_SPLIT], BF16)
            sm_ = fw.sb(es, "smra", [128, SM_COLS], F32)
            fm_ = fw.sb(es, "fmra", [128, 8, 128], BF16)
            aq_ = fw.sb(es, "aqra", [64, 8, 128], BF16)
            ak_ = fw.sb(es, "akra", [64, 2, 128], BF16)
            tmpA = fw.ring(es, "tmpAa", [128, 512], F32, 3)
            tmpB = fw.ring(es, "tmpBa", [128, 512], F32, 2)
            qb_ = fw.sb(es, "qba", [128, 640], BF16)
            g_ = fw.sb(es, "gtsa", [128, 64], F32)
            sl_ = fw.sb(es, "smla", [128, 32], F32)
            fw.op(fw.dve, lambda e: e.memset(tmA[:, R_AV:R_AV + 130].rearrange("p (g c) -> p g c", g=2)[:, :, 64:65], 1.0), [], [tmA])

            ps_tr, ps_fm, ps_sm, ps_aq = ps[0], ps[1], ps[2], ps[3]
            ps_tm = ps[4:8]
            self._tmk = 0

            def s1(t):
                if t == NCT:
                    load_mod(0)
                fw.dma(fw.sp, xt.ap, self.X[t * 128:(t + 1) * 128, :], reads=[self.xbuf], writes=[xt], ds=xt.ds)
                for hh in range(2):
                    fw.op(fw.dve, lambda e, hh=hh: e.bn_stats(out=st6[:, hh, :], in_=xt[:, hh * 512:(hh + 1) * 512]), [xt], [st6])
                fw.op(fw.dve, lambda e: e.bn_aggr(out=mv[:, 0:2], in_=st6.ap.rearrange("p a b -> p (a b)")), [st6], [mv])
                rstd_from(fw, mv[:, 2:3], mv[:, 1:2], [mv], [mv])
                ts(fw, fw.dve, xn.ap, xt.ap, mv[:, 0:1], mv[:, 2:3], ALU.subtract, ALU.mult, [xt, mv], [xn])
                tt(fw, fw.pool, xn.ap, xn.ap, modt[:, 1, :], ALU.mult, [xn, modt], [xn])
                tt(fw, fw.dve, xm.ap, xn.ap, modt[:, 0, :], ALU.add, [xn, modt], [xm])

            def s2(t):
                xT_ = xmT[t % 2]
                pb = ps_tr.ap.bitcast(BF16).rearrange("p (a b) -> p a b", a=8)
                for kc in range(8):
                    tr(fw, pb[:, kc, :], xm[:, kc * 128:(kc + 1) * 128], self.identb.ap, [xm, self.identb], [ps_tr], inc=(kc == 7))
                cp(fw, fw.act, xT_.ap.rearrange("p a b -> p (a b)"), ps_tr.ap.bitcast(BF16), [ps_tr], [xT_])

            def tm_matmul(t, name, n):
                xT_ = xmT[t % 2]
                pst = ps_tm[self._tmk % len(ps_tm)]
                self._tmk += 1
                o = TMW0 + TM_OFF[name]
                for kc in range(8):
                    mm(fw, pst[:, 0:n], xT_[:, kc, :], W[:, kc, o:o + n], kc == 0, kc == 7, [xT_, W], [pst])
                return pst

            def bias_of(name, n, off=0):
                o = TM_OFF[name] - 16 + off
                return BB[:, o:o + n]

            def s3(t):
                xT_ = xmT[t % 2]
                for half in range(2):
                    for i4 in range(4):
                        i = half * 4 + i4
                        for kc in range(8):
                            mm(fw, ps_fm[:, i4 * 128:(i4 + 1) * 128], W[:, kc, i * 128:(i + 1) * 128], xT_[:, kc, :], kc == 0, kc == 7, [xT_, W], [ps_fm],
                               inc=(kc == 7 and i4 == 3))
                    for i4 in range(4):
                        i = half * 4 + i4
                        sc_ = 0.125 if i in (2, 3, 6, 7) else 1.0
                        ts(fw, fw.dve, fm_[:, i, :], ps_fm[:, i4 * 128:(i4 + 1) * 128], bfm[:, i:i + 1], sc_, ALU.add, ALU.mult, [ps_fm, bfm], [fm_])
                fw.dma(fw.sp, self.FMd[t], fm_.ap.rearrange("p a b -> p (a b)"), reads=[fm_], ds=fm_.ds)
                if lim is not None and len(lim) > 2 and lim[2] <= 1:
                    return
                pbk = ps_sm.ap.bitcast(BF16)
                for n_, i in enumerate((2, 3, 6, 7)):
                    tr(fw, pbk[:, 512 + n_ * 128:512 + (n_ + 1) * 128], fm_[:, i, :], self.identb.ap, [fm_, self.identb], [ps_sm], inc=(n_ == 3))
                cp(fw, fw.act, tmA[:, R_RK:R_RK + 512], pbk[:, 512:1024], [ps_sm], [tmA])
                if lim is not None and len(lim) > 2 and lim[2] <= 2:
                    return
                pst = tm_matmul(t, "MG", 16)
                tt(fw, fw.dve, g_[:, 0:16], pst[:, 0:16], BG.ap, ALU.add, [pst, BG], [g_])
                e_ = g_[:, 16:24]
                sp_ = g_[:, 24:32]
                act(fw, e_, g_[:, 8:16], AF.Exp, [g_], [g_], scale=-1.0)
                act(fw, sp_, e_, AF.Ln, [g_], [g_], bias=self.one_t[:, 0:1])
                mm(fw, ps_sm[:, 0:4], self.triF, sp_[:, 0:4], True, True, [g_, self.cst], [ps_sm], inc=False)
                mm(fw, ps_sm[:, 4:8], self.triB, sp_[:, 4:8], True, True, [g_, self.cst], [ps_sm], inc=False)
                mm(fw, ps_sm[:, 8:16], self.ones_f, sp_, True, True, [g_, self.cst], [ps_sm], inc=True)
                ta = g_[:, 32:40]
                EA = g_[:, 40:48]
                tt(fw, fw.dve, ta, g_[:, 0:8], ps_sm[:, 0:8], ALU.add, [g_, ps_sm], [g_])
                act(fw, EA, ta, AF.Exp, [g_], [g_])
                act(fw, sm_[:, 0:8], ps_sm[:, 0:8], AF.Exp, [ps_sm], [sm_], scale=-1.0)
                ebe = g_[:, 48:56]
                act(fw, ebe, ps_sm[:, 8:16], AF.Exp, [ps_sm], [g_], scale=-1.0)
                for hf in range(2):
                    cp(fw, fw.dve, sm_[hf * 64:(hf + 1) * 64, 8:12].rearrange("p (d j) -> p d j", d=2),
                       ebe[hf * 64:(hf + 1) * 64, :].rearrange("p (d j two) -> p d j two", d=2, j=2)[:, :, :, hf], [g_], [sm_])
                fw.dma(fw.sp, self.SMd[t], sm_.ap, reads=[sm_], ds=sm_.ds)
                if lim is not None and len(lim) > 2 and lim[2] <= 3:
                    return
                pst = tm_matmul(t, "RV", 512)
                v_ = tmpA[0]
                tt(fw, fw.dve, v_.ap, pst.ap, bias_of("RV", 512), ALU.add, [pst, BB], [v_])
                for d in range(2):
                    eng = fw.dve if d == 0 else fw.pool
                    tt(fw, eng, tmA[:, R_RV + d * 512:R_RV + (d + 1) * 512].rearrange("p (h e) -> p h e", h=4),
                       v_.ap.rearrange("p (h e) -> p h e", h=4), rEA[:, d * 4:(d + 1) * 4].unsqueeze(2).to_broadcast([128, 4, 128]), ALU.mult, [v_, dk], [tmA])
                if lim is not None and len(lim) > 2 and lim[2] <= 4:
                    return
                pst = tm_matmul(t, "RG", 512)
                a_, b_ = tmpA[1], tmpB[0]
                tt(fw, fw.dve, a_.ap, pst.ap, bias_of("RG", 512), ALU.add, [pst, BB], [a_])
                act(fw, b_.ap, a_.ap, AF.Silu, [a_], [b_])
                tt(fw, fw.pool, tmA[:, R_RG:R_RG + 512], b_.ap, gn[:, 0, :], ALU.mult, [b_, gn], [tmA])
                if lim is not None and len(lim) > 2 and lim[2] <= 5:
                    return
                q_ = tmpA[2]
                pst = tm_matmul(t, "AQ", 512)
                tt(fw, fw.dve, q_.ap, pst.ap, bias_of("AQ", 512), ALU.add, [pst, BB], [q_])
                self.norm_rope(t, q_, q_.ap, 8, qk[:, 0, :], qb_, qb_[:, 0:512], sl_, sl_[:, 0:8], tmpB[1], colT, rowT, qk)
                if lim is not None and len(lim) > 2 and lim[2] <= 6:
                    return
                pst = tm_matmul(t, "AKV", 256)
                k_ = tmpA[0]
                tt(fw, fw.dve, k_[:, 0:128], pst[:, 0:128], bias_of("AKV", 128), ALU.add, [pst, BB], [k_])
                self.norm_rope(t, k_, k_[:, 0:128], 2, qk[:, 1, :], qb_, qb_[:, 512:640], sl_, sl_[:, 8:10], tmpB[1], colT, rowT, qk)
                tt(fw, fw.dve, tmA[:, R_AV:R_AV + 130].rearrange("p (g c) -> p g c", g=2)[:, :, 0:64],
                   pst[:, 128:256].rearrange("p (g c) -> p g c", g=2), bias_of("AKV", 128, 128).rearrange("p (g c) -> p g c", g=2), ALU.add, [pst, BB], [tmA])
                if lim is not None and len(lim) > 2 and lim[2] <= 7:
                    return
                pbq = ps_aq.ap.bitcast(BF16)
                for h in range(8):
                    tr(fw, pbq[0:64, h * 128:(h + 1) * 128], qb_[:, h * 64:(h + 1) * 64], self.identb.ap, [qb_, self.identb], [ps_aq], inc=(h == 7))
                for h in range(2):
                    tr(fw, pbk[0:64, 256 + h * 128:256 + (h + 1) * 128], qb_[:, 512 + h * 64:512 + (h + 1) * 64], self.identb.ap, [qb_, self.identb], [ps_sm], inc=(h == 1))
                cp(fw, fw.act, aq_.ap.rearrange("p a b -> p (a b)"), pbq[0:64, :], [ps_aq], [aq_])
                cp(fw, fw.act, ak_.ap.rearrange("p a b -> p (a b)"), pbk[0:64, 256:512], [ps_sm], [ak_])
                fw.dma(fw.sp, self.AQd[t], aq_.ap.rearrange("p a b -> p (a b)"), reads=[aq_], ds=aq_.ds)
                fw.dma(fw.sp, self.AKd[:, :, t * 128:(t + 1) * 128], ak_.ap, reads=[ak_], ds=ak_.ds)
                if lim is not None and len(lim) > 2 and lim[2] <= 8:
                    return
                pst = tm_matmul(t, "MV", 512)
                v_ = tmpA[1]
                tt(fw, fw.dve, v_.ap, pst.ap, bias_of("MV", 512), ALU.add, [pst, BB], [v_])
                for d in range(2):
                    eng = fw.dve if d == 0 else fw.pool
                    dst = tmA[:, R_MV + d * 516:R_MV + (d + 1) * 516].rearrange("p (h e) -> p h e", h=4)
                    tt(fw, eng, dst[:, :, 0:128], v_.ap.rearrange("p (h e) -> p h e", h=4),
                       EA[:, d * 4:(d + 1) * 4].unsqueeze(2).to_broadcast([128, 4, 128]), ALU.mult, [v_, g_], [tmA])
                    cp(fw, eng, dst[:, :, 128:129], EA[:, d * 4:(d + 1) * 4].unsqueeze(2), [g_], [tmA])
                if lim is not None and len(lim) > 2 and lim[2] <= 9:
                    return
                pst = tm_matmul(t, "MO", 512)
                a_, b_ = tmpA[2], tmpB[0]
                tt(fw, fw.dve, a_.ap, pst.ap, bias_of("MO", 512), ALU.add, [pst, BB], [a_])
                act(fw, b_.ap, a_.ap, AF.Sigmoid, [a_], [b_])
                tt(fw, fw.pool, tmA[:, R_MO:R_MO + 512], b_.ap, gn[:, 1, :], ALU.mult, [b_, gn], [tmA])
                fw.dma(fw.sp, self.TMd[t][:, 0:R_SPLIT], tmA.ap, reads=[tmA], ds=tmA.ds)
                if lim is not None and len(lim) > 2 and lim[2] <= 10:
                    return
                for i in range(6):
                    pst = tm_matmul(t, "G%d" % i, 512)
                    a_ = tmpA[i % 3]
                    tt(fw, fw.dve, a_.ap, pst.ap, bias_of("G%d" % i, 512), ALU.add, [pst, BB], [a_])
                    act(fw, tmB[:, i * 512:(i + 1) * 512], a_.ap, AF.Sigmoid, [a_], [tmB])
                fw.dma(fw.sp, self.TMd[t][:, R_SPLIT:R_COLS], tmB.ap, reads=[tmB], ds=tmB.ds)

            lim = getattr(self, "a_lim", None)
            if lim == "pre":
                fw.barrier()
                return
            nt = NT if lim is None else lim[0]
            s1(0)
            s2(0)
            for t in range(nt):
                if t + 1 < nt:
                    s1(t + 1)
                    s2(t + 1)
                if lim is None or lim[1] >= 3:
                    s3(t)
            fw.barrier()

    def phase_b(self, l):
        nc, fw = self.nc, self.fw
        ps = self.ps
        last = (l == DEPTH - 1)
        with ExitStack() as es:
            WB = fw.sb(es, "WBb", [128, 3, 4, D], BF16)
            for b in range(3):
                fw.dma(fw.pool, WB[:, b, :, :], self.w_br[l, b].rearrange("(kc p) c -> p kc c", p=128), writes=[WB], ds=WB.ds, serialize=False)
            WO = fw.sb(es, "WOb", [128, 8, D], BF16)
            wo_src = self.w_out[l].rearrange("(kc p) c -> p kc c", p=128)
            for hh in range(2):
                fw.dma(fw.pool, WO[:, hh * 4:(hh + 1) * 4, :], wo_src[:, hh * 4:(hh + 1) * 4, :], writes=[WO], ds=WO.ds, serialize=False)
            AKT = fw.sb(es, "AKTb", [64, 2, T], BF16)
            fw.dma(fw.sp, AKT.ap, self.AKd, writes=[AKT], ds=AKT.ds)
            AVa = fw.sb(es, "AVab", [128, NT, 130], BF16)
            for q in range(0, NT, 8):
                q1 = min(NT, q + 8)
                fw.dma(fw.sp, AVa[:, q:q1, :], self.TMd[q:q1, :, R_AV:R_AV + 130].rearrange("t p c -> p t c"), writes=[AVa], ds=AVa.ds, serialize=False)
            retc = fw.sb(es, "retcb", [128, 16], F32)
            fw.dma(fw.sp, retc.ap, self.RETCd[l], writes=[retc], ds=retc.ds)
            gms = fw.sb(es, "gmsb", [128, D], F32)
            ln1t = fw.sb(es, "ln1tb", [128, 2, D], F32)
            for i in range(2):
                self.load_bcast(ln1t, ln1t[:, i, :], self.ln1[l, i:i + 1, :])

            def load_gms(j):
                self.load_bcast(gms, gms.ap, self.MOD[l, j:j + 1, 2 * D:3 * D], reads=[self.modbuf])
            load_gms(1)
            orders = {0: list(range(NT)), 1: [1, 0] + list(range(NT - 1, 1, -1))}
            SXd = (self.SFd, self.SBd)

            with ExitStack() as es2:
                S = [[fw.sb(es2, "Sst%d_%d" % (d, k), [128, 258], F32) for k in range(4)] for d in range(2)]
                for d in range(2):
                    for k in range(4):
                        fw.op(fw.dve if k % 2 == 0 else fw.pool, lambda e, d=d, k=k: e.memset(S[d][k].ap, 0.0), [], [S[d][k]])
                Sbf = [[fw.ring(es2, "Sbf%d_%d" % (d, k), [128, 258], BF16, 2) for k in range(4)] for d in range(2)]
                ldr = [fw.ring(es2, "ldr%d" % d, [128, 1540], BF16, 3) for d in range(2)]
                smr = [fw.ring(es2, "smr%d" % d, [128, SM_COLS], F32, 3) for d in range(2)]
                for i in range(NT):
                    for d in range(2):
                        t = orders[d][i]
                        L_ = ldr[d][i % 3]
                        sm_ = smr[d][i % 3]
                        fw.dma(fw.sp, L_[:, 0:512], self.TMd[t][:, R_RK:R_RK + 512], writes=[L_], ds=L_.ds)
                        fw.dma(fw.sp, L_[:, 512:1024], self.TMd[t][:, R_RV + d * 512:R_RV + (d + 1) * 512], writes=[L_], ds=L_.ds, serialize=False)
                        fw.dma(fw.sp, L_[:, 1024:1540], self.TMd[t][:, R_MV + d * 516:R_MV + (d + 1) * 516], writes=[L_], ds=L_.ds, serialize=False)
                        fw.dma(fw.sp, sm_.ap, self.SMd[t], writes=[sm_], ds=sm_.ds)
                        for mxj in range(4):
                            mx, j = mxj // 2, mxj % 2
                            W_ = 128 if mx == 0 else 129
                            pst = ps[d * 4 + mxj]
                            K_ = L_[:, mx * 256 + j * 128:mx * 256 + (j + 1) * 128]
                            for blk in range(2):
                                h = 2 * j + blk
                                V_ = L_[:, 512 + h * 128:512 + (h + 1) * 128] if mx == 0 else L_[:, 1024 + h * 129:1024 + (h + 1) * 129]
                                mm(fw, pst[:, blk * 129:blk * 129 + W_], K_, V_, True, True, [L_], [pst], inc=(blk == 1))
                        for mxj in range(4):
                            mx, j = mxj // 2, mxj % 2
                            W_ = 128 if mx == 0 else 129
                            pst = ps[d * 4 + mxj]
                            St = S[d][mxj]
                            sb_ = Sbf[d][mxj][i % 2]
                            cp(fw, fw.act, sb_.ap, St.ap, [St], [sb_])
                            fw.dma(fw.sp, SXd[d][t][:, mxj * 258:(mxj + 1) * 258], sb_.ap, reads=[sb_], ds=sb_.ds)
                            e_ = retc[:, 8 + d * 2 + j:9 + d * 2 + j] if mx == 0 else sm_[:, 8 + d * 2 + j:9 + d * 2 + j]
                            e_src = retc if mx == 0 else sm_
                            Sv = St.ap.rearrange("p (b w) -> p b w", b=2)[:, :, 0:W_]
                            Pv = pst[:, 0:258].rearrange("p (b w) -> p b w", b=2)[:, :, 0:W_]
                            act(fw, Sv, Sv, AF.Identity, [St, e_src], [St], scale=e_)
                            stt(fw, fw.dve, Sv, Pv, e_, Sv, ALU.mult, ALU.add, [pst, e_src, St], [St])
                fw.barrier()
            self.sxbuf = Buf("SX")

            tmA = fw.ring(es, "tmAb", [128, R_SPLIT], BF16, 2)
            tmG = fw.ring(es, "tmGb", [128, R_COLS - R_SPLIT], BF16, 3)
            smr = fw.ring(es, "smrb", [128, SM_COLS], F32, 2)
            fmr = fw.ring(es, "fmrb", [128, 8, 128], BF16, 2)
            aqr = fw.ring(es, "aqrb", [64, 8, 128], BF16, 2)
            sfr = [fw.ring(es, "sxr%d" % d, [128, 4, 258], BF16, 2) for d in range(2)]
            xr = fw.ring(es, "xrb", [128, D], F32, 2)
            PT = fw.ring(es, "PTb", [128, 2, 512], BF16, 2)
            pTr = fw.ring(es, "pTb", [128, 512], BF16, 3)
            yf = fw.ring(es, "yfb", [128, 512], F32, 4)
            sml = fw.ring(es, "smlb", [128, 64], F32, 2)
            st4 = fw.sb(es, "st4b", [128, 4, 6], F32)
            ymix = fw.ring(es, "ymixb", [128, 3, 512], BF16, 2)
            yT = fw.ring(es, "yTb", [128, 12, 128], BF16, 2)
            zt = fw.ring(es, "ztb", [128, D], F32, 2)
            zb = fw.sb(es, "zbb", [128, D], BF16)
            zT = fw.sb(es, "zTb", [128, 8, 128], BF16)
            rr = fw.ring(es, "rrb", [128, D], F32, 2)
            st6 = fw.sb(es, "st6b", [128, 2, 6], F32)
            mv = fw.sb(es, "mvb", [128, 4], F32)
            ps_s, ps_of, ps_ob, ps_sc, ps_acc = ps[0], (ps[1], ps[2]), (ps[3], ps[4]), (ps[5], ps[6]), ps[7]
            self._yk = 0

            def nexty():
                self._yk += 1
                return yf[self._yk % 4]

            def loads(t):
                fw.dma(fw.sp, tmA[t % 2].ap, self.TMd[t][:, 0:R_SPLIT], writes=[tmA[t % 2]], ds=tmA[t % 2].ds)
                fw.dma(fw.sp, tmG[t % 3].ap, self.TMd[t][:, R_SPLIT:R_COLS], writes=[tmG[t % 3]], ds=tmG[t % 3].ds)
                fw.dma(fw.sp, smr[t % 2].ap, self.SMd[t], writes=[smr[t % 2]], ds=smr[t % 2].ds)
                fw.dma(fw.sp, fmr[t % 2].ap.rearrange("p a b -> p (a b)"), self.FMd[t], writes=[fmr[t % 2]], ds=fmr[t % 2].ds)
                fw.dma(fw.sp, aqr[t % 2].ap.rearrange("p a b -> p (a b)"), self.AQd[t], writes=[aqr[t % 2]], ds=aqr[t % 2].ds)
                for d in range(2):
                    fw.dma(fw.sp, sfr[d][t % 2].ap.rearrange("p a b -> p (a b)"), SXd[d][t], writes=[sfr[d][t % 2]], ds=sfr[d][t % 2].ds)

            def par(ap2, hp, n=2):
                return ap2.rearrange("p (j two k) -> p j two k", j=2, two=2)[:, :, hp, :]

            def linattn(t, mx):
                ta_, sm_, fm_ = tmA[t % 2], smr[t % 2], fmr[t % 2]
                W_ = 128 if mx == 0 else 129
                qi, ki = (0, 2) if mx == 0 else (4, 6)
                pt_ = PT[(2 * t + mx) % 2]
                for h in range(4):
                    j, hf = h // 2, h % 2
                    mm(fw, ps_sc[hf][:, j * 128:(j + 1) * 128], fm_[hf * 64:(hf + 1) * 64, ki + j, :], fm_[hf * 64:(hf + 1) * 64, qi + j, :], True, True, [fm_], [ps_sc[hf]], inc=(h >= 2))
                for d, msk in ((0, self.maskF), (1, self.maskB)):
                    for hp in range(2):
                        tt(fw, fw.dve, par(pt_[:, d, :], hp), ps_sc[hp][:, 0:256].rearrange("p (j k) -> p j k", j=2), msk[:, 0:2, :], ALU.mult, [ps_sc[hp], msk], [pt_])
                for d in range(2):
                    banks = ps_of if d == 0 else ps_ob
                    S_ = sfr[d][t % 2]
                    for h in (0, 2, 1, 3):
                        j, hf = h // 2, h % 2
                        bank = banks[hf]
                        if mx == 0:
                            V_ = ta_[:, R_RV + d * 512 + h * 128:R_RV + d * 512 + (h + 1) * 128]
                        else:
                            V_ = ta_[:, R_MV + d * 516 + h * 129:R_MV + d * 516 + (h + 1) * 129]
                        out = bank[:, j * 129:j * 129 + W_]
                        mm(fw, out, pt_[:, d, h * 128:(h + 1) * 128], V_, (j == 0), False, [pt_, ta_], [bank], inc=False, skip_group_check=True)
                        Sv = S_[hf * 64:(hf + 1) * 64, mx * 2 + j, hf * 129:hf * 129 + W_]
                        mm(fw, out, fm_[hf * 64:(hf + 1) * 64, qi + j, :], Sv, False, True, [fm_, S_], [bank], inc=(j == 1), skip_group_check=True)
                y = nexty()
                sl = sml[t % 2]
                if mx == 0:
                    t1 = nexty()
                    for hp in range(2):
                        o_f = ps_of[hp][:, 0:258].rearrange("p (j w) -> p j w", j=2)[:, :, 0:128]
                        o_b = ps_ob[hp][:, 0:258].rearrange("p (j w) -> p j w", j=2)[:, :, 0:128]
                        ebf = par(retc[:, 0:4], hp).to_broadcast([128, 2, 128])
                        ebb = par(retc[:, 4:8], hp).to_broadcast([128, 2, 128])
                        tt(fw, fw.dve, par(t1.ap, hp), o_f, ebf, ALU.mult, [ps_of[hp], retc], [t1])
                        tt(fw, fw.dve, par(y.ap, hp), o_b, ebb, ALU.mult, [ps_ob[hp], retc], [y])
                    tt(fw, fw.pool, y.ap, y.ap, t1.ap, ALU.add, [y, t1], [y])
                else:
                    hd = []
                    for d in range(2):
                        banks = ps_of if d == 0 else ps_ob
                        q1 = sl[:, d * 16:d * 16 + 4]
                        q2 = sl[:, d * 16 + 4:d * 16 + 8]
                        r_ = sl[:, d * 16 + 8:d * 16 + 12]
                        eb = sm_[:, d * 4:(d + 1) * 4]
                        for hp in range(2):
                            den = banks[hp][:, 0:258].rearrange("p (j w) -> p j w", j=2)[:, :, 128:129]
                            tt(fw, fw.dve, par(q1, hp), den, par(eb, hp), ALU.mult, [banks[hp], sm_], [sl])
                        stt(fw, fw.dve, q2, q1, -1.0, q1, ALU.mult, ALU.max, [sl], [sl])
                        ts(fw, fw.dve, q2, q2, 1.0, None, ALU.max, None, [sl], [sl])
                        fw.op(fw.dve, lambda e, q2=q2: e.reciprocal(out=q2, in_=q2), [sl], [sl])
                        tt(fw, fw.dve, r_, q2, eb, ALU.mult, [sl, sm_], [sl])
                        hdt = y if d == 0 else nexty()
                        for hp in range(2):
                            num = banks[hp][:, 0:258].rearrange("p (j w) -> p j w", j=2)[:, :, 0:128]
                            tt(fw, fw.dve, par(hdt.ap, hp), num, par(r_, hp).to_broadcast([128, 2, 128]), ALU.mult, [banks[hp], sl], [hdt])
                        hd.append(hdt)
                    tt(fw, fw.pool, y.ap, hd[0].ap, hd[1].ap, ALU.add, [hd[0], hd[1]], [y])
                y3 = y.ap.rearrange("p (h e) -> p h e", h=4)
                for h in range(4):
                    fw.op(fw.dve, lambda e, h=h: e.bn_stats(out=st4[:, h, :], in_=y3[:, h, :]), [y], [st4])
                mvh = sl[:, 32:40].rearrange("p (h two) -> p h two", h=4)
                for h in range(4):
                    fw.op(fw.dve, lambda e, h=h: e.bn_aggr(out=mvh[:, h, :], in_=st4[:, h, :]), [st4], [sl])
                rs = sl[:, 40:44]
                act(fw, rs, mvh[:, :, 1], AF.Ln, [sl], [sl], bias=self.eps_t[:, 0:1])
                act(fw, rs, rs, AF.Exp, [sl], [sl], scale=-0.5)
                for h in range(4):
                    ts(fw, fw.dve, y3[:, h, :], y3[:, h, :], mvh[:, h, 0:1], rs[:, h:h + 1], ALU.subtract, ALU.mult, [y, sl], [y])
                gcol = R_RG if mx == 0 else R_MO
                ym = ymix[t % 2]
                tt(fw, fw.pool, ym[:, 0 if mx == 0 else 2, :], y.ap, ta_[:, gcol:gcol + 512], ALU.mult, [y, ta_], [ym])

            def attention(t, hooks):
                aq_ = aqr[t % 2]
                ym = ymix[t % 2]
                kts = list(range(NCT)) if t < NCT else list(range(NT))
                its = [(g, n_, kt) for g in range(2) for n_, kt in enumerate(kts)]
                nk = len(kts)
                hk = list(hooks)
                every = max(1, (len(its) - 4) // max(1, len(hk))) if hk else 0

                def score(idx):
                    g, n_, kt = its[idx]
                    psc = ps_sc[idx % 2]
                    mm(fw, psc.ap, AKT[:, g, kt * 128:(kt + 1) * 128], aq_[:, g * 4:(g + 1) * 4, :].rearrange("p a b -> p (a b)"), True, True, [AKT, aq_], [psc])
                score(0)
                for idx, (g, n_, kt) in enumerate(its):
                    if idx + 1 < len(its):
                        score(idx + 1)
                    psc = ps_sc[idx % 2]
                    p_ = pTr[idx % 3]
                    act(fw, p_.ap, psc.ap, AF.Exp, [psc], [p_])
                    for r in range(4):
                        mm(fw, ps_acc[:, r * 65:(r + 1) * 65], p_[:, r * 128:(r + 1) * 128], AVa[:, kt, g * 65:(g + 1) * 65],
                           (n_ == 0 and r == 0), (n_ == nk - 1), [p_, AVa], [ps_acc], inc=(r == 3), skip_group_check=True)
                    if n_ == nk - 1:
                        sl = sml[t % 2]
                        rd = sl[:, 48 + g * 4:52 + g * 4]
                        acc3 = ps_acc[:, 0:260].rearrange("p (r c) -> p r c", r=4)
                        fw.op(fw.dve, lambda e, rd=rd, acc3=acc3: e.reciprocal(out=rd, in_=acc3[:, :, 64]), [ps_acc], [sl])
                        tt(fw, fw.dve, ym[:, 1, g * 256:(g + 1) * 256].rearrange("p (r c) -> p r c", r=4), acc3[:, :, 0:64],
                           rd.unsqueeze(2).to_broadcast([128, 4, 64]), ALU.mult, [ps_acc, sl], [ym])
                    if hk and every and idx >= 2 and (idx - 2) % every == 0:
                        hk.pop(0)()
                for h_ in hk:
                    h_()

            def merge_stages(t):
                ym, yT_, tg_, x_ = ymix[t % 2], yT[t % 2], tmG[t % 3], xr[t % 2]
                pb = ps_s.ap.bitcast(BF16)
                zsum = zt[0]

                def m_tr(grp):
                    def f():
                        if grp == 0:
                            if t == NCT:
                                load_gms(0)
                            if "YDd" in self.debug:
                                fw.dma(fw.sp, self.YDd[t], ym.ap.rearrange("p a b -> p (a b)"), reads=[ym], ds=ym.ds)
                            fw.dma(fw.sp, x_.ap, self.X[t * 128:(t + 1) * 128, :], reads=[self.xbuf], writes=[x_], ds=x_.ds)
                        n = 8 if grp == 0 else 4
                        for i in range(n):
                            ii = grp * 8 + i
                            tr(fw, pb[:, i * 128:(i + 1) * 128], ym[:, ii // 4, (ii % 4) * 128:(ii % 4 + 1) * 128], self.identb.ap, [ym, self.identb], [ps_s], inc=(i == n - 1))
                        cp(fw, fw.act, yT_[:, grp * 8:grp * 8 + n, :].rearrange("p a b -> p (a b)"), pb[:, 0:n * 128], [ps_s], [yT_])
                    return f

                def m_br(b):
                    def f():
                        banks = ps_of if b % 2 == 0 else ps_ob
                        for n in range(2):
                            for kc in range(4):
                                mm(fw, banks[n].ap, yT_[:, b * 4 + kc, :], WB[:, b, kc, n * 512:(n + 1) * 512], kc == 0, kc == 3, [yT_, WB], [banks[n]])
                        dst = zsum if b == 0 else zt[1]
                        for n in range(2):
                            tt(fw, fw.dve, dst[:, n * 512:(n + 1) * 512], banks[n].ap, tg_[:, b * D + n * 512:b * D + (n + 1) * 512], ALU.mult, [banks[n], tg_], [dst])
                        if b == 1:
                            tt(fw, fw.pool, zsum.ap, zsum.ap, dst.ap, ALU.add, [zsum, dst], [zsum])
                        if b == 2:
                            tt(fw, fw.pool, zb.ap, zsum.ap, dst.ap, ALU.add, [zsum, dst], [zb])
                    return f

                def m_zt():
                    pb8 = pb.rearrange("p (a b) -> p a b", a=8)
                    for kc in range(8):
                        tr(fw, pb8[:, kc, :], zb[:, kc * 128:(kc + 1) * 128], self.identb.ap, [zb, self.identb], [ps_s], inc=(kc == 7))
                    cp(fw, fw.act, zT.ap.rearrange("p a b -> p (a b)"), pb, [ps_s], [zT])

                def m_out():
                    for n in range(2):
                        for kc in range(8):
                            mm(fw, ps_ob[n].ap, zT[:, kc, :], WO[:, kc, n * 512:(n + 1) * 512], kc == 0, kc == 7, [zT, WO], [ps_ob[n]])
                    r_ = rr[0]
                    for n in range(2):
                        tt(fw, fw.dve, r_[:, n * 512:(n + 1) * 512], ps_ob[n].ap, gms[:, n * 512:(n + 1) * 512], ALU.mult, [ps_ob[n], gms], [r_])
                    stt(fw, fw.dve, r_.ap, x_.ap, ALPHA, r_.ap, ALU.mult, ALU.add, [x_, r_], [r_])

                def m_ln():
                    self.ln_affine(rr[0], rr[1], st6, mv, ln1t)
                    fw.dma(fw.sp, self.X1[t * 128:(t + 1) * 128, :], rr[1].ap, reads=[rr[1]], ds=rr[1].ds)
                return [m_tr(0), m_tr(1), m_br(0), m_br(1), m_br(2), m_zt, m_out, m_ln]

            t0 = NCT if last else 0
            loads(t0)
            for t in range(t0, NT):
                if t + 1 < NT:
                    loads(t + 1)
                linattn(t, 0)
                linattn(t, 1)
                attention(t, merge_stages(t - 1) if t - 1 >= t0 else [])
            for f_ in merge_stages(NT - 1):
                f_()
            self.x1buf = Buf("X1")
            fw.barrier()

    def phase_d(self, l):
        nc, fw = self.nc, self.fw
        ps = self.ps
        last = (l == DEPTH - 1)
        with ExitStack() as es:
            import os
            dsk = os.environ.get("D_SKIP", "").split(",")
            WU = fw.sb(es, "WUd", [128, 8, 2 * DFF], BF16)
            wu_src = self.w_up[l].rearrange("(kc p) c -> p kc c", p=128)
            for c0 in range(0, 2 * DFF if "wu" not in dsk else 0, 512):
                fw.dma(fw.pool, WU[:, :, c0:c0 + 512], wu_src[:, :, c0:c0 + 512], writes=[WU], ds=WU.ds, serialize=False)
            WD = fw.sb(es, "WDd", [128, NF, D], BF16)
            wd_src = self.w_down[l].rearrange("(f p) c -> p f c", p=128)
            for f0 in range(0, NF if "wd" not in dsk else 0, 4):
                f1 = min(NF, f0 + 4)
                fw.dma(fw.pool, WD[:, f0:f1, :], wd_src[:, f0:f1, :], writes=[WD], ds=WD.ds, serialize=False)
            cvp = fw.sb(es, "cvpd", [128, 4, 44], F32)
            if "cvp" not in dsk:
                fw.dma(fw.sp, cvp.ap, self.convp[l], writes=[cvp], ds=cvp.ds)
            modt = fw.sb(es, "modd", [128, 2, D], F32)
            ln2t = fw.sb(es, "ln2td", [128, 2, D], F32)
            for i in range(2):
                self.load_bcast(ln2t, ln2t[:, i, :], self.ln2[l, i:i + 1, :])

            def load_mod(j):
                for i in range(2):
                    self.load_bcast(modt, modt[:, i, :], self.MOD[l, j:j + 1, (3 + i) * D:(4 + i) * D], reads=[self.modbuf])

            def load_gate(j):
                self.load_bcast(gmlp, gmlp.ap, self.MOD[l, j:j + 1, 5 * D:6 * D], reads=[self.modbuf])
            gmlp = fw.sb(es, "gmlpd", [128, D], F32)
            load_mod(1)
            load_gate(1)
            x1r = fw.ring(es, "x1rd", [128, D], F32, 2)
            xn = fw.sb(es, "xnd", [128, D], F32)
            hb = fw.sb(es, "hbd", [128, D], BF16)
            HTB = fw.ring(es, "HTBd", [128, 8, 132], BF16, 3)
            cA = [fw.ring(es, "cAd%d" % n_, [128, 128], F32, 3) for n_ in range(2)]
            cB = [fw.ring(es, "cBd%d" % n_, [128, 128], F32, 3) for n_ in range(2)]
            sg = fw.ring(es, "sgd", [128, 128], F32, 3)
            actT_t = fw.ring(es, "actTd", [128, NF, 128], BF16, 2)
            actT = [[Tl(fw, a_[:, i, :], "aT%d" % i) for i in range(NF)] for a_ in actT_t]
            rr = [xn, fw.sb(es, "rrd1", [128, D], F32)]
            st6 = fw.sb(es, "st6d", [128, 2, 6], F32)
            mv = fw.sb(es, "mvd", [128, 4], F32)
            ps_tr = ps[0]
            ps_up = ps[1:5]
            ps_dn = (ps[5], ps[6])
            t0 = NCT if last else 0

            def seq_first(t):
                return t == 0 or t == NCT

            def seq_last(t):
                return t == NCT - 1 or t == NT - 1

            def f1(t):
                if t == NCT:
                    load_mod(0)
                x_ = x1r[t % 2]
                fw.dma(fw.sp, x_.ap, self.X1[t * 128:(t + 1) * 128, :], reads=[self.x1buf], writes=[x_], ds=x_.ds)
                for hh in range(2):
                    fw.op(fw.dve, lambda e, hh=hh: e.bn_stats(out=st6[:, hh, :], in_=x_[:, hh * 512:(hh + 1) * 512]), [x_], [st6])
                fw.op(fw.dve, lambda e: e.bn_aggr(out=mv[:, 0:2], in_=st6.ap.rearrange("p a b -> p (a b)")), [st6], [mv])
                rstd_from(fw, mv[:, 2:3], mv[:, 1:2], [mv], [mv])
                ts(fw, fw.dve, xn.ap, x_.ap, mv[:, 0:1], mv[:, 2:3], ALU.subtract, ALU.mult, [x_, mv], [xn])
                tt(fw, fw.pool, xn.ap, xn.ap, modt[:, 1, :], ALU.mult, [xn, modt], [xn])
                tt(fw, fw.dve, hb.ap, xn.ap, modt[:, 0, :], ALU.add, [xn, modt], [hb])
                pb = ps_tr.ap.bitcast(BF16).rearrange("p (a b) -> p a b", a=8)
                for kc in range(8):
                    tr(fw, pb[:, kc, :], hb[:, kc * 128:(kc + 1) * 128], self.identb.ap, [hb, self.identb], [ps_tr], inc=(kc == 7))
                H_ = HTB[t % 3]
                if "cpH" in dsk:
                    return
                cp(fw, fw.act, H_[:, :, 2:130], pb, [ps_tr], [H_])
                if "halo" in dsk:
                    return
                if seq_first(t):
                    fw.op(fw.dve, lambda e: e.memset(H_[:, :, 1:2], 0.0), [], [H_])
                else:
                    Hp = HTB[(t - 1) % 3]
                    cp(fw, fw.dve, Hp[:, :, 130:131], H_[:, :, 2:3], [H_], [Hp])
                if seq_last(t):
                    fw.op(fw.dve, lambda e: e.memset(H_[:, :, 130:131], 0.0), [], [H_])
                elif t + 1 < NT:
                    Hn = HTB[(t + 1) % 3]
                    cp(fw, fw.dve, Hn[:, :, 1:2], H_[:, :, 129:130], [H_], [Hn])

            def f2(t):
                if t == NCT:
                    load_gate(0)
                H_ = HTB[t % 3]
                x_ = x1r[t % 2]
                aT = actT[t % 2]
                SK = 5

                def down(i):
                    for n in range(2):
                        mm(fw, ps_dn[n].ap, aT[i].ap, WD[:, i, n * 512:(n + 1) * 512], i == 0, i == NF - 1, [aT[i], WD], [ps_dn[n]], inc=True)

                npair = NF if self.d_lim is None else self.d_lim[1]
                for i in range(npair):
                    pu = ps_up[i % 4]
                    for n_, ch in enumerate((NF + i, i)):
                        for kc in range(8):
                            mm(fw, pu[:, n_ * 130:(n_ + 1) * 130], WU[:, kc, ch * 128:(ch + 1) * 128], H_[:, kc, 1:131], kc == 0, kc == 7, [WU, H_], [pu],
                               inc=(kc == 7 and n_ == 1))
                    if i >= SK and npair == NF:
                        down(i - SK)
                    for n_, ch in enumerate((NF + i, i)):
                        u = pu[:, n_ * 130:(n_ + 1) * 130]
                        a_, b_ = cA[n_][i % 3], cB[n_][i % 3]
                        act(fw, a_.ap, u[:, 1:129], AF.Identity, [pu, cvp], [a_], bias=cvp[:, 3, ch:ch + 1], scale=cvp[:, 1, ch:ch + 1])
                        stt(fw, fw.dve, a_.ap, u[:, 0:128], cvp[:, 0, ch:ch + 1], a_.ap, ALU.mult, ALU.add, [pu, cvp, a_], [a_])
                        stt(fw, fw.dve, a_.ap, u[:, 2:130], cvp[:, 2, ch:ch + 1], a_.ap, ALU.mult, ALU.add, [pu, cvp, a_], [a_])
                    s_ = sg[i % 3]
                    act(fw, s_.ap, cA[0][i % 3].ap, AF.Silu, [cA[0][i % 3]], [s_])
                    tt(fw, fw.pool, aT[i].ap, cA[1][i % 3].ap, s_.ap, ALU.mult, [cA[1][i % 3], s_], [aT[i]])
                if npair < NF:
                    return
                for i in range(NF - SK, NF):
                    down(i)
                r_ = rr[0]
                for n in range(2):
                    tt(fw, fw.dve, r_[:, n * 512:(n + 1) * 512], ps_dn[n].ap, gmlp[:, n * 512:(n + 1) * 512], ALU.mult, [ps_dn[n], gmlp], [r_])
                stt(fw, fw.dve, r_.ap, x_.ap, ALPHA, r_.ap, ALU.mult, ALU.add, [x_, r_], [r_])
                self.ln_affine(r_, rr[1], st6, mv, ln2t)
                if last:
                    fw.dma(fw.sp, self.out[(t - NCT) * 128:(t - NCT + 1) * 128, :], rr[1].ap, reads=[rr[1]], ds=rr[1].ds)
                else:
                    fw.dma(fw.sp, self.X[t * 128:(t + 1) * 128, :], rr[1].ap, reads=[rr[1]], writes=[self.xbuf], ds=rr[1].ds)

            dl = self.d_lim
            nt_ = NT if dl is None else t0 + dl[0]
            if "f1" in dsk:
                fw.barrier()
                return
            f1(t0)
            for t in range(t0, nt_):
                if t + 1 < NT:
                    f1(t + 1)
                if dl is None or dl[1] > 0:
                    f2(t)
            fw.barrier()

    def ln_affine(self, src, dst, st6, mv, gbt):
        fw = self.fw
        for hh in range(2):
            fw.op(fw.dve, lambda e, hh=hh: e.bn_stats(out=st6[:, hh, :], in_=src[:, hh * 512:(hh + 1) * 512]), [src], [st6])
        fw.op(fw.dve, lambda e: e.bn_aggr(out=mv[:, 0:2], in_=st6.ap.rearrange("p a b -> p (a b)")), [st6], [mv])
        rstd_from(fw, mv[:, 2:3], mv[:, 1:2], [mv], [mv])
        ts(fw, fw.dve, dst.ap, src.ap, mv[:, 0:1], mv[:, 2:3], ALU.subtract, ALU.mult, [src, mv], [dst])
        tt(fw, fw.pool, dst.ap, dst.ap, gbt[:, 0, :], ALU.mult, [dst, gbt], [dst])
        tt(fw, fw.pool, dst.ap, dst.ap, gbt[:, 1, :], ALU.add, [dst, gbt], [dst])

    def norm_rope(self, t, src_tl, src, nh, gain, dst_tl, dst, sl_tl, ss, tmp_tl, colT, rowT, qk):
        fw = self.fw
        n = nh * 64
        sq = tmp_tl[:, 0:n]
        s3 = src.rearrange("p (h d) -> p h d", h=nh)
        tt(fw, fw.pool, sq, src, src, ALU.mult, [src_tl], [tmp_tl])
        fw.op(fw.dve, lambda e: e.tensor_reduce(out=ss, in_=sq.rearrange("p (h d) -> p h d", h=nh), axis=AX.X, op=ALU.add), [tmp_tl], [sl_tl])
        rstd_from(fw, ss, ss, [sl_tl], [sl_tl], scale=1.0 / 64.0)
        tt(fw, fw.dve, s3, s3, ss.unsqueeze(2).to_broadcast([128, nh, 64]), ALU.mult, [src_tl, sl_tl], [src_tl])
        is_lat = t >= NCT
        gb = gain.unsqueeze(1).to_broadcast([128, nh, 64])
        if not is_lat:
            tt(fw, fw.dve, dst.rearrange("p (h d) -> p h d", h=nh), s3, gb, ALU.mult, [src_tl, qk], [dst_tl])
            return
        tt(fw, fw.pool, s3, s3, gb, ALU.mult, [src_tl, qk], [src_tl])
        j = t - NCT
        s5 = src.rearrange("p (h a two i) -> p h a two i", h=nh, a=2, two=2)
        o5 = tmp_tl[:, 0:n].rearrange("p (h a two i) -> p h a two i", h=nh, a=2, two=2)
        d5 = dst.rearrange("p (h a two i) -> p h a two i", h=nh, a=2, two=2)
        for a, tab in ((0, rowT[:, j, :]), (1, colT.ap)):
            cos_b = tab[:, 0:16].unsqueeze(1).to_broadcast([128, nh, 16])
            sin_b = tab[:, 16:32].unsqueeze(1).to_broadcast([128, nh, 16])
            tabt = rowT if a == 0 else colT
            x1 = s5[:, :, a, 0, :]
            x2 = s5[:, :, a, 1, :]
            tt(fw, fw.dve, o5[:, :, a, 0, :], x2, sin_b, ALU.mult, [src_tl, tabt], [tmp_tl])
            tt(fw, fw.pool, o5[:, :, a, 1, :], x1, sin_b, ALU.mult, [src_tl, tabt], [tmp_tl])
            tt(fw, fw.dve, x1, x1, cos_b, ALU.mult, [src_tl, tabt], [src_tl])
            tt(fw, fw.pool, x2, x2, cos_b, ALU.mult, [src_tl, tabt], [src_tl])
            tt(fw, fw.dve, d5[:, :, a, 0, :], x1, o5[:, :, a, 0, :], ALU.subtract, [src_tl, tmp_tl], [dst_tl])
            tt(fw, fw.dve, d5[:, :, a, 1, :], x2, o5[:, :, a, 1, :], ALU.add, [src_tl, tmp_tl], [dst_tl])


def make_consts():
    c = np.zeros((128, 1024), np.float32)
    p = np.arange(128)
    c[:, 0:128] = np.eye(128, dtype=np.float32)
    c[:, 128:256] = (p[:, None] <= p[None, :])
    c[:, 256:384] = (p[:, None] >= p[None, :])
    c[:, 384:512] = 1.0
    c[:, 512] = p + 1
    c[:, 513] = 128 - p
    c[:, 514] = -(p + 1)
    c[:, 515] = -(128 - p)
    c[:, 516] = p % 64
    c[:, 517:533] = np.arange(16)[None, :]
    return c


def host_inputs(inputs, b, L=DEPTH):
    f = lambda a: np.ascontiguousarray(np.asarray(a, dtype=np.float32))
    inputs = {k: (np.asarray(v)[:L] if k not in ('x', 'c', 'ctx', 'c_ctx') else v) for k, v in inputs.items()}
    m = {}
    m["xin"] = f(np.concatenate([inputs["ctx"][b], inputs["x"][b]], axis=0))
    cc = np.stack([np.asarray(inputs["c"][b]), np.asarray(inputs["c_ctx"])], axis=-1)
    m["ccT"] = f(cc.reshape(8, 128, 2).transpose(1, 0, 2))
    m["w_mod"] = f(inputs["w_mod"])
    m["b_mod"] = f(inputs["b_mod"])
    m["w_in"] = f(inputs["w_in"])
    m["b_in"] = f(inputs["b_in"])
    fmcols = np.concatenate([np.arange(O_RQ, O_RQ + 256), np.arange(O_RK, O_RK + 256), np.arange(O_MQ, O_MQ + 256), np.arange(O_MK, O_MK + 256)])
    m["b_in_fm"] = f(np.asarray(inputs["b_in"])[:, fmcols].reshape(L, 8, 128).transpose(0, 2, 1))
    m["decay"] = f(np.asarray(inputs["ret_decay_logit"]).reshape(L, 8))
    m["ret_gn"] = f(inputs["ret_gn_g"])
    m["qn_g"] = f(inputs["attn_qn_g"])
    m["kn_g"] = f(inputs["attn_kn_g"])
    m["m_gn"] = f(inputs["mlstm_gn_g"])
    m["w_br"] = f(np.stack([inputs["w_br_ret"], inputs["w_br_att"], inputs["w_br_mlstm"]], axis=1))
    m["w_out"] = f(inputs["w_out"])
    m["ln1"] = f(np.stack([inputs["ln1_g"], inputs["ln1_b"]], axis=1))
    m["w_up"] = f(inputs["w_up"])
    cw = np.asarray(inputs["conv_w"])
    cb = np.asarray(inputs["conv_b"])
    cpk = np.concatenate([cw, cb[:, None, :]], axis=1)
    m["convp"] = f(cpk.reshape(L, 4, 44, 128).transpose(0, 3, 1, 2))
    m["w_down"] = f(inputs["w_down"])
    m["ln2"] = f(np.stack([inputs["ln2_g"], inputs["ln2_b"]], axis=1))
    m["consts"] = make_consts()
    return m


_PROG = {}


def kernel(**inputs):
    if "p" not in _PROG:
        _PROG["p"] = Prog()
    prog = _PROG["p"]
    in_maps = [host_inputs(inputs, b) for b in range(NCORES)]
    res = run_bass_kernel_spmd(prog.nc, in_maps, core_ids=list(range(NCORES)))
    out = np.stack([np.asarray(r["out"]).reshape(32 * 128, D) for r in res.results], axis=0)
    return out.astype(np.float32)
```

```python
import math
import numpy as np
from contextlib import ExitStack
import concourse.bass as bass
import concourse.mybir as mybir
from concourse.bass_utils import run_bass_kernel_spmd

F32 = mybir.dt.float32
BF16 = mybir.dt.bfloat16
I32 = mybir.dt.int32
AF = mybir.ActivationFunctionType
ALU = mybir.AluOpType
AX = mybir.AxisListType

D = 1024
DEPTH = 4
NT = 34
NCT = 2
T = NT * 128
DFF = 2816
NF = 22
EPS = 1e-6
ALPHA = (2.0 * DEPTH) ** 0.25
NCORES = 4

O_RQ, O_RK, O_RV, O_RG = 0, 256, 512, 1024
O_AQ, O_AK, O_AV = 1536, 2048, 2176
O_MQ, O_MK, O_MV, O_MO, O_MI, O_MF, O_GATE = 2304, 2560, 2816, 3328, 3840, 3848, 3856
N_IN = 6928

FM_PIECES = [(0, O_RQ, 256), (256, O_RK, 256), (512, O_MQ, 256), (768, O_MK, 256)]
TMW0 = 1024
TM_GROUPS = [
    ("MG", [(O_MI, 16)]),
    ("RV", [(O_RV, 512)]),
    ("RG", [(O_RG, 512)]),
    ("AQ", [(O_AQ, 512)]),
    ("AKV", [(O_AK, 128), (O_AV, 128)]),
    ("MV", [(O_MV, 512)]),
    ("MO", [(O_MO, 512)]),
] + [("G%d" % i, [(O_GATE + 512 * i, 512)]) for i in range(6)]
TM_OFF = {}
_o = 0
for _n, _p in TM_GROUPS:
    TM_OFF[_n] = _o
    _o += sum(n for _, n in _p)
TM_COLS = _o
W_COLS = TMW0 + TM_COLS

R_RV, R_RG, R_AV, R_RK, R_MK, R_MV, R_MO, R_MG = 0, 1024, 1536, 1666, 1922, 2178, 3210, 3722
R_COLS = 6794
SM_COLS = 16
R_SPLIT = R_MG


class Sem:
    def __init__(self, h, name):
        self.h = h
        self.name = name
        self.owner = None
        self.total = 0


class Buf:
    __slots__ = ("w", "r", "name")

    def __init__(self, name=""):
        self.w = {}
        self.r = {}
        self.name = name


class Tl:
    def __init__(self, fw, ap, name, buf=None):
        self.fw = fw
        self.ap = ap
        self.name = name
        self.buf = buf if buf is not None else Buf(name)
        self._ds = None

    @property
    def ds(self):
        if self._ds is None:
            self._ds = self.fw.pool_sem()
        return self._ds

    def __getitem__(self, idx):
        return self.ap[idx]


class Eng:
    def __init__(self, fw, name, eng):
        self.fw = fw
        self.name = name
        self.eng = eng
        self.sem = fw.new_sem("c_" + name)
        self.sem.owner = self
        self.cnt = 0
        self.waited = {}

    def wait(self, sem, val):
        if val <= 0:
            return
        if self.waited.get(sem, 0) >= val:
            return
        if sem.owner is not None:
            assert val <= sem.owner.cnt, ("wait on unissued instr", self.name, sem.name, val, sem.owner.cnt)
        else:
            assert val <= sem.total
        self.eng.wait_ge(sem.h, val)
        self.waited[sem] = val


class FW:
    def __init__(self, nc):
        self.nc = nc
        self.es = ExitStack()
        self.nsem = 0
        self.pe = Eng(self, "pe", nc.tensor)
        self.act = Eng(self, "act", nc.scalar)
        self.dve = Eng(self, "dve", nc.vector)
        self.pool = Eng(self, "pool", nc.gpsimd)
        self.sp = Eng(self, "sp", nc.sync)
        self.engs = [self.pe, self.act, self.dve, self.pool, self.sp]
        self.dsems = []
        self.swq = []
        self.ndram = 0
        self.sem_pool = []
        self.sem_idx = 0
        self.pool_base = 0
        self.nsb = 0

    def pool_sem(self):
        if self.sem_idx == len(self.sem_pool):
            self.sem_pool.append(self.new_sem("dp%d" % self.sem_idx))
        s = self.sem_pool[self.sem_idx]
        self.sem_idx += 1
        return s

    def reset_pool(self):
        self.sem_idx = self.pool_base

    def new_sem(self, name):
        h = self.es.enter_context(self.nc.semaphore(name + "_%d" % self.nsem))
        self.nsem += 1
        s = Sem(h, name)
        return s

    def sb(self, es, name, shape, dtype):
        self.nsb += 1
        t = es.enter_context(self.nc.sbuf_tensor("%s_%d" % (name, self.nsb), list(shape), dtype))
        return Tl(self, t[:], name)

    def ring(self, es, name, shape, dtype, n):
        return [self.sb(es, "%s%d" % (name, i), shape, dtype) for i in range(n)]

    def dram(self, name, shape, dtype, kind="Internal"):
        t = self.nc.dram_tensor(name, list(shape), dtype, kind=kind)
        return t.ap()

    def _deps(self, E, reads, writes, skip_sem=None):
        for b in reads:
            b = b.buf if isinstance(b, Tl) else b
            for sem, v in b.w.items():
                if sem is E.sem and E is self.pe:
                    continue
                E.wait(sem, v)
        for b in writes:
            b = b.buf if isinstance(b, Tl) else b
            for sem, v in list(b.w.items()) + list(b.r.items()):
                if sem is E.sem or sem is skip_sem:
                    continue
                E.wait(sem, v)

    def _mark(self, sem, tok, reads, writes):
        for b in reads:
            b = b.buf if isinstance(b, Tl) else b
            if b.r.get(sem, 0) < tok:
                b.r[sem] = tok
        for b in writes:
            b = b.buf if isinstance(b, Tl) else b
            b.w = {sem: tok}
            b.r = {}

    def op(self, E, fn, reads=(), writes=(), inc=True):
        self._deps(E, reads, writes)
        ins = fn(E.eng)
        if inc:
            E.cnt += 1
            ins.then_inc(E.sem.h, 1)
            tok = E.cnt
        else:
            tok = E.cnt + 1
        self._mark(E.sem, tok, reads, writes)
        return ins

    def dma(self, Q, out, in_, reads=(), writes=(), ds=None, serialize=True, **kw):
        self._deps(Q, reads, writes, skip_sem=(None if serialize else ds))
        if serialize and ds.total > 0:
            Q.wait(ds, ds.total)
        if Q is self.pool:
            while len(self.swq) >= 2:
                s_, v_ = self.swq.pop(0)
                Q.wait(s_, v_)
        ins = Q.eng.dma_start(out=out, in_=in_, **kw)
        ds.total += 16
        if Q is self.pool:
            self.swq.append((ds, ds.total))
        ins.then_inc(ds.h, 16)
        if ds not in self.dsems:
            self.dsems.append(ds)
        self._mark(ds, ds.total, reads, writes)
        return ins

    def barrier(self):
        for E in self.engs:
            for P in self.engs:
                if P is not E:
                    E.wait(P.sem, P.cnt)
            for ds in self.dsems:
                E.wait(ds, ds.total)


def tt(fw, E, out, in0, in1, op, reads, writes):
    return fw.op(E, lambda e: e.tensor_tensor(out=out, in0=in0, in1=in1, op=op), reads, writes)


def ts(fw, E, out, in0, s1, s2, op0, op1, reads, writes):
    if op1 is None:
        return fw.op(E, lambda e: e.tensor_scalar(out=out, in0=in0, scalar1=s1, scalar2=None, op0=op0), reads, writes)
    return fw.op(E, lambda e: e.tensor_scalar(out=out, in0=in0, scalar1=s1, scalar2=s2, op0=op0, op1=op1), reads, writes)


def stt(fw, E, out, in0, scalar, in1, op0, op1, reads, writes):
    return fw.op(E, lambda e: e.scalar_tensor_tensor(out=out, in0=in0, scalar=scalar, in1=in1, op0=op0, op1=op1), reads, writes)


def act(fw, out, in_, func, reads, writes, bias=None, scale=None):
    kw = {}
    if bias is not None:
        kw["bias"] = bias
    if scale is not None:
        kw["scale"] = scale
    return fw.op(fw.act, lambda e: e.activation(out=out, in_=in_, func=func, **kw), reads, writes)


def cp(fw, E, out, in_, reads, writes):
    if E is fw.act:
        return fw.op(E, lambda e: e.copy(out=out, in_=in_), reads, writes)
    return fw.op(E, lambda e: e.tensor_copy(out=out, in_=in_), reads, writes)


def mm(fw, out, lhsT, rhs, start, stop, reads, writes, inc=None, **kw):
    if inc is None:
        inc = stop
    return fw.op(fw.pe, lambda e: e.matmul(out, lhsT=lhsT, rhs=rhs, start=start, stop=stop, **kw), reads, writes, inc=inc)


def tr(fw, out, in_, ident, reads, writes, inc=True):
    return fw.op(fw.pe, lambda e: e.transpose(out, in_, ident), reads, writes, inc=inc)


def bcast_rows(ap2d, nparts):
    return ap2d.to_broadcast([nparts, ap2d.shape[-1]])


def rstd_from(fw, out, in_, reads, writes, scale=1.0):
    act(fw, out, in_, AF.Ln, reads, writes, bias=fw.eps_t[:, 0:1] if in_.shape[0] == 128 else fw.eps_t[0:in_.shape[0], 0:1], scale=scale)
    act(fw, out, out, AF.Exp, writes, writes, scale=-0.5)


class Prog:
    def __init__(self, n_layers=DEPTH, debug=None, stop_after=None, a_lim=None, skip="", d_lim=None):
        self.a_lim = a_lim
        self.skip = skip
        self.d_lim = d_lim
        self.n_layers = n_layers
        self.debug = debug or []
        self.stop_after = stop_after
        self.nc = bass.Bass("TRN2", target_bir_lowering=False)
        self.fw = FW(self.nc)
        self.inputs = {}
        self.build()

    def din(self, name, shape, dtype=F32):
        ap = self.fw.dram(name, shape, dtype, kind="ExternalInput")
        self.inputs[name] = ap
        return ap

    def dscr(self, name, shape, dtype):
        kind = "ExternalOutput" if name in self.debug else "Internal"
        return self.fw.dram(name, shape, dtype, kind=kind)

    def build(self):
        nc, fw = self.nc, self.fw
        L = self.n_layers
        self.xin = self.din("xin", [T, D])
        self.ccT = self.din("ccT", [128, 8, 2])
        self.w_mod = self.din("w_mod", [L, D, 6 * D])
        self.b_mod = self.din("b_mod", [L, 6 * D])
        self.w_in = self.din("w_in", [L, D, N_IN])
        self.b_in = self.din("b_in", [L, N_IN])
        self.b_in_fm = self.din("b_in_fm", [L, 128, 8])
        self.decay = self.din("decay", [L, 8])
        self.ret_gn = self.din("ret_gn", [L, 512])
        self.qn_g = self.din("qn_g", [L, 64])
        self.kn_g = self.din("kn_g", [L, 64])
        self.m_gn = self.din("m_gn", [L, 512])
        self.w_br = self.din("w_br", [L, 3, 512, D])
        self.w_out = self.din("w_out", [L, D, D])
        self.ln1 = self.din("ln1", [L, 2, D])
        self.w_up = self.din("w_up", [L, D, 2 * DFF])
        self.convp = self.din("convp", [L, 128, 4, 44])
        self.w_down = self.din("w_down", [L, DFF, D])
        self.ln2 = self.din("ln2", [L, 2, D])
        self.consts = self.din("consts", [128, 1024])
        self.out = self.fw.dram("out", [32 * 128, D], F32, kind="ExternalOutput")
        self.X = self.dscr("X", [T, D], F32)
        self.X1 = self.dscr("X1", [T, D], F32)
        self.MOD = self.dscr("MOD", [L, 2, 6 * D], F32)
        self.TMd = self.dscr("TMd", [NT, 128, R_COLS], BF16)
        self.SMd = self.dscr("SMd", [NT, 128, SM_COLS], F32)
        self.FMd = self.dscr("FMd", [NT, 128, 1024], BF16)
        self.AQd = self.dscr("AQd", [NT, 64, 1024], BF16)
        self.AKd = self.dscr("AKd", [64, 2, T], BF16)
        self.ROPEd = self.dscr("ROPEd", [64, 32], F32)
        self.RETCd = self.dscr("RETCd", [L, 128, 16], F32)
        self.SFd = self.dscr("SFd", [NT, 128, 4 * 258], BF16)
        self.SBd = self.dscr("SBd", [NT, 128, 4 * 258], BF16)
        self.YDd = self.dscr("YDd", [NT, 128, 1536], BF16)

        with ExitStack() as es0:
            self.setup_globals(es0)
            fw.barrier()
            fw.pool_base = fw.sem_idx
            if self.stop_after == "S":
                return self.finish()
            for l in range(self.n_layers):
                if "A" not in self.skip:
                    self.phase_a(l)
                fw.barrier()
                fw.reset_pool()
                if self.stop_after == "A%d" % l:
                    return self.finish()
                if "B" not in self.skip:
                    self.phase_b(l)
                else:
                    self.x1buf = Buf("X1")
                fw.barrier()
                fw.reset_pool()
                if self.stop_after == "B%d" % l:
                    return self.finish()
                self.phase_d(l)
                fw.barrier()
                fw.reset_pool()
                if self.stop_after == "D%d" % l:
                    return self.finish()
            self.finish()

    def finish(self):
        fw = self.fw
        fw.barrier()
        for ds in fw.dsems:
            fw.sp.wait(ds, ds.total)

    def setup_globals(self, es):
        nc, fw = self.nc, self.fw
        self.cst = fw.sb(es, "cst", [128, 1024], F32)
        fw.dma(fw.sp, self.cst.ap, self.consts, writes=[self.cst], ds=self.cst.ds)
        self.ident_f = self.cst[:, 0:128]
        self.triF = self.cst[:, 128:256]
        self.triB = self.cst[:, 256:384]
        self.ones_f = self.cst[:, 384:512]
        self.identb = fw.sb(es, "identb", [128, 128], BF16)
        cp(fw, fw.dve, self.identb.ap, self.ident_f, [self.cst], [self.identb])
        self.eps_t = fw.sb(es, "eps_t", [128, 1], F32)
        fw.eps_t = self.eps_t
        fw.op(fw.dve, lambda e: e.memset(self.eps_t.ap, EPS), [], [self.eps_t])
        self.onesb = fw.sb(es, "onesb", [128, 128], BF16)
        fw.op(fw.dve, lambda e: e.memset(self.onesb.ap, 1.0), [], [self.onesb])
        self.one_t = fw.sb(es, "one_t", [128, 1], F32)
        fw.op(fw.dve, lambda e: e.memset(self.one_t.ap, 1.0), [], [self.one_t])
        self.maskF = fw.sb(es, "maskF", [128, 4, 128], BF16)
        self.maskB = fw.sb(es, "maskB", [128, 4, 128], BF16)
        for h in range(4):
            cp(fw, fw.dve, self.maskF[:, h, :], self.triF, [self.cst], [self.maskF])
            cp(fw, fw.dve, self.maskB[:, h, :], self.triB, [self.cst], [self.maskB])
        self.ps = []
        for i in range(8):
            t = es.enter_context(nc.psum_tensor("psb%d" % i, [128, 512], F32))
            self.ps.append(Tl(fw, t[:], "psb%d" % i))
        xds = fw.new_sem("xcopy")
        fw.dma(fw.sp, self.X, self.xin, ds=xds)
        self.xbuf = Buf("Xall")
        self.xbuf.w = {xds: xds.total}
        self.compute_mod(es)
        self.compute_rope(es)

    def compute_mod(self, es0):
        nc, fw = self.nc, self.fw
        with ExitStack() as es:
            cc = fw.sb(es, "cc", [128, 8, 2], F32)
            sc = fw.sb(es, "sc", [128, 8, 2], F32)
            fw.dma(fw.sp, cc.ap, self.ccT, writes=[cc], ds=cc.ds)
            act(fw, sc.ap, cc.ap, AF.Silu, [cc], [sc])
            wring = fw.ring(es, "wm", [128, 8, 512], F32, 3)
            bm = fw.sb(es, "bm", [2, 6 * D], F32)
            orow = fw.ring(es, "orow", [2, 6 * D], F32, 2)
            k = 0
            for l in range(self.n_layers):
                fw.dma(fw.sp, bm.ap, self.b_mod[l:l + 1, :].to_broadcast([2, 6 * D]), writes=[bm], ds=bm.ds)
                orw = orow[l % 2]
                for g in range(12):
                    wt = wring[k % 3]
                    src = self.w_mod[l].rearrange("(kc p) c -> p kc c", p=128)[:, :, g * 512:(g + 1) * 512]
                    fw.dma(fw.sp, wt.ap, src, writes=[wt], ds=wt.ds)
                    pst = self.ps[k % 2]
                    for kc in range(8):
                        mm(fw, pst[0:2, :], sc[:, kc, :], wt[:, kc, :], kc == 0, kc == 7, [sc, wt], [pst])
                    tt(fw, fw.dve, orw[:, g * 512:(g + 1) * 512], pst[0:2, :], bm[:, g * 512:(g + 1) * 512], ALU.add, [pst, bm], [orw])
                    k += 1
                for ch in (1, 4):
                    ts(fw, fw.dve, orw[:, ch * D:(ch + 1) * D], orw[:, ch * D:(ch + 1) * D], 1.0, None, ALU.add, None, [orw], [orw])
                fw.dma(fw.sp, self.MOD[l], orw.ap, reads=[orw], ds=orw.ds)
            self.modbuf = Buf("MOD")
            for o in orow:
                self.modbuf.w[o.ds] = o.ds.total
            fw.barrier()

    def compute_rope(self, es0):
        nc, fw = self.nc, self.fw
        with ExitStack() as es:
            tl = fw.sb(es, "rp", [128, 8, 32], F32)
            itl = fw.sb(es, "rpi", [128, 32], I32)
            c = self.cst
            fr, u, r, fx, ang = (tl[:, i, :] for i in range(5))
            nidx = c[:, 516:517]
            act(fw, fr[:, 0:16], c[:, 517:533], AF.Exp, [c], [tl], scale=-math.log(10000.0) / 16.0)
            ts(fw, fw.dve, ang[:, 0:16], fr[:, 0:16], nidx, 1.0 / (2 * math.pi), ALU.mult, ALU.mult, [tl, c], [tl])
            ts(fw, fw.dve, u[:, 0:16], ang[:, 0:16], 0.25, None, ALU.add, None, [tl], [tl])
            cp(fw, fw.dve, u[:, 16:32], ang[:, 0:16], [tl], [tl])
            cp(fw, fw.dve, itl.ap, u, [tl], [itl])
            cp(fw, fw.dve, r, itl.ap, [itl], [tl])
            tt(fw, fw.dve, r, u, r, ALU.subtract, [tl], [tl])
            ts(fw, fw.dve, fx, r, 0.5, None, ALU.is_gt, None, [tl], [tl])
            tt(fw, fw.dve, r, r, fx, ALU.subtract, [tl], [tl])
            ts(fw, fw.dve, fx, r, -0.5, None, ALU.is_lt, None, [tl], [tl])
            tt(fw, fw.dve, r, r, fx, ALU.add, [tl], [tl])
            res = tl[:, 5, :]
            act(fw, res, r, AF.Sin, [tl], [tl], scale=2 * math.pi)
            fw.dma(fw.sp, self.ROPEd, tl[0:64, 5, :], reads=[tl], ds=tl.ds)
            self.ropebuf = Buf("rope")
            self.ropebuf.w = {tl.ds: tl.ds.total}
            fw.barrier()

    def load_bcast(self, dst_tl, dst_ap, src_row_ap, q=None, reads=()):
        fw = self.fw
        q = q or fw.sp
        n = src_row_ap.shape[-1]
        fw.dma(q, dst_ap, src_row_ap.to_broadcast([dst_ap.shape[0], n]), reads=list(reads), writes=[dst_tl], ds=dst_tl.ds, serialize=False)

    def phase_a(self, l):
        nc, fw = self.nc, self.fw
        ps = self.ps
        with ExitStack() as es:
            W = fw.sb(es, "Wa", [128, 8, W_COLS], BF16)
            wsrc = self.w_in[l].rearrange("(kc p) c -> p kc c", p=128)
            pieces = list(FM_PIECES)
            for name, pl_ in TM_GROUPS:
                o = TMW0 + TM_OFF[name]
                for (src, n) in pl_:
                    pieces.append((o, src, n))
                    o += n
            for (dst, src, n) in pieces:
                fw.dma(fw.pool, W[:, :, dst:dst + n], wsrc[:, :, src:src + n], writes=[W], ds=W.ds, serialize=False)
            BB = fw.sb(es, "BBa", [128, TM_COLS - 16], BF16)
            BG = fw.sb(es, "BGa", [128, 16], F32)
            for name, pl_ in TM_GROUPS:
                o = TM_OFF[name]
                for (src, n) in pl_:
                    row = self.b_in[l:l + 1, src:src + n]
                    if name == "MG":
                        self.load_bcast(BG, BG[:, o:o + n], row)
                    else:
                        self.load_bcast(BB, BB[:, o - 16:o - 16 + n], row, q=fw.pool)
                    o += n
            bfm = fw.sb(es, "bfm", [128, 8], F32)
            fw.dma(fw.sp, bfm.ap, self.b_in_fm[l], writes=[bfm], ds=bfm.ds)
            modt = fw.sb(es, "moda", [128, 2, D], F32)

            def load_mod(j):
                for ch in range(2):
                    self.load_bcast(modt, modt[:, ch, :], self.MOD[l, j:j + 1, ch * D:(ch + 1) * D], reads=[self.modbuf])
            load_mod(1)
            gn = fw.sb(es, "gna", [128, 2, 512], F32)
            self.load_bcast(gn, gn[:, 0, :], self.ret_gn[l:l + 1, :])
            self.load_bcast(gn, gn[:, 1, :], self.m_gn[l:l + 1, :])
            qk = fw.sb(es, "qka", [128, 2, 64], F32)
            self.load_bcast(qk, qk[:, 0, :], self.qn_g[l:l + 1, :])
            self.load_bcast(qk, qk[:, 1, :], self.kn_g[l:l + 1, :])
            ts(fw, fw.dve, qk[:, 0, :], qk[:, 0, :], 0.125, None, ALU.mult, None, [qk], [qk])
            colT = fw.sb(es, "colT", [128, 32], F32)
            rowT = fw.sb(es, "rowT", [128, 32, 32], F32)
            for hf in range(2):
                fw.dma(fw.sp, colT[hf * 64:(hf + 1) * 64, :], self.ROPEd, reads=[self.ropebuf], writes=[colT], ds=colT.ds, serialize=False)
                src = self.ROPEd.rearrange("(j two) c -> two j c", two=2)[hf:hf + 1]
                fw.dma(fw.sp, rowT[hf * 64:(hf + 1) * 64, :, :], src.to_broadcast([64, 32, 32]), reads=[self.ropebuf], writes=[rowT], ds=rowT.ds, serialize=False)
            dk = fw.sb(es, "dka", [128, 8, 8], F32)
            c = self.cst
            self.load_bcast(dk, dk[:, 0, :], self.decay[l:l + 1, :])
            act(fw, dk[:, 1, :], dk[:, 0, :], AF.Exp, [dk], [dk], scale=-1.0)
            act(fw, dk[:, 2, :], dk[:, 1, :], AF.Ln, [dk], [dk], bias=self.one_t[:, 0:1])
            rEA = dk[:, 3, :]
            rEB = dk[:, 4, :]
            rEE = dk[:, 5, :]
            act(fw, rEA[:, 0:4], dk[:, 2, 0:4], AF.Exp, [dk, c], [dk], scale=c[:, 512:513])
            act(fw, rEA[:, 4:8], dk[:, 2, 4:8], AF.Exp, [dk, c], [dk], scale=c[:, 513:514])
            act(fw, rEB[:, 0:4], dk[:, 2, 0:4], AF.Exp, [dk, c], [dk], scale=c[:, 514:515])
            act(fw, rEB[:, 4:8], dk[:, 2, 4:8], AF.Exp, [dk, c], [dk], scale=c[:, 515:516])
            act(fw, rEE, dk[:, 2, :], AF.Exp, [dk], [dk], scale=-128.0)
            retc = fw.sb(es, "retc", [128, 16], F32)
            cp(fw, fw.dve, retc[:, 0:8], rEB, [dk], [retc])
            for hf in range(2):
                cp(fw, fw.dve, retc[hf * 64:(hf + 1) * 64, 8:12].rearrange("p (d j) -> p d j", d=2),
                   rEE[hf * 64:(hf + 1) * 64, :].rearrange("p (d j two) -> p d j two", d=2, j=2)[:, :, :, hf], [dk], [retc])
            fw.dma(fw.sp, self.RETCd[l], retc.ap, reads=[retc], ds=retc.ds)

            xt = fw.sb(es, "xta", [128, D], F32)
            st6 = fw.sb(es, "st6a", [128, 2, 6], F32)
            mv = fw.sb(es, "mva", [128, 4], F32)
            xn = fw.sb(es, "xna", [128, D], F32)
            xm = fw.sb(es, "xma", [128, D], BF16)
            xmT = fw.ring(es, "xmTa", [128, 8, 128], BF16, 2)
            tmA = fw.sb(es, "tmAa", [128, R_SPLIT], BF16)
            tmB = fw.sb(es, "tmBa", [128, R_COLS - R_SPLIT], BF16)
            sm_ = fw.sb(es, "smra", [128, SM_COLS], F32)
            fm_ = fw.sb(es, "fmra", [128, 8, 128], BF16)
            aq_ = fw.sb(es, "aqra", [64, 8, 128], BF16)
            ak_ = fw.sb(es, "akra", [64, 2, 128], BF16)
            tmpA = fw.ring(es, "tmpAa", [128, 512], F32, 2)
            qpriv = fw.sb(es, "qpriva", [128, 512], F32)
            kpriv = fw.sb(es, "kpriva", [128, 128], F32)
            tmpB = fw.ring(es, "tmpBa", [128, 512], F32, 2)
            qb_ = fw.sb(es, "qba", [128, 640], BF16)
            g_ = fw.sb(es, "gtsa", [128, 64], F32)
            g16 = fw.sb(es, "g16a", [128, 16], BF16)
            sl_ = fw.sb(es, "smla", [128, 32], F32)
            fw.op(fw.dve, lambda e: e.memset(tmA[:, R_AV:R_AV + 130].rearrange("p (g c) -> p g c", g=2)[:, :, 64:65], 1.0), [], [tmA])

            ps_tr, ps_fm, ps_sm, ps_aq = ps[0], ps[1], ps[2], ps[3]
            ps_tm = ps[4:8]
            self._tmk = 0

            def s1(t):
                if t == NCT:
                    load_mod(0)
                fw.dma(fw.sp, xt.ap, self.X[t * 128:(t + 1) * 128, :], reads=[self.xbuf], writes=[xt], ds=xt.ds)
                for hh in range(2):
                    fw.op(fw.dve, lambda e, hh=hh: e.bn_stats(out=st6[:, hh, :], in_=xt[:, hh * 512:(hh + 1) * 512]), [xt], [st6])
                fw.op(fw.dve, lambda e: e.bn_aggr(out=mv[:, 0:2], in_=st6.ap.rearrange("p a b -> p (a b)")), [st6], [mv])
                rstd_from(fw, mv[:, 2:3], mv[:, 1:2], [mv], [mv])
                ts(fw, fw.dve, xn.ap, xt.ap, mv[:, 0:1], mv[:, 2:3], ALU.subtract, ALU.mult, [xt, mv], [xn])
                tt(fw, fw.pool, xn.ap, xn.ap, modt[:, 1, :], ALU.mult, [xn, modt], [xn])
                tt(fw, fw.dve, xm.ap, xn.ap, modt[:, 0, :], ALU.add, [xn, modt], [xm])

            def s2(t):
                xT_ = xmT[t % 2]
                pb = ps_tr.ap.bitcast(BF16).rearrange("p (a b) -> p a b", a=8)
                for kc in range(8):
                    tr(fw, pb[:, kc, :], xm[:, kc * 128:(kc + 1) * 128], self.identb.ap, [xm, self.identb], [ps_tr], inc=(kc == 7))
                cp(fw, fw.act, xT_.ap.rearrange("p a b -> p (a b)"), ps_tr.ap.bitcast(BF16), [ps_tr], [xT_])

            def tm_matmul(t, name, n):
                xT_ = xmT[t % 2]
                pst = ps_tm[self._tmk % len(ps_tm)]
                self._tmk += 1
                o = TMW0 + TM_OFF[name]
                for kc in range(8):
                    mm(fw, pst[:, 0:n], xT_[:, kc, :], W[:, kc, o:o + n], kc == 0, kc == 7, [xT_, W], [pst])
                return pst

            def bias_of(name, n, off=0):
                o = TM_OFF[name] - 16 + off
                return BB[:, o:o + n]

            def s3(t, mid_hook=None):
                xT_ = xmT[t % 2]
                for half in range(2):
                    for i4 in range(4):
                        i = half * 4 + i4
                        for kc in range(8):
                            mm(fw, ps_fm[:, i4 * 128:(i4 + 1) * 128], W[:, kc, i * 128:(i + 1) * 128], xT_[:, kc, :], kc == 0, kc == 7, [xT_, W], [ps_fm],
                               inc=(kc == 7 and i4 == 3))
                    for i4 in range(4):
                        i = half * 4 + i4
                        sc_ = 0.125 if i in (2, 3, 6, 7) else 1.0
                        ts(fw, fw.dve, fm_[:, i, :], ps_fm[:, i4 * 128:(i4 + 1) * 128], bfm[:, i:i + 1], sc_, ALU.add, ALU.mult, [ps_fm, bfm], [fm_])
                fw.dma(fw.sp, self.FMd[t], fm_.ap.rearrange("p a b -> p (a b)"), reads=[fm_], ds=fm_.ds)
                if lim is not None and len(lim) > 2 and lim[2] <= 1:
                    return
                pbk = ps_sm.ap.bitcast(BF16)
                for n_, i in enumerate((2, 3, 6, 7)):
                    tr(fw, pbk[:, 512 + n_ * 128:512 + (n_ + 1) * 128], fm_[:, i, :], self.identb.ap, [fm_, self.identb], [ps_sm], inc=(n_ == 3))
                cp(fw, fw.act, tmA[:, R_RK:R_RK + 512], pbk[:, 512:1024], [ps_sm], [tmA])
                if lim is not None and len(lim) > 2 and lim[2] <= 2:
                    return
                pst = tm_matmul(t, "MG", 16)
                tt(fw, fw.dve, g_[:, 0:16], pst[:, 0:16], BG.ap, ALU.add, [pst, BG], [g_])
                e_ = g_[:, 16:24]
                sp_ = g_[:, 24:32]
                act(fw, e_, g_[:, 8:16], AF.Exp, [g_], [g_], scale=-1.0)
                act(fw, sp_, e_, AF.Ln, [g_], [g_], bias=self.one_t[:, 0:1])
                hi32, lo32 = g_[:, 56:64], g_[:, 16:24]
                cp(fw, fw.dve, g16[:, 0:8], sp_, [g_], [g16])
                cp(fw, fw.dve, hi32, g16[:, 0:8], [g16], [g_])
                tt(fw, fw.dve, lo32, sp_, hi32, ALU.subtract, [g_], [g_])
                cp(fw, fw.dve, g16[:, 8:16], lo32, [g_], [g16])
                for part in range(2):
                    o = part * 8
                    mm(fw, ps_sm[:, 0:4], self.maskF[:, 0, :], g16[:, o:o + 4], part == 0, part == 1, [g16, self.maskF], [ps_sm], inc=False)
                for part in range(2):
                    o = part * 8
                    mm(fw, ps_sm[:, 4:8], self.maskB[:, 0, :], g16[:, o + 4:o + 8], part == 0, part == 1, [g16, self.maskB], [ps_sm], inc=False)
                for part in range(2):
                    o = part * 8
                    mm(fw, ps_sm[:, 8:16], self.onesb.ap, g16[:, o:o + 8], part == 0, part == 1, [g16, self.onesb], [ps_sm], inc=(part == 1))
                ta = g_[:, 32:40]
                EA = g_[:, 40:48]
                tt(fw, fw.dve, ta, g_[:, 0:8], ps_sm[:, 0:8], ALU.add, [g_, ps_sm], [g_])
                act(fw, EA, ta, AF.Exp, [g_], [g_])
                act(fw, sm_[:, 0:8], ps_sm[:, 0:8], AF.Exp, [ps_sm], [sm_], scale=-1.0)
                ebe = g_[:, 48:56]
                act(fw, ebe, ps_sm[:, 8:16], AF.Exp, [ps_sm], [g_], scale=-1.0)
                for hf in range(2):
                    cp(fw, fw.dve, sm_[hf * 64:(hf + 1) * 64, 8:12].rearrange("p (d j) -> p d j", d=2),
                       ebe[hf * 64:(hf + 1) * 64, :].rearrange("p (d j two) -> p d j two", d=2, j=2)[:, :, :, hf], [g_], [sm_])
                fw.dma(fw.sp, self.SMd[t], sm_.ap, reads=[sm_], ds=sm_.ds)
                if lim is not None and len(lim) > 2 and lim[2] <= 3:
                    return
                pst = tm_matmul(t, "RV", 512)
                v_ = tmpA[0]
                tt(fw, fw.dve, v_.ap, pst.ap, bias_of("RV", 512), ALU.add, [pst, BB], [v_])
                for d in range(2):
                    eng = fw.dve if d == 0 else fw.pool
                    tt(fw, eng, tmA[:, R_RV + d * 512:R_RV + (d + 1) * 512].rearrange("p (h e) -> p h e", h=4),
                       v_.ap.rearrange("p (h e) -> p h e", h=4), rEA[:, d * 4:(d + 1) * 4].unsqueeze(2).to_broadcast([128, 4, 128]), ALU.mult, [v_, dk], [tmA])
                if lim is not None and len(lim) > 2 and lim[2] <= 4:
                    return
                pst = tm_matmul(t, "RG", 512)
                a_, b_ = tmpA[1], tmpB[0]
                tt(fw, fw.dve, a_.ap, pst.ap, bias_of("RG", 512), ALU.add, [pst, BB], [a_])
                act(fw, b_.ap, a_.ap, AF.Silu, [a_], [b_])
                tt(fw, fw.pool, tmA[:, R_RG:R_RG + 512], b_.ap, gn[:, 0, :], ALU.mult, [b_, gn], [tmA])
                if lim is not None and len(lim) > 2 and lim[2] <= 5:
                    return
                q_ = qpriv
                pst = tm_matmul(t, "AQ", 512)
                tt(fw, fw.dve, q_.ap, pst.ap, bias_of("AQ", 512), ALU.add, [pst, BB], [q_])
                self.norm_rope(t, q_, q_.ap, 8, qk[:, 0, :], qb_, qb_[:, 0:512], sl_, sl_[:, 0:8], tmpB[1], colT, rowT, qk)
                if lim is not None and len(lim) > 2 and lim[2] <= 6:
                    return
                pst = tm_matmul(t, "AKV", 256)
                k_ = kpriv
                tt(fw, fw.dve, k_[:, 0:128], pst[:, 0:128], bias_of("AKV", 128), ALU.add, [pst, BB], [k_])
                self.norm_rope(t, k_, k_[:, 0:128], 2, qk[:, 1, :], qb_, qb_[:, 512:640], sl_, sl_[:, 8:10], tmpB[1], colT, rowT, qk)
                tt(fw, fw.dve, tmA[:, R_AV:R_AV + 130].rearrange("p (g c) -> p g c", g=2)[:, :, 0:64],
                   pst[:, 128:256].rearrange("p (g c) -> p g c", g=2), bias_of("AKV", 128, 128).rearrange("p (g c) -> p g c", g=2), ALU.add, [pst, BB], [tmA])
                if lim is not None and len(lim) > 2 and lim[2] <= 7:
                    return
                pst = tm_matmul(t, "MV", 512)
                v_ = tmpA[0]
                tt(fw, fw.dve, v_.ap, pst.ap, bias_of("MV", 512), ALU.add, [pst, BB], [v_])
                for d in range(2):
                    eng = fw.dve if d == 0 else fw.pool
                    dst = tmA[:, R_MV + d * 516:R_MV + (d + 1) * 516].rearrange("p (h e) -> p h e", h=4)
                    tt(fw, eng, dst[:, :, 0:128], v_.ap.rearrange("p (h e) -> p h e", h=4),
                       EA[:, d * 4:(d + 1) * 4].unsqueeze(2).to_broadcast([128, 4, 128]), ALU.mult, [v_, g_], [tmA])
                    cp(fw, eng, dst[:, :, 128:129], EA[:, d * 4:(d + 1) * 4].unsqueeze(2), [g_], [tmA])
                if lim is not None and len(lim) > 2 and lim[2] <= 9:
                    return
                pst = tm_matmul(t, "MO", 512)
                a_, b_ = tmpA[1], tmpB[0]
                tt(fw, fw.dve, a_.ap, pst.ap, bias_of("MO", 512), ALU.add, [pst, BB], [a_])
                act(fw, b_.ap, a_.ap, AF.Sigmoid, [a_], [b_])
                tt(fw, fw.pool, tmA[:, R_MO:R_MO + 512], b_.ap, gn[:, 1, :], ALU.mult, [b_, gn], [tmA])
                fw.dma(fw.sp, self.TMd[t][:, 0:R_SPLIT], tmA.ap, reads=[tmA], ds=tmA.ds)
                if mid_hook is not None:
                    mid_hook()
                if lim is not None and len(lim) > 2 and lim[2] <= 10:
                    return
                for i in range(6):
                    pst = tm_matmul(t, "G%d" % i, 512)
                    a_ = tmpA[i % 2]
                    tt(fw, fw.dve, a_.ap, pst.ap, bias_of("G%d" % i, 512), ALU.add, [pst, BB], [a_])
                    act(fw, tmB[:, i * 512:(i + 1) * 512], a_.ap, AF.Sigmoid, [a_], [tmB])
                fw.dma(fw.sp, self.TMd[t][:, R_SPLIT:R_COLS], tmB.ap, reads=[tmB], ds=tmB.ds)
                pbq = ps_aq.ap.bitcast(BF16)
                for h in range(8):
                    tr(fw, pbq[0:64, h * 128:(h + 1) * 128], qb_[:, h * 64:(h + 1) * 64], self.identb.ap, [qb_, self.identb], [ps_aq], inc=(h == 7))
                for h in range(2):
                    tr(fw, pbk[0:64, 256 + h * 128:256 + (h + 1) * 128], qb_[:, 512 + h * 64:512 + (h + 1) * 64], self.identb.ap, [qb_, self.identb], [ps_sm], inc=(h == 1))
                cp(fw, fw.act, aq_.ap.rearrange("p a b -> p (a b)"), pbq[0:64, :], [ps_aq], [aq_])
                cp(fw, fw.act, ak_.ap.rearrange("p a b -> p (a b)"), pbk[0:64, 256:512], [ps_sm], [ak_])
                fw.dma(fw.sp, self.AQd[t], aq_.ap.rearrange("p a b -> p (a b)"), reads=[aq_], ds=aq_.ds)
                fw.dma(fw.sp, self.AKd[:, :, t * 128:(t + 1) * 128], ak_.ap, reads=[ak_], ds=ak_.ds)
                if lim is not None and len(lim) > 2 and lim[2] <= 8:
                    return

            lim = getattr(self, "a_lim", None)
            if lim == "pre":
                fw.barrier()
                return
            nt = NT if lim is None else lim[0]
            s1(0)
            s2(0)
            for t in range(nt):
                if t + 1 < nt:
                    s1(t + 1)
                    s3(t, (lambda t=t: s2(t + 1)))
                else:
                    s3(t)
            fw.barrier()

    def phase_b(self, l):
        nc, fw = self.nc, self.fw
        ps = self.ps
        last = (l == DEPTH - 1)
        with ExitStack() as es:
            WB = fw.sb(es, "WBb", [128, 3, 4, D], BF16)
            for b in range(3):
                fw.dma(fw.pool, WB[:, b, :, :], self.w_br[l, b].rearrange("(kc p) c -> p kc c", p=128), writes=[WB], ds=WB.ds, serialize=False)
            WO = fw.sb(es, "WOb", [128, 8, D], BF16)
            wo_src = self.w_out[l].rearrange("(kc p) c -> p kc c", p=128)
            for hh in range(2):
                fw.dma(fw.pool, WO[:, hh * 4:(hh + 1) * 4, :], wo_src[:, hh * 4:(hh + 1) * 4, :], writes=[WO], ds=WO.ds, serialize=False)
            AKT = fw.sb(es, "AKTb", [64, 2, T], BF16)
            fw.dma(fw.sp, AKT.ap, self.AKd, writes=[AKT], ds=AKT.ds)
            AVa = fw.sb(es, "AVab", [128, NT, 130], BF16)
            for q in range(0, NT, 8):
                q1 = min(NT, q + 8)
                fw.dma(fw.sp, AVa[:, q:q1, :], self.TMd[q:q1, :, R_AV:R_AV + 130].rearrange("t p c -> p t c"), writes=[AVa], ds=AVa.ds, serialize=False)
            retc = fw.sb(es, "retcb", [128, 16], F32)
            fw.dma(fw.sp, retc.ap, self.RETCd[l], writes=[retc], ds=retc.ds)
            gms = fw.sb(es, "gmsb", [128, D], F32)
            ln1t = fw.sb(es, "ln1tb", [128, 2, D], F32)
            for i in range(2):
                self.load_bcast(ln1t, ln1t[:, i, :], self.ln1[l, i:i + 1, :])

            def load_gms(j):
                self.load_bcast(gms, gms.ap, self.MOD[l, j:j + 1, 2 * D:3 * D], reads=[self.modbuf])
            load_gms(1)
            orders = {0: list(range(NT)), 1: [1, 0] + list(range(NT - 1, 1, -1))}
            SXd = (self.SFd, self.SBd)

            with ExitStack() as es2:
                S = [[fw.sb(es2, "Sst%d_%d" % (d, k), [128, 258], F32) for k in range(4)] for d in range(2)]
                for d in range(2):
                    for k in range(4):
                        fw.op(fw.dve if k % 2 == 0 else fw.pool, lambda e, d=d, k=k: e.memset(S[d][k].ap, 0.0), [], [S[d][k]])
                Sbf = [fw.ring(es2, "Sbf%d" % d, [128, 4, 258], BF16, 2) for d in range(2)]
                ldr = [fw.ring(es2, "ldr%d" % d, [128, 1540], BF16, 3) for d in range(2)]
                smr = [fw.ring(es2, "smr%d" % d, [128, SM_COLS], F32, 3) for d in range(2)]
                for i in range(NT):
                    for d in range(2):
                        t = orders[d][i]
                        L_ = ldr[d][i % 3]
                        sm_ = smr[d][i % 3]
                        fw.dma(fw.sp, L_[:, 0:512], self.TMd[t][:, R_RK:R_RK + 512], writes=[L_], ds=L_.ds)
                        fw.dma(fw.sp, L_[:, 512:1024], self.TMd[t][:, R_RV + d * 512:R_RV + (d + 1) * 512], writes=[L_], ds=L_.ds, serialize=False)
                        fw.dma(fw.sp, L_[:, 1024:1540], self.TMd[t][:, R_MV + d * 516:R_MV + (d + 1) * 516], writes=[L_], ds=L_.ds, serialize=False)
                        fw.dma(fw.sp, sm_.ap, self.SMd[t], writes=[sm_], ds=sm_.ds)
                        for mxj in range(4):
                            mx, j = mxj // 2, mxj % 2
                            W_ = 128 if mx == 0 else 129
                            pst = ps[d * 4 + mxj]
                            K_ = L_[:, mx * 256 + j * 128:mx * 256 + (j + 1) * 128]
                            for blk in range(2):
                                h = 2 * j + blk
                                V_ = L_[:, 512 + h * 128:512 + (h + 1) * 128] if mx == 0 else L_[:, 1024 + h * 129:1024 + (h + 1) * 129]
                                mm(fw, pst[:, blk * 129:blk * 129 + W_], K_, V_, True, True, [L_], [pst], inc=(blk == 1))
                        for mxj in range(4):
                            mx, j = mxj // 2, mxj % 2
                            W_ = 128 if mx == 0 else 129
                            pst = ps[d * 4 + mxj]
                            St = S[d][mxj]
                            sb_ = Sbf[d][i % 2]
                            cp(fw, fw.act, sb_[:, mxj, :], St.ap, [St], [sb_])
                            if mxj == 3:
                                fw.dma(fw.pool, SXd[d][t], sb_.ap.rearrange("p a b -> p (a b)"), reads=[sb_], ds=sb_.ds)
                            e_ = retc[:, 8 + d * 2 + j:9 + d * 2 + j] if mx == 0 else sm_[:, 8 + d * 2 + j:9 + d * 2 + j]
                            e_src = retc if mx == 0 else sm_
                            Sv = St.ap.rearrange("p (b w) -> p b w", b=2)[:, :, 0:W_]
                            Pv = pst[:, 0:258].rearrange("p (b w) -> p b w", b=2)[:, :, 0:W_]
                            act(fw, Sv, Sv, AF.Identity, [St, e_src], [St], scale=e_)
                            stt(fw, fw.dve, Sv, Pv, e_, Sv, ALU.mult, ALU.add, [pst, e_src, St], [St])
                fw.barrier()
            self.sxbuf = Buf("SX")

            tmA = fw.ring(es, "tmAb", [128, R_SPLIT], BF16, 2)
            tmG = fw.ring(es, "tmGb", [128, R_COLS - R_SPLIT], BF16, 3)
            smr = fw.ring(es, "smrb", [128, SM_COLS], F32, 2)
            fmr = fw.ring(es, "fmrb", [128, 8, 128], BF16, 2)
            aqr = fw.ring(es, "aqrb", [64, 8, 128], BF16, 2)
            sfr = [fw.ring(es, "sxr%d" % d, [128, 4, 258], BF16, 2) for d in range(2)]
            xr = fw.ring(es, "xrb", [128, D], F32, 2)
            PT = fw.ring(es, "PTb", [128, 2, 512], BF16, 2)
            pTr = fw.ring(es, "pTb", [128, 512], BF16, 3)
            yf = fw.ring(es, "yfb", [128, 512], F32, 4)
            sml = fw.ring(es, "smlb", [128, 64], F32, 2)
            st4 = fw.sb(es, "st4b", [128, 4, 6], F32)
            ymix = fw.ring(es, "ymixb", [128, 3, 512], BF16, 2)
            yT = fw.ring(es, "yTb", [128, 12, 128], BF16, 2)
            zt = fw.ring(es, "ztb", [128, D], F32, 2)
            zb = fw.sb(es, "zbb", [128, D], BF16)
            zT = fw.sb(es, "zTb", [128, 8, 128], BF16)
            rr = fw.ring(es, "rrb", [128, D], F32, 2)
            st6 = fw.sb(es, "st6b", [128, 2, 6], F32)
            mv = fw.sb(es, "mvb", [128, 4], F32)
            ps_s, ps_of, ps_ob, ps_sc, ps_acc = ps[0], (ps[1], ps[2]), (ps[3], ps[4]), (ps[5], ps[6]), ps[7]
            self._yk = 0

            def nexty():
                self._yk += 1
                return yf[self._yk % 4]

            def loads(t):
                fw.dma(fw.sp, tmA[t % 2].ap, self.TMd[t][:, 0:R_SPLIT], writes=[tmA[t % 2]], ds=tmA[t % 2].ds)
                fw.dma(fw.sp, tmG[t % 3].ap, self.TMd[t][:, R_SPLIT:R_COLS], writes=[tmG[t % 3]], ds=tmG[t % 3].ds)
                fw.dma(fw.sp, smr[t % 2].ap, self.SMd[t], writes=[smr[t % 2]], ds=smr[t % 2].ds)
                fw.dma(fw.sp, fmr[t % 2].ap.rearrange("p a b -> p (a b)"), self.FMd[t], writes=[fmr[t % 2]], ds=fmr[t % 2].ds)
                fw.dma(fw.sp, aqr[t % 2].ap.rearrange("p a b -> p (a b)"), self.AQd[t], writes=[aqr[t % 2]], ds=aqr[t % 2].ds)
                for d in range(2):
                    fw.dma(fw.sp, sfr[d][t % 2].ap.rearrange("p a b -> p (a b)"), SXd[d][t], writes=[sfr[d][t % 2]], ds=sfr[d][t % 2].ds)

            def par(ap2, hp, n=2):
                return ap2.rearrange("p (j two k) -> p j two k", j=2, two=2)[:, :, hp, :]

            def linattn(t, mx):
                ta_, sm_, fm_ = tmA[t % 2], smr[t % 2], fmr[t % 2]
                W_ = 128 if mx == 0 else 129
                qi, ki = (0, 2) if mx == 0 else (4, 6)
                pt_ = PT[(2 * t + mx) % 2]
                for h in range(4):
                    j, hf = h // 2, h % 2
                    mm(fw, ps_sc[hf][:, j * 128:(j + 1) * 128], fm_[hf * 64:(hf + 1) * 64, ki + j, :], fm_[hf * 64:(hf + 1) * 64, qi + j, :], True, True, [fm_], [ps_sc[hf]], inc=(h >= 2))
                for d, msk in ((0, self.maskF), (1, self.maskB)):
                    for hp in range(2):
                        tt(fw, fw.dve, par(pt_[:, d, :], hp), ps_sc[hp][:, 0:256].rearrange("p (j k) -> p j k", j=2), msk[:, 0:2, :], ALU.mult, [ps_sc[hp], msk], [pt_])
                for d in range(2):
                    banks = ps_of if d == 0 else ps_ob
                    S_ = sfr[d][t % 2]
                    for h in (0, 2, 1, 3):
                        j, hf = h // 2, h % 2
                        bank = banks[hf]
                        if mx == 0:
                            V_ = ta_[:, R_RV + d * 512 + h * 128:R_RV + d * 512 + (h + 1) * 128]
                        else:
                            V_ = ta_[:, R_MV + d * 516 + h * 129:R_MV + d * 516 + (h + 1) * 129]
                        out = bank[:, j * 129:j * 129 + W_]
                        mm(fw, out, pt_[:, d, h * 128:(h + 1) * 128], V_, (j == 0), False, [pt_, ta_], [bank], inc=False, skip_group_check=True)
                        Sv = S_[hf * 64:(hf + 1) * 64, mx * 2 + j, hf * 129:hf * 129 + W_]
                        mm(fw, out, fm_[hf * 64:(hf + 1) * 64, qi + j, :], Sv, False, True, [fm_, S_], [bank], inc=(j == 1), skip_group_check=True)
                y = nexty()
                sl = sml[t % 2]
                if mx == 0:
                    t1 = nexty()
                    for hp in range(2):
                        o_f = ps_of[hp][:, 0:258].rearrange("p (j w) -> p j w", j=2)[:, :, 0:128]
                        o_b = ps_ob[hp][:, 0:258].rearrange("p (j w) -> p j w", j=2)[:, :, 0:128]
                        ebf = par(retc[:, 0:4], hp).to_broadcast([128, 2, 128])
                        ebb = par(retc[:, 4:8], hp).to_broadcast([128, 2, 128])
                        tt(fw, fw.dve, par(t1.ap, hp), o_f, ebf, ALU.mult, [ps_of[hp], retc], [t1])
                        tt(fw, fw.dve, par(y.ap, hp), o_b, ebb, ALU.mult, [ps_ob[hp], retc], [y])
                    tt(fw, fw.pool, y.ap, y.ap, t1.ap, ALU.add, [y, t1], [y])
                else:
                    hd = []
                    for d in range(2):
                        banks = ps_of if d == 0 else ps_ob
                        q1 = sl[:, d * 16:d * 16 + 4]
                        q2 = sl[:, d * 16 + 4:d * 16 + 8]
                        r_ = sl[:, d * 16 + 8:d * 16 + 12]
                        eb = sm_[:, d * 4:(d + 1) * 4]
                        for hp in range(2):
                            den = banks[hp][:, 0:258].rearrange("p (j w) -> p j w", j=2)[:, :, 128:129]
                            tt(fw, fw.dve, par(q1, hp), den, par(eb, hp), ALU.mult, [banks[hp], sm_], [sl])
                        stt(fw, fw.dve, q2, q1, -1.0, q1, ALU.mult, ALU.max, [sl], [sl])
                        ts(fw, fw.dve, q2, q2, 1.0, None, ALU.max, None, [sl], [sl])
                        fw.op(fw.dve, lambda e, q2=q2: e.reciprocal(out=q2, in_=q2), [sl], [sl])
                        tt(fw, fw.dve, r_, q2, eb, ALU.mult, [sl, sm_], [sl])
                        hdt = y if d == 0 else nexty()
                        for hp in range(2):
                            num = banks[hp][:, 0:258].rearrange("p (j w) -> p j w", j=2)[:, :, 0:128]
                            tt(fw, fw.dve, par(hdt.ap, hp), num, par(r_, hp).to_broadcast([128, 2, 128]), ALU.mult, [banks[hp], sl], [hdt])
                        hd.append(hdt)
                    tt(fw, fw.pool, y.ap, hd[0].ap, hd[1].ap, ALU.add, [hd[0], hd[1]], [y])
                y3 = y.ap.rearrange("p (h e) -> p h e", h=4)
                for h in range(4):
                    fw.op(fw.dve, lambda e, h=h: e.bn_stats(out=st4[:, h, :], in_=y3[:, h, :]), [y], [st4])
                mvh = sl[:, 32:40].rearrange("p (h two) -> p h two", h=4)
                for h in range(4):
                    fw.op(fw.dve, lambda e, h=h: e.bn_aggr(out=mvh[:, h, :], in_=st4[:, h, :]), [st4], [sl])
                rs = sl[:, 40:44]
                act(fw, rs, mvh[:, :, 1], AF.Ln, [sl], [sl], bias=self.eps_t[:, 0:1])
                act(fw, rs, rs, AF.Exp, [sl], [sl], scale=-0.5)
                for h in range(4):
                    ts(fw, fw.dve, y3[:, h, :], y3[:, h, :], mvh[:, h, 0:1], rs[:, h:h + 1], ALU.subtract, ALU.mult, [y, sl], [y])
                gcol = R_RG if mx == 0 else R_MO
                ym = ymix[t % 2]
                tt(fw, fw.pool, ym[:, 0 if mx == 0 else 2, :], y.ap, ta_[:, gcol:gcol + 512], ALU.mult, [y, ta_], [ym])

            def attention(t, hooks):
                aq_ = aqr[t % 2]
                ym = ymix[t % 2]
                kts = list(range(NCT)) if t < NCT else list(range(NT))
                its = [(g, n_, kt) for g in range(2) for n_, kt in enumerate(kts)]
                nk = len(kts)
                hk = list(hooks)
                every = max(1, (len(its) - 4) // max(1, len(hk))) if hk else 0

                def score(idx):
                    g, n_, kt = its[idx]
                    psc = ps_sc[idx % 2]
                    mm(fw, psc.ap, AKT[:, g, kt * 128:(kt + 1) * 128], aq_[:, g * 4:(g + 1) * 4, :].rearrange("p a b -> p (a b)"), True, True, [AKT, aq_], [psc])
                score(0)
                for idx, (g, n_, kt) in enumerate(its):
                    if idx + 1 < len(its):
                        score(idx + 1)
                    psc = ps_sc[idx % 2]
                    p_ = pTr[idx % 3]
                    act(fw, p_.ap, psc.ap, AF.Exp, [psc], [p_])
                    for r in range(4):
                        mm(fw, ps_acc[:, r * 65:(r + 1) * 65], p_[:, r * 128:(r + 1) * 128], AVa[:, kt, g * 65:(g + 1) * 65],
                           (n_ == 0 and r == 0), (n_ == nk - 1), [p_, AVa], [ps_acc], inc=(r == 3), skip_group_check=True)
                    if n_ == nk - 1:
                        sl = sml[t % 2]
                        rd = sl[:, 48 + g * 4:52 + g * 4]
                        acc3 = ps_acc[:, 0:260].rearrange("p (r c) -> p r c", r=4)
                        fw.op(fw.dve, lambda e, rd=rd, acc3=acc3: e.reciprocal(out=rd, in_=acc3[:, :, 64]), [ps_acc], [sl])
                        tt(fw, fw.dve, ym[:, 1, g * 256:(g + 1) * 256].rearrange("p (r c) -> p r c", r=4), acc3[:, :, 0:64],
                           rd.unsqueeze(2).to_broadcast([128, 4, 64]), ALU.mult, [ps_acc, sl], [ym])
                    if hk and every and idx >= 2 and (idx - 2) % every == 0:
                        hk.pop(0)()
                for h_ in hk:
                    h_()

            def merge_stages(t):
                ym, yT_, tg_, x_ = ymix[t % 2], yT[t % 2], tmG[t % 3], xr[t % 2]
                pb = ps_s.ap.bitcast(BF16)
                zsum = zt[0]

                def m_tr(grp):
                    def f():
                        if grp == 0:
                            if t == NCT:
                                load_gms(0)
                            if "YDd" in self.debug:
                                fw.dma(fw.sp, self.YDd[t], ym.ap.rearrange("p a b -> p (a b)"), reads=[ym], ds=ym.ds)
                            fw.dma(fw.sp, x_.ap, self.X[t * 128:(t + 1) * 128, :], reads=[self.xbuf], writes=[x_], ds=x_.ds)
                        n = 8 if grp == 0 else 4
                        for i in range(n):
                            ii = grp * 8 + i
                            tr(fw, pb[:, i * 128:(i + 1) * 128], ym[:, ii // 4, (ii % 4) * 128:(ii % 4 + 1) * 128], self.identb.ap, [ym, self.identb], [ps_s], inc=(i == n - 1))
                        cp(fw, fw.act, yT_[:, grp * 8:grp * 8 + n, :].rearrange("p a b -> p (a b)"), pb[:, 0:n * 128], [ps_s], [yT_])
                    return f

                def m_br(b):
                    def f():
                        banks = ps_of if b % 2 == 0 else ps_ob
                        for n in range(2):
                            for kc in range(4):
                                mm(fw, banks[n].ap, yT_[:, b * 4 + kc, :], WB[:, b, kc, n * 512:(n + 1) * 512], kc == 0, kc == 3, [yT_, WB], [banks[n]])
                        dst = zsum if b == 0 else zt[1]
                        for n in range(2):
                            tt(fw, fw.dve, dst[:, n * 512:(n + 1) * 512], banks[n].ap, tg_[:, b * D + n * 512:b * D + (n + 1) * 512], ALU.mult, [banks[n], tg_], [dst])
                        if b == 1:
                            tt(fw, fw.pool, zsum.ap, zsum.ap, dst.ap, ALU.add, [zsum, dst], [zsum])
                        if b == 2:
                            tt(fw, fw.pool, zb.ap, zsum.ap, dst.ap, ALU.add, [zsum, dst], [zb])
                    return f

                def m_zt():
                    pb8 = pb.rearrange("p (a b) -> p a b", a=8)
                    for kc in range(8):
                        tr(fw, pb8[:, kc, :], zb[:, kc * 128:(kc + 1) * 128], self.identb.ap, [zb, self.identb], [ps_s], inc=(kc == 7))
                    cp(fw, fw.act, zT.ap.rearrange("p a b -> p (a b)"), pb, [ps_s], [zT])

                def m_out():
                    for n in range(2):
                        for kc in range(8):
                            mm(fw, ps_ob[n].ap, zT[:, kc, :], WO[:, kc, n * 512:(n + 1) * 512], kc == 0, kc == 7, [zT, WO], [ps_ob[n]])
                    r_ = rr[0]
                    for n in range(2):
                        tt(fw, fw.dve, r_[:, n * 512:(n + 1) * 512], ps_ob[n].ap, gms[:, n * 512:(n + 1) * 512], ALU.mult, [ps_ob[n], gms], [r_])
                    stt(fw, fw.dve, r_.ap, x_.ap, ALPHA, r_.ap, ALU.mult, ALU.add, [x_, r_], [r_])

                def m_ln():
                    self.ln_affine(rr[0], rr[1], st6, mv, ln1t)
                    fw.dma(fw.sp, self.X1[t * 128:(t + 1) * 128, :], rr[1].ap, reads=[rr[1]], ds=rr[1].ds)
                return [m_tr(0), m_tr(1), m_br(0), m_br(1), m_br(2), m_zt, m_out, m_ln]

            t0 = NCT if last else 0
            loads(t0)
            for t in range(t0, NT):
                if t + 1 < NT:
                    loads(t + 1)
                linattn(t, 0)
                linattn(t, 1)
                attention(t, merge_stages(t - 1) if t - 1 >= t0 else [])
            for f_ in merge_stages(NT - 1):
                f_()
            self.x1buf = Buf("X1")
            fw.barrier()

    def phase_d(self, l):
        nc, fw = self.nc, self.fw
        ps = self.ps
        last = (l == DEPTH - 1)
        with ExitStack() as es:
            import os
            dsk = os.environ.get("D_SKIP", "").split(",")
            WU = fw.sb(es, "WUd", [128, 8, 2 * DFF], BF16)
            wu_src = self.w_up[l].rearrange("(kc p) c -> p kc c", p=128)
            for c0 in range(0, 2 * DFF if "wu" not in dsk else 0, 512):
                fw.dma(fw.pool, WU[:, :, c0:c0 + 512], wu_src[:, :, c0:c0 + 512], writes=[WU], ds=WU.ds, serialize=False)
            WD = fw.sb(es, "WDd", [128, NF, D], BF16)
            wd_src = self.w_down[l].rearrange("(f p) c -> p f c", p=128)
            for f0 in range(0, NF if "wd" not in dsk else 0, 4):
                f1 = min(NF, f0 + 4)
                fw.dma(fw.pool, WD[:, f0:f1, :], wd_src[:, f0:f1, :], writes=[WD], ds=WD.ds, serialize=False)
            cvp = fw.sb(es, "cvpd", [128, 4, 44], F32)
            if "cvp" not in dsk:
                fw.dma(fw.sp, cvp.ap, self.convp[l], writes=[cvp], ds=cvp.ds)
            modt = fw.sb(es, "modd", [128, 2, D], F32)
            ln2t = fw.sb(es, "ln2td", [128, 2, D], F32)
            for i in range(2):
                self.load_bcast(ln2t, ln2t[:, i, :], self.ln2[l, i:i + 1, :])

            def load_mod(j):
                for i in range(2):
                    self.load_bcast(modt, modt[:, i, :], self.MOD[l, j:j + 1, (3 + i) * D:(4 + i) * D], reads=[self.modbuf])

            def load_gate(j):
                self.load_bcast(gmlp, gmlp.ap, self.MOD[l, j:j + 1, 5 * D:6 * D], reads=[self.modbuf])
            gmlp = fw.sb(es, "gmlpd", [128, D], F32)
            load_mod(1)
            load_gate(1)
            x1r = fw.ring(es, "x1rd", [128, D], F32, 2)
            xn = fw.sb(es, "xnd", [128, D], F32)
            hb = fw.sb(es, "hbd", [128, D], BF16)
            HTB = fw.ring(es, "HTBd", [128, 8, 132], BF16, 3)
            cA = [fw.ring(es, "cAd%d" % n_, [128, 128], F32, 3) for n_ in range(2)]
            cB = [fw.ring(es, "cBd%d" % n_, [128, 128], F32, 3) for n_ in range(2)]
            sg = fw.ring(es, "sgd", [128, 128], F32, 3)
            actT_t = fw.ring(es, "actTd", [128, NF, 128], BF16, 2)
            actT = [[Tl(fw, a_[:, i, :], "aT%d" % i) for i in range(NF)] for a_ in actT_t]
            rr = [xn, fw.sb(es, "rrd1", [128, D], F32)]
            st6 = fw.sb(es, "st6d", [128, 2, 6], F32)
            mv = fw.sb(es, "mvd", [128, 4], F32)
            ps_tr = ps[0]
            ps_up = ps[1:5]
            ps_dn = (ps[5], ps[6])
            t0 = NCT if last else 0

            def seq_first(t):
                return t == 0 or t == NCT

            def seq_last(t):
                return t == NCT - 1 or t == NT - 1

            def f1(t):
                if t == NCT:
                    load_mod(0)
                x_ = x1r[t % 2]
                fw.dma(fw.sp, x_.ap, self.X1[t * 128:(t + 1) * 128, :], reads=[self.x1buf], writes=[x_], ds=x_.ds)
                for hh in range(2):
                    fw.op(fw.dve, lambda e, hh=hh: e.bn_stats(out=st6[:, hh, :], in_=x_[:, hh * 512:(hh + 1) * 512]), [x_], [st6])
                fw.op(fw.dve, lambda e: e.bn_aggr(out=mv[:, 0:2], in_=st6.ap.rearrange("p a b -> p (a b)")), [st6], [mv])
                rstd_from(fw, mv[:, 2:3], mv[:, 1:2], [mv], [mv])
                ts(fw, fw.dve, xn.ap, x_.ap, mv[:, 0:1], mv[:, 2:3], ALU.subtract, ALU.mult, [x_, mv], [xn])
                tt(fw, fw.pool, xn.ap, xn.ap, modt[:, 1, :], ALU.mult, [xn, modt], [xn])
                tt(fw, fw.dve, hb.ap, xn.ap, modt[:, 0, :], ALU.add, [xn, modt], [hb])
                pb = ps_tr.ap.bitcast(BF16).rearrange("p (a b) -> p a b", a=8)
                for kc in range(8):
                    tr(fw, pb[:, kc, :], hb[:, kc * 128:(kc + 1) * 128], self.identb.ap, [hb, self.identb], [ps_tr], inc=(kc == 7))
                H_ = HTB[t % 3]
                if "cpH" in dsk:
                    return
                cp(fw, fw.act, H_[:, :, 2:130], pb, [ps_tr], [H_])
                if "halo" in dsk:
                    return
                if seq_first(t):
                    fw.op(fw.dve, lambda e: e.memset(H_[:, :, 1:2], 0.0), [], [H_])
                else:
                    Hp = HTB[(t - 1) % 3]
                    cp(fw, fw.dve, Hp[:, :, 130:131], H_[:, :, 2:3], [H_], [Hp])
                if seq_last(t):
                    fw.op(fw.dve, lambda e: e.memset(H_[:, :, 130:131], 0.0), [], [H_])
                elif t + 1 < NT:
                    Hn = HTB[(t + 1) % 3]
                    cp(fw, fw.dve, Hn[:, :, 1:2], H_[:, :, 129:130], [H_], [Hn])

            def f2(t):
                if t == NCT:
                    load_gate(0)
                H_ = HTB[t % 3]
                x_ = x1r[t % 2]
                aT = actT[t % 2]
                SK = 5

                def down(i):
                    for n in range(2):
                        mm(fw, ps_dn[n].ap, aT[i].ap, WD[:, i, n * 512:(n + 1) * 512], i == 0, i == NF - 1, [aT[i], WD], [ps_dn[n]], inc=True)

                npair = NF if self.d_lim is None else self.d_lim[1]
                for i in range(npair):
                    pu = ps_up[i % 4]
                    for n_, ch in enumerate((NF + i, i)):
                        for kc in range(8):
                            mm(fw, pu[:, n_ * 130:(n_ + 1) * 130], WU[:, kc, ch * 128:(ch + 1) * 128], H_[:, kc, 1:131], kc == 0, kc == 7, [WU, H_], [pu],
                               inc=(kc == 7 and n_ == 1))
                    if i >= SK and npair == NF:
                        down(i - SK)
                    for n_, ch in enumerate((NF + i, i)):
                        u = pu[:, n_ * 130:(n_ + 1) * 130]
                        a_, b_ = cA[n_][i % 3], cB[n_][i % 3]
                        act(fw, a_.ap, u[:, 1:129], AF.Identity, [pu, cvp], [a_], bias=cvp[:, 3, ch:ch + 1], scale=cvp[:, 1, ch:ch + 1])
                        stt(fw, fw.dve, a_.ap, u[:, 0:128], cvp[:, 0, ch:ch + 1], a_.ap, ALU.mult, ALU.add, [pu, cvp, a_], [a_])
                        stt(fw, fw.dve, a_.ap, u[:, 2:130], cvp[:, 2, ch:ch + 1], a_.ap, ALU.mult, ALU.add, [pu, cvp, a_], [a_])
                    s_ = sg[i % 3]
                    act(fw, s_.ap, cA[0][i % 3].ap, AF.Silu, [cA[0][i % 3]], [s_])
                    tt(fw, fw.pool, aT[i].ap, cA[1][i % 3].ap, s_.ap, ALU.mult, [cA[1][i % 3], s_], [aT[i]])
                if npair < NF:
                    return
                for i in range(NF - SK, NF):
                    down(i)
                r_ = rr[0]
                for n in range(2):
                    tt(fw, fw.dve, r_[:, n * 512:(n + 1) * 512], ps_dn[n].ap, gmlp[:, n * 512:(n + 1) * 512], ALU.mult, [ps_dn[n], gmlp], [r_])
                stt(fw, fw.dve, r_.ap, x_.ap, ALPHA, r_.ap, ALU.mult, ALU.add, [x_, r_], [r_])
                self.ln_affine(r_, rr[1], st6, mv, ln2t)
                if last:
                    fw.dma(fw.sp, self.out[(t - NCT) * 128:(t - NCT + 1) * 128, :], rr[1].ap, reads=[rr[1]], ds=rr[1].ds)
                else:
                    fw.dma(fw.sp, self.X[t * 128:(t + 1) * 128, :], rr[1].ap, reads=[rr[1]], writes=[self.xbuf], ds=rr[1].ds)

            dl = self.d_lim
            nt_ = NT if dl is None else t0 + dl[0]
            if "f1" in dsk:
                fw.barrier()
                return
            f1(t0)
            for t in range(t0, nt_):
                if t + 1 < NT:
                    f1(t + 1)
                if dl is None or dl[1] > 0:
                    f2(t)
            fw.barrier()

    def ln_affine(self, src, dst, st6, mv, gbt):
        fw = self.fw
        for hh in range(2):
            fw.op(fw.dve, lambda e, hh=hh: e.bn_stats(out=st6[:, hh, :], in_=src[:, hh * 512:(hh + 1) * 512]), [src], [st6])
        fw.op(fw.dve, lambda e: e.bn_aggr(out=mv[:, 0:2], in_=st6.ap.rearrange("p a b -> p (a b)")), [st6], [mv])
        rstd_from(fw, mv[:, 2:3], mv[:, 1:2], [mv], [mv])
        ts(fw, fw.dve, dst.ap, src.ap, mv[:, 0:1], mv[:, 2:3], ALU.subtract, ALU.mult, [src, mv], [dst])
        tt(fw, fw.pool, dst.ap, dst.ap, gbt[:, 0, :], ALU.mult, [dst, gbt], [dst])
        tt(fw, fw.pool, dst.ap, dst.ap, gbt[:, 1, :], ALU.add, [dst, gbt], [dst])

    def norm_rope(self, t, src_tl, src, nh, gain, dst_tl, dst, sl_tl, ss, tmp_tl, colT, rowT, qk):
        fw = self.fw
        n = nh * 64
        sq = tmp_tl[:, 0:n]
        s3 = src.rearrange("p (h d) -> p h d", h=nh)
        tt(fw, fw.pool, sq, src, src, ALU.mult, [src_tl], [tmp_tl])
        fw.op(fw.dve, lambda e: e.tensor_reduce(out=ss, in_=sq.rearrange("p (h d) -> p h d", h=nh), axis=AX.X, op=ALU.add), [tmp_tl], [sl_tl])
        rstd_from(fw, ss, ss, [sl_tl], [sl_tl], scale=1.0 / 64.0)
        tt(fw, fw.pool, s3, s3, ss.unsqueeze(2).to_broadcast([128, nh, 64]), ALU.mult, [src_tl, sl_tl], [src_tl])
        is_lat = t >= NCT
        gb = gain.unsqueeze(1).to_broadcast([128, nh, 64])
        if not is_lat:
            tt(fw, fw.pool, dst.rearrange("p (h d) -> p h d", h=nh), s3, gb, ALU.mult, [src_tl, qk], [dst_tl])
            return
        tt(fw, fw.pool, s3, s3, gb, ALU.mult, [src_tl, qk], [src_tl])
        j = t - NCT
        s5 = src.rearrange("p (h a two i) -> p h a two i", h=nh, a=2, two=2)
        o5 = tmp_tl[:, 0:n].rearrange("p (h a two i) -> p h a two i", h=nh, a=2, two=2)
        d5 = dst.rearrange("p (h a two i) -> p h a two i", h=nh, a=2, two=2)
        for a, tab in ((0, rowT[:, j, :]), (1, colT.ap)):
            cos_b = tab[:, 0:16].unsqueeze(1).to_broadcast([128, nh, 16])
            sin_b = tab[:, 16:32].unsqueeze(1).to_broadcast([128, nh, 16])
            tabt = rowT if a == 0 else colT
            x1 = s5[:, :, a, 0, :]
            x2 = s5[:, :, a, 1, :]
            tt(fw, fw.pool, o5[:, :, a, 0, :], x2, sin_b, ALU.mult, [src_tl, tabt], [tmp_tl])
            tt(fw, fw.pool, o5[:, :, a, 1, :], x1, sin_b, ALU.mult, [src_tl, tabt], [tmp_tl])
            tt(fw, fw.pool, x1, x1, cos_b, ALU.mult, [src_tl, tabt], [src_tl])
            tt(fw, fw.pool, x2, x2, cos_b, ALU.mult, [src_tl, tabt], [src_tl])
            tt(fw, fw.pool, d5[:, :, a, 0, :], x1, o5[:, :, a, 0, :], ALU.subtract, [src_tl, tmp_tl], [dst_tl])
            tt(fw, fw.pool, d5[:, :, a, 1, :], x2, o5[:, :, a, 1, :], ALU.add, [src_tl, tmp_tl], [dst_tl])


def make_consts():
    c = np.zeros((128, 1024), np.float32)
    p = np.arange(128)
    c[:, 0:128] = np.eye(128, dtype=np.float32)
    c[:, 128:256] = (p[:, None] <= p[None, :])
    c[:, 256:384] = (p[:, None] >= p[None, :])
    c[:, 384:512] = 1.0
    c[:, 512] = p + 1
    c[:, 513] = 128 - p
    c[:, 514] = -(p + 1)
    c[:, 515] = -(128 - p)
    c[:, 516] = p % 64
    c[:, 517:533] = np.arange(16)[None, :]
    return c


def host_inputs(inputs, b, L=DEPTH):
    f = lambda a: np.ascontiguousarray(np.asarray(a, dtype=np.float32))
    inputs = {k: (np.asarray(v)[:L] if k not in ('x', 'c', 'ctx', 'c_ctx') else v) for k, v in inputs.items()}
    m = {}
    m["xin"] = f(np.concatenate([inputs["ctx"][b], inputs["x"][b]], axis=0))
    cc = np.stack([np.asarray(inputs["c"][b]), np.asarray(inputs["c_ctx"])], axis=-1)
    m["ccT"] = f(cc.reshape(8, 128, 2).transpose(1, 0, 2))
    m["w_mod"] = f(inputs["w_mod"])
    m["b_mod"] = f(inputs["b_mod"])
    m["w_in"] = f(inputs["w_in"])
    m["b_in"] = f(inputs["b_in"])
    fmcols = np.concatenate([np.arange(O_RQ, O_RQ + 256), np.arange(O_RK, O_RK + 256), np.arange(O_MQ, O_MQ + 256), np.arange(O_MK, O_MK + 256)])
    m["b_in_fm"] = f(np.asarray(inputs["b_in"])[:, fmcols].reshape(L, 8, 128).transpose(0, 2, 1))
    m["decay"] = f(np.asarray(inputs["ret_decay_logit"]).reshape(L, 8))
    m["ret_gn"] = f(inputs["ret_gn_g"])
    m["qn_g"] = f(inputs["attn_qn_g"])
    m["kn_g"] = f(inputs["attn_kn_g"])
    m["m_gn"] = f(inputs["mlstm_gn_g"])
    m["w_br"] = f(np.stack([inputs["w_br_ret"], inputs["w_br_att"], inputs["w_br_mlstm"]], axis=1))
    m["w_out"] = f(inputs["w_out"])
    m["ln1"] = f(np.stack([inputs["ln1_g"], inputs["ln1_b"]], axis=1))
    m["w_up"] = f(inputs["w_up"])
    cw = np.asarray(inputs["conv_w"])
    cb = np.asarray(inputs["conv_b"])
    cpk = np.concatenate([cw, cb[:, None, :]], axis=1)
    m["convp"] = f(cpk.reshape(L, 4, 44, 128).transpose(0, 3, 1, 2))
    m["w_down"] = f(inputs["w_down"])
    m["ln2"] = f(np.stack([inputs["ln2_g"], inputs["ln2_b"]], axis=1))
    m["consts"] = make_consts()
    return m


_PROG = {}


def kernel(**inputs):
    if "p" not in _PROG:
        _PROG["p"] = Prog()
    prog = _PROG["p"]
    in_maps = [host_inputs(inputs, b) for b in range(NCORES)]
    res = run_bass_kernel_spmd(prog.nc, in_maps, core_ids=list(range(NCORES)))
    out = np.stack([np.asarray(r["out"]).reshape(32 * 128, D) for r in res.results], axis=0)
    return out.astype(np.float32)
```

```python
import math
import numpy as np
from contextlib import ExitStack
import concourse.bass as bass
import concourse.mybir as mybir
from concourse.bass_utils import run_bass_kernel_spmd

F32 = mybir.dt.float32
BF16 = mybir.dt.bfloat16
I32 = mybir.dt.int32
AF = mybir.ActivationFunctionType
ALU = mybir.AluOpType
AX = mybir.AxisListType

D = 1024
DEPTH = 4
NT = 34
NCT = 2
T = NT * 128
DFF = 2816
NF = 22
EPS = 1e-6
ALPHA = (2.0 * DEPTH) ** 0.25
NCORES = 4

O_RQ, O_RK, O_RV, O_RG = 0, 256, 512, 1024
O_AQ, O_AK, O_AV = 1536, 2048, 2176
O_MQ, O_MK, O_MV, O_MO, O_MI, O_MF, O_GATE = 2304, 2560, 2816, 3328, 3840, 3848, 3856
N_IN = 6928

FM_PIECES = [(0, O_RQ, 256), (256, O_RK, 256), (512, O_MQ, 256), (768, O_MK, 256)]
TMW0 = 1024
TM_GROUPS = [
    ("MG", [(O_MI, 16)]),
    ("RV", [(O_RV, 512)]),
    ("RG", [(O_RG, 512)]),
    ("AQ", [(O_AQ, 512)]),
    ("AKV", [(O_AK, 128), (O_AV, 128)]),
    ("MV", [(O_MV, 512)]),
    ("MO", [(O_MO, 512)]),
] + [("G%d" % i, [(O_GATE + 512 * i, 512)]) for i in range(6)]
TM_OFF = {}
_o = 0
for _n, _p in TM_GROUPS:
    TM_OFF[_n] = _o
    _o += sum(n for _, n in _p)
TM_COLS = _o
W_COLS = TMW0 + TM_COLS

R_RV, R_RG, R_AV, R_RK, R_MK, R_MV, R_MO, R_MG = 0, 1024, 1536, 1666, 1922, 2178, 3210, 3722
R_COLS = 6794
SM_COLS = 16
R_SPLIT = R_MG


class Sem:
    def __init__(self, h, name):
        self.h = h
        self.name = name
        self.owner = None
        self.total = 0


class Buf:
    __slots__ = ("w", "r", "name")

    def __init__(self, name=""):
        self.w = {}
        self.r = {}
        self.name = name


class Tl:
    def __init__(self, fw, ap, name, buf=None):
        self.fw = fw
        self.ap = ap
        self.name = name
        self.buf = buf if buf is not None else Buf(name)
        self._ds = None

    @property
    def ds(self):
        if self._ds is None:
            self._ds = self.fw.pool_sem()
        return self._ds

    def __getitem__(self, idx):
        return self.ap[idx]


class Eng:
    def __init__(self, fw, name, eng):
        self.fw = fw
        self.name = name
        self.eng = eng
        self.sem = fw.new_sem("c_" + name)
        self.sem.owner = self
        self.cnt = 0
        self.waited = {}

    def wait(self, sem, val):
        if val <= 0:
            return
        if self.waited.get(sem, 0) >= val:
            return
        if sem.owner is not None:
            assert val <= sem.owner.cnt, ("wait on unissued instr", self.name, sem.name, val, sem.owner.cnt)
        else:
            assert val <= sem.total
        self.eng.wait_ge(sem.h, val)
        self.waited[sem] = val


class FW:
    def __init__(self, nc):
        self.nc = nc
        self.es = ExitStack()
        self.nsem = 0
        self.pe = Eng(self, "pe", nc.tensor)
        self.act = Eng(self, "act", nc.scalar)
        self.dve = Eng(self, "dve", nc.vector)
        self.pool = Eng(self, "pool", nc.gpsimd)
        self.sp = Eng(self, "sp", nc.sync)
        self.engs = [self.pe, self.act, self.dve, self.pool, self.sp]
        self.dsems = []
        self.swq = []
        self.ndram = 0
        self.sem_pool = []
        self.sem_idx = 0
        self.pool_base = 0
        self.nsb = 0

    def pool_sem(self):
        if self.sem_idx == len(self.sem_pool):
            self.sem_pool.append(self.new_sem("dp%d" % self.sem_idx))
        s = self.sem_pool[self.sem_idx]
        self.sem_idx += 1
        return s

    def reset_pool(self):
        self.sem_idx = self.pool_base

    def new_sem(self, name):
        h = self.es.enter_context(self.nc.semaphore(name + "_%d" % self.nsem))
        self.nsem += 1
        s = Sem(h, name)
        return s

    def sb(self, es, name, shape, dtype):
        self.nsb += 1
        t = es.enter_context(self.nc.sbuf_tensor("%s_%d" % (name, self.nsb), list(shape), dtype))
        return Tl(self, t[:], name)

    def ring(self, es, name, shape, dtype, n):
        return [self.sb(es, "%s%d" % (name, i), shape, dtype) for i in range(n)]

    def dram(self, name, shape, dtype, kind="Internal"):
        t = self.nc.dram_tensor(name, list(shape), dtype, kind=kind)
        return t.ap()

    def _deps(self, E, reads, writes, skip_sem=None):
        for b in reads:
            b = b.buf if isinstance(b, Tl) else b
            for sem, v in b.w.items():
                if sem is E.sem and E is self.pe:
                    continue
                E.wait(sem, v)
        for b in writes:
            b = b.buf if isinstance(b, Tl) else b
            for sem, v in list(b.w.items()) + list(b.r.items()):
                if sem is E.sem or sem is skip_sem:
                    continue
                E.wait(sem, v)

    def _mark(self, sem, tok, reads, writes):
        for b in reads:
            b = b.buf if isinstance(b, Tl) else b
            if b.r.get(sem, 0) < tok:
                b.r[sem] = tok
        for b in writes:
            b = b.buf if isinstance(b, Tl) else b
            b.w = {sem: tok}
            b.r = {}

    def op(self, E, fn, reads=(), writes=(), inc=True):
        self._deps(E, reads, writes)
        ins = fn(E.eng)
        if inc:
            E.cnt += 1
            ins.then_inc(E.sem.h, 1)
            tok = E.cnt
        else:
            tok = E.cnt + 1
        self._mark(E.sem, tok, reads, writes)
        return ins

    def dma(self, Q, out, in_, reads=(), writes=(), ds=None, serialize=True, **kw):
        self._deps(Q, reads, writes, skip_sem=(None if serialize else ds))
        if serialize and ds.total > 0:
            Q.wait(ds, ds.total)
        if Q is self.pool:
            while len(self.swq) >= 2:
                s_, v_ = self.swq.pop(0)
                Q.wait(s_, v_)
        ins = Q.eng.dma_start(out=out, in_=in_, **kw)
        ds.total += 16
        if Q is self.pool:
            self.swq.append((ds, ds.total))
        ins.then_inc(ds.h, 16)
        if ds not in self.dsems:
            self.dsems.append(ds)
        self._mark(ds, ds.total, reads, writes)
        return ins

    def barrier(self):
        for E in self.engs:
            for P in self.engs:
                if P is not E:
                    E.wait(P.sem, P.cnt)
            for ds in self.dsems:
                E.wait(ds, ds.total)


def tt(fw, E, out, in0, in1, op, reads, writes):
    return fw.op(E, lambda e: e.tensor_tensor(out=out, in0=in0, in1=in1, op=op), reads, writes)


def ts(fw, E, out, in0, s1, s2, op0, op1, reads, writes):
    if op1 is None:
        return fw.op(E, lambda e: e.tensor_scalar(out=out, in0=in0, scalar1=s1, scalar2=None, op0=op0), reads, writes)
    return fw.op(E, lambda e: e.tensor_scalar(out=out, in0=in0, scalar1=s1, scalar2=s2, op0=op0, op1=op1), reads, writes)


def stt(fw, E, out, in0, scalar, in1, op0, op1, reads, writes):
    return fw.op(E, lambda e: e.scalar_tensor_tensor(out=out, in0=in0, scalar=scalar, in1=in1, op0=op0, op1=op1), reads, writes)


def act(fw, out, in_, func, reads, writes, bias=None, scale=None):
    kw = {}
    if bias is not None:
        kw["bias"] = bias
    if scale is not None:
        kw["scale"] = scale
    return fw.op(fw.act, lambda e: e.activation(out=out, in_=in_, func=func, **kw), reads, writes)


def cp(fw, E, out, in_, reads, writes):
    if E is fw.act:
        return fw.op(E, lambda e: e.copy(out=out, in_=in_), reads, writes)
    return fw.op(E, lambda e: e.tensor_copy(out=out, in_=in_), reads, writes)


def mm(fw, out, lhsT, rhs, start, stop, reads, writes, inc=None, **kw):
    if inc is None:
        inc = stop
    return fw.op(fw.pe, lambda e: e.matmul(out, lhsT=lhsT, rhs=rhs, start=start, stop=stop, **kw), reads, writes, inc=inc)


def tr(fw, out, in_, ident, reads, writes, inc=True):
    return fw.op(fw.pe, lambda e: e.transpose(out, in_, ident), reads, writes, inc=inc)


def bcast_rows(ap2d, nparts):
    return ap2d.to_broadcast([nparts, ap2d.shape[-1]])


def rstd_from(fw, out, in_, reads, writes, scale=1.0):
    act(fw, out, in_, AF.Ln, reads, writes, bias=fw.eps_t[:, 0:1] if in_.shape[0] == 128 else fw.eps_t[0:in_.shape[0], 0:1], scale=scale)
    act(fw, out, out, AF.Exp, writes, writes, scale=-0.5)


class Prog:
    def __init__(self, n_layers=DEPTH, debug=None, stop_after=None, a_lim=None, skip="", d_lim=None):
        self.a_lim = a_lim
        self.skip = skip
        self.d_lim = d_lim
        self.n_layers = n_layers
        self.debug = debug or []
        self.stop_after = stop_after
        self.nc = bass.Bass("TRN2", target_bir_lowering=False)
        self.fw = FW(self.nc)
        self.inputs = {}
        self.build()

    def din(self, name, shape, dtype=F32):
        ap = self.fw.dram(name, shape, dtype, kind="ExternalInput")
        self.inputs[name] = ap
        return ap

    def dscr(self, name, shape, dtype):
        kind = "ExternalOutput" if name in self.debug else "Internal"
        return self.fw.dram(name, shape, dtype, kind=kind)

    def build(self):
        nc, fw = self.nc, self.fw
        L = self.n_layers
        self.xin = self.din("xin", [T, D])
        self.ccT = self.din("ccT", [128, 8, 2])
        self.w_mod = self.din("w_mod", [L, D, 6 * D])
        self.b_mod = self.din("b_mod", [L, 6 * D])
        self.w_in = self.din("w_in", [L, D, N_IN])
        self.b_in = self.din("b_in", [L, N_IN])
        self.b_in_fm = self.din("b_in_fm", [L, 128, 8])
        self.decay = self.din("decay", [L, 8])
        self.ret_gn = self.din("ret_gn", [L, 512])
        self.qn_g = self.din("qn_g", [L, 64])
        self.kn_g = self.din("kn_g", [L, 64])
        self.m_gn = self.din("m_gn", [L, 512])
        self.w_br = self.din("w_br", [L, 3, 512, D])
        self.w_out = self.din("w_out", [L, D, D])
        self.ln1 = self.din("ln1", [L, 2, D])
        self.w_up = self.din("w_up", [L, D, 2 * DFF])
        self.convp = self.din("convp", [L, 128, 4, 44])
        self.w_down = self.din("w_down", [L, DFF, D])
        self.ln2 = self.din("ln2", [L, 2, D])
        self.consts = self.din("consts", [128, 1024])
        self.out = self.fw.dram("out", [32 * 128, D], F32, kind="ExternalOutput")
        self.X = self.dscr("X", [T, D], F32)
        self.X1 = self.dscr("X1", [T, D], F32)
        self.MOD = self.dscr("MOD", [L, 2, 6 * D], F32)
        self.TMd = self.dscr("TMd", [NT, 128, R_COLS], BF16)
        self.SMd = self.dscr("SMd", [NT, 128, SM_COLS], F32)
        self.FMd = self.dscr("FMd", [NT, 128, 1024], BF16)
        self.AQd = self.dscr("AQd", [NT, 64, 1024], BF16)
        self.AKd = self.dscr("AKd", [64, 2, T], BF16)
        self.ROPEd = self.dscr("ROPEd", [64, 32], F32)
        self.RETCd = self.dscr("RETCd", [L, 128, 16], F32)
        self.SFd = self.dscr("SFd", [NT, 128, 4 * 258], BF16)
        self.SBd = self.dscr("SBd", [NT, 128, 4 * 258], BF16)
        self.YDd = self.dscr("YDd", [NT, 128, 1536], BF16)

        with ExitStack() as es0:
            self.setup_globals(es0)
            fw.barrier()
            fw.pool_base = fw.sem_idx
            if self.stop_after == "S":
                return self.finish()
            for l in range(self.n_layers):
                if "A" not in self.skip:
                    self.phase_a(l)
                fw.barrier()
                fw.reset_pool()
                if self.stop_after == "A%d" % l:
                    return self.finish()
                if "B" not in self.skip:
                    self.phase_b(l)
                else:
                    self.x1buf = Buf("X1")
                fw.barrier()
                fw.reset_pool()
                if self.stop_after == "B%d" % l:
                    return self.finish()
                self.phase_d(l)
                fw.barrier()
                fw.reset_pool()
                if self.stop_after == "D%d" % l:
                    return self.finish()
            self.finish()

    def finish(self):
        fw = self.fw
        fw.barrier()
        for ds in fw.dsems:
            fw.sp.wait(ds, ds.total)

    def setup_globals(self, es):
        nc, fw = self.nc, self.fw
        self.cst = fw.sb(es, "cst", [128, 1024], F32)
        fw.dma(fw.sp, self.cst.ap, self.consts, writes=[self.cst], ds=self.cst.ds)
        self.ident_f = self.cst[:, 0:128]
        self.triF = self.cst[:, 128:256]
        self.triB = self.cst[:, 256:384]
        self.ones_f = self.cst[:, 384:512]
        self.identb = fw.sb(es, "identb", [128, 128], BF16)
        cp(fw, fw.dve, self.identb.ap, self.ident_f, [self.cst], [self.identb])
        self.eps_t = fw.sb(es, "eps_t", [128, 1], F32)
        fw.eps_t = self.eps_t
        fw.op(fw.dve, lambda e: e.memset(self.eps_t.ap, EPS), [], [self.eps_t])
        self.onesb = fw.sb(es, "onesb", [128, 128], BF16)
        fw.op(fw.dve, lambda e: e.memset(self.onesb.ap, 1.0), [], [self.onesb])
        self.one_t = fw.sb(es, "one_t", [128, 1], F32)
        fw.op(fw.dve, lambda e: e.memset(self.one_t.ap, 1.0), [], [self.one_t])
        self.maskF = fw.sb(es, "maskF", [128, 4, 128], BF16)
        self.maskB = fw.sb(es, "maskB", [128, 4, 128], BF16)
        for h in range(4):
            cp(fw, fw.dve, self.maskF[:, h, :], self.triF, [self.cst], [self.maskF])
            cp(fw, fw.dve, self.maskB[:, h, :], self.triB, [self.cst], [self.maskB])
        self.ps = []
        for i in range(8):
            t = es.enter_context(nc.psum_tensor("psb%d" % i, [128, 512], F32))
            self.ps.append(Tl(fw, t[:], "psb%d" % i))
        xds = fw.new_sem("xcopy")
        fw.dma(fw.sp, self.X, self.xin, ds=xds)
        self.xbuf = Buf("Xall")
        self.xbuf.w = {xds: xds.total}
        self.compute_mod(es)
        self.compute_rope(es)

    def compute_mod(self, es0):
        nc, fw = self.nc, self.fw
        with ExitStack() as es:
            cc = fw.sb(es, "cc", [128, 8, 2], F32)
            sc = fw.sb(es, "sc", [128, 8, 2], F32)
            fw.dma(fw.sp, cc.ap, self.ccT, writes=[cc], ds=cc.ds)
            act(fw, sc.ap, cc.ap, AF.Silu, [cc], [sc])
            wring = fw.ring(es, "wm", [128, 8, 512], F32, 3)
            bm = fw.sb(es, "bm", [2, 6 * D], F32)
            orow = fw.ring(es, "orow", [2, 6 * D], F32, 2)
            k = 0
            for l in range(self.n_layers):
                fw.dma(fw.sp, bm.ap, self.b_mod[l:l + 1, :].to_broadcast([2, 6 * D]), writes=[bm], ds=bm.ds)
                orw = orow[l % 2]
                for g in range(12):
                    wt = wring[k % 3]
                    src = self.w_mod[l].rearrange("(kc p) c -> p kc c", p=128)[:, :, g * 512:(g + 1) * 512]
                    fw.dma(fw.sp, wt.ap, src, writes=[wt], ds=wt.ds)
                    pst = self.ps[k % 2]
                    for kc in range(8):
                        mm(fw, pst[0:2, :], sc[:, kc, :], wt[:, kc, :], kc == 0, kc == 7, [sc, wt], [pst])
                    tt(fw, fw.dve, orw[:, g * 512:(g + 1) * 512], pst[0:2, :], bm[:, g * 512:(g + 1) * 512], ALU.add, [pst, bm], [orw])
                    k += 1
                for ch in (1, 4):
                    ts(fw, fw.dve, orw[:, ch * D:(ch + 1) * D], orw[:, ch * D:(ch + 1) * D], 1.0, None, ALU.add, None, [orw], [orw])
                fw.dma(fw.sp, self.MOD[l], orw.ap, reads=[orw], ds=orw.ds)
            self.modbuf = Buf("MOD")
            for o in orow:
                self.modbuf.w[o.ds] = o.ds.total
            fw.barrier()

    def compute_rope(self, es0):
        nc, fw = self.nc, self.fw
        with ExitStack() as es:
            tl = fw.sb(es, "rp", [128, 8, 32], F32)
            itl = fw.sb(es, "rpi", [128, 32], I32)
            c = self.cst
            fr, u, r, fx, ang = (tl[:, i, :] for i in range(5))
            nidx = c[:, 516:517]
            act(fw, fr[:, 0:16], c[:, 517:533], AF.Exp, [c], [tl], scale=-math.log(10000.0) / 16.0)
            ts(fw, fw.dve, ang[:, 0:16], fr[:, 0:16], nidx, 1.0 / (2 * math.pi), ALU.mult, ALU.mult, [tl, c], [tl])
            ts(fw, fw.dve, u[:, 0:16], ang[:, 0:16], 0.25, None, ALU.add, None, [tl], [tl])
            cp(fw, fw.dve, u[:, 16:32], ang[:, 0:16], [tl], [tl])
            cp(fw, fw.dve, itl.ap, u, [tl], [itl])
            cp(fw, fw.dve, r, itl.ap, [itl], [tl])
            tt(fw, fw.dve, r, u, r, ALU.subtract, [tl], [tl])
            ts(fw, fw.dve, fx, r, 0.5, None, ALU.is_gt, None, [tl], [tl])
            tt(fw, fw.dve, r, r, fx, ALU.subtract, [tl], [tl])
            ts(fw, fw.dve, fx, r, -0.5, None, ALU.is_lt, None, [tl], [tl])
            tt(fw, fw.dve, r, r, fx, ALU.add, [tl], [tl])
            res = tl[:, 5, :]
            act(fw, res, r, AF.Sin, [tl], [tl], scale=2 * math.pi)
            fw.dma(fw.sp, self.ROPEd, tl[0:64, 5, :], reads=[tl], ds=tl.ds)
            self.ropebuf = Buf("rope")
            self.ropebuf.w = {tl.ds: tl.ds.total}
            fw.barrier()

    def load_bcast(self, dst_tl, dst_ap, src_row_ap, q=None, reads=()):
        fw = self.fw
        q = q or fw.sp
        n = src_row_ap.shape[-1]
        fw.dma(q, dst_ap, src_row_ap.to_broadcast([dst_ap.shape[0], n]), reads=list(reads), writes=[dst_tl], ds=dst_tl.ds, serialize=False)

    def phase_a(self, l):
        nc, fw = self.nc, self.fw
        ps = self.ps
        with ExitStack() as es:
            W = fw.sb(es, "Wa", [128, 8, W_COLS], BF16)
            wsrc = self.w_in[l].rearrange("(kc p) c -> p kc c", p=128)
            pieces = list(FM_PIECES)
            for name, pl_ in TM_GROUPS:
                o = TMW0 + TM_OFF[name]
                for (src, n) in pl_:
                    pieces.append((o, src, n))
                    o += n
            for (dst, src, n) in pieces:
                fw.dma(fw.pool, W[:, :, dst:dst + n], wsrc[:, :, src:src + n], writes=[W], ds=W.ds, serialize=False)
            BB = fw.sb(es, "BBa", [128, TM_COLS - 16], BF16)
            BG = fw.sb(es, "BGa", [128, 16], F32)
            for name, pl_ in TM_GROUPS:
                o = TM_OFF[name]
                for (src, n) in pl_:
                    row = self.b_in[l:l + 1, src:src + n]
                    if name == "MG":
                        self.load_bcast(BG, BG[:, o:o + n], row)
                    else:
                        self.load_bcast(BB, BB[:, o - 16:o - 16 + n], row, q=fw.pool)
                    o += n
            bfm = fw.sb(es, "bfm", [128, 8], F32)
            fw.dma(fw.sp, bfm.ap, self.b_in_fm[l], writes=[bfm], ds=bfm.ds)
            modt = fw.sb(es, "moda", [128, 2, D], F32)

            def load_mod(j):
                for ch in range(2):
                    self.load_bcast(modt, modt[:, ch, :], self.MOD[l, j:j + 1, ch * D:(ch + 1) * D], reads=[self.modbuf])
            load_mod(1)
            gn = fw.sb(es, "gna", [128, 2, 512], F32)
            self.load_bcast(gn, gn[:, 0, :], self.ret_gn[l:l + 1, :])
            self.load_bcast(gn, gn[:, 1, :], self.m_gn[l:l + 1, :])
            qk = fw.sb(es, "qka", [128, 2, 64], F32)
            self.load_bcast(qk, qk[:, 0, :], self.qn_g[l:l + 1, :])
            self.load_bcast(qk, qk[:, 1, :], self.kn_g[l:l + 1, :])
            ts(fw, fw.dve, qk[:, 0, :], qk[:, 0, :], 0.125, None, ALU.mult, None, [qk], [qk])
            colT = fw.sb(es, "colT", [128, 32], F32)
            rowT = fw.sb(es, "rowT", [128, 32, 32], F32)
            for hf in range(2):
                fw.dma(fw.sp, colT[hf * 64:(hf + 1) * 64, :], self.ROPEd, reads=[self.ropebuf], writes=[colT], ds=colT.ds, serialize=False)
                src = self.ROPEd.rearrange("(j two) c -> two j c", two=2)[hf:hf + 1]
                fw.dma(fw.sp, rowT[hf * 64:(hf + 1) * 64, :, :], src.to_broadcast([64, 32, 32]), reads=[self.ropebuf], writes=[rowT], ds=rowT.ds, serialize=False)
            dk = fw.sb(es, "dka", [128, 8, 8], F32)
            c = self.cst
            self.load_bcast(dk, dk[:, 0, :], self.decay[l:l + 1, :])
            act(fw, dk[:, 1, :], dk[:, 0, :], AF.Exp, [dk], [dk], scale=-1.0)
            act(fw, dk[:, 2, :], dk[:, 1, :], AF.Ln, [dk], [dk], bias=self.one_t[:, 0:1])
            rEA = dk[:, 3, :]
            rEB = dk[:, 4, :]
            rEE = dk[:, 5, :]
            act(fw, rEA[:, 0:4], dk[:, 2, 0:4], AF.Exp, [dk, c], [dk], scale=c[:, 512:513])
            act(fw, rEA[:, 4:8], dk[:, 2, 4:8], AF.Exp, [dk, c], [dk], scale=c[:, 513:514])
            act(fw, rEB[:, 0:4], dk[:, 2, 0:4], AF.Exp, [dk, c], [dk], scale=c[:, 514:515])
            act(fw, rEB[:, 4:8], dk[:, 2, 4:8], AF.Exp, [dk, c], [dk], scale=c[:, 515:516])
            act(fw, rEE, dk[:, 2, :], AF.Exp, [dk], [dk], scale=-128.0)
            retc = fw.sb(es, "retc", [128, 16], F32)
            cp(fw, fw.dve, retc[:, 0:8], rEB, [dk], [retc])
            for hf in range(2):
                cp(fw, fw.dve, retc[hf * 64:(hf + 1) * 64, 8:12].rearrange("p (d j) -> p d j", d=2),
                   rEE[hf * 64:(hf + 1) * 64, :].rearrange("p (d j two) -> p d j two", d=2, j=2)[:, :, :, hf], [dk], [retc])
            fw.dma(fw.sp, self.RETCd[l], retc.ap, reads=[retc], ds=retc.ds)

            xt = fw.sb(es, "xta", [128, D], F32)
            st6 = fw.sb(es, "st6a", [128, 2, 6], F32)
            mv = fw.sb(es, "mva", [128, 4], F32)
            xn = fw.sb(es, "xna", [128, D], F32)
            xm = fw.sb(es, "xma", [128, D], BF16)
            xmT = fw.ring(es, "xmTa", [128, 8, 128], BF16, 2)
            tmA = fw.sb(es, "tmAa", [128, R_SPLIT], BF16)
            tmB = fw.sb(es, "tmBa", [128, R_COLS - R_SPLIT], BF16)
            sm_ = fw.sb(es, "smra", [128, SM_COLS], F32)
            fm_ = fw.sb(es, "fmra", [128, 8, 128], BF16)
            aq_ = fw.sb(es, "aqra", [64, 8, 128], BF16)
            ak_ = fw.sb(es, "akra", [64, 2, 128], BF16)
            tmpA = fw.ring(es, "tmpAa", [128, 512], F32, 2)
            qpriv = fw.sb(es, "qpriva", [128, 512], F32)
            kpriv = fw.sb(es, "kpriva", [128, 128], F32)
            tmpB = fw.ring(es, "tmpBa", [128, 512], F32, 2)
            qb_ = fw.sb(es, "qba", [128, 640], BF16)
            g_ = fw.sb(es, "gtsa", [128, 64], F32)
            g16 = fw.sb(es, "g16a", [128, 16], BF16)
            sl_ = fw.sb(es, "smla", [128, 32], F32)
            fw.op(fw.dve, lambda e: e.memset(tmA[:, R_AV:R_AV + 130].rearrange("p (g c) -> p g c", g=2)[:, :, 64:65], 1.0), [], [tmA])

            ps_tr, ps_fm, ps_sm, ps_aq = ps[0], ps[1], ps[2], ps[3]
            ps_tm = ps[4:8]
            self._tmk = 0

            def s1(t):
                if t == NCT:
                    load_mod(0)
                fw.dma(fw.sp, xt.ap, self.X[t * 128:(t + 1) * 128, :], reads=[self.xbuf], writes=[xt], ds=xt.ds)
                for hh in range(2):
                    fw.op(fw.dve, lambda e, hh=hh: e.bn_stats(out=st6[:, hh, :], in_=xt[:, hh * 512:(hh + 1) * 512]), [xt], [st6])
                fw.op(fw.dve, lambda e: e.bn_aggr(out=mv[:, 0:2], in_=st6.ap.rearrange("p a b -> p (a b)")), [st6], [mv])
                rstd_from(fw, mv[:, 2:3], mv[:, 1:2], [mv], [mv])
                ts(fw, fw.dve, xn.ap, xt.ap, mv[:, 0:1], mv[:, 2:3], ALU.subtract, ALU.mult, [xt, mv], [xn])
                tt(fw, fw.dve, xn.ap, xn.ap, modt[:, 1, :], ALU.mult, [xn, modt], [xn])
                tt(fw, fw.dve, xm.ap, xn.ap, modt[:, 0, :], ALU.add, [xn, modt], [xm])

            def s2(t):
                xT_ = xmT[t % 2]
                pb = ps_tr.ap.bitcast(BF16).rearrange("p (a b) -> p a b", a=8)
                for kc in range(8):
                    tr(fw, pb[:, kc, :], xm[:, kc * 128:(kc + 1) * 128], self.identb.ap, [xm, self.identb], [ps_tr], inc=(kc == 7))
                cp(fw, fw.act, xT_.ap.rearrange("p a b -> p (a b)"), ps_tr.ap.bitcast(BF16), [ps_tr], [xT_])

            def tm_matmul(t, name, n):
                xT_ = xmT[t % 2]
                pst = ps_tm[self._tmk % len(ps_tm)]
                self._tmk += 1
                o = TMW0 + TM_OFF[name]
                for kc in range(8):
                    mm(fw, pst[:, 0:n], xT_[:, kc, :], W[:, kc, o:o + n], kc == 0, kc == 7, [xT_, W], [pst])
                return pst

            def bias_of(name, n, off=0):
                o = TM_OFF[name] - 16 + off
                return BB[:, o:o + n]

            def s3(t, mid_hook=None):
                xT_ = xmT[t % 2]
                for half in range(2):
                    for i4 in range(4):
                        i = half * 4 + i4
                        for kc in range(8):
                            mm(fw, ps_fm[:, i4 * 128:(i4 + 1) * 128], W[:, kc, i * 128:(i + 1) * 128], xT_[:, kc, :], kc == 0, kc == 7, [xT_, W], [ps_fm],
                               inc=(kc == 7 and i4 == 3))
                    for i4 in range(4):
                        i = half * 4 + i4
                        sc_ = 0.125 if i in (2, 3, 6, 7) else 1.0
                        ts(fw, fw.dve, fm_[:, i, :], ps_fm[:, i4 * 128:(i4 + 1) * 128], bfm[:, i:i + 1], sc_, ALU.add, ALU.mult, [ps_fm, bfm], [fm_])
                fw.dma(fw.sp, self.FMd[t], fm_.ap.rearrange("p a b -> p (a b)"), reads=[fm_], ds=fm_.ds)
                if lim is not None and len(lim) > 2 and lim[2] <= 1:
                    return
                pbk = ps_sm.ap.bitcast(BF16)
                for n_, i in enumerate((2, 3, 6, 7)):
                    tr(fw, pbk[:, 512 + n_ * 128:512 + (n_ + 1) * 128], fm_[:, i, :], self.identb.ap, [fm_, self.identb], [ps_sm], inc=(n_ == 3))
                cp(fw, fw.act, tmA[:, R_RK:R_RK + 512], pbk[:, 512:1024], [ps_sm], [tmA])
                if lim is not None and len(lim) > 2 and lim[2] <= 2:
                    return
                pst = tm_matmul(t, "MG", 16)
                tt(fw, fw.dve, g_[:, 0:16], pst[:, 0:16], BG.ap, ALU.add, [pst, BG], [g_])
                e_ = g_[:, 16:24]
                sp_ = g_[:, 24:32]
                act(fw, e_, g_[:, 8:16], AF.Exp, [g_], [g_], scale=-1.0)
                act(fw, sp_, e_, AF.Ln, [g_], [g_], bias=self.one_t[:, 0:1])
                hi32, lo32 = g_[:, 56:64], g_[:, 16:24]
                cp(fw, fw.dve, g16[:, 0:8], sp_, [g_], [g16])
                cp(fw, fw.dve, hi32, g16[:, 0:8], [g16], [g_])
                tt(fw, fw.dve, lo32, sp_, hi32, ALU.subtract, [g_], [g_])
                cp(fw, fw.dve, g16[:, 8:16], lo32, [g_], [g16])
                if lim is not None and len(lim) > 2 and lim[2] <= 3:
                    return
                pst = tm_matmul(t, "RV", 512)
                v_ = tmpA[0]
                tt(fw, fw.dve, v_.ap, pst.ap, bias_of("RV", 512), ALU.add, [pst, BB], [v_])
                for d in range(2):
                    eng = fw.dve if d == 0 else fw.pool
                    tt(fw, eng, tmA[:, R_RV + d * 512:R_RV + (d + 1) * 512].rearrange("p (h e) -> p h e", h=4),
                       v_.ap.rearrange("p (h e) -> p h e", h=4), rEA[:, d * 4:(d + 1) * 4].unsqueeze(2).to_broadcast([128, 4, 128]), ALU.mult, [v_, dk], [tmA])
                if lim is not None and len(lim) > 2 and lim[2] <= 4:
                    return
                pst = tm_matmul(t, "RG", 512)
                a_, b_ = tmpA[1], tmpB[0]
                tt(fw, fw.dve, a_.ap, pst.ap, bias_of("RG", 512), ALU.add, [pst, BB], [a_])
                act(fw, b_.ap, a_.ap, AF.Silu, [a_], [b_])
                tt(fw, fw.pool, tmA[:, R_RG:R_RG + 512], b_.ap, gn[:, 0, :], ALU.mult, [b_, gn], [tmA])
                if lim is not None and len(lim) > 2 and lim[2] <= 5:
                    return
                q_ = qpriv
                pst = tm_matmul(t, "AQ", 512)
                tt(fw, fw.dve, q_.ap, pst.ap, bias_of("AQ", 512), ALU.add, [pst, BB], [q_])
                self.norm_rope(t, q_, q_.ap, 8, qk[:, 0, :], qb_, qb_[:, 0:512], sl_, sl_[:, 0:8], tmpB[1], colT, rowT, qk)
                if lim is not None and len(lim) > 2 and lim[2] <= 6:
                    return
                pst = tm_matmul(t, "AKV", 256)
                k_ = kpriv
                tt(fw, fw.dve, k_[:, 0:128], pst[:, 0:128], bias_of("AKV", 128), ALU.add, [pst, BB], [k_])
                self.norm_rope(t, k_, k_[:, 0:128], 2, qk[:, 1, :], qb_, qb_[:, 512:640], sl_, sl_[:, 8:10], tmpB[1], colT, rowT, qk)
                tt(fw, fw.dve, tmA[:, R_AV:R_AV + 130].rearrange("p (g c) -> p g c", g=2)[:, :, 0:64],
                   pst[:, 128:256].rearrange("p (g c) -> p g c", g=2), bias_of("AKV", 128, 128).rearrange("p (g c) -> p g c", g=2), ALU.add, [pst, BB], [tmA])
                if lim is not None and len(lim) > 2 and lim[2] <= 7:
                    return
                for part in range(2):
                    o = part * 8
                    mm(fw, ps_sm[:, 0:4], self.maskF[:, 0, :], g16[:, o:o + 4], part == 0, part == 1, [g16, self.maskF], [ps_sm], inc=False)
                for part in range(2):
                    o = part * 8
                    mm(fw, ps_sm[:, 4:8], self.maskB[:, 0, :], g16[:, o + 4:o + 8], part == 0, part == 1, [g16, self.maskB], [ps_sm], inc=False)
                for part in range(2):
                    o = part * 8
                    mm(fw, ps_sm[:, 8:16], self.onesb.ap, g16[:, o:o + 8], part == 0, part == 1, [g16, self.onesb], [ps_sm], inc=(part == 1))
                ta = g_[:, 32:40]
                EA = g_[:, 40:48]
                tt(fw, fw.dve, ta, g_[:, 0:8], ps_sm[:, 0:8], ALU.add, [g_, ps_sm], [g_])
                act(fw, EA, ta, AF.Exp, [g_], [g_])
                act(fw, sm_[:, 0:8], ps_sm[:, 0:8], AF.Exp, [ps_sm], [sm_], scale=-1.0)
                ebe = g_[:, 48:56]
                act(fw, ebe, ps_sm[:, 8:16], AF.Exp, [ps_sm], [g_], scale=-1.0)
                for hf in range(2):
                    cp(fw, fw.dve, sm_[hf * 64:(hf + 1) * 64, 8:12].rearrange("p (d j) -> p d j", d=2),
                       ebe[hf * 64:(hf + 1) * 64, :].rearrange("p (d j two) -> p d j two", d=2, j=2)[:, :, :, hf], [g_], [sm_])
                fw.dma(fw.sp, self.SMd[t], sm_.ap, reads=[sm_], ds=sm_.ds)
                pst = tm_matmul(t, "MV", 512)
                v_ = tmpA[0]
                tt(fw, fw.dve, v_.ap, pst.ap, bias_of("MV", 512), ALU.add, [pst, BB], [v_])
                for d in range(2):
                    eng = fw.dve if d == 0 else fw.pool
                    dst = tmA[:, R_MV + d * 516:R_MV + (d + 1) * 516].rearrange("p (h e) -> p h e", h=4)
                    tt(fw, eng, dst[:, :, 0:128], v_.ap.rearrange("p (h e) -> p h e", h=4),
                       EA[:, d * 4:(d + 1) * 4].unsqueeze(2).to_broadcast([128, 4, 128]), ALU.mult, [v_, g_], [tmA])
                    cp(fw, eng, dst[:, :, 128:129], EA[:, d * 4:(d + 1) * 4].unsqueeze(2), [g_], [tmA])
                if lim is not None and len(lim) > 2 and lim[2] <= 9:
                    return
                pst = tm_matmul(t, "MO", 512)
                a_, b_ = tmpA[1], tmpB[0]
                tt(fw, fw.dve, a_.ap, pst.ap, bias_of("MO", 512), ALU.add, [pst, BB], [a_])
                act(fw, b_.ap, a_.ap, AF.Sigmoid, [a_], [b_])
                tt(fw, fw.pool, tmA[:, R_MO:R_MO + 512], b_.ap, gn[:, 1, :], ALU.mult, [b_, gn], [tmA])
                fw.dma(fw.sp, self.TMd[t][:, 0:R_SPLIT], tmA.ap, reads=[tmA], ds=tmA.ds)
                if mid_hook is not None:
                    mid_hook()
                if lim is not None and len(lim) > 2 and lim[2] <= 10:
                    return
                for i in range(6):
                    pst = tm_matmul(t, "G%d" % i, 512)
                    a_ = tmpA[i % 2]
                    tt(fw, fw.dve, a_.ap, pst.ap, bias_of("G%d" % i, 512), ALU.add, [pst, BB], [a_])
                    act(fw, tmB[:, i * 512:(i + 1) * 512], a_.ap, AF.Sigmoid, [a_], [tmB])
                fw.dma(fw.sp, self.TMd[t][:, R_SPLIT:R_COLS], tmB.ap, reads=[tmB], ds=tmB.ds)
                pbq = ps_aq.ap.bitcast(BF16)
                for h in range(8):
                    tr(fw, pbq[0:64, h * 128:(h + 1) * 128], qb_[:, h * 64:(h + 1) * 64], self.identb.ap, [qb_, self.identb], [ps_aq], inc=(h == 7))
                for h in range(2):
                    tr(fw, pbk[0:64, 256 + h * 128:256 + (h + 1) * 128], qb_[:, 512 + h * 64:512 + (h + 1) * 64], self.identb.ap, [qb_, self.identb], [ps_sm], inc=(h == 1))
                cp(fw, fw.act, aq_.ap.rearrange("p a b -> p (a b)"), pbq[0:64, :], [ps_aq], [aq_])
                cp(fw, fw.act, ak_.ap.rearrange("p a b -> p (a b)"), pbk[0:64, 256:512], [ps_sm], [ak_])
                fw.dma(fw.sp, self.AQd[t], aq_.ap.rearrange("p a b -> p (a b)"), reads=[aq_], ds=aq_.ds)
                fw.dma(fw.sp, self.AKd[:, :, t * 128:(t + 1) * 128], ak_.ap, reads=[ak_], ds=ak_.ds)
                if lim is not None and len(lim) > 2 and lim[2] <= 8:
                    return

            lim = getattr(self, "a_lim", None)
            if lim == "pre":
                fw.barrier()
                return
            nt = NT if lim is None else lim[0]
            s1(0)
            s2(0)
            for t in range(nt):
                if t + 1 < nt:
                    s1(t + 1)
                    s3(t, (lambda t=t: s2(t + 1)))
                else:
                    s3(t)
            fw.barrier()

    def phase_b(self, l):
        nc, fw = self.nc, self.fw
        ps = self.ps
        last = (l == DEPTH - 1)
        with ExitStack() as es:
            WB = fw.sb(es, "WBb", [128, 3, 4, D], BF16)
            for b in range(3):
                fw.dma(fw.pool, WB[:, b, :, :], self.w_br[l, b].rearrange("(kc p) c -> p kc c", p=128), writes=[WB], ds=WB.ds, serialize=False)
            WO = fw.sb(es, "WOb", [128, 8, D], BF16)
            wo_src = self.w_out[l].rearrange("(kc p) c -> p kc c", p=128)
            for hh in range(2):
                fw.dma(fw.pool, WO[:, hh * 4:(hh + 1) * 4, :], wo_src[:, hh * 4:(hh + 1) * 4, :], writes=[WO], ds=WO.ds, serialize=False)
            AKT = fw.sb(es, "AKTb", [64, 2, T], BF16)
            fw.dma(fw.sp, AKT.ap, self.AKd, writes=[AKT], ds=AKT.ds)
            AVa = fw.sb(es, "AVab", [128, NT, 130], BF16)
            for q in range(0, NT, 8):
                q1 = min(NT, q + 8)
                fw.dma(fw.sp, AVa[:, q:q1, :], self.TMd[q:q1, :, R_AV:R_AV + 130].rearrange("t p c -> p t c"), writes=[AVa], ds=AVa.ds, serialize=False)
            retc = fw.sb(es, "retcb", [128, 16], F32)
            fw.dma(fw.sp, retc.ap, self.RETCd[l], writes=[retc], ds=retc.ds)
            gms = fw.sb(es, "gmsb", [128, D], F32)
            ln1t = fw.sb(es, "ln1tb", [128, 2, D], F32)
            for i in range(2):
                self.load_bcast(ln1t, ln1t[:, i, :], self.ln1[l, i:i + 1, :])

            def load_gms(j):
                self.load_bcast(gms, gms.ap, self.MOD[l, j:j + 1, 2 * D:3 * D], reads=[self.modbuf])
            load_gms(1)
            orders = {0: list(range(NT)), 1: [1, 0] + list(range(NT - 1, 1, -1))}
            SXd = (self.SFd, self.SBd)

            with ExitStack() as es2:
                S = [[fw.sb(es2, "Sst%d_%d" % (d, k), [128, 258], F32) for k in range(4)] for d in range(2)]
                for d in range(2):
                    for k in range(4):
                        fw.op(fw.dve if k % 2 == 0 else fw.pool, lambda e, d=d, k=k: e.memset(S[d][k].ap, 0.0), [], [S[d][k]])
                Sbf = [fw.ring(es2, "Sbf%d" % d, [128, 4, 258], BF16, 2) for d in range(2)]
                ldr = [fw.ring(es2, "ldr%d" % d, [128, 1540], BF16, 3) for d in range(2)]
                smr = [fw.ring(es2, "smr%d" % d, [128, SM_COLS], F32, 3) for d in range(2)]
                for i in range(NT):
                    for d in range(2):
                        t = orders[d][i]
                        L_ = ldr[d][i % 3]
                        sm_ = smr[d][i % 3]
                        fw.dma(fw.sp, L_[:, 0:512], self.TMd[t][:, R_RK:R_RK + 512], writes=[L_], ds=L_.ds)
                        fw.dma(fw.sp, L_[:, 512:1024], self.TMd[t][:, R_RV + d * 512:R_RV + (d + 1) * 512], writes=[L_], ds=L_.ds, serialize=False)
                        fw.dma(fw.sp, L_[:, 1024:1540], self.TMd[t][:, R_MV + d * 516:R_MV + (d + 1) * 516], writes=[L_], ds=L_.ds, serialize=False)
                        fw.dma(fw.sp, sm_.ap, self.SMd[t], writes=[sm_], ds=sm_.ds)
                        for mxj in range(4):
                            mx, j = mxj // 2, mxj % 2
                            W_ = 128 if mx == 0 else 129
                            pst = ps[d * 4 + mxj]
                            K_ = L_[:, mx * 256 + j * 128:mx * 256 + (j + 1) * 128]
                            for blk in range(2):
                                h = 2 * j + blk
                                V_ = L_[:, 512 + h * 128:512 + (h + 1) * 128] if mx == 0 else L_[:, 1024 + h * 129:1024 + (h + 1) * 129]
                                mm(fw, pst[:, blk * 129:blk * 129 + W_], K_, V_, True, True, [L_], [pst], inc=(blk == 1))
                        for mxj in range(4):
                            mx, j = mxj // 2, mxj % 2
                            W_ = 128 if mx == 0 else 129
                            pst = ps[d * 4 + mxj]
                            St = S[d][mxj]
                            sb_ = Sbf[d][i % 2]
                            cp(fw, fw.act, sb_[:, mxj, :], St.ap, [St], [sb_])
                            if mxj == 3:
                                fw.dma(fw.pool, SXd[d][t], sb_.ap.rearrange("p a b -> p (a b)"), reads=[sb_], ds=sb_.ds)
                            e_ = retc[:, 8 + d * 2 + j:9 + d * 2 + j] if mx == 0 else sm_[:, 8 + d * 2 + j:9 + d * 2 + j]
                            e_src = retc if mx == 0 else sm_
                            Sv = St.ap.rearrange("p (b w) -> p b w", b=2)[:, :, 0:W_]
                            Pv = pst[:, 0:258].rearrange("p (b w) -> p b w", b=2)[:, :, 0:W_]
                            act(fw, Sv, Sv, AF.Identity, [St, e_src], [St], scale=e_)
                            stt(fw, fw.dve, Sv, Pv, e_, Sv, ALU.mult, ALU.add, [pst, e_src, St], [St])
                fw.barrier()
            self.sxbuf = Buf("SX")

            tmA = fw.ring(es, "tmAb", [128, R_SPLIT], BF16, 2)
            tmG = fw.ring(es, "tmGb", [128, R_COLS - R_SPLIT], BF16, 3)
            smr = fw.ring(es, "smrb", [128, SM_COLS], F32, 2)
            fmr = fw.ring(es, "fmrb", [128, 8, 128], BF16, 2)
            aqr = fw.ring(es, "aqrb", [64, 8, 128], BF16, 2)
            sfr = [fw.ring(es, "sxr%d" % d, [128, 4, 258], BF16, 2) for d in range(2)]
            xr = fw.ring(es, "xrb", [128, D], F32, 2)
            PT = fw.ring(es, "PTb", [128, 2, 512], BF16, 2)
            pTr = fw.ring(es, "pTb", [128, 512], BF16, 3)
            yf = fw.ring(es, "yfb", [128, 512], F32, 4)
            sml = fw.ring(es, "smlb", [128, 64], F32, 2)
            st4 = fw.sb(es, "st4b", [128, 4, 6], F32)
            ymix = fw.ring(es, "ymixb", [128, 3, 512], BF16, 2)
            yT = fw.ring(es, "yTb", [128, 12, 128], BF16, 2)
            zt = fw.ring(es, "ztb", [128, D], F32, 2)
            zb = fw.sb(es, "zbb", [128, D], BF16)
            zT = fw.sb(es, "zTb", [128, 8, 128], BF16)
            rr = fw.ring(es, "rrb", [128, D], F32, 2)
            st6 = fw.sb(es, "st6b", [128, 2, 6], F32)
            mv = fw.sb(es, "mvb", [128, 4], F32)
            ps_s, ps_of, ps_ob, ps_sc, ps_acc = ps[0], (ps[1], ps[2]), (ps[3], ps[4]), (ps[5], ps[6]), ps[7]
            self._yk = 0

            def nexty():
                self._yk += 1
                return yf[self._yk % 4]

            def loads(t):
                fw.dma(fw.sp, tmA[t % 2].ap, self.TMd[t][:, 0:R_SPLIT], writes=[tmA[t % 2]], ds=tmA[t % 2].ds)
                fw.dma(fw.sp, tmG[t % 3].ap, self.TMd[t][:, R_SPLIT:R_COLS], writes=[tmG[t % 3]], ds=tmG[t % 3].ds)
                fw.dma(fw.sp, smr[t % 2].ap, self.SMd[t], writes=[smr[t % 2]], ds=smr[t % 2].ds)
                fw.dma(fw.sp, fmr[t % 2].ap.rearrange("p a b -> p (a b)"), self.FMd[t], writes=[fmr[t % 2]], ds=fmr[t % 2].ds)
                fw.dma(fw.sp, aqr[t % 2].ap.rearrange("p a b -> p (a b)"), self.AQd[t], writes=[aqr[t % 2]], ds=aqr[t % 2].ds)
                for d in range(2):
                    fw.dma(fw.sp, sfr[d][t % 2].ap.rearrange("p a b -> p (a b)"), SXd[d][t], writes=[sfr[d][t % 2]], ds=sfr[d][t % 2].ds)

            def par(ap2, hp, n=2):
                return ap2.rearrange("p (j two k) -> p j two k", j=2, two=2)[:, :, hp, :]

            def linattn(t, mx):
                ta_, sm_, fm_ = tmA[t % 2], smr[t % 2], fmr[t % 2]
                W_ = 128 if mx == 0 else 129
                qi, ki = (0, 2) if mx == 0 else (4, 6)
                pt_ = PT[(2 * t + mx) % 2]
                for h in range(4):
                    j, hf = h // 2, h % 2
                    mm(fw, ps_sc[hf][:, j * 128:(j + 1) * 128], fm_[hf * 64:(hf + 1) * 64, ki + j, :], fm_[hf * 64:(hf + 1) * 64, qi + j, :], True, True, [fm_], [ps_sc[hf]], inc=(h >= 2))
                for d, msk in ((0, self.maskF), (1, self.maskB)):
                    for hp in range(2):
                        tt(fw, fw.dve, par(pt_[:, d, :], hp), ps_sc[hp][:, 0:256].rearrange("p (j k) -> p j k", j=2), msk[:, 0:2, :], ALU.mult, [ps_sc[hp], msk], [pt_])
                for d in range(2):
                    banks = ps_of if d == 0 else ps_ob
                    S_ = sfr[d][t % 2]
                    for h in (0, 2, 1, 3):
                        j, hf = h // 2, h % 2
                        bank = banks[hf]
                        if mx == 0:
                            V_ = ta_[:, R_RV + d * 512 + h * 128:R_RV + d * 512 + (h + 1) * 128]
                        else:
                            V_ = ta_[:, R_MV + d * 516 + h * 129:R_MV + d * 516 + (h + 1) * 129]
                        out = bank[:, j * 129:j * 129 + W_]
                        mm(fw, out, pt_[:, d, h * 128:(h + 1) * 128], V_, (j == 0), False, [pt_, ta_], [bank], inc=False, skip_group_check=True)
                        Sv = S_[hf * 64:(hf + 1) * 64, mx * 2 + j, hf * 129:hf * 129 + W_]
                        mm(fw, out, fm_[hf * 64:(hf + 1) * 64, qi + j, :], Sv, False, True, [fm_, S_], [bank], inc=(j == 1), skip_group_check=True)
                y = nexty()
                sl = sml[t % 2]
                if mx == 0:
                    t1 = nexty()
                    for hp in range(2):
                        o_f = ps_of[hp][:, 0:258].rearrange("p (j w) -> p j w", j=2)[:, :, 0:128]
                        o_b = ps_ob[hp][:, 0:258].rearrange("p (j w) -> p j w", j=2)[:, :, 0:128]
                        ebf = par(retc[:, 0:4], hp).to_broadcast([128, 2, 128])
                        ebb = par(retc[:, 4:8], hp).to_broadcast([128, 2, 128])
                        tt(fw, fw.dve, par(t1.ap, hp), o_f, ebf, ALU.mult, [ps_of[hp], retc], [t1])
                        tt(fw, fw.dve, par(y.ap, hp), o_b, ebb, ALU.mult, [ps_ob[hp], retc], [y])
                    tt(fw, fw.pool, y.ap, y.ap, t1.ap, ALU.add, [y, t1], [y])
                else:
                    hd = []
                    for d in range(2):
                        banks = ps_of if d == 0 else ps_ob
                        q1 = sl[:, d * 16:d * 16 + 4]
                        q2 = sl[:, d * 16 + 4:d * 16 + 8]
                        r_ = sl[:, d * 16 + 8:d * 16 + 12]
                        eb = sm_[:, d * 4:(d + 1) * 4]
                        for hp in range(2):
                            den = banks[hp][:, 0:258].rearrange("p (j w) -> p j w", j=2)[:, :, 128:129]
                            tt(fw, fw.dve, par(q1, hp), den, par(eb, hp), ALU.mult, [banks[hp], sm_], [sl])
                        stt(fw, fw.dve, q2, q1, -1.0, q1, ALU.mult, ALU.max, [sl], [sl])
                        ts(fw, fw.dve, q2, q2, 1.0, None, ALU.max, None, [sl], [sl])
                        fw.op(fw.dve, lambda e, q2=q2: e.reciprocal(out=q2, in_=q2), [sl], [sl])
                        tt(fw, fw.dve, r_, q2, eb, ALU.mult, [sl, sm_], [sl])
                        hdt = y if d == 0 else nexty()
                        for hp in range(2):
                            num = banks[hp][:, 0:258].rearrange("p (j w) -> p j w", j=2)[:, :, 0:128]
                            tt(fw, fw.dve, par(hdt.ap, hp), num, par(r_, hp).to_broadcast([128, 2, 128]), ALU.mult, [banks[hp], sl], [hdt])
                        hd.append(hdt)
                    tt(fw, fw.pool, y.ap, hd[0].ap, hd[1].ap, ALU.add, [hd[0], hd[1]], [y])
                y3 = y.ap.rearrange("p (h e) -> p h e", h=4)
                for h in range(4):
                    fw.op(fw.dve, lambda e, h=h: e.bn_stats(out=st4[:, h, :], in_=y3[:, h, :]), [y], [st4])
                mvh = sl[:, 32:40].rearrange("p (h two) -> p h two", h=4)
                for h in range(4):
                    fw.op(fw.dve, lambda e, h=h: e.bn_aggr(out=mvh[:, h, :], in_=st4[:, h, :]), [st4], [sl])
                rs = sl[:, 40:44]
                act(fw, rs, mvh[:, :, 1], AF.Ln, [sl], [sl], bias=self.eps_t[:, 0:1])
                act(fw, rs, rs, AF.Exp, [sl], [sl], scale=-0.5)
                for h in range(4):
                    ts(fw, fw.dve, y3[:, h, :], y3[:, h, :], mvh[:, h, 0:1], rs[:, h:h + 1], ALU.subtract, ALU.mult, [y, sl], [y])
                gcol = R_RG if mx == 0 else R_MO
                ym = ymix[t % 2]
                tt(fw, fw.pool, ym[:, 0 if mx == 0 else 2, :], y.ap, ta_[:, gcol:gcol + 512], ALU.mult, [y, ta_], [ym])

            def attention(t, hooks):
                aq_ = aqr[t % 2]
                ym = ymix[t % 2]
                kts = list(range(NCT)) if t < NCT else list(range(NT))
                its = [(g, n_, kt) for g in range(2) for n_, kt in enumerate(kts)]
                nk = len(kts)
                hk = list(hooks)
                every = max(1, (len(its) - 4) // max(1, len(hk))) if hk else 0

                def score(idx):
                    g, n_, kt = its[idx]
                    psc = ps_sc[idx % 2]
                    mm(fw, psc.ap, AKT[:, g, kt * 128:(kt + 1) * 128], aq_[:, g * 4:(g + 1) * 4, :].rearrange("p a b -> p (a b)"), True, True, [AKT, aq_], [psc])
                score(0)
                for idx, (g, n_, kt) in enumerate(its):
                    if idx + 1 < len(its):
                        score(idx + 1)
                    psc = ps_sc[idx % 2]
                    p_ = pTr[idx % 3]
                    act(fw, p_.ap, psc.ap, AF.Exp, [psc], [p_])
                    for r in range(4):
                        mm(fw, ps_acc[:, r * 65:(r + 1) * 65], p_[:, r * 128:(r + 1) * 128], AVa[:, kt, g * 65:(g + 1) * 65],
                           (n_ == 0 and r == 0), (n_ == nk - 1), [p_, AVa], [ps_acc], inc=(r == 3), skip_group_check=True)
                    if n_ == nk - 1:
                        sl = sml[t % 2]
                        rd = sl[:, 48 + g * 4:52 + g * 4]
                        acc3 = ps_acc[:, 0:260].rearrange("p (r c) -> p r c", r=4)
                        fw.op(fw.dve, lambda e, rd=rd, acc3=acc3: e.reciprocal(out=rd, in_=acc3[:, :, 64]), [ps_acc], [sl])
                        tt(fw, fw.dve, ym[:, 1, g * 256:(g + 1) * 256].rearrange("p (r c) -> p r c", r=4), acc3[:, :, 0:64],
                           rd.unsqueeze(2).to_broadcast([128, 4, 64]), ALU.mult, [ps_acc, sl], [ym])
                    if hk and every and idx >= 2 and (idx - 2) % every == 0:
                        hk.pop(0)()
                for h_ in hk:
                    h_()

            def merge_stages(t):
                ym, yT_, tg_, x_ = ymix[t % 2], yT[t % 2], tmG[t % 3], xr[t % 2]
                pb = ps_s.ap.bitcast(BF16)
                zsum = zt[0]

                def m_tr(grp):
                    def f():
                        if grp == 0:
                            if t == NCT:
                                load_gms(0)
                            if "YDd" in self.debug:
                                fw.dma(fw.sp, self.YDd[t], ym.ap.rearrange("p a b -> p (a b)"), reads=[ym], ds=ym.ds)
                            fw.dma(fw.sp, x_.ap, self.X[t * 128:(t + 1) * 128, :], reads=[self.xbuf], writes=[x_], ds=x_.ds)
                        n = 8 if grp == 0 else 4
                        for i in range(n):
                            ii = grp * 8 + i
                            tr(fw, pb[:, i * 128:(i + 1) * 128], ym[:, ii // 4, (ii % 4) * 128:(ii % 4 + 1) * 128], self.identb.ap, [ym, self.identb], [ps_s], inc=(i == n - 1))
                        cp(fw, fw.act, yT_[:, grp * 8:grp * 8 + n, :].rearrange("p a b -> p (a b)"), pb[:, 0:n * 128], [ps_s], [yT_])
                    return f

                def m_br(b):
                    def f():
                        banks = ps_of if b % 2 == 0 else ps_ob
                        for n in range(2):
                            for kc in range(4):
                                mm(fw, banks[n].ap, yT_[:, b * 4 + kc, :], WB[:, b, kc, n * 512:(n + 1) * 512], kc == 0, kc == 3, [yT_, WB], [banks[n]])
                        dst = zsum if b == 0 else zt[1]
                        for n in range(2):
                            tt(fw, fw.dve, dst[:, n * 512:(n + 1) * 512], banks[n].ap, tg_[:, b * D + n * 512:b * D + (n + 1) * 512], ALU.mult, [banks[n], tg_], [dst])
                        if b == 1:
                            tt(fw, fw.pool, zsum.ap, zsum.ap, dst.ap, ALU.add, [zsum, dst], [zsum])
                        if b == 2:
                            tt(fw, fw.pool, zb.ap, zsum.ap, dst.ap, ALU.add, [zsum, dst], [zb])
                    return f

                def m_zt():
                    pb8 = pb.rearrange("p (a b) -> p a b", a=8)
                    for kc in range(8):
                        tr(fw, pb8[:, kc, :], zb[:, kc * 128:(kc + 1) * 128], self.identb.ap, [zb, self.identb], [ps_s], inc=(kc == 7))
                    cp(fw, fw.act, zT.ap.rearrange("p a b -> p (a b)"), pb, [ps_s], [zT])

                def m_out():
                    for n in range(2):
                        for kc in range(8):
                            mm(fw, ps_ob[n].ap, zT[:, kc, :], WO[:, kc, n * 512:(n + 1) * 512], kc == 0, kc == 7, [zT, WO], [ps_ob[n]])
                    r_ = rr[0]
                    for n in range(2):
                        tt(fw, fw.dve, r_[:, n * 512:(n + 1) * 512], ps_ob[n].ap, gms[:, n * 512:(n + 1) * 512], ALU.mult, [ps_ob[n], gms], [r_])
                    stt(fw, fw.dve, r_.ap, x_.ap, ALPHA, r_.ap, ALU.mult, ALU.add, [x_, r_], [r_])

                def m_ln():
                    self.ln_affine(rr[0], rr[1], st6, mv, ln1t)
                    fw.dma(fw.sp, self.X1[t * 128:(t + 1) * 128, :], rr[1].ap, reads=[rr[1]], ds=rr[1].ds)
                return [m_tr(0), m_tr(1), m_br(0), m_br(1), m_br(2), m_zt, m_out, m_ln]

            t0 = NCT if last else 0
            loads(t0)
            for t in range(t0, NT):
                if t + 1 < NT:
                    loads(t + 1)
                linattn(t, 0)
                linattn(t, 1)
                attention(t, merge_stages(t - 1) if t - 1 >= t0 else [])
            for f_ in merge_stages(NT - 1):
                f_()
            self.x1buf = Buf("X1")
            fw.barrier()

    def phase_d(self, l):
        nc, fw = self.nc, self.fw
        ps = self.ps
        last = (l == DEPTH - 1)
        with ExitStack() as es:
            import os
            dsk = os.environ.get("D_SKIP", "").split(",")
            WU = fw.sb(es, "WUd", [128, 8, 2 * DFF], BF16)
            wu_src = self.w_up[l].rearrange("(kc p) c -> p kc c", p=128)
            for c0 in range(0, 2 * DFF if "wu" not in dsk else 0, 512):
                fw.dma(fw.pool, WU[:, :, c0:c0 + 512], wu_src[:, :, c0:c0 + 512], writes=[WU], ds=WU.ds, serialize=False)
            WD = fw.sb(es, "WDd", [128, NF, D], BF16)
            wd_src = self.w_down[l].rearrange("(f p) c -> p f c", p=128)
            for f0 in range(0, NF if "wd" not in dsk else 0, 4):
                f1 = min(NF, f0 + 4)
                fw.dma(fw.pool, WD[:, f0:f1, :], wd_src[:, f0:f1, :], writes=[WD], ds=WD.ds, serialize=False)
            cvp = fw.sb(es, "cvpd", [128, 4, 44], F32)
            if "cvp" not in dsk:
                fw.dma(fw.sp, cvp.ap, self.convp[l], writes=[cvp], ds=cvp.ds)
            modt = fw.sb(es, "modd", [128, 2, D], F32)
            ln2t = fw.sb(es, "ln2td", [128, 2, D], F32)
            for i in range(2):
                self.load_bcast(ln2t, ln2t[:, i, :], self.ln2[l, i:i + 1, :])

            def load_mod(j):
                for i in range(2):
                    self.load_bcast(modt, modt[:, i, :], self.MOD[l, j:j + 1, (3 + i) * D:(4 + i) * D], reads=[self.modbuf])

            def load_gate(j):
                self.load_bcast(gmlp, gmlp.ap, self.MOD[l, j:j + 1, 5 * D:6 * D], reads=[self.modbuf])
            gmlp = fw.sb(es, "gmlpd", [128, D], F32)
            load_mod(1)
            load_gate(1)
            x1r = fw.ring(es, "x1rd", [128, D], F32, 2)
            xn = fw.sb(es, "xnd", [128, D], F32)
            hb = fw.sb(es, "hbd", [128, D], BF16)
            HTB = fw.ring(es, "HTBd", [128, 8, 132], BF16, 3)
            cA = [fw.ring(es, "cAd%d" % n_, [128, 128], F32, 3) for n_ in range(2)]
            cB = [fw.ring(es, "cBd%d" % n_, [128, 128], F32, 3) for n_ in range(2)]
            sg = fw.ring(es, "sgd", [128, 128], F32, 3)
            actT_t = fw.ring(es, "actTd", [128, NF, 128], BF16, 2)
            actT = [[Tl(fw, a_[:, i, :], "aT%d" % i) for i in range(NF)] for a_ in actT_t]
            rr = [xn, fw.sb(es, "rrd1", [128, D], F32)]
            st6 = fw.sb(es, "st6d", [128, 2, 6], F32)
            mv = fw.sb(es, "mvd", [128, 4], F32)
            ps_tr = ps[0]
            ps_up = ps[1:5]
            ps_dn = (ps[5], ps[6])
            t0 = NCT if last else 0

            def seq_first(t):
                return t == 0 or t == NCT

            def seq_last(t):
                return t == NCT - 1 or t == NT - 1

            def f1(t):
                if t == NCT:
                    load_mod(0)
                x_ = x1r[t % 2]
                fw.dma(fw.sp, x_.ap, self.X1[t * 128:(t + 1) * 128, :], reads=[self.x1buf], writes=[x_], ds=x_.ds)
                for hh in range(2):
                    fw.op(fw.dve, lambda e, hh=hh: e.bn_stats(out=st6[:, hh, :], in_=x_[:, hh * 512:(hh + 1) * 512]), [x_], [st6])
                fw.op(fw.dve, lambda e: e.bn_aggr(out=mv[:, 0:2], in_=st6.ap.rearrange("p a b -> p (a b)")), [st6], [mv])
                rstd_from(fw, mv[:, 2:3], mv[:, 1:2], [mv], [mv])
                ts(fw, fw.dve, xn.ap, x_.ap, mv[:, 0:1], mv[:, 2:3], ALU.subtract, ALU.mult, [x_, mv], [xn])
                tt(fw, fw.pool, xn.ap, xn.ap, modt[:, 1, :], ALU.mult, [xn, modt], [xn])
                tt(fw, fw.dve, hb.ap, xn.ap, modt[:, 0, :], ALU.add, [xn, modt], [hb])
                pb = ps_tr.ap.bitcast(BF16).rearrange("p (a b) -> p a b", a=8)
                for kc in range(8):
                    tr(fw, pb[:, kc, :], hb[:, kc * 128:(kc + 1) * 128], self.identb.ap, [hb, self.identb], [ps_tr], inc=(kc == 7))
                H_ = HTB[t % 3]
                if "cpH" in dsk:
                    return
                cp(fw, fw.act, H_[:, :, 2:130], pb, [ps_tr], [H_])
                if "halo" in dsk:
                    return
                if seq_first(t):
                    fw.op(fw.dve, lambda e: e.memset(H_[:, :, 1:2], 0.0), [], [H_])
                else:
                    Hp = HTB[(t - 1) % 3]
                    cp(fw, fw.dve, Hp[:, :, 130:131], H_[:, :, 2:3], [H_], [Hp])
                if seq_last(t):
                    fw.op(fw.dve, lambda e: e.memset(H_[:, :, 130:131], 0.0), [], [H_])
                elif t + 1 < NT:
                    Hn = HTB[(t + 1) % 3]
                    cp(fw, fw.dve, Hn[:, :, 1:2], H_[:, :, 129:130], [H_], [Hn])

            def f2(t):
                if t == NCT:
                    load_gate(0)
                H_ = HTB[t % 3]
                x_ = x1r[t % 2]
                aT = actT[t % 2]
                SK = 5

                def down(i):
                    for n in range(2):
                        mm(fw, ps_dn[n].ap, aT[i].ap, WD[:, i, n * 512:(n + 1) * 512], i == 0, i == NF - 1, [aT[i], WD], [ps_dn[n]], inc=True)

                def tail(i):
                    s_ = sg[i % 3]
                    act(fw, s_.ap, cA[0][i % 3].ap, AF.Silu, [cA[0][i % 3]], [s_])
                    tt(fw, fw.pool, aT[i].ap, cA[1][i % 3].ap, s_.ap, ALU.mult, [cA[1][i % 3], s_], [aT[i]])

                npair = NF if self.d_lim is None else self.d_lim[1]
                for i in range(npair):
                    pu = ps_up[i % 4]
                    for n_, ch in enumerate((NF + i, i)):
                        for kc in range(8):
                            mm(fw, pu[:, n_ * 130:(n_ + 1) * 130], WU[:, kc, ch * 128:(ch + 1) * 128], H_[:, kc, 1:131], kc == 0, kc == 7, [WU, H_], [pu],
                               inc=(kc == 7 and n_ == 1))
                    if i >= SK and npair == NF:
                        down(i - SK)
                    for n_, ch in enumerate((NF + i, i)):
                        u = pu[:, n_ * 130:(n_ + 1) * 130]
                        a_, b_ = cA[n_][i % 3], cB[n_][i % 3]
                        act(fw, a_.ap, u[:, 1:129], AF.Identity, [pu, cvp], [a_], bias=cvp[:, 3, ch:ch + 1], scale=cvp[:, 1, ch:ch + 1])
                        stt(fw, fw.dve, a_.ap, u[:, 0:128], cvp[:, 0, ch:ch + 1], a_.ap, ALU.mult, ALU.add, [pu, cvp, a_], [a_])
                        stt(fw, fw.dve, a_.ap, u[:, 2:130], cvp[:, 2, ch:ch + 1], a_.ap, ALU.mult, ALU.add, [pu, cvp, a_], [a_])
                    if i >= 1:
                        tail(i - 1)
                if npair < NF:
                    return
                tail(NF - 1)
                for i in range(NF - SK, NF):
                    down(i)
                r_ = rr[0]
                for n in range(2):
                    tt(fw, fw.dve, r_[:, n * 512:(n + 1) * 512], ps_dn[n].ap, gmlp[:, n * 512:(n + 1) * 512], ALU.mult, [ps_dn[n], gmlp], [r_])
                stt(fw, fw.dve, r_.ap, x_.ap, ALPHA, r_.ap, ALU.mult, ALU.add, [x_, r_], [r_])
                self.ln_affine(r_, rr[1], st6, mv, ln2t)
                if last:
                    fw.dma(fw.sp, self.out[(t - NCT) * 128:(t - NCT + 1) * 128, :], rr[1].ap, reads=[rr[1]], ds=rr[1].ds)
                else:
                    fw.dma(fw.sp, self.X[t * 128:(t + 1) * 128, :], rr[1].ap, reads=[rr[1]], writes=[self.xbuf], ds=rr[1].ds)

            dl = self.d_lim
            nt_ = NT if dl is None else t0 + dl[0]
            if "f1" in dsk:
                fw.barrier()
                return
            f1(t0)
            for t in range(t0, nt_):
                if t + 1 < NT:
                    f1(t + 1)
                if dl is None or dl[1] > 0:
                    f2(t)
            fw.barrier()

    def ln_affine(self, src, dst, st6, mv, gbt):
        fw = self.fw
        for hh in range(2):
            fw.op(fw.dve, lambda e, hh=hh: e.bn_stats(out=st6[:, hh, :], in_=src[:, hh * 512:(hh + 1) * 512]), [src], [st6])
        fw.op(fw.dve, lambda e: e.bn_aggr(out=mv[:, 0:2], in_=st6.ap.rearrange("p a b -> p (a b)")), [st6], [mv])
        rstd_from(fw, mv[:, 2:3], mv[:, 1:2], [mv], [mv])
        ts(fw, fw.dve, dst.ap, src.ap, mv[:, 0:1], mv[:, 2:3], ALU.subtract, ALU.mult, [src, mv], [dst])
        tt(fw, fw.pool, dst.ap, dst.ap, gbt[:, 0, :], ALU.mult, [dst, gbt], [dst])
        tt(fw, fw.pool, dst.ap, dst.ap, gbt[:, 1, :], ALU.add, [dst, gbt], [dst])

    def norm_rope(self, t, src_tl, src, nh, gain, dst_tl, dst, sl_tl, ss, tmp_tl, colT, rowT, qk):
        fw = self.fw
        n = nh * 64
        sq = tmp_tl[:, 0:n]
        s3 = src.rearrange("p (h d) -> p h d", h=nh)
        tt(fw, fw.pool, sq, src, src, ALU.mult, [src_tl], [tmp_tl])
        fw.op(fw.dve, lambda e: e.tensor_reduce(out=ss, in_=sq.rearrange("p (h d) -> p h d", h=nh), axis=AX.X, op=ALU.add), [tmp_tl], [sl_tl])
        rstd_from(fw, ss, ss, [sl_tl], [sl_tl], scale=1.0 / 64.0)
        tt(fw, fw.pool, s3, s3, ss.unsqueeze(2).to_broadcast([128, nh, 64]), ALU.mult, [src_tl, sl_tl], [src_tl])
        is_lat = t >= NCT
        gb = gain.unsqueeze(1).to_broadcast([128, nh, 64])
        if not is_lat:
            tt(fw, fw.pool, dst.rearrange("p (h d) -> p h d", h=nh), s3, gb, ALU.mult, [src_tl, qk], [dst_tl])
            return
        tt(fw, fw.pool, s3, s3, gb, ALU.mult, [src_tl, qk], [src_tl])
        j = t - NCT
        s5 = src.rearrange("p (h a two i) -> p h a two i", h=nh, a=2, two=2)
        o5 = tmp_tl[:, 0:n].rearrange("p (h a two i) -> p h a two i", h=nh, a=2, two=2)
        d5 = dst.rearrange("p (h a two i) -> p h a two i", h=nh, a=2, two=2)
        for a, tab in ((0, rowT[:, j, :]), (1, colT.ap)):
            cos_b = tab[:, 0:16].unsqueeze(1).to_broadcast([128, nh, 16])
            sin_b = tab[:, 16:32].unsqueeze(1).to_broadcast([128, nh, 16])
            tabt = rowT if a == 0 else colT
            x1 = s5[:, :, a, 0, :]
            x2 = s5[:, :, a, 1, :]
            tt(fw, fw.pool, o5[:, :, a, 0, :], x2, sin_b, ALU.mult, [src_tl, tabt], [tmp_tl])
            tt(fw, fw.pool, o5[:, :, a, 1, :], x1, sin_b, ALU.mult, [src_tl, tabt], [tmp_tl])
            tt(fw, fw.pool, x1, x1, cos_b, ALU.mult, [src_tl, tabt], [src_tl])
            tt(fw, fw.pool, x2, x2, cos_b, ALU.mult, [src_tl, tabt], [src_tl])
            tt(fw, fw.pool, d5[:, :, a, 0, :], x1, o5[:, :, a, 0, :], ALU.subtract, [src_tl, tmp_tl], [dst_tl])
            tt(fw, fw.pool, d5[:, :, a, 1, :], x2, o5[:, :, a, 1, :], ALU.add, [src_tl, tmp_tl], [dst_tl])


def make_consts():
    c = np.zeros((128, 1024), np.float32)
    p = np.arange(128)
    c[:, 0:128] = np.eye(128, dtype=np.float32)
    c[:, 128:256] = (p[:, None] <= p[None, :])
    c[:, 256:384] = (p[:, None] >= p[None, :])
    c[:, 384:512] = 1.0
    c[:, 512] = p + 1
    c[:, 513] = 128 - p
    c[:, 514] = -(p + 1)
    c[:, 515] = -(128 - p)
    c[:, 516] = p % 64
    c[:, 517:533] = np.arange(16)[None, :]
    return c


def host_inputs(inputs, b, L=DEPTH):
    f = lambda a: np.ascontiguousarray(np.asarray(a, dtype=np.float32))
    inputs = {k: (np.asarray(v)[:L] if k not in ('x', 'c', 'ctx', 'c_ctx') else v) for k, v in inputs.items()}
    m = {}
    m["xin"] = f(np.concatenate([inputs["ctx"][b], inputs["x"][b]], axis=0))
    cc = np.stack([np.asarray(inputs["c"][b]), np.asarray(inputs["c_ctx"])], axis=-1)
    m["ccT"] = f(cc.reshape(8, 128, 2).transpose(1, 0, 2))
    m["w_mod"] = f(inputs["w_mod"])
    m["b_mod"] = f(inputs["b_mod"])
    m["w_in"] = f(inputs["w_in"])
    m["b_in"] = f(inputs["b_in"])
    fmcols = np.concatenate([np.arange(O_RQ, O_RQ + 256), np.arange(O_RK, O_RK + 256), np.arange(O_MQ, O_MQ + 256), np.arange(O_MK, O_MK + 256)])
    m["b_in_fm"] = f(np.asarray(inputs["b_in"])[:, fmcols].reshape(L, 8, 128).transpose(0, 2, 1))
    m["decay"] = f(np.asarray(inputs["ret_decay_logit"]).reshape(L, 8))
    m["ret_gn"] = f(inputs["ret_gn_g"])
    m["qn_g"] = f(inputs["attn_qn_g"])
    m["kn_g"] = f(inputs["attn_kn_g"])
    m["m_gn"] = f(inputs["mlstm_gn_g"])
    m["w_br"] = f(np.stack([inputs["w_br_ret"], inputs["w_br_att"], inputs["w_br_mlstm"]], axis=1))
    m["w_out"] = f(inputs["w_out"])
    m["ln1"] = f(np.stack([inputs["ln1_g"], inputs["ln1_b"]], axis=1))
    m["w_up"] = f(inputs["w_up"])
    cw = np.asarray(inputs["conv_w"])
    cb = np.asarray(inputs["conv_b"])
    cpk = np.concatenate([cw, cb[:, None, :]], axis=1)
    m["convp"] = f(cpk.reshape(L, 4, 44, 128).transpose(0, 3, 1, 2))
    m["w_down"] = f(inputs["w_down"])
    m["ln2"] = f(np.stack([inputs["ln2_g"], inputs["ln2_b"]], axis=1))
    m["consts"] = make_consts()
    return m


_PROG = {}


def kernel(**inputs):
    if "p" not in _PROG:
        _PROG["p"] = Prog()
    prog = _PROG["p"]
    in_maps = [host_inputs(inputs, b) for b in range(NCORES)]
    res = run_bass_kernel_spmd(prog.nc, in_maps, core_ids=list(range(NCORES)))
    out = np.stack([np.asarray(r["out"]).reshape(32 * 128, D) for r in res.results], axis=0)
    return out.astype(np.float32)
```

```python
import math
import numpy as np
from contextlib import ExitStack
import concourse.bass as bass
import concourse.mybir as mybir
from concourse.bass_utils import run_bass_kernel_spmd

F32 = mybir.dt.float32
BF16 = mybir.dt.bfloat16
I32 = mybir.dt.int32
AF = mybir.ActivationFunctionType
ALU = mybir.AluOpType
AX = mybir.AxisListType

D = 1024
DEPTH = 4
NT = 34
NCT = 2
T = NT * 128
DFF = 2816
NF = 22
EPS = 1e-6
ALPHA = (2.0 * DEPTH) ** 0.25
NCORES = 4

O_RQ, O_RK, O_RV, O_RG = 0, 256, 512, 1024
O_AQ, O_AK, O_AV = 1536, 2048, 2176
O_MQ, O_MK, O_MV, O_MO, O_MI, O_MF, O_GATE = 2304, 2560, 2816, 3328, 3840, 3848, 3856
N_IN = 6928

FM_PIECES = [(0, O_RQ, 256), (256, O_RK, 256), (512, O_MQ, 256), (768, O_MK, 256)]
TMW0 = 1024
TM_GROUPS = [
    ("MG", [(O_MI, 16)]),
    ("RV", [(O_RV, 512)]),
    ("RG", [(O_RG, 512)]),
    ("AQ", [(O_AQ, 512)]),
    ("AKV", [(O_AK, 128), (O_AV, 128)]),
    ("MV", [(O_MV, 512)]),
    ("MO", [(O_MO, 512)]),
] + [("G%d" % i, [(O_GATE + 512 * i, 512)]) for i in range(6)]
TM_OFF = {}
_o = 0
for _n, _p in TM_GROUPS:
    TM_OFF[_n] = _o
    _o += sum(n for _, n in _p)
TM_COLS = _o
W_COLS = TMW0 + TM_COLS

R_RV, R_RG, R_AV, R_RK, R_MK, R_MV, R_MO, R_MG = 0, 1024, 1536, 1666, 1922, 2178, 3210, 3722
R_COLS = 6794
SM_COLS = 16
R_SPLIT = R_MG


class Sem:
    def __init__(self, h, name):
        self.h = h
        self.name = name
        self.owner = None
        self.total = 0


class Buf:
    __slots__ = ("w", "r", "name")

    def __init__(self, name=""):
        self.w = {}
        self.r = {}
        self.name = name


class Tl:
    def __init__(self, fw, ap, name, buf=None):
        self.fw = fw
        self.ap = ap
        self.name = name
        self.buf = buf if buf is not None else Buf(name)
        self._ds = None

    @property
    def ds(self):
        if self._ds is None:
            self._ds = self.fw.pool_sem()
        return self._ds

    def __getitem__(self, idx):
        return self.ap[idx]


class Eng:
    def __init__(self, fw, name, eng):
        self.fw = fw
        self.name = name
        self.eng = eng
        self.sem = fw.new_sem("c_" + name)
        self.sem.owner = self
        self.cnt = 0
        self.waited = {}

    def wait(self, sem, val):
        if val <= 0:
            return
        if self.waited.get(sem, 0) >= val:
            return
        if sem.owner is not None:
            assert val <= sem.owner.cnt, ("wait on unissued instr", self.name, sem.name, val, sem.owner.cnt)
        else:
            assert val <= sem.total
        self.eng.wait_ge(sem.h, val)
        self.waited[sem] = val


class FW:
    def __init__(self, nc):
        self.nc = nc
        self.es = ExitStack()
        self.nsem = 0
        self.pe = Eng(self, "pe", nc.tensor)
        self.act = Eng(self, "act", nc.scalar)
        self.dve = Eng(self, "dve", nc.vector)
        self.pool = Eng(self, "pool", nc.gpsimd)
        self.sp = Eng(self, "sp", nc.sync)
        self.engs = [self.pe, self.act, self.dve, self.pool, self.sp]
        self.dsems = []
        self.swq = []
        self.ndram = 0
        self.sem_pool = []
        self.sem_idx = 0
        self.pool_base = 0
        self.nsb = 0

    def pool_sem(self):
        if self.sem_idx == len(self.sem_pool):
            self.sem_pool.append(self.new_sem("dp%d" % self.sem_idx))
        s = self.sem_pool[self.sem_idx]
        self.sem_idx += 1
        return s

    def reset_pool(self):
        self.sem_idx = self.pool_base

    def new_sem(self, name):
        h = self.es.enter_context(self.nc.semaphore(name + "_%d" % self.nsem))
        self.nsem += 1
        s = Sem(h, name)
        return s

    def sb(self, es, name, shape, dtype):
        self.nsb += 1
        t = es.enter_context(self.nc.sbuf_tensor("%s_%d" % (name, self.nsb), list(shape), dtype))
        return Tl(self, t[:], name)

    def ring(self, es, name, shape, dtype, n):
        return [self.sb(es, "%s%d" % (name, i), shape, dtype) for i in range(n)]

    def dram(self, name, shape, dtype, kind="Internal"):
        t = self.nc.dram_tensor(name, list(shape), dtype, kind=kind)
        return t.ap()

    def _deps(self, E, reads, writes, skip_sem=None):
        for b in reads:
            b = b.buf if isinstance(b, Tl) else b
            for sem, v in b.w.items():
                if sem is E.sem and E is self.pe:
                    continue
                E.wait(sem, v)
        for b in writes:
            b = b.buf if isinstance(b, Tl) else b
            for sem, v in list(b.w.items()) + list(b.r.items()):
                if sem is E.sem or sem is skip_sem:
                    continue
                E.wait(sem, v)

    def _mark(self, sem, tok, reads, writes):
        for b in reads:
            b = b.buf if isinstance(b, Tl) else b
            if b.r.get(sem, 0) < tok:
                b.r[sem] = tok
        for b in writes:
            b = b.buf if isinstance(b, Tl) else b
            b.w = {sem: tok}
            b.r = {}

    def op(self, E, fn, reads=(), writes=(), inc=True):
        self._deps(E, reads, writes)
        ins = fn(E.eng)
        if inc:
            E.cnt += 1
            ins.then_inc(E.sem.h, 1)
            tok = E.cnt
        else:
            tok = E.cnt + 1
        self._mark(E.sem, tok, reads, writes)
        return ins

    def dma(self, Q, out, in_, reads=(), writes=(), ds=None, serialize=True, **kw):
        self._deps(Q, reads, writes, skip_sem=(None if serialize else ds))
        if serialize and ds.total > 0:
            Q.wait(ds, ds.total)
        if Q is self.pool:
            while len(self.swq) >= 3:
                s_, v_ = self.swq.pop(0)
                Q.wait(s_, v_)
        ins = Q.eng.dma_start(out=out, in_=in_, **kw)
        ds.total += 16
        if Q is self.pool:
            self.swq.append((ds, ds.total))
        ins.then_inc(ds.h, 16)
        if ds not in self.dsems:
            self.dsems.append(ds)
        self._mark(ds, ds.total, reads, writes)
        return ins

    def barrier(self):
        for E in self.engs:
            for P in self.engs:
                if P is not E:
                    E.wait(P.sem, P.cnt)
            for ds in self.dsems:
                E.wait(ds, ds.total)


def tt(fw, E, out, in0, in1, op, reads, writes):
    return fw.op(E, lambda e: e.tensor_tensor(out=out, in0=in0, in1=in1, op=op), reads, writes)


def ts(fw, E, out, in0, s1, s2, op0, op1, reads, writes):
    if op1 is None:
        return fw.op(E, lambda e: e.tensor_scalar(out=out, in0=in0, scalar1=s1, scalar2=None, op0=op0), reads, writes)
    return fw.op(E, lambda e: e.tensor_scalar(out=out, in0=in0, scalar1=s1, scalar2=s2, op0=op0, op1=op1), reads, writes)


def stt(fw, E, out, in0, scalar, in1, op0, op1, reads, writes):
    return fw.op(E, lambda e: e.scalar_tensor_tensor(out=out, in0=in0, scalar=scalar, in1=in1, op0=op0, op1=op1), reads, writes)


def act(fw, out, in_, func, reads, writes, bias=None, scale=None):
    kw = {}
    if bias is not None:
        kw["bias"] = bias
    if scale is not None:
        kw["scale"] = scale
    return fw.op(fw.act, lambda e: e.activation(out=out, in_=in_, func=func, **kw), reads, writes)


def cp(fw, E, out, in_, reads, writes):
    if E is fw.act:
        return fw.op(E, lambda e: e.copy(out=out, in_=in_), reads, writes)
    return fw.op(E, lambda e: e.tensor_copy(out=out, in_=in_), reads, writes)


def mm(fw, out, lhsT, rhs, start, stop, reads, writes, inc=None, **kw):
    if inc is None:
        inc = stop
    return fw.op(fw.pe, lambda e: e.matmul(out, lhsT=lhsT, rhs=rhs, start=start, stop=stop, **kw), reads, writes, inc=inc)


def tr(fw, out, in_, ident, reads, writes, inc=True):
    return fw.op(fw.pe, lambda e: e.transpose(out, in_, ident), reads, writes, inc=inc)


def bcast_rows(ap2d, nparts):
    return ap2d.to_broadcast([nparts, ap2d.shape[-1]])


def rstd_from(fw, out, in_, reads, writes, scale=1.0):
    act(fw, out, in_, AF.Ln, reads, writes, bias=fw.eps_t[:, 0:1] if in_.shape[0] == 128 else fw.eps_t[0:in_.shape[0], 0:1], scale=scale)
    act(fw, out, out, AF.Exp, writes, writes, scale=-0.5)


class Prog:
    def __init__(self, n_layers=DEPTH, debug=None, stop_after=None, a_lim=None, skip="", d_lim=None):
        self.a_lim = a_lim
        self.skip = skip
        self.d_lim = d_lim
        self.n_layers = n_layers
        self.debug = debug or []
        self.stop_after = stop_after
        self.nc = bass.Bass("TRN2", target_bir_lowering=False)
        self.fw = FW(self.nc)
        self.inputs = {}
        self.build()

    def din(self, name, shape, dtype=F32):
        ap = self.fw.dram(name, shape, dtype, kind="ExternalInput")
        self.inputs[name] = ap
        return ap

    def dscr(self, name, shape, dtype):
        kind = "ExternalOutput" if name in self.debug else "Internal"
        return self.fw.dram(name, shape, dtype, kind=kind)

    def build(self):
        nc, fw = self.nc, self.fw
        L = self.n_layers
        self.xin = self.din("xin", [T, D])
        self.ccT = self.din("ccT", [128, 8, 2])
        self.w_mod = self.din("w_mod", [L, D, 6 * D])
        self.b_mod = self.din("b_mod", [L, 6 * D])
        self.w_in = self.din("w_in", [L, D, N_IN])
        self.b_in = self.din("b_in", [L, N_IN])
        self.b_in_fm = self.din("b_in_fm", [L, 128, 8])
        self.decay = self.din("decay", [L, 8])
        self.ret_gn = self.din("ret_gn", [L, 512])
        self.qn_g = self.din("qn_g", [L, 64])
        self.kn_g = self.din("kn_g", [L, 64])
        self.m_gn = self.din("m_gn", [L, 512])
        self.w_br = self.din("w_br", [L, 3, 512, D])
        self.w_out = self.din("w_out", [L, D, D])
        self.ln1 = self.din("ln1", [L, 2, D])
        self.w_up = self.din("w_up", [L, D, 2 * DFF])
        self.convp = self.din("convp", [L, 128, 4, 44])
        self.w_down = self.din("w_down", [L, DFF, D])
        self.ln2 = self.din("ln2", [L, 2, D])
        self.consts = self.din("consts", [128, 1024])
        self.out = self.fw.dram("out", [32 * 128, D], F32, kind="ExternalOutput")
        self.X = self.dscr("X", [T, D], F32)
        self.X1 = self.dscr("X1", [T, D], F32)
        self.MOD = self.dscr("MOD", [L, 2, 6 * D], F32)
        self.TMd = self.dscr("TMd", [NT, 128, R_COLS], BF16)
        self.SMd = self.dscr("SMd", [NT, 128, SM_COLS], F32)
        self.FMd = self.dscr("FMd", [NT, 128, 1024], BF16)
        self.AQd = self.dscr("AQd", [NT, 64, 1024], BF16)
        self.AKd = self.dscr("AKd", [64, 2, T], BF16)
        self.ROPEd = self.dscr("ROPEd", [64, 32], F32)
        self.RETCd = self.dscr("RETCd", [L, 128, 16], F32)
        self.SFd = self.dscr("SFd", [NT, 128, 4 * 258], BF16)
        self.SBd = self.dscr("SBd", [NT, 128, 4 * 258], BF16)
        self.YDd = self.dscr("YDd", [NT, 128, 1536], BF16)

        with ExitStack() as es0:
            self.setup_globals(es0)
            fw.barrier()
            fw.pool_base = fw.sem_idx
            if self.stop_after == "S":
                return self.finish()
            for l in range(self.n_layers):
                if "A" not in self.skip:
                    self.phase_a(l)
                fw.barrier()
                fw.reset_pool()
                if self.stop_after == "A%d" % l:
                    return self.finish()
                if "B" not in self.skip:
                    self.phase_b(l)
                else:
                    self.x1buf = Buf("X1")
                fw.barrier()
                fw.reset_pool()
                if self.stop_after == "B%d" % l:
                    return self.finish()
                self.phase_d(l)
                fw.barrier()
                fw.reset_pool()
                if self.stop_after == "D%d" % l:
                    return self.finish()
            self.finish()

    def finish(self):
        fw = self.fw
        fw.barrier()
        for ds in fw.dsems:
            fw.sp.wait(ds, ds.total)

    def setup_globals(self, es):
        nc, fw = self.nc, self.fw
        self.cst = fw.sb(es, "cst", [128, 1024], F32)
        fw.dma(fw.sp, self.cst.ap, self.consts, writes=[self.cst], ds=self.cst.ds)
        self.ident_f = self.cst[:, 0:128]
        self.triF = self.cst[:, 128:256]
        self.triB = self.cst[:, 256:384]
        self.ones_f = self.cst[:, 384:512]
        self.identb = fw.sb(es, "identb", [128, 128], BF16)
        cp(fw, fw.dve, self.identb.ap, self.ident_f, [self.cst], [self.identb])
        self.eps_t = fw.sb(es, "eps_t", [128, 1], F32)
        fw.eps_t = self.eps_t
        fw.op(fw.dve, lambda e: e.memset(self.eps_t.ap, EPS), [], [self.eps_t])
        self.onesb = fw.sb(es, "onesb", [128, 128], BF16)
        fw.op(fw.dve, lambda e: e.memset(self.onesb.ap, 1.0), [], [self.onesb])
        self.one_t = fw.sb(es, "one_t", [128, 1], F32)
        fw.op(fw.dve, lambda e: e.memset(self.one_t.ap, 1.0), [], [self.one_t])
        self.maskF = fw.sb(es, "maskF", [128, 4, 128], BF16)
        self.maskB = fw.sb(es, "maskB", [128, 4, 128], BF16)
        for h in range(4):
            cp(fw, fw.dve, self.maskF[:, h, :], self.triF, [self.cst], [self.maskF])
            cp(fw, fw.dve, self.maskB[:, h, :], self.triB, [self.cst], [self.maskB])
        self.ps = []
        for i in range(8):
            t = es.enter_context(nc.psum_tensor("psb%d" % i, [128, 512], F32))
            self.ps.append(Tl(fw, t[:], "psb%d" % i))
        xds = fw.new_sem("xcopy")
        fw.dma(fw.sp, self.X, self.xin, ds=xds)
        self.xbuf = Buf("Xall")
        self.xbuf.w = {xds: xds.total}
        self.compute_mod(es)
        self.compute_rope(es)

    def compute_mod(self, es0):
        nc, fw = self.nc, self.fw
        with ExitStack() as es:
            cc = fw.sb(es, "cc", [128, 8, 2], F32)
            sc = fw.sb(es, "sc", [128, 8, 2], F32)
            fw.dma(fw.sp, cc.ap, self.ccT, writes=[cc], ds=cc.ds)
            act(fw, sc.ap, cc.ap, AF.Silu, [cc], [sc])
            wring = fw.ring(es, "wm", [128, 8, 512], F32, 3)
            bm = fw.sb(es, "bm", [2, 6 * D], F32)
            orow = fw.ring(es, "orow", [2, 6 * D], F32, 2)
            k = 0
            for l in range(self.n_layers):
                fw.dma(fw.sp, bm.ap, self.b_mod[l:l + 1, :].to_broadcast([2, 6 * D]), writes=[bm], ds=bm.ds)
                orw = orow[l % 2]
                for g in range(12):
                    wt = wring[k % 3]
                    src = self.w_mod[l].rearrange("(kc p) c -> p kc c", p=128)[:, :, g * 512:(g + 1) * 512]
                    fw.dma(fw.sp, wt.ap, src, writes=[wt], ds=wt.ds)
                    pst = self.ps[k % 2]
                    for kc in range(8):
                        mm(fw, pst[0:2, :], sc[:, kc, :], wt[:, kc, :], kc == 0, kc == 7, [sc, wt], [pst])
                    tt(fw, fw.dve, orw[:, g * 512:(g + 1) * 512], pst[0:2, :], bm[:, g * 512:(g + 1) * 512], ALU.add, [pst, bm], [orw])
                    k += 1
                for ch in (1, 4):
                    ts(fw, fw.dve, orw[:, ch * D:(ch + 1) * D], orw[:, ch * D:(ch + 1) * D], 1.0, None, ALU.add, None, [orw], [orw])
                fw.dma(fw.sp, self.MOD[l], orw.ap, reads=[orw], ds=orw.ds)
            self.modbuf = Buf("MOD")
            for o in orow:
                self.modbuf.w[o.ds] = o.ds.total
            fw.barrier()

    def compute_rope(self, es0):
        nc, fw = self.nc, self.fw
        with ExitStack() as es:
            tl = fw.sb(es, "rp", [128, 8, 32], F32)
            itl = fw.sb(es, "rpi", [128, 32], I32)
            c = self.cst
            fr, u, r, fx, ang = (tl[:, i, :] for i in range(5))
            nidx = c[:, 516:517]
            act(fw, fr[:, 0:16], c[:, 517:533], AF.Exp, [c], [tl], scale=-math.log(10000.0) / 16.0)
            ts(fw, fw.dve, ang[:, 0:16], fr[:, 0:16], nidx, 1.0 / (2 * math.pi), ALU.mult, ALU.mult, [tl, c], [tl])
            ts(fw, fw.dve, u[:, 0:16], ang[:, 0:16], 0.25, None, ALU.add, None, [tl], [tl])
            cp(fw, fw.dve, u[:, 16:32], ang[:, 0:16], [tl], [tl])
            cp(fw, fw.dve, itl.ap, u, [tl], [itl])
            cp(fw, fw.dve, r, itl.ap, [itl], [tl])
            tt(fw, fw.dve, r, u, r, ALU.subtract, [tl], [tl])
            ts(fw, fw.dve, fx, r, 0.5, None, ALU.is_gt, None, [tl], [tl])
            tt(fw, fw.dve, r, r, fx, ALU.subtract, [tl], [tl])
            ts(fw, fw.dve, fx, r, -0.5, None, ALU.is_lt, None, [tl], [tl])
            tt(fw, fw.dve, r, r, fx, ALU.add, [tl], [tl])
            res = tl[:, 5, :]
            act(fw, res, r, AF.Sin, [tl], [tl], scale=2 * math.pi)
            fw.dma(fw.sp, self.ROPEd, tl[0:64, 5, :], reads=[tl], ds=tl.ds)
            self.ropebuf = Buf("rope")
            self.ropebuf.w = {tl.ds: tl.ds.total}
            fw.barrier()

    def load_bcast(self, dst_tl, dst_ap, src_row_ap, q=None, reads=()):
        fw = self.fw
        q = q or fw.sp
        n = src_row_ap.shape[-1]
        fw.dma(q, dst_ap, src_row_ap.to_broadcast([dst_ap.shape[0], n]), reads=list(reads), writes=[dst_tl], ds=dst_tl.ds, serialize=False)

    def phase_a(self, l):
        nc, fw = self.nc, self.fw
        ps = self.ps
        with ExitStack() as es:
            W = fw.sb(es, "Wa", [128, 8, W_COLS], BF16)
            wsrc = self.w_in[l].rearrange("(kc p) c -> p kc c", p=128)
            pieces = list(FM_PIECES)
            for name, pl_ in TM_GROUPS:
                o = TMW0 + TM_OFF[name]
                for (src, n) in pl_:
                    pieces.append((o, src, n))
                    o += n
            for (dst, src, n) in pieces:
                fw.dma(fw.pool, W[:, :, dst:dst + n], wsrc[:, :, src:src + n], writes=[W], ds=W.ds, serialize=False)
            BB = fw.sb(es, "BBa", [128, TM_COLS - 16], BF16)
            BG = fw.sb(es, "BGa", [128, 16], F32)
            for name, pl_ in TM_GROUPS:
                o = TM_OFF[name]
                for (src, n) in pl_:
                    row = self.b_in[l:l + 1, src:src + n]
                    if name == "MG":
                        self.load_bcast(BG, BG[:, o:o + n], row)
                    else:
                        self.load_bcast(BB, BB[:, o - 16:o - 16 + n], row, q=fw.pool)
                    o += n
            bfm = fw.sb(es, "bfm", [128, 8], F32)
            fw.dma(fw.sp, bfm.ap, self.b_in_fm[l], writes=[bfm], ds=bfm.ds)
            modt = fw.sb(es, "moda", [128, 2, D], F32)

            def load_mod(j):
                for ch in range(2):
                    self.load_bcast(modt, modt[:, ch, :], self.MOD[l, j:j + 1, ch * D:(ch + 1) * D], reads=[self.modbuf])
            load_mod(1)
            gn = fw.sb(es, "gna", [128, 2, 512], F32)
            self.load_bcast(gn, gn[:, 0, :], self.ret_gn[l:l + 1, :])
            self.load_bcast(gn, gn[:, 1, :], self.m_gn[l:l + 1, :])
            qk = fw.sb(es, "qka", [128, 2, 64], F32)
            self.load_bcast(qk, qk[:, 0, :], self.qn_g[l:l + 1, :])
            self.load_bcast(qk, qk[:, 1, :], self.kn_g[l:l + 1, :])
            ts(fw, fw.dve, qk[:, 0, :], qk[:, 0, :], 0.125, None, ALU.mult, None, [qk], [qk])
            colT = fw.sb(es, "colT", [128, 32], F32)
            rowT = fw.sb(es, "rowT", [128, 32, 32], F32)
            for hf in range(2):
                fw.dma(fw.sp, colT[hf * 64:(hf + 1) * 64, :], self.ROPEd, reads=[self.ropebuf], writes=[colT], ds=colT.ds, serialize=False)
                src = self.ROPEd.rearrange("(j two) c -> two j c", two=2)[hf:hf + 1]
                fw.dma(fw.sp, rowT[hf * 64:(hf + 1) * 64, :, :], src.to_broadcast([64, 32, 32]), reads=[self.ropebuf], writes=[rowT], ds=rowT.ds, serialize=False)
            dk = fw.sb(es, "dka", [128, 8, 8], F32)
            c = self.cst
            self.load_bcast(dk, dk[:, 0, :], self.decay[l:l + 1, :])
            act(fw, dk[:, 1, :], dk[:, 0, :], AF.Exp, [dk], [dk], scale=-1.0)
            act(fw, dk[:, 2, :], dk[:, 1, :], AF.Ln, [dk], [dk], bias=self.one_t[:, 0:1])
            rEA = dk[:, 3, :]
            rEB = dk[:, 4, :]
            rEE = dk[:, 5, :]
            act(fw, rEA[:, 0:4], dk[:, 2, 0:4], AF.Exp, [dk, c], [dk], scale=c[:, 512:513])
            act(fw, rEA[:, 4:8], dk[:, 2, 4:8], AF.Exp, [dk, c], [dk], scale=c[:, 513:514])
            act(fw, rEB[:, 0:4], dk[:, 2, 0:4], AF.Exp, [dk, c], [dk], scale=c[:, 514:515])
            act(fw, rEB[:, 4:8], dk[:, 2, 4:8], AF.Exp, [dk, c], [dk], scale=c[:, 515:516])
            act(fw, rEE, dk[:, 2, :], AF.Exp, [dk], [dk], scale=-128.0)
            retc = fw.sb(es, "retc", [128, 16], F32)
            cp(fw, fw.dve, retc[:, 0:8], rEB, [dk], [retc])
            for hf in range(2):
                cp(fw, fw.dve, retc[hf * 64:(hf + 1) * 64, 8:12].rearrange("p (d j) -> p d j", d=2),
                   rEE[hf * 64:(hf + 1) * 64, :].rearrange("p (d j two) -> p d j two", d=2, j=2)[:, :, :, hf], [dk], [retc])
            fw.dma(fw.sp, self.RETCd[l], retc.ap, reads=[retc], ds=retc.ds)

            xt = fw.sb(es, "xta", [128, D], F32)
            st6 = fw.sb(es, "st6a", [128, 2, 6], F32)
            mv = fw.sb(es, "mva", [128, 4], F32)
            xn = fw.sb(es, "xna", [128, D], F32)
            xm = fw.sb(es, "xma", [128, D], BF16)
            xmT = fw.ring(es, "xmTa", [128, 8, 128], BF16, 2)
            tmA = fw.sb(es, "tmAa", [128, R_SPLIT], BF16)
            tmB = fw.sb(es, "tmBa", [128, R_COLS - R_SPLIT], BF16)
            sm_ = fw.sb(es, "smra", [128, SM_COLS], F32)
            fm_ = fw.sb(es, "fmra", [128, 8, 128], BF16)
            aq_ = fw.sb(es, "aqra", [64, 8, 128], BF16)
            ak_ = fw.sb(es, "akra", [64, 2, 128], BF16)
            tmpA = fw.ring(es, "tmpAa", [128, 512], F32, 2)
            qpriv = fw.sb(es, "qpriva", [128, 512], F32)
            kpriv = fw.sb(es, "kpriva", [128, 128], F32)
            tmpB = fw.ring(es, "tmpBa", [128, 512], F32, 2)
            qb_ = fw.sb(es, "qba", [128, 640], BF16)
            g_ = fw.sb(es, "gtsa", [128, 64], F32)
            g16 = fw.sb(es, "g16a", [128, 16], BF16)
            sl_ = fw.sb(es, "smla", [128, 32], F32)
            fw.op(fw.dve, lambda e: e.memset(tmA[:, R_AV:R_AV + 130].rearrange("p (g c) -> p g c", g=2)[:, :, 64:65], 1.0), [], [tmA])

            ps_tr, ps_fm, ps_sm, ps_aq = ps[0], ps[1], ps[2], ps[3]
            ps_tm = ps[4:8]
            self._tmk = 0

            def s1(t):
                if t == NCT:
                    load_mod(0)
                fw.dma(fw.sp, xt.ap, self.X[t * 128:(t + 1) * 128, :], reads=[self.xbuf], writes=[xt], ds=xt.ds)
                for hh in range(2):
                    fw.op(fw.dve, lambda e, hh=hh: e.bn_stats(out=st6[:, hh, :], in_=xt[:, hh * 512:(hh + 1) * 512]), [xt], [st6])
                fw.op(fw.dve, lambda e: e.bn_aggr(out=mv[:, 0:2], in_=st6.ap.rearrange("p a b -> p (a b)")), [st6], [mv])
                rstd_from(fw, mv[:, 2:3], mv[:, 1:2], [mv], [mv])
                ts(fw, fw.dve, xn.ap, xt.ap, mv[:, 0:1], mv[:, 2:3], ALU.subtract, ALU.mult, [xt, mv], [xn])
                tt(fw, fw.dve, xn.ap, xn.ap, modt[:, 1, :], ALU.mult, [xn, modt], [xn])
                tt(fw, fw.dve, xm.ap, xn.ap, modt[:, 0, :], ALU.add, [xn, modt], [xm])

            def s2(t):
                xT_ = xmT[t % 2]
                pb = ps_tr.ap.bitcast(BF16).rearrange("p (a b) -> p a b", a=8)
                for kc in range(8):
                    tr(fw, pb[:, kc, :], xm[:, kc * 128:(kc + 1) * 128], self.identb.ap, [xm, self.identb], [ps_tr], inc=(kc == 7))
                cp(fw, fw.act, xT_.ap.rearrange("p a b -> p (a b)"), ps_tr.ap.bitcast(BF16), [ps_tr], [xT_])

            def tm_matmul(t, name, n):
                xT_ = xmT[t % 2]
                pst = ps_tm[self._tmk % len(ps_tm)]
                self._tmk += 1
                o = TMW0 + TM_OFF[name]
                for kc in range(8):
                    mm(fw, pst[:, 0:n], xT_[:, kc, :], W[:, kc, o:o + n], kc == 0, kc == 7, [xT_, W], [pst])
                return pst

            def bias_of(name, n, off=0):
                o = TM_OFF[name] - 16 + off
                return BB[:, o:o + n]

            def s3(t, mid_hook=None):
                xT_ = xmT[t % 2]
                for half in range(2):
                    for i4 in range(4):
                        i = half * 4 + i4
                        for kc in range(8):
                            mm(fw, ps_fm[:, i4 * 128:(i4 + 1) * 128], W[:, kc, i * 128:(i + 1) * 128], xT_[:, kc, :], kc == 0, kc == 7, [xT_, W], [ps_fm],
                               inc=(kc == 7 and i4 == 3))
                    for i4 in range(4):
                        i = half * 4 + i4
                        sc_ = 0.125 if i in (2, 3, 6, 7) else 1.0
                        ts(fw, fw.dve, fm_[:, i, :], ps_fm[:, i4 * 128:(i4 + 1) * 128], bfm[:, i:i + 1], sc_, ALU.add, ALU.mult, [ps_fm, bfm], [fm_])
                fw.dma(fw.sp, self.FMd[t], fm_.ap.rearrange("p a b -> p (a b)"), reads=[fm_], ds=fm_.ds)
                if lim is not None and len(lim) > 2 and lim[2] <= 1:
                    return
                pbk = ps_sm.ap.bitcast(BF16)
                for n_, i in enumerate((2, 3, 6, 7)):
                    tr(fw, pbk[:, 512 + n_ * 128:512 + (n_ + 1) * 128], fm_[:, i, :], self.identb.ap, [fm_, self.identb], [ps_sm], inc=(n_ == 3))
                cp(fw, fw.act, tmA[:, R_RK:R_RK + 512], pbk[:, 512:1024], [ps_sm], [tmA])
                if lim is not None and len(lim) > 2 and lim[2] <= 2:
                    return
                pst = tm_matmul(t, "MG", 16)
                tt(fw, fw.dve, g_[:, 0:16], pst[:, 0:16], BG.ap, ALU.add, [pst, BG], [g_])
                e_ = g_[:, 16:24]
                sp_ = g_[:, 24:32]
                act(fw, e_, g_[:, 8:16], AF.Exp, [g_], [g_], scale=-1.0)
                act(fw, sp_, e_, AF.Ln, [g_], [g_], bias=self.one_t[:, 0:1])
                hi32, lo32 = g_[:, 56:64], g_[:, 16:24]
                cp(fw, fw.dve, g16[:, 0:8], sp_, [g_], [g16])
                cp(fw, fw.dve, hi32, g16[:, 0:8], [g16], [g_])
                tt(fw, fw.dve, lo32, sp_, hi32, ALU.subtract, [g_], [g_])
                cp(fw, fw.dve, g16[:, 8:16], lo32, [g_], [g16])
                if lim is not None and len(lim) > 2 and lim[2] <= 3:
                    return
                pst = tm_matmul(t, "RV", 512)
                v_ = tmpA[0]
                tt(fw, fw.dve, v_.ap, pst.ap, bias_of("RV", 512), ALU.add, [pst, BB], [v_])
                for d in range(2):
                    eng = fw.dve if d == 0 else fw.pool
                    tt(fw, eng, tmA[:, R_RV + d * 512:R_RV + (d + 1) * 512].rearrange("p (h e) -> p h e", h=4),
                       v_.ap.rearrange("p (h e) -> p h e", h=4), rEA[:, d * 4:(d + 1) * 4].unsqueeze(2).to_broadcast([128, 4, 128]), ALU.mult, [v_, dk], [tmA])
                if lim is not None and len(lim) > 2 and lim[2] <= 4:
                    return
                pst = tm_matmul(t, "RG", 512)
                a_, b_ = tmpA[1], tmpB[0]
                tt(fw, fw.dve, a_.ap, pst.ap, bias_of("RG", 512), ALU.add, [pst, BB], [a_])
                act(fw, b_.ap, a_.ap, AF.Silu, [a_], [b_])
                tt(fw, fw.pool, tmA[:, R_RG:R_RG + 512], b_.ap, gn[:, 0, :], ALU.mult, [b_, gn], [tmA])
                if lim is not None and len(lim) > 2 and lim[2] <= 5:
                    return
                q_ = qpriv
                pst = tm_matmul(t, "AQ", 512)
                tt(fw, fw.dve, q_.ap, pst.ap, bias_of("AQ", 512), ALU.add, [pst, BB], [q_])
                self.norm_rope(t, q_, q_.ap, 8, qk[:, 0, :], qb_, qb_[:, 0:512], sl_, sl_[:, 0:8], tmpB[1], colT, rowT, qk)
                if lim is not None and len(lim) > 2 and lim[2] <= 6:
                    return
                pst = tm_matmul(t, "AKV", 256)
                k_ = kpriv
                tt(fw, fw.dve, k_[:, 0:128], pst[:, 0:128], bias_of("AKV", 128), ALU.add, [pst, BB], [k_])
                self.norm_rope(t, k_, k_[:, 0:128], 2, qk[:, 1, :], qb_, qb_[:, 512:640], sl_, sl_[:, 8:10], tmpB[1], colT, rowT, qk)
                tt(fw, fw.dve, tmA[:, R_AV:R_AV + 130].rearrange("p (g c) -> p g c", g=2)[:, :, 0:64],
                   pst[:, 128:256].rearrange("p (g c) -> p g c", g=2), bias_of("AKV", 128, 128).rearrange("p (g c) -> p g c", g=2), ALU.add, [pst, BB], [tmA])
                if lim is not None and len(lim) > 2 and lim[2] <= 7:
                    return
                for part in range(2):
                    o = part * 8
                    mm(fw, ps_sm[:, 0:4], self.maskF[:, 0, :], g16[:, o:o + 4], part == 0, part == 1, [g16, self.maskF], [ps_sm], inc=False)
                for part in range(2):
                    o = part * 8
                    mm(fw, ps_sm[:, 4:8], self.maskB[:, 0, :], g16[:, o + 4:o + 8], part == 0, part == 1, [g16, self.maskB], [ps_sm], inc=False)
                for part in range(2):
                    o = part * 8
                    mm(fw, ps_sm[:, 8:16], self.onesb.ap, g16[:, o:o + 8], part == 0, part == 1, [g16, self.onesb], [ps_sm], inc=(part == 1))
                ta = g_[:, 32:40]
                EA = g_[:, 40:48]
                tt(fw, fw.dve, ta, g_[:, 0:8], ps_sm[:, 0:8], ALU.add, [g_, ps_sm], [g_])
                act(fw, EA, ta, AF.Exp, [g_], [g_])
                act(fw, sm_[:, 0:8], ps_sm[:, 0:8], AF.Exp, [ps_sm], [sm_], scale=-1.0)
                ebe = g_[:, 48:56]
                act(fw, ebe, ps_sm[:, 8:16], AF.Exp, [ps_sm], [g_], scale=-1.0)
                for hf in range(2):
                    cp(fw, fw.dve, sm_[hf * 64:(hf + 1) * 64, 8:12].rearrange("p (d j) -> p d j", d=2),
                       ebe[hf * 64:(hf + 1) * 64, :].rearrange("p (d j two) -> p d j two", d=2, j=2)[:, :, :, hf], [g_], [sm_])
                fw.dma(fw.sp, self.SMd[t], sm_.ap, reads=[sm_], ds=sm_.ds)
                pst = tm_matmul(t, "MV", 512)
                v_ = tmpA[0]
                tt(fw, fw.dve, v_.ap, pst.ap, bias_of("MV", 512), ALU.add, [pst, BB], [v_])
                for d in range(2):
                    eng = fw.dve if d == 0 else fw.pool
                    dst = tmA[:, R_MV + d * 516:R_MV + (d + 1) * 516].rearrange("p (h e) -> p h e", h=4)
                    tt(fw, eng, dst[:, :, 0:128], v_.ap.rearrange("p (h e) -> p h e", h=4),
                       EA[:, d * 4:(d + 1) * 4].unsqueeze(2).to_broadcast([128, 4, 128]), ALU.mult, [v_, g_], [tmA])
                    cp(fw, eng, dst[:, :, 128:129], EA[:, d * 4:(d + 1) * 4].unsqueeze(2), [g_], [tmA])
                if lim is not None and len(lim) > 2 and lim[2] <= 9:
                    return
                pst = tm_matmul(t, "MO", 512)
                a_, b_ = tmpA[1], tmpB[0]
                tt(fw, fw.dve, a_.ap, pst.ap, bias_of("MO", 512), ALU.add, [pst, BB], [a_])
                act(fw, b_.ap, a_.ap, AF.Sigmoid, [a_], [b_])
                tt(fw, fw.pool, tmA[:, R_MO:R_MO + 512], b_.ap, gn[:, 1, :], ALU.mult, [b_, gn], [tmA])
                fw.dma(fw.sp, self.TMd[t][:, 0:R_SPLIT], tmA.ap, reads=[tmA], ds=tmA.ds)
                if mid_hook is not None:
                    mid_hook()
                if lim is not None and len(lim) > 2 and lim[2] <= 10:
                    return
                for i in range(6):
                    pst = tm_matmul(t, "G%d" % i, 512)
                    a_ = tmpA[i % 2]
                    tt(fw, fw.dve, a_.ap, pst.ap, bias_of("G%d" % i, 512), ALU.add, [pst, BB], [a_])
                    act(fw, tmB[:, i * 512:(i + 1) * 512], a_.ap, AF.Sigmoid, [a_], [tmB])
                fw.dma(fw.sp, self.TMd[t][:, R_SPLIT:R_COLS], tmB.ap, reads=[tmB], ds=tmB.ds)
                pbq = ps_aq.ap.bitcast(BF16)
                for h in range(8):
                    tr(fw, pbq[0:64, h * 128:(h + 1) * 128], qb_[:, h * 64:(h + 1) * 64], self.identb.ap, [qb_, self.identb], [ps_aq], inc=(h == 7))
                for h in range(2):
                    tr(fw, pbk[0:64, 256 + h * 128:256 + (h + 1) * 128], qb_[:, 512 + h * 64:512 + (h + 1) * 64], self.identb.ap, [qb_, self.identb], [ps_sm], inc=(h == 1))
                cp(fw, fw.act, aq_.ap.rearrange("p a b -> p (a b)"), pbq[0:64, :], [ps_aq], [aq_])
                cp(fw, fw.act, ak_.ap.rearrange("p a b -> p (a b)"), pbk[0:64, 256:512], [ps_sm], [ak_])
                fw.dma(fw.sp, self.AQd[t], aq_.ap.rearrange("p a b -> p (a b)"), reads=[aq_], ds=aq_.ds)
                fw.dma(fw.sp, self.AKd[:, :, t * 128:(t + 1) * 128], ak_.ap, reads=[ak_], ds=ak_.ds)
                if lim is not None and len(lim) > 2 and lim[2] <= 8:
                    return

            lim = getattr(self, "a_lim", None)
            if lim == "pre":
                fw.barrier()
                return
            nt = NT if lim is None else lim[0]
            s1(0)
            s2(0)
            for t in range(nt):
                if t + 1 < nt:
                    s1(t + 1)
                    s3(t, (lambda t=t: s2(t + 1)))
                else:
                    s3(t)
            fw.barrier()

    def phase_b(self, l):
        nc, fw = self.nc, self.fw
        ps = self.ps
        last = (l == DEPTH - 1)
        with ExitStack() as es:
            WB = fw.sb(es, "WBb", [128, 3, 4, D], BF16)
            for b in range(3):
                fw.dma(fw.pool, WB[:, b, :, :], self.w_br[l, b].rearrange("(kc p) c -> p kc c", p=128), writes=[WB], ds=WB.ds, serialize=False)
            WO = fw.sb(es, "WOb", [128, 8, D], BF16)
            wo_src = self.w_out[l].rearrange("(kc p) c -> p kc c", p=128)
            for hh in range(2):
                fw.dma(fw.pool, WO[:, hh * 4:(hh + 1) * 4, :], wo_src[:, hh * 4:(hh + 1) * 4, :], writes=[WO], ds=WO.ds, serialize=False)
            AKT = fw.sb(es, "AKTb", [64, 2, T], BF16)
            fw.dma(fw.sp, AKT.ap, self.AKd, writes=[AKT], ds=AKT.ds)
            AVa = fw.sb(es, "AVab", [128, NT, 130], BF16)
            for q in range(0, NT, 8):
                q1 = min(NT, q + 8)
                fw.dma(fw.sp, AVa[:, q:q1, :], self.TMd[q:q1, :, R_AV:R_AV + 130].rearrange("t p c -> p t c"), writes=[AVa], ds=AVa.ds, serialize=False)
            retc = fw.sb(es, "retcb", [128, 16], F32)
            fw.dma(fw.sp, retc.ap, self.RETCd[l], writes=[retc], ds=retc.ds)
            gms = fw.sb(es, "gmsb", [128, D], F32)
            ln1t = fw.sb(es, "ln1tb", [128, 2, D], F32)
            for i in range(2):
                self.load_bcast(ln1t, ln1t[:, i, :], self.ln1[l, i:i + 1, :])

            def load_gms(j):
                self.load_bcast(gms, gms.ap, self.MOD[l, j:j + 1, 2 * D:3 * D], reads=[self.modbuf])
            load_gms(1)
            orders = {0: list(range(NT)), 1: [1, 0] + list(range(NT - 1, 1, -1))}
            SXd = (self.SFd, self.SBd)

            with ExitStack() as es2:
                S = [[fw.sb(es2, "Sst%d_%d" % (d, k), [128, 258], F32) for k in range(4)] for d in range(2)]
                for d in range(2):
                    for k in range(4):
                        fw.op(fw.dve if k % 2 == 0 else fw.pool, lambda e, d=d, k=k: e.memset(S[d][k].ap, 0.0), [], [S[d][k]])
                Sbf = [fw.ring(es2, "Sbf%d" % d, [128, 4, 258], BF16, 2) for d in range(2)]
                ldr = [fw.ring(es2, "ldr%d" % d, [128, 1540], BF16, 3) for d in range(2)]
                smr = [fw.ring(es2, "smr%d" % d, [128, SM_COLS], F32, 3) for d in range(2)]
                for i in range(NT):
                    for d in range(2):
                        t = orders[d][i]
                        L_ = ldr[d][i % 3]
                        sm_ = smr[d][i % 3]
                        fw.dma(fw.sp, L_[:, 0:512], self.TMd[t][:, R_RK:R_RK + 512], writes=[L_], ds=L_.ds)
                        fw.dma(fw.sp, L_[:, 512:1024], self.TMd[t][:, R_RV + d * 512:R_RV + (d + 1) * 512], writes=[L_], ds=L_.ds, serialize=False)
                        fw.dma(fw.sp, L_[:, 1024:1540], self.TMd[t][:, R_MV + d * 516:R_MV + (d + 1) * 516], writes=[L_], ds=L_.ds, serialize=False)
                        fw.dma(fw.sp, sm_.ap, self.SMd[t], writes=[sm_], ds=sm_.ds)
                        for mxj in range(4):
                            mx, j = mxj // 2, mxj % 2
                            W_ = 128 if mx == 0 else 129
                            pst = ps[d * 4 + mxj]
                            K_ = L_[:, mx * 256 + j * 128:mx * 256 + (j + 1) * 128]
                            for blk in range(2):
                                h = 2 * j + blk
                                V_ = L_[:, 512 + h * 128:512 + (h + 1) * 128] if mx == 0 else L_[:, 1024 + h * 129:1024 + (h + 1) * 129]
                                mm(fw, pst[:, blk * 129:blk * 129 + W_], K_, V_, True, True, [L_], [pst], inc=(blk == 1))
                        for mxj in range(4):
                            mx, j = mxj // 2, mxj % 2
                            W_ = 128 if mx == 0 else 129
                            pst = ps[d * 4 + mxj]
                            St = S[d][mxj]
                            sb_ = Sbf[d][i % 2]
                            cp(fw, fw.act, sb_[:, mxj, :], St.ap, [St], [sb_])
                            if mxj == 3:
                                fw.dma(fw.pool, SXd[d][t], sb_.ap.rearrange("p a b -> p (a b)"), reads=[sb_], ds=sb_.ds)
                            e_ = retc[:, 8 + d * 2 + j:9 + d * 2 + j] if mx == 0 else sm_[:, 8 + d * 2 + j:9 + d * 2 + j]
                            e_src = retc if mx == 0 else sm_
                            Sv = St.ap.rearrange("p (b w) -> p b w", b=2)[:, :, 0:W_]
                            Pv = pst[:, 0:258].rearrange("p (b w) -> p b w", b=2)[:, :, 0:W_]
                            act(fw, Sv, Sv, AF.Identity, [St, e_src], [St], scale=e_)
                            stt(fw, fw.dve, Sv, Pv, e_, Sv, ALU.mult, ALU.add, [pst, e_src, St], [St])
                fw.barrier()
            self.sxbuf = Buf("SX")

            tmA = fw.ring(es, "tmAb", [128, R_SPLIT], BF16, 2)
            tmG = fw.ring(es, "tmGb", [128, R_COLS - R_SPLIT], BF16, 3)
            smr = fw.ring(es, "smrb", [128, SM_COLS], F32, 2)
            fmr = fw.ring(es, "fmrb", [128, 8, 128], BF16, 2)
            aqr = fw.ring(es, "aqrb", [64, 8, 128], BF16, 2)
            sfr = [fw.ring(es, "sxr%d" % d, [128, 4, 258], BF16, 2) for d in range(2)]
            xr = fw.ring(es, "xrb", [128, D], F32, 2)
            PT = fw.ring(es, "PTb", [128, 2, 512], BF16, 2)
            pTr = fw.ring(es, "pTb", [128, 512], BF16, 3)
            yf = fw.ring(es, "yfb", [128, 512], F32, 4)
            sml = fw.ring(es, "smlb", [128, 64], F32, 2)
            st4 = fw.sb(es, "st4b", [128, 4, 6], F32)
            ymix = fw.ring(es, "ymixb", [128, 3, 512], BF16, 2)
            yT = fw.ring(es, "yTb", [128, 12, 128], BF16, 2)
            zt = fw.ring(es, "ztb", [128, D], F32, 2)
            zb = fw.sb(es, "zbb", [128, D], BF16)
            zT = fw.sb(es, "zTb", [128, 8, 128], BF16)
            rr = fw.ring(es, "rrb", [128, D], F32, 2)
            st6 = fw.sb(es, "st6b", [128, 2, 6], F32)
            mv = fw.sb(es, "mvb", [128, 4], F32)
            ps_s, ps_of, ps_ob, ps_sc, ps_acc = ps[0], (ps[1], ps[2]), (ps[3], ps[4]), (ps[5], ps[6]), ps[7]
            self._yk = 0

            def nexty():
                self._yk += 1
                return yf[self._yk % 4]

            def loads(t):
                fw.dma(fw.sp, tmA[t % 2].ap, self.TMd[t][:, 0:R_SPLIT], writes=[tmA[t % 2]], ds=tmA[t % 2].ds)
                fw.dma(fw.sp, tmG[t % 3].ap, self.TMd[t][:, R_SPLIT:R_COLS], writes=[tmG[t % 3]], ds=tmG[t % 3].ds)
                fw.dma(fw.sp, smr[t % 2].ap, self.SMd[t], writes=[smr[t % 2]], ds=smr[t % 2].ds)
                fw.dma(fw.sp, fmr[t % 2].ap.rearrange("p a b -> p (a b)"), self.FMd[t], writes=[fmr[t % 2]], ds=fmr[t % 2].ds)
                fw.dma(fw.sp, aqr[t % 2].ap.rearrange("p a b -> p (a b)"), self.AQd[t], writes=[aqr[t % 2]], ds=aqr[t % 2].ds)
                for d in range(2):
                    fw.dma(fw.sp, sfr[d][t % 2].ap.rearrange("p a b -> p (a b)"), SXd[d][t], writes=[sfr[d][t % 2]], ds=sfr[d][t % 2].ds)

            def par(ap2, hp, n=2):
                return ap2.rearrange("p (j two k) -> p j two k", j=2, two=2)[:, :, hp, :]

            def linattn(t, mx):
                ta_, sm_, fm_ = tmA[t % 2], smr[t % 2], fmr[t % 2]
                W_ = 128 if mx == 0 else 129
                qi, ki = (0, 2) if mx == 0 else (4, 6)
                pt_ = PT[(2 * t + mx) % 2]
                for h in range(4):
                    j, hf = h // 2, h % 2
                    mm(fw, ps_sc[hf][:, j * 128:(j + 1) * 128], fm_[hf * 64:(hf + 1) * 64, ki + j, :], fm_[hf * 64:(hf + 1) * 64, qi + j, :], True, True, [fm_], [ps_sc[hf]], inc=(h >= 2))
                for d, msk in ((0, self.maskF), (1, self.maskB)):
                    for hp in range(2):
                        tt(fw, fw.dve, par(pt_[:, d, :], hp), ps_sc[hp][:, 0:256].rearrange("p (j k) -> p j k", j=2), msk[:, 0:2, :], ALU.mult, [ps_sc[hp], msk], [pt_])
                for d in range(2):
                    banks = ps_of if d == 0 else ps_ob
                    S_ = sfr[d][t % 2]
                    for h in (0, 2, 1, 3):
                        j, hf = h // 2, h % 2
                        bank = banks[hf]
                        if mx == 0:
                            V_ = ta_[:, R_RV + d * 512 + h * 128:R_RV + d * 512 + (h + 1) * 128]
                        else:
                            V_ = ta_[:, R_MV + d * 516 + h * 129:R_MV + d * 516 + (h + 1) * 129]
                        out = bank[:, j * 129:j * 129 + W_]
                        mm(fw, out, pt_[:, d, h * 128:(h + 1) * 128], V_, (j == 0), False, [pt_, ta_], [bank], inc=False, skip_group_check=True)
                        Sv = S_[hf * 64:(hf + 1) * 64, mx * 2 + j, hf * 129:hf * 129 + W_]
                        mm(fw, out, fm_[hf * 64:(hf + 1) * 64, qi + j, :], Sv, False, True, [fm_, S_], [bank], inc=(j == 1), skip_group_check=True)
                y = nexty()
                sl = sml[t % 2]
                if mx == 0:
                    t1 = nexty()
                    for hp in range(2):
                        o_f = ps_of[hp][:, 0:258].rearrange("p (j w) -> p j w", j=2)[:, :, 0:128]
                        o_b = ps_ob[hp][:, 0:258].rearrange("p (j w) -> p j w", j=2)[:, :, 0:128]
                        ebf = par(retc[:, 0:4], hp).to_broadcast([128, 2, 128])
                        ebb = par(retc[:, 4:8], hp).to_broadcast([128, 2, 128])
                        tt(fw, fw.dve, par(t1.ap, hp), o_f, ebf, ALU.mult, [ps_of[hp], retc], [t1])
                        tt(fw, fw.dve, par(y.ap, hp), o_b, ebb, ALU.mult, [ps_ob[hp], retc], [y])
                    tt(fw, fw.pool, y.ap, y.ap, t1.ap, ALU.add, [y, t1], [y])
                else:
                    hd = []
                    for d in range(2):
                        banks = ps_of if d == 0 else ps_ob
                        q1 = sl[:, d * 16:d * 16 + 4]
                        q2 = sl[:, d * 16 + 4:d * 16 + 8]
                        r_ = sl[:, d * 16 + 8:d * 16 + 12]
                        eb = sm_[:, d * 4:(d + 1) * 4]
                        for hp in range(2):
                            den = banks[hp][:, 0:258].rearrange("p (j w) -> p j w", j=2)[:, :, 128:129]
                            tt(fw, fw.dve, par(q1, hp), den, par(eb, hp), ALU.mult, [banks[hp], sm_], [sl])
                        stt(fw, fw.dve, q2, q1, -1.0, q1, ALU.mult, ALU.max, [sl], [sl])
                        ts(fw, fw.dve, q2, q2, 1.0, None, ALU.max, None, [sl], [sl])
                        fw.op(fw.dve, lambda e, q2=q2: e.reciprocal(out=q2, in_=q2), [sl], [sl])
                        tt(fw, fw.dve, r_, q2, eb, ALU.mult, [sl, sm_], [sl])
                        hdt = y if d == 0 else nexty()
                        for hp in range(2):
                            num = banks[hp][:, 0:258].rearrange("p (j w) -> p j w", j=2)[:, :, 0:128]
                            tt(fw, fw.dve, par(hdt.ap, hp), num, par(r_, hp).to_broadcast([128, 2, 128]), ALU.mult, [banks[hp], sl], [hdt])
                        hd.append(hdt)
                    tt(fw, fw.pool, y.ap, hd[0].ap, hd[1].ap, ALU.add, [hd[0], hd[1]], [y])
                y3 = y.ap.rearrange("p (h e) -> p h e", h=4)
                for h in range(4):
                    fw.op(fw.dve, lambda e, h=h: e.bn_stats(out=st4[:, h, :], in_=y3[:, h, :]), [y], [st4])
                mvh = sl[:, 32:40].rearrange("p (h two) -> p h two", h=4)
                for h in range(4):
                    fw.op(fw.dve, lambda e, h=h: e.bn_aggr(out=mvh[:, h, :], in_=st4[:, h, :]), [st4], [sl])
                rs = sl[:, 40:44]
                act(fw, rs, mvh[:, :, 1], AF.Ln, [sl], [sl], bias=self.eps_t[:, 0:1])
                act(fw, rs, rs, AF.Exp, [sl], [sl], scale=-0.5)
                for h in range(4):
                    ts(fw, fw.dve, y3[:, h, :], y3[:, h, :], mvh[:, h, 0:1], rs[:, h:h + 1], ALU.subtract, ALU.mult, [y, sl], [y])
                gcol = R_RG if mx == 0 else R_MO
                ym = ymix[t % 2]
                tt(fw, fw.pool, ym[:, 0 if mx == 0 else 2, :], y.ap, ta_[:, gcol:gcol + 512], ALU.mult, [y, ta_], [ym])

            def attention(t, hooks):
                aq_ = aqr[t % 2]
                ym = ymix[t % 2]
                kts = list(range(NCT)) if t < NCT else list(range(NT))
                its = [(g, n_, kt) for g in range(2) for n_, kt in enumerate(kts)]
                nk = len(kts)
                hk = list(hooks)
                every = max(1, (len(its) - 4) // max(1, len(hk))) if hk else 0

                def score(idx):
                    g, n_, kt = its[idx]
                    psc = ps_sc[idx % 2]
                    mm(fw, psc.ap, AKT[:, g, kt * 128:(kt + 1) * 128], aq_[:, g * 4:(g + 1) * 4, :].rearrange("p a b -> p (a b)"), True, True, [AKT, aq_], [psc])
                score(0)
                for idx, (g, n_, kt) in enumerate(its):
                    if idx + 1 < len(its):
                        score(idx + 1)
                    psc = ps_sc[idx % 2]
                    p_ = pTr[idx % 3]
                    act(fw, p_.ap, psc.ap, AF.Exp, [psc], [p_])
                    for r in range(4):
                        mm(fw, ps_acc[:, r * 65:(r + 1) * 65], p_[:, r * 128:(r + 1) * 128], AVa[:, kt, g * 65:(g + 1) * 65],
                           (n_ == 0 and r == 0), (n_ == nk - 1), [p_, AVa], [ps_acc], inc=(r == 3), skip_group_check=True)
                    if n_ == nk - 1:
                        sl = sml[t % 2]
                        rd = sl[:, 48 + g * 4:52 + g * 4]
                        acc3 = ps_acc[:, 0:260].rearrange("p (r c) -> p r c", r=4)
                        fw.op(fw.dve, lambda e, rd=rd, acc3=acc3: e.reciprocal(out=rd, in_=acc3[:, :, 64]), [ps_acc], [sl])
                        tt(fw, fw.dve, ym[:, 1, g * 256:(g + 1) * 256].rearrange("p (r c) -> p r c", r=4), acc3[:, :, 0:64],
                           rd.unsqueeze(2).to_broadcast([128, 4, 64]), ALU.mult, [ps_acc, sl], [ym])
                    if hk and every and idx >= 2 and (idx - 2) % every == 0:
                        hk.pop(0)()
                for h_ in hk:
                    h_()

            def merge_stages(t):
                ym, yT_, tg_, x_ = ymix[t % 2], yT[t % 2], tmG[t % 3], xr[t % 2]
                pb = ps_s.ap.bitcast(BF16)
                zsum = zt[0]

                def m_tr(grp):
                    def f():
                        if grp == 0:
                            if t == NCT:
                                load_gms(0)
                            if "YDd" in self.debug:
                                fw.dma(fw.sp, self.YDd[t], ym.ap.rearrange("p a b -> p (a b)"), reads=[ym], ds=ym.ds)
                            fw.dma(fw.sp, x_.ap, self.X[t * 128:(t + 1) * 128, :], reads=[self.xbuf], writes=[x_], ds=x_.ds)
                        n = 8 if grp == 0 else 4
                        for i in range(n):
                            ii = grp * 8 + i
                            tr(fw, pb[:, i * 128:(i + 1) * 128], ym[:, ii // 4, (ii % 4) * 128:(ii % 4 + 1) * 128], self.identb.ap, [ym, self.identb], [ps_s], inc=(i == n - 1))
                        cp(fw, fw.act, yT_[:, grp * 8:grp * 8 + n, :].rearrange("p a b -> p (a b)"), pb[:, 0:n * 128], [ps_s], [yT_])
                    return f

                def m_br(b):
                    def f():
                        banks = ps_of if b % 2 == 0 else ps_ob
                        for n in range(2):
                            for kc in range(4):
                                mm(fw, banks[n].ap, yT_[:, b * 4 + kc, :], WB[:, b, kc, n * 512:(n + 1) * 512], kc == 0, kc == 3, [yT_, WB], [banks[n]])
                        dst = zsum if b == 0 else zt[1]
                        for n in range(2):
                            tt(fw, fw.dve, dst[:, n * 512:(n + 1) * 512], banks[n].ap, tg_[:, b * D + n * 512:b * D + (n + 1) * 512], ALU.mult, [banks[n], tg_], [dst])
                        if b == 1:
                            tt(fw, fw.pool, zsum.ap, zsum.ap, dst.ap, ALU.add, [zsum, dst], [zsum])
                        if b == 2:
                            tt(fw, fw.pool, zb.ap, zsum.ap, dst.ap, ALU.add, [zsum, dst], [zb])
                    return f

                def m_zt():
                    pb8 = pb.rearrange("p (a b) -> p a b", a=8)
                    for kc in range(8):
                        tr(fw, pb8[:, kc, :], zb[:, kc * 128:(kc + 1) * 128], self.identb.ap, [zb, self.identb], [ps_s], inc=(kc == 7))
                    cp(fw, fw.act, zT.ap.rearrange("p a b -> p (a b)"), pb, [ps_s], [zT])

                def m_out():
                    for n in range(2):
                        for kc in range(8):
                            mm(fw, ps_ob[n].ap, zT[:, kc, :], WO[:, kc, n * 512:(n + 1) * 512], kc == 0, kc == 7, [zT, WO], [ps_ob[n]])
                    r_ = rr[0]
                    for n in range(2):
                        tt(fw, fw.dve, r_[:, n * 512:(n + 1) * 512], ps_ob[n].ap, gms[:, n * 512:(n + 1) * 512], ALU.mult, [ps_ob[n], gms], [r_])
                    stt(fw, fw.dve, r_.ap, x_.ap, ALPHA, r_.ap, ALU.mult, ALU.add, [x_, r_], [r_])

                def m_ln():
                    self.ln_affine(rr[0], rr[1], st6, mv, ln1t)
                    fw.dma(fw.sp, self.X1[t * 128:(t + 1) * 128, :], rr[1].ap, reads=[rr[1]], ds=rr[1].ds)
                return [m_tr(0), m_tr(1), m_br(0), m_br(1), m_br(2), m_zt, m_out, m_ln]

            t0 = NCT if last else 0
            loads(t0)
            for t in range(t0, NT):
                if t + 1 < NT:
                    loads(t + 1)
                linattn(t, 0)
                linattn(t, 1)
                attention(t, merge_stages(t - 1) if t - 1 >= t0 else [])
            for f_ in merge_stages(NT - 1):
                f_()
            self.x1buf = Buf("X1")
            fw.barrier()

    def phase_d(self, l):
        nc, fw = self.nc, self.fw
        ps = self.ps
        last = (l == DEPTH - 1)
        with ExitStack() as es:
            import os
            dsk = os.environ.get("D_SKIP", "").split(",")
            WU = fw.sb(es, "WUd", [128, 8, 2 * DFF], BF16)
            wu_src = self.w_up[l].rearrange("(kc p) c -> p kc c", p=128)
            for c0 in range(0, 2 * DFF if "wu" not in dsk else 0, 512):
                fw.dma(fw.pool, WU[:, :, c0:c0 + 512], wu_src[:, :, c0:c0 + 512], writes=[WU], ds=WU.ds, serialize=False)
            WD = fw.sb(es, "WDd", [128, NF, D], BF16)
            wd_src = self.w_down[l].rearrange("(f p) c -> p f c", p=128)
            for f0 in range(0, NF if "wd" not in dsk else 0, 4):
                f1 = min(NF, f0 + 4)
                fw.dma(fw.pool, WD[:, f0:f1, :], wd_src[:, f0:f1, :], writes=[WD], ds=WD.ds, serialize=False)
            cvp = fw.sb(es, "cvpd", [128, 4, 44], F32)
            if "cvp" not in dsk:
                fw.dma(fw.sp, cvp.ap, self.convp[l], writes=[cvp], ds=cvp.ds)
            modt = fw.sb(es, "modd", [128, 2, D], F32)
            ln2t = fw.sb(es, "ln2td", [128, 2, D], F32)
            for i in range(2):
                self.load_bcast(ln2t, ln2t[:, i, :], self.ln2[l, i:i + 1, :])

            def load_mod(j):
                for i in range(2):
                    self.load_bcast(modt, modt[:, i, :], self.MOD[l, j:j + 1, (3 + i) * D:(4 + i) * D], reads=[self.modbuf])

            def load_gate(j):
                self.load_bcast(gmlp, gmlp.ap, self.MOD[l, j:j + 1, 5 * D:6 * D], reads=[self.modbuf])
            gmlp = fw.sb(es, "gmlpd", [128, D], F32)
            load_mod(1)
            load_gate(1)
            x1r = fw.ring(es, "x1rd", [128, D], F32, 2)
            xn = fw.sb(es, "xnd", [128, D], F32)
            hb = fw.sb(es, "hbd", [128, D], BF16)
            HTB = fw.ring(es, "HTBd", [128, 8, 132], BF16, 3)
            cA = [fw.ring(es, "cAd%d" % n_, [128, 128], F32, 3) for n_ in range(2)]
            cB = [fw.ring(es, "cBd%d" % n_, [128, 128], F32, 3) for n_ in range(2)]
            sg = fw.ring(es, "sgd", [128, 128], F32, 3)
            actT_t = fw.ring(es, "actTd", [128, NF, 128], BF16, 2)
            actT = [[Tl(fw, a_[:, i, :], "aT%d" % i) for i in range(NF)] for a_ in actT_t]
            rr = [xn, fw.sb(es, "rrd1", [128, D], F32)]
            st6 = fw.sb(es, "st6d", [128, 2, 6], F32)
            mv = fw.sb(es, "mvd", [128, 4], F32)
            ps_tr = ps[0]
            ps_up = ps[1:5]
            ps_dn = (ps[5], ps[6])
            t0 = NCT if last else 0

            def seq_first(t):
                return t == 0 or t == NCT

            def seq_last(t):
                return t == NCT - 1 or t == NT - 1

            def f1(t):
                if t == NCT:
                    load_mod(0)
                x_ = x1r[t % 2]
                fw.dma(fw.sp, x_.ap, self.X1[t * 128:(t + 1) * 128, :], reads=[self.x1buf], writes=[x_], ds=x_.ds)
                for hh in range(2):
                    fw.op(fw.dve, lambda e, hh=hh: e.bn_stats(out=st6[:, hh, :], in_=x_[:, hh * 512:(hh + 1) * 512]), [x_], [st6])
                fw.op(fw.dve, lambda e: e.bn_aggr(out=mv[:, 0:2], in_=st6.ap.rearrange("p a b -> p (a b)")), [st6], [mv])
                rstd_from(fw, mv[:, 2:3], mv[:, 1:2], [mv], [mv])
                ts(fw, fw.dve, xn.ap, x_.ap, mv[:, 0:1], mv[:, 2:3], ALU.subtract, ALU.mult, [x_, mv], [xn])
                tt(fw, fw.pool, xn.ap, xn.ap, modt[:, 1, :], ALU.mult, [xn, modt], [xn])
                tt(fw, fw.dve, hb.ap, xn.ap, modt[:, 0, :], ALU.add, [xn, modt], [hb])
                pb = ps_tr.ap.bitcast(BF16).rearrange("p (a b) -> p a b", a=8)
                for kc in range(8):
                    tr(fw, pb[:, kc, :], hb[:, kc * 128:(kc + 1) * 128], self.identb.ap, [hb, self.identb], [ps_tr], inc=(kc == 7))
                H_ = HTB[t % 3]
                if "cpH" in dsk:
                    return
                cp(fw, fw.act, H_[:, :, 2:130], pb, [ps_tr], [H_])
                if "halo" in dsk:
                    return
                if seq_first(t):
                    fw.op(fw.dve, lambda e: e.memset(H_[:, :, 1:2], 0.0), [], [H_])
                else:
                    Hp = HTB[(t - 1) % 3]
                    cp(fw, fw.dve, Hp[:, :, 130:131], H_[:, :, 2:3], [H_], [Hp])
                if seq_last(t):
                    fw.op(fw.dve, lambda e: e.memset(H_[:, :, 130:131], 0.0), [], [H_])
                elif t + 1 < NT:
                    Hn = HTB[(t + 1) % 3]
                    cp(fw, fw.dve, Hn[:, :, 1:2], H_[:, :, 129:130], [H_], [Hn])

            def f2(t):
                if t == NCT:
                    load_gate(0)
                H_ = HTB[t % 3]
                x_ = x1r[t % 2]
                aT = actT[t % 2]
                SK = 5

                def down(i):
                    for n in range(2):
                        mm(fw, ps_dn[n].ap, aT[i].ap, WD[:, i, n * 512:(n + 1) * 512], i == 0, i == NF - 1, [aT[i], WD], [ps_dn[n]], inc=True)

                def tail(i):
                    s_ = sg[i % 3]
                    act(fw, s_.ap, cA[0][i % 3].ap, AF.Silu, [cA[0][i % 3]], [s_])
                    tt(fw, fw.pool, aT[i].ap, cA[1][i % 3].ap, s_.ap, ALU.mult, [cA[1][i % 3], s_], [aT[i]])

                npair = NF if self.d_lim is None else self.d_lim[1]
                for i in range(npair):
                    pu = ps_up[i % 4]
                    for n_, ch in enumerate((NF + i, i)):
                        for kc in range(8):
                            mm(fw, pu[:, n_ * 130:(n_ + 1) * 130], WU[:, kc, ch * 128:(ch + 1) * 128], H_[:, kc, 1:131], kc == 0, kc == 7, [WU, H_], [pu],
                               inc=(kc == 7 and n_ == 1))
                    if i >= SK and npair == NF:
                        down(i - SK)
                    for n_, ch in enumerate((NF + i, i)):
                        u = pu[:, n_ * 130:(n_ + 1) * 130]
                        a_, b_ = cA[n_][i % 3], cB[n_][i % 3]
                        act(fw, a_.ap, u[:, 1:129], AF.Identity, [pu, cvp], [a_], bias=cvp[:, 3, ch:ch + 1], scale=cvp[:, 1, ch:ch + 1])
                        stt(fw, fw.dve, a_.ap, u[:, 0:128], cvp[:, 0, ch:ch + 1], a_.ap, ALU.mult, ALU.add, [pu, cvp, a_], [a_])
                        stt(fw, fw.dve, a_.ap, u[:, 2:130], cvp[:, 2, ch:ch + 1], a_.ap, ALU.mult, ALU.add, [pu, cvp, a_], [a_])
                    if i >= 1:
                        tail(i - 1)
                if npair < NF:
                    return
                tail(NF - 1)
                for i in range(NF - SK, NF):
                    down(i)
                r_ = rr[0]
                for n in range(2):
                    tt(fw, fw.dve, r_[:, n * 512:(n + 1) * 512], ps_dn[n].ap, gmlp[:, n * 512:(n + 1) * 512], ALU.mult, [ps_dn[n], gmlp], [r_])
                stt(fw, fw.dve, r_.ap, x_.ap, ALPHA, r_.ap, ALU.mult, ALU.add, [x_, r_], [r_])
                self.ln_affine(r_, rr[1], st6, mv, ln2t)
                if last:
                    fw.dma(fw.sp, self.out[(t - NCT) * 128:(t - NCT + 1) * 128, :], rr[1].ap, reads=[rr[1]], ds=rr[1].ds)
                else:
                    fw.dma(fw.sp, self.X[t * 128:(t + 1) * 128, :], rr[1].ap, reads=[rr[1]], writes=[self.xbuf], ds=rr[1].ds)

            dl = self.d_lim
            nt_ = NT if dl is None else t0 + dl[0]
            if "f1" in dsk:
                fw.barrier()
                return
            f1(t0)
            for t in range(t0, nt_):
                if t + 1 < NT:
                    f1(t + 1)
                if dl is None or dl[1] > 0:
                    f2(t)
            fw.barrier()

    def ln_affine(self, src, dst, st6, mv, gbt):
        fw = self.fw
        for hh in range(2):
            fw.op(fw.dve, lambda e, hh=hh: e.bn_stats(out=st6[:, hh, :], in_=src[:, hh * 512:(hh + 1) * 512]), [src], [st6])
        fw.op(fw.dve, lambda e: e.bn_aggr(out=mv[:, 0:2], in_=st6.ap.rearrange("p a b -> p (a b)")), [st6], [mv])
        rstd_from(fw, mv[:, 2:3], mv[:, 1:2], [mv], [mv])
        ts(fw, fw.dve, dst.ap, src.ap, mv[:, 0:1], mv[:, 2:3], ALU.subtract, ALU.mult, [src, mv], [dst])
        tt(fw, fw.pool, dst.ap, dst.ap, gbt[:, 0, :], ALU.mult, [dst, gbt], [dst])
        tt(fw, fw.pool, dst.ap, dst.ap, gbt[:, 1, :], ALU.add, [dst, gbt], [dst])

    def norm_rope(self, t, src_tl, src, nh, gain, dst_tl, dst, sl_tl, ss, tmp_tl, colT, rowT, qk):
        fw = self.fw
        n = nh * 64
        sq = tmp_tl[:, 0:n]
        s3 = src.rearrange("p (h d) -> p h d", h=nh)
        tt(fw, fw.pool, sq, src, src, ALU.mult, [src_tl], [tmp_tl])
        fw.op(fw.dve, lambda e: e.tensor_reduce(out=ss, in_=sq.rearrange("p (h d) -> p h d", h=nh), axis=AX.X, op=ALU.add), [tmp_tl], [sl_tl])
        rstd_from(fw, ss, ss, [sl_tl], [sl_tl], scale=1.0 / 64.0)
        tt(fw, fw.pool, s3, s3, ss.unsqueeze(2).to_broadcast([128, nh, 64]), ALU.mult, [src_tl, sl_tl], [src_tl])
        is_lat = t >= NCT
        gb = gain.unsqueeze(1).to_broadcast([128, nh, 64])
        if not is_lat:
            tt(fw, fw.pool, dst.rearrange("p (h d) -> p h d", h=nh), s3, gb, ALU.mult, [src_tl, qk], [dst_tl])
            return
        tt(fw, fw.pool, s3, s3, gb, ALU.mult, [src_tl, qk], [src_tl])
        j = t - NCT
        s5 = src.rearrange("p (h a two i) -> p h a two i", h=nh, a=2, two=2)
        o5 = tmp_tl[:, 0:n].rearrange("p (h a two i) -> p h a two i", h=nh, a=2, two=2)
        d5 = dst.rearrange("p (h a two i) -> p h a two i", h=nh, a=2, two=2)
        for a, tab in ((0, rowT[:, j, :]), (1, colT.ap)):
            cos_b = tab[:, 0:16].unsqueeze(1).to_broadcast([128, nh, 16])
            sin_b = tab[:, 16:32].unsqueeze(1).to_broadcast([128, nh, 16])
            tabt = rowT if a == 0 else colT
            x1 = s5[:, :, a, 0, :]
            x2 = s5[:, :, a, 1, :]
            tt(fw, fw.pool, o5[:, :, a, 0, :], x2, sin_b, ALU.mult, [src_tl, tabt], [tmp_tl])
            tt(fw, fw.pool, o5[:, :, a, 1, :], x1, sin_b, ALU.mult, [src_tl, tabt], [tmp_tl])
            tt(fw, fw.pool, x1, x1, cos_b, ALU.mult, [src_tl, tabt], [src_tl])
            tt(fw, fw.pool, x2, x2, cos_b, ALU.mult, [src_tl, tabt], [src_tl])
            tt(fw, fw.pool, d5[:, :, a, 0, :], x1, o5[:, :, a, 0, :], ALU.subtract, [src_tl, tmp_tl], [dst_tl])
            tt(fw, fw.pool, d5[:, :, a, 1, :], x2, o5[:, :, a, 1, :], ALU.add, [src_tl, tmp_tl], [dst_tl])


def make_consts():
    c = np.zeros((128, 1024), np.float32)
    p = np.arange(128)
    c[:, 0:128] = np.eye(128, dtype=np.float32)
    c[:, 128:256] = (p[:, None] <= p[None, :])
    c[:, 256:384] = (p[:, None] >= p[None, :])
    c[:, 384:512] = 1.0
    c[:, 512] = p + 1
    c[:, 513] = 128 - p
    c[:, 514] = -(p + 1)
    c[:, 515] = -(128 - p)
    c[:, 516] = p % 64
    c[:, 517:533] = np.arange(16)[None, :]
    return c


def host_inputs(inputs, b, L=DEPTH):
    f = lambda a: np.ascontiguousarray(np.asarray(a, dtype=np.float32))
    inputs = {k: (np.asarray(v)[:L] if k not in ('x', 'c', 'ctx', 'c_ctx') else v) for k, v in inputs.items()}
    m = {}
    m["xin"] = f(np.concatenate([inputs["ctx"][b], inputs["x"][b]], axis=0))
    cc = np.stack([np.asarray(inputs["c"][b]), np.asarray(inputs["c_ctx"])], axis=-1)
    m["ccT"] = f(cc.reshape(8, 128, 2).transpose(1, 0, 2))
    m["w_mod"] = f(inputs["w_mod"])
    m["b_mod"] = f(inputs["b_mod"])
    m["w_in"] = f(inputs["w_in"])
    m["b_in"] = f(inputs["b_in"])
    fmcols = np.concatenate([np.arange(O_RQ, O_RQ + 256), np.arange(O_RK, O_RK + 256), np.arange(O_MQ, O_MQ + 256), np.arange(O_MK, O_MK + 256)])
    m["b_in_fm"] = f(np.asarray(inputs["b_in"])[:, fmcols].reshape(L, 8, 128).transpose(0, 2, 1))
    m["decay"] = f(np.asarray(inputs["ret_decay_logit"]).reshape(L, 8))
    m["ret_gn"] = f(inputs["ret_gn_g"])
    m["qn_g"] = f(inputs["attn_qn_g"])
    m["kn_g"] = f(inputs["attn_kn_g"])
    m["m_gn"] = f(inputs["mlstm_gn_g"])
    m["w_br"] = f(np.stack([inputs["w_br_ret"], inputs["w_br_att"], inputs["w_br_mlstm"]], axis=1))
    m["w_out"] = f(inputs["w_out"])
    m["ln1"] = f(np.stack([inputs["ln1_g"], inputs["ln1_b"]], axis=1))
    m["w_up"] = f(inputs["w_up"])
    cw = np.asarray(inputs["conv_w"])
    cb = np.asarray(inputs["conv_b"])
    cpk = np.concatenate([cw, cb[:, None, :]], axis=1)
    m["convp"] = f(cpk.reshape(L, 4, 44, 128).transpose(0, 3, 1, 2))
    m["w_down"] = f(inputs["w_down"])
    m["ln2"] = f(np.stack([inputs["ln2_g"], inputs["ln2_b"]], axis=1))
    m["consts"] = make_consts()
    return m


_PROG = {}


def kernel(**inputs):
    if "p" not in _PROG:
        _PROG["p"] = Prog()
    prog = _PROG["p"]
    in_maps = [host_inputs(inputs, b) for b in range(NCORES)]
    res = run_bass_kernel_spmd(prog.nc, in_maps, core_ids=list(range(NCORES)))
    out = np.stack([np.asarray(r["out"]).reshape(32 * 128, D) for r in res.results], axis=0)
    return out.astype(np.float32)
```

```python
import math
import numpy as np
from contextlib import ExitStack
import concourse.bass as bass
import concourse.mybir as mybir
from concourse.bass_utils import run_bass_kernel_spmd

F32 = mybir.dt.float32
BF16 = mybir.dt.bfloat16
I32 = mybir.dt.int32
AF = mybir.ActivationFunctionType
ALU = mybir.AluOpType
AX = mybir.AxisListType

D = 1024
DEPTH = 4
NT = 34
NCT = 2
T = NT * 128
DFF = 2816
NF = 22
EPS = 1e-6
ALPHA = (2.0 * DEPTH) ** 0.25
NCORES = 4

O_RQ, O_RK, O_RV, O_RG = 0, 256, 512, 1024
O_AQ, O_AK, O_AV = 1536, 2048, 2176
O_MQ, O_MK, O_MV, O_MO, O_MI, O_MF, O_GATE = 2304, 2560, 2816, 3328, 3840, 3848, 3856
N_IN = 6928

FM_PIECES = [(0, O_RQ, 256), (256, O_RK, 256), (512, O_MQ, 256), (768, O_MK, 256)]
TMW0 = 1024
TM_GROUPS = [
    ("MG", [(O_MI, 16)]),
    ("RV", [(O_RV, 512)]),
    ("RG", [(O_RG, 512)]),
    ("AQ", [(O_AQ, 512)]),
    ("AKV", [(O_AK, 128), (O_AV, 128)]),
    ("MV", [(O_MV, 512)]),
    ("MO", [(O_MO, 512)]),
] + [("G%d" % i, [(O_GATE + 512 * i, 512)]) for i in range(6)]
TM_OFF = {}
_o = 0
for _n, _p in TM_GROUPS:
    TM_OFF[_n] = _o
    _o += sum(n for _, n in _p)
TM_COLS = _o
W_COLS = TMW0 + TM_COLS

R_RV, R_RG, R_AV, R_RK, R_MK, R_MV, R_MO, R_MG = 0, 1024, 1536, 1666, 1922, 2178, 3210, 3722
R_COLS = 6794
SM_COLS = 16
R_SPLIT = R_MG


class Sem:
    def __init__(self, h, name):
        self.h = h
        self.name = name
        self.owner = None
        self.total = 0


class Buf:
    __slots__ = ("w", "r", "name")

    def __init__(self, name=""):
        self.w = {}
        self.r = {}
        self.name = name


class Tl:
    def __init__(self, fw, ap, name, buf=None):
        self.fw = fw
        self.ap = ap
        self.name = name
        self.buf = buf if buf is not None else Buf(name)
        self._ds = None

    @property
    def ds(self):
        if self._ds is None:
            self._ds = self.fw.pool_sem()
        return self._ds

    def __getitem__(self, idx):
        return self.ap[idx]


class Eng:
    def __init__(self, fw, name, eng):
        self.fw = fw
        self.name = name
        self.eng = eng
        self.sem = fw.new_sem("c_" + name)
        self.sem.owner = self
        self.cnt = 0
        self.waited = {}

    def wait(self, sem, val):
        if val <= 0:
            return
        if self.waited.get(sem, 0) >= val:
            return
        if sem.owner is not None:
            assert val <= sem.owner.cnt, ("wait on unissued instr", self.name, sem.name, val, sem.owner.cnt)
        else:
            assert val <= sem.total
        self.eng.wait_ge(sem.h, val)
        self.waited[sem] = val


class FW:
    def __init__(self, nc):
        self.nc = nc
        self.es = ExitStack()
        self.nsem = 0
        self.pe = Eng(self, "pe", nc.tensor)
        self.act = Eng(self, "act", nc.scalar)
        self.dve = Eng(self, "dve", nc.vector)
        self.pool = Eng(self, "pool", nc.gpsimd)
        self.sp = Eng(self, "sp", nc.sync)
        self.engs = [self.pe, self.act, self.dve, self.pool, self.sp]
        self.dsems = []
        self.swq = []
        self.ndram = 0
        self.sem_pool = []
        self.sem_idx = 0
        self.pool_base = 0
        self.nsb = 0

    def pool_sem(self):
        if self.sem_idx == len(self.sem_pool):
            self.sem_pool.append(self.new_sem("dp%d" % self.sem_idx))
        s = self.sem_pool[self.sem_idx]
        self.sem_idx += 1
        return s

    def reset_pool(self):
        self.sem_idx = self.pool_base

    def new_sem(self, name):
        h = self.es.enter_context(self.nc.semaphore(name + "_%d" % self.nsem))
        self.nsem += 1
        s = Sem(h, name)
        return s

    def sb(self, es, name, shape, dtype):
        self.nsb += 1
        t = es.enter_context(self.nc.sbuf_tensor("%s_%d" % (name, self.nsb), list(shape), dtype))
        return Tl(self, t[:], name)

    def ring(self, es, name, shape, dtype, n):
        return [self.sb(es, "%s%d" % (name, i), shape, dtype) for i in range(n)]

    def dram(self, name, shape, dtype, kind="Internal"):
        t = self.nc.dram_tensor(name, list(shape), dtype, kind=kind)
        return t.ap()

    def _deps(self, E, reads, writes, skip_sem=None):
        for b in reads:
            b = b.buf if isinstance(b, Tl) else b
            for sem, v in b.w.items():
                if sem is E.sem and E is self.pe:
                    continue
                E.wait(sem, v)
        for b in writes:
            b = b.buf if isinstance(b, Tl) else b
            for sem, v in list(b.w.items()) + list(b.r.items()):
                if sem is E.sem or sem is skip_sem:
                    continue
                E.wait(sem, v)

    def _mark(self, sem, tok, reads, writes):
        for b in reads:
            b = b.buf if isinstance(b, Tl) else b
            if b.r.get(sem, 0) < tok:
                b.r[sem] = tok
        for b in writes:
            b = b.buf if isinstance(b, Tl) else b
            b.w = {sem: tok}
            b.r = {}

    def op(self, E, fn, reads=(), writes=(), inc=True):
        self._deps(E, reads, writes)
        ins = fn(E.eng)
        if inc:
            E.cnt += 1
            ins.then_inc(E.sem.h, 1)
            tok = E.cnt
        else:
            tok = E.cnt + 1
        self._mark(E.sem, tok, reads, writes)
        return ins

    def dma(self, Q, out, in_, reads=(), writes=(), ds=None, serialize=True, **kw):
        self._deps(Q, reads, writes, skip_sem=(None if serialize else ds))
        if serialize and ds.total > 0:
            Q.wait(ds, ds.total)
        if Q is self.pool:
            while len(self.swq) >= 2:
                s_, v_ = self.swq.pop(0)
                Q.wait(s_, v_)
        ins = Q.eng.dma_start(out=out, in_=in_, **kw)
        ds.total += 16
        if Q is self.pool:
            self.swq.append((ds, ds.total))
        ins.then_inc(ds.h, 16)
        if ds not in self.dsems:
            self.dsems.append(ds)
        self._mark(ds, ds.total, reads, writes)
        return ins

    def barrier(self):
        for E in self.engs:
            for P in self.engs:
                if P is not E:
                    E.wait(P.sem, P.cnt)
            for ds in self.dsems:
                E.wait(ds, ds.total)


def tt(fw, E, out, in0, in1, op, reads, writes):
    return fw.op(E, lambda e: e.tensor_tensor(out=out, in0=in0, in1=in1, op=op), reads, writes)


def ts(fw, E, out, in0, s1, s2, op0, op1, reads, writes):
    if op1 is None:
        return fw.op(E, lambda e: e.tensor_scalar(out=out, in0=in0, scalar1=s1, scalar2=None, op0=op0), reads, writes)
    return fw.op(E, lambda e: e.tensor_scalar(out=out, in0=in0, scalar1=s1, scalar2=s2, op0=op0, op1=op1), reads, writes)


def stt(fw, E, out, in0, scalar, in1, op0, op1, reads, writes):
    return fw.op(E, lambda e: e.scalar_tensor_tensor(out=out, in0=in0, scalar=scalar, in1=in1, op0=op0, op1=op1), reads, writes)


def act(fw, out, in_, func, reads, writes, bias=None, scale=None):
    kw = {}
    if bias is not None:
        kw["bias"] = bias
    if scale is not None:
        kw["scale"] = scale
    return fw.op(fw.act, lambda e: e.activation(out=out, in_=in_, func=func, **kw), reads, writes)


def cp(fw, E, out, in_, reads, writes):
    if E is fw.act:
        return fw.op(E, lambda e: e.copy(out=out, in_=in_), reads, writes)
    return fw.op(E, lambda e: e.tensor_copy(out=out, in_=in_), reads, writes)


def mm(fw, out, lhsT, rhs, start, stop, reads, writes, inc=None, **kw):
    if inc is None:
        inc = stop
    return fw.op(fw.pe, lambda e: e.matmul(out, lhsT=lhsT, rhs=rhs, start=start, stop=stop, **kw), reads, writes, inc=inc)


def tr(fw, out, in_, ident, reads, writes, inc=True):
    return fw.op(fw.pe, lambda e: e.transpose(out, in_, ident), reads, writes, inc=inc)


def bcast_rows(ap2d, nparts):
    return ap2d.to_broadcast([nparts, ap2d.shape[-1]])


def rstd_from(fw, out, in_, reads, writes, scale=1.0):
    act(fw, out, in_, AF.Ln, reads, writes, bias=fw.eps_t[:, 0:1] if in_.shape[0] == 128 else fw.eps_t[0:in_.shape[0], 0:1], scale=scale)
    act(fw, out, out, AF.Exp, writes, writes, scale=-0.5)


class Prog:
    def __init__(self, n_layers=DEPTH, debug=None, stop_after=None, a_lim=None, skip="", d_lim=None):
        self.a_lim = a_lim
        self.skip = skip
        self.d_lim = d_lim
        self.n_layers = n_layers
        self.debug = debug or []
        self.stop_after = stop_after
        self.nc = bass.Bass("TRN2", target_bir_lowering=False)
        self.fw = FW(self.nc)
        self.inputs = {}
        self.build()

    def din(self, name, shape, dtype=F32):
        ap = self.fw.dram(name, shape, dtype, kind="ExternalInput")
        self.inputs[name] = ap
        return ap

    def dscr(self, name, shape, dtype):
        kind = "ExternalOutput" if name in self.debug else "Internal"
        return self.fw.dram(name, shape, dtype, kind=kind)

    def build(self):
        nc, fw = self.nc, self.fw
        L = self.n_layers
        self.xin = self.din("xin", [T, D])
        self.ccT = self.din("ccT", [128, 8, 2])
        self.w_mod = self.din("w_mod", [L, D, 6 * D])
        self.b_mod = self.din("b_mod", [L, 6 * D])
        self.w_in = self.din("w_in", [L, D, N_IN])
        self.b_in = self.din("b_in", [L, N_IN])
        self.b_in_fm = self.din("b_in_fm", [L, 128, 8])
        self.decay = self.din("decay", [L, 8])
        self.ret_gn = self.din("ret_gn", [L, 512])
        self.qn_g = self.din("qn_g", [L, 64])
        self.kn_g = self.din("kn_g", [L, 64])
        self.m_gn = self.din("m_gn", [L, 512])
        self.w_br = self.din("w_br", [L, 3, 512, D])
        self.w_out = self.din("w_out", [L, D, D])
        self.ln1 = self.din("ln1", [L, 2, D])
        self.w_up = self.din("w_up", [L, D, 2 * DFF])
        self.convp = self.din("convp", [L, 128, 4, 44])
        self.w_down = self.din("w_down", [L, DFF, D])
        self.ln2 = self.din("ln2", [L, 2, D])
        self.consts = self.din("consts", [128, 1024])
        self.out = self.fw.dram("out", [32 * 128, D], F32, kind="ExternalOutput")
        self.X = self.dscr("X", [T, D], F32)
        self.X1 = self.dscr("X1", [T, D], F32)
        self.MOD = self.dscr("MOD", [L, 2, 6 * D], F32)
        self.TMd = self.dscr("TMd", [NT, 128, R_COLS], BF16)
        self.SMd = self.dscr("SMd", [NT, 128, SM_COLS], F32)
        self.FMd = self.dscr("FMd", [NT, 128, 1024], BF16)
        self.AQd = self.dscr("AQd", [NT, 64, 1024], BF16)
        self.AKd = self.dscr("AKd", [64, 2, T], BF16)
        self.ROPEd = self.dscr("ROPEd", [64, 32], F32)
        self.RETCd = self.dscr("RETCd", [L, 128, 16], F32)
        self.SFd = self.dscr("SFd", [NT, 128, 4 * 258], BF16)
        self.SBd = self.dscr("SBd", [NT, 128, 4 * 258], BF16)
        self.YDd = self.dscr("YDd", [NT, 128, 1536], BF16)

        with ExitStack() as es0:
            self.setup_globals(es0)
            fw.barrier()
            fw.pool_base = fw.sem_idx
            if self.stop_after == "S":
                return self.finish()
            for l in range(self.n_layers):
                if "A" not in self.skip:
                    self.phase_a(l)
                fw.barrier()
                fw.reset_pool()
                if self.stop_after == "A%d" % l:
                    return self.finish()
                if "B" not in self.skip:
                    self.phase_b(l)
                else:
                    self.x1buf = Buf("X1")
                fw.barrier()
                fw.reset_pool()
                if self.stop_after == "B%d" % l:
                    return self.finish()
                self.phase_d(l)
                fw.barrier()
                fw.reset_pool()
                if self.stop_after == "D%d" % l:
                    return self.finish()
            self.finish()

    def finish(self):
        fw = self.fw
        fw.barrier()
        for ds in fw.dsems:
            fw.sp.wait(ds, ds.total)

    def setup_globals(self, es):
        nc, fw = self.nc, self.fw
        self.cst = fw.sb(es, "cst", [128, 1024], F32)
        fw.dma(fw.sp, self.cst.ap, self.consts, writes=[self.cst], ds=self.cst.ds)
        self.ident_f = self.cst[:, 0:128]
        self.triF = self.cst[:, 128:256]
        self.triB = self.cst[:, 256:384]
        self.ones_f = self.cst[:, 384:512]
        self.identb = fw.sb(es, "identb", [128, 128], BF16)
        cp(fw, fw.dve, self.identb.ap, self.ident_f, [self.cst], [self.identb])
        self.eps_t = fw.sb(es, "eps_t", [128, 1], F32)
        fw.eps_t = self.eps_t
        fw.op(fw.dve, lambda e: e.memset(self.eps_t.ap, EPS), [], [self.eps_t])
        self.onesb = fw.sb(es, "onesb", [128, 128], BF16)
        fw.op(fw.dve, lambda e: e.memset(self.onesb.ap, 1.0), [], [self.onesb])
        self.one_t = fw.sb(es, "one_t", [128, 1], F32)
        fw.op(fw.dve, lambda e: e.memset(self.one_t.ap, 1.0), [], [self.one_t])
        self.maskF = fw.sb(es, "maskF", [128, 4, 128], BF16)
        self.maskB = fw.sb(es, "maskB", [128, 4, 128], BF16)
        for h in range(4):
            cp(fw, fw.dve, self.maskF[:, h, :], self.triF, [self.cst], [self.maskF])
            cp(fw, fw.dve, self.maskB[:, h, :], self.triB, [self.cst], [self.maskB])
        self.ps = []
        for i in range(8):
            t = es.enter_context(nc.psum_tensor("psb%d" % i, [128, 512], F32))
            self.ps.append(Tl(fw, t[:], "psb%d" % i))
        xds = fw.new_sem("xcopy")
        fw.dma(fw.sp, self.X, self.xin, ds=xds)
        self.xbuf = Buf("Xall")
        self.xbuf.w = {xds: xds.total}
        self.compute_mod(es)
        self.compute_rope(es)

    def compute_mod(self, es0):
        nc, fw = self.nc, self.fw
        with ExitStack() as es:
            cc = fw.sb(es, "cc", [128, 8, 2], F32)
            sc = fw.sb(es, "sc", [128, 8, 2], F32)
            fw.dma(fw.sp, cc.ap, self.ccT, writes=[cc], ds=cc.ds)
            act(fw, sc.ap, cc.ap, AF.Silu, [cc], [sc])
            wring = fw.ring(es, "wm", [128, 8, 512], F32, 3)
            bm = fw.sb(es, "bm", [2, 6 * D], F32)
            orow = fw.ring(es, "orow", [2, 6 * D], F32, 2)
            k = 0
            for l in range(self.n_layers):
                fw.dma(fw.sp, bm.ap, self.b_mod[l:l + 1, :].to_broadcast([2, 6 * D]), writes=[bm], ds=bm.ds)
                orw = orow[l % 2]
                for g in range(12):
                    wt = wring[k % 3]
                    src = self.w_mod[l].rearrange("(kc p) c -> p kc c", p=128)[:, :, g * 512:(g + 1) * 512]
                    fw.dma(fw.sp, wt.ap, src, writes=[wt], ds=wt.ds)
                    pst = self.ps[k % 2]
                    for kc in range(8):
                        mm(fw, pst[0:2, :], sc[:, kc, :], wt[:, kc, :], kc == 0, kc == 7, [sc, wt], [pst])
                    tt(fw, fw.dve, orw[:, g * 512:(g + 1) * 512], pst[0:2, :], bm[:, g * 512:(g + 1) * 512], ALU.add, [pst, bm], [orw])
                    k += 1
                for ch in (1, 4):
                    ts(fw, fw.dve, orw[:, ch * D:(ch + 1) * D], orw[:, ch * D:(ch + 1) * D], 1.0, None, ALU.add, None, [orw], [orw])
                fw.dma(fw.sp, self.MOD[l], orw.ap, reads=[orw], ds=orw.ds)
            self.modbuf = Buf("MOD")
            for o in orow:
                self.modbuf.w[o.ds] = o.ds.total
            fw.barrier()

    def compute_rope(self, es0):
        nc, fw = self.nc, self.fw
        with ExitStack() as es:
            tl = fw.sb(es, "rp", [128, 8, 32], F32)
            itl = fw.sb(es, "rpi", [128, 32], I32)
            c = self.cst
            fr, u, r, fx, ang = (tl[:, i, :] for i in range(5))
            nidx = c[:, 516:517]
            act(fw, fr[:, 0:16], c[:, 517:533], AF.Exp, [c], [tl], scale=-math.log(10000.0) / 16.0)
            ts(fw, fw.dve, ang[:, 0:16], fr[:, 0:16], nidx, 1.0 / (2 * math.pi), ALU.mult, ALU.mult, [tl, c], [tl])
            ts(fw, fw.dve, u[:, 0:16], ang[:, 0:16], 0.25, None, ALU.add, None, [tl], [tl])
            cp(fw, fw.dve, u[:, 16:32], ang[:, 0:16], [tl], [tl])
            cp(fw, fw.dve, itl.ap, u, [tl], [itl])
            cp(fw, fw.dve, r, itl.ap, [itl], [tl])
            tt(fw, fw.dve, r, u, r, ALU.subtract, [tl], [tl])
            ts(fw, fw.dve, fx, r, 0.5, None, ALU.is_gt, None, [tl], [tl])
            tt(fw, fw.dve, r, r, fx, ALU.subtract, [tl], [tl])
            ts(fw, fw.dve, fx, r, -0.5, None, ALU.is_lt, None, [tl], [tl])
            tt(fw, fw.dve, r, r, fx, ALU.add, [tl], [tl])
            res = tl[:, 5, :]
            act(fw, res, r, AF.Sin, [tl], [tl], scale=2 * math.pi)
            fw.dma(fw.sp, self.ROPEd, tl[0:64, 5, :], reads=[tl], ds=tl.ds)
            self.ropebuf = Buf("rope")
            self.ropebuf.w = {tl.ds: tl.ds.total}
            fw.barrier()

    def load_bcast(self, dst_tl, dst_ap, src_row_ap, q=None, reads=()):
        fw = self.fw
        q = q or fw.sp
        n = src_row_ap.shape[-1]
        fw.dma(q, dst_ap, src_row_ap.to_broadcast([dst_ap.shape[0], n]), reads=list(reads), writes=[dst_tl], ds=dst_tl.ds, serialize=False)

    def phase_a(self, l):
        nc, fw = self.nc, self.fw
        ps = self.ps
        with ExitStack() as es:
            W = fw.sb(es, "Wa", [128, 8, W_COLS], BF16)
            wsrc = self.w_in[l].rearrange("(kc p) c -> p kc c", p=128)
            pieces = list(FM_PIECES)
            for name, pl_ in TM_GROUPS:
                o = TMW0 + TM_OFF[name]
                for (src, n) in pl_:
                    pieces.append((o, src, n))
                    o += n
            for (dst, src, n) in pieces:
                fw.dma(fw.pool, W[:, :, dst:dst + n], wsrc[:, :, src:src + n], writes=[W], ds=W.ds, serialize=False)
            BB = fw.sb(es, "BBa", [128, TM_COLS - 16], BF16)
            BG = fw.sb(es, "BGa", [128, 16], F32)
            for name, pl_ in TM_GROUPS:
                o = TM_OFF[name]
                for (src, n) in pl_:
                    row = self.b_in[l:l + 1, src:src + n]
                    if name == "MG":
                        self.load_bcast(BG, BG[:, o:o + n], row)
                    else:
                        self.load_bcast(BB, BB[:, o - 16:o - 16 + n], row, q=fw.pool)
                    o += n
            bfm = fw.sb(es, "bfm", [128, 8], F32)
            fw.dma(fw.sp, bfm.ap, self.b_in_fm[l], writes=[bfm], ds=bfm.ds)
            modt = fw.sb(es, "moda", [128, 2, D], F32)

            def load_mod(j):
                for ch in range(2):
                    self.load_bcast(modt, modt[:, ch, :], self.MOD[l, j:j + 1, ch * D:(ch + 1) * D], reads=[self.modbuf])
            load_mod(1)
            gn = fw.sb(es, "gna", [128, 2, 512], F32)
            self.load_bcast(gn, gn[:, 0, :], self.ret_gn[l:l + 1, :])
            self.load_bcast(gn, gn[:, 1, :], self.m_gn[l:l + 1, :])
            qk = fw.sb(es, "qka", [128, 2, 64], F32)
            self.load_bcast(qk, qk[:, 0, :], self.qn_g[l:l + 1, :])
            self.load_bcast(qk, qk[:, 1, :], self.kn_g[l:l + 1, :])
            ts(fw, fw.dve, qk[:, 0, :], qk[:, 0, :], 0.125, None, ALU.mult, None, [qk], [qk])
            colT = fw.sb(es, "colT", [128, 32], F32)
            rowT = fw.sb(es, "rowT", [128, 32, 32], F32)
            for hf in range(2):
                fw.dma(fw.sp, colT[hf * 64:(hf + 1) * 64, :], self.ROPEd, reads=[self.ropebuf], writes=[colT], ds=colT.ds, serialize=False)
                src = self.ROPEd.rearrange("(j two) c -> two j c", two=2)[hf:hf + 1]
                fw.dma(fw.sp, rowT[hf * 64:(hf + 1) * 64, :, :], src.to_broadcast([64, 32, 32]), reads=[self.ropebuf], writes=[rowT], ds=rowT.ds, serialize=False)
            dk = fw.sb(es, "dka", [128, 8, 8], F32)
            c = self.cst
            self.load_bcast(dk, dk[:, 0, :], self.decay[l:l + 1, :])
            act(fw, dk[:, 1, :], dk[:, 0, :], AF.Exp, [dk], [dk], scale=-1.0)
            act(fw, dk[:, 2, :], dk[:, 1, :], AF.Ln, [dk], [dk], bias=self.one_t[:, 0:1])
            rEA = dk[:, 3, :]
            rEB = dk[:, 4, :]
            rEE = dk[:, 5, :]
            act(fw, rEA[:, 0:4], dk[:, 2, 0:4], AF.Exp, [dk, c], [dk], scale=c[:, 512:513])
            act(fw, rEA[:, 4:8], dk[:, 2, 4:8], AF.Exp, [dk, c], [dk], scale=c[:, 513:514])
            act(fw, rEB[:, 0:4], dk[:, 2, 0:4], AF.Exp, [dk, c], [dk], scale=c[:, 514:515])
            act(fw, rEB[:, 4:8], dk[:, 2, 4:8], AF.Exp, [dk, c], [dk], scale=c[:, 515:516])
            act(fw, rEE, dk[:, 2, :], AF.Exp, [dk], [dk], scale=-128.0)
            retc = fw.sb(es, "retc", [128, 16], F32)
            cp(fw, fw.dve, retc[:, 0:8], rEB, [dk], [retc])
            for hf in range(2):
                cp(fw, fw.dve, retc[hf * 64:(hf + 1) * 64, 8:12].rearrange("p (d j) -> p d j", d=2),
                   rEE[hf * 64:(hf + 1) * 64, :].rearrange("p (d j two) -> p d j two", d=2, j=2)[:, :, :, hf], [dk], [retc])
            fw.dma(fw.sp, self.RETCd[l], retc.ap, reads=[retc], ds=retc.ds)

            xt = fw.sb(es, "xta", [128, D], F32)
            st6 = fw.sb(es, "st6a", [128, 2, 6], F32)
            mv = fw.sb(es, "mva", [128, 4], F32)
            xn = fw.sb(es, "xna", [128, D], F32)
            xm = fw.sb(es, "xma", [128, D], BF16)
            xmT = fw.ring(es, "xmTa", [128, 8, 128], BF16, 2)
            tmA = fw.sb(es, "tmAa", [128, R_SPLIT], BF16)
            tmB = fw.sb(es, "tmBa", [128, R_COLS - R_SPLIT], BF16)
            sm_ = fw.sb(es, "smra", [128, SM_COLS], F32)
            fm_ = fw.sb(es, "fmra", [128, 8, 128], BF16)
            aq_ = fw.sb(es, "aqra", [64, 8, 128], BF16)
            ak_ = fw.sb(es, "akra", [64, 2, 128], BF16)
            tmpA = fw.ring(es, "tmpAa", [128, 512], F32, 2)
            qpriv = fw.sb(es, "qpriva", [128, 512], F32)
            kpriv = fw.sb(es, "kpriva", [128, 128], F32)
            tmpB = fw.ring(es, "tmpBa", [128, 512], F32, 2)
            qb_ = fw.sb(es, "qba", [128, 640], BF16)
            g_ = fw.sb(es, "gtsa", [128, 64], F32)
            g16 = fw.sb(es, "g16a", [128, 16], BF16)
            sl_ = fw.sb(es, "smla", [128, 32], F32)
            fw.op(fw.dve, lambda e: e.memset(tmA[:, R_AV:R_AV + 130].rearrange("p (g c) -> p g c", g=2)[:, :, 64:65], 1.0), [], [tmA])

            ps_tr, ps_fm, ps_sm, ps_aq = ps[0], ps[1], ps[2], ps[3]
            ps_tm = ps[4:8]
            self._tmk = 0

            def s1(t):
                if t == NCT:
                    load_mod(0)
                fw.dma(fw.sp, xt.ap, self.X[t * 128:(t + 1) * 128, :], reads=[self.xbuf], writes=[xt], ds=xt.ds)
                for hh in range(2):
                    fw.op(fw.dve, lambda e, hh=hh: e.bn_stats(out=st6[:, hh, :], in_=xt[:, hh * 512:(hh + 1) * 512]), [xt], [st6])
                fw.op(fw.dve, lambda e: e.bn_aggr(out=mv[:, 0:2], in_=st6.ap.rearrange("p a b -> p (a b)")), [st6], [mv])
                rstd_from(fw, mv[:, 2:3], mv[:, 1:2], [mv], [mv])
                ts(fw, fw.dve, xn.ap, xt.ap, mv[:, 0:1], mv[:, 2:3], ALU.subtract, ALU.mult, [xt, mv], [xn])
                tt(fw, fw.dve, xn.ap, xn.ap, modt[:, 1, :], ALU.mult, [xn, modt], [xn])
                tt(fw, fw.dve, xm.ap, xn.ap, modt[:, 0, :], ALU.add, [xn, modt], [xm])

            def s2(t):
                xT_ = xmT[t % 2]
                pb = ps_tr.ap.bitcast(BF16).rearrange("p (a b) -> p a b", a=8)
                for kc in range(8):
                    tr(fw, pb[:, kc, :], xm[:, kc * 128:(kc + 1) * 128], self.identb.ap, [xm, self.identb], [ps_tr], inc=(kc == 7))
                cp(fw, fw.act, xT_.ap.rearrange("p a b -> p (a b)"), ps_tr.ap.bitcast(BF16), [ps_tr], [xT_])

            def tm_matmul(t, name, n):
                xT_ = xmT[t % 2]
                pst = ps_tm[self._tmk % len(ps_tm)]
                self._tmk += 1
                o = TMW0 + TM_OFF[name]
                for kc in range(8):
                    mm(fw, pst[:, 0:n], xT_[:, kc, :], W[:, kc, o:o + n], kc == 0, kc == 7, [xT_, W], [pst])
                return pst

            def bias_of(name, n, off=0):
                o = TM_OFF[name] - 16 + off
                return BB[:, o:o + n]

            def s3(t, mid_hook=None):
                xT_ = xmT[t % 2]
                for half in range(2):
                    for i4 in range(4):
                        i = half * 4 + i4
                        for kc in range(8):
                            mm(fw, ps_fm[:, i4 * 128:(i4 + 1) * 128], W[:, kc, i * 128:(i + 1) * 128], xT_[:, kc, :], kc == 0, kc == 7, [xT_, W], [ps_fm],
                               inc=(kc == 7 and i4 == 3))
                    for i4 in range(4):
                        i = half * 4 + i4
                        sc_ = 0.125 if i in (2, 3, 6, 7) else 1.0
                        ts(fw, fw.dve, fm_[:, i, :], ps_fm[:, i4 * 128:(i4 + 1) * 128], bfm[:, i:i + 1], sc_, ALU.add, ALU.mult, [ps_fm, bfm], [fm_])
                fw.dma(fw.sp, self.FMd[t], fm_.ap.rearrange("p a b -> p (a b)"), reads=[fm_], ds=fm_.ds)
                if lim is not None and len(lim) > 2 and lim[2] <= 1:
                    return
                pbk = ps_sm.ap.bitcast(BF16)
                for n_, i in enumerate((2, 3, 6, 7)):
                    tr(fw, pbk[:, 512 + n_ * 128:512 + (n_ + 1) * 128], fm_[:, i, :], self.identb.ap, [fm_, self.identb], [ps_sm], inc=(n_ == 3))
                cp(fw, fw.act, tmA[:, R_RK:R_RK + 512], pbk[:, 512:1024], [ps_sm], [tmA])
                if lim is not None and len(lim) > 2 and lim[2] <= 2:
                    return
                pst = tm_matmul(t, "MG", 16)
                tt(fw, fw.dve, g_[:, 0:16], pst[:, 0:16], BG.ap, ALU.add, [pst, BG], [g_])
                e_ = g_[:, 16:24]
                sp_ = g_[:, 24:32]
                act(fw, e_, g_[:, 8:16], AF.Exp, [g_], [g_], scale=-1.0)
                act(fw, sp_, e_, AF.Ln, [g_], [g_], bias=self.one_t[:, 0:1])
                hi32, lo32 = g_[:, 56:64], g_[:, 16:24]
                cp(fw, fw.dve, g16[:, 0:8], sp_, [g_], [g16])
                cp(fw, fw.dve, hi32, g16[:, 0:8], [g16], [g_])
                tt(fw, fw.dve, lo32, sp_, hi32, ALU.subtract, [g_], [g_])
                cp(fw, fw.dve, g16[:, 8:16], lo32, [g_], [g16])
                if lim is not None and len(lim) > 2 and lim[2] <= 3:
                    return
                pst = tm_matmul(t, "RV", 512)
                v_ = tmpA[0]
                tt(fw, fw.dve, v_.ap, pst.ap, bias_of("RV", 512), ALU.add, [pst, BB], [v_])
                for d in range(2):
                    eng = fw.dve if d == 0 else fw.pool
                    tt(fw, eng, tmA[:, R_RV + d * 512:R_RV + (d + 1) * 512].rearrange("p (h e) -> p h e", h=4),
                       v_.ap.rearrange("p (h e) -> p h e", h=4), rEA[:, d * 4:(d + 1) * 4].unsqueeze(2).to_broadcast([128, 4, 128]), ALU.mult, [v_, dk], [tmA])
                if lim is not None and len(lim) > 2 and lim[2] <= 4:
                    return
                pst = tm_matmul(t, "RG", 512)
                a_, b_ = tmpA[1], tmpB[0]
                tt(fw, fw.dve, a_.ap, pst.ap, bias_of("RG", 512), ALU.add, [pst, BB], [a_])
                act(fw, b_.ap, a_.ap, AF.Silu, [a_], [b_])
                tt(fw, fw.pool, tmA[:, R_RG:R_RG + 512], b_.ap, gn[:, 0, :], ALU.mult, [b_, gn], [tmA])
                if lim is not None and len(lim) > 2 and lim[2] <= 5:
                    return
                q_ = qpriv
                pst = tm_matmul(t, "AQ", 512)
                tt(fw, fw.dve, q_.ap, pst.ap, bias_of("AQ", 512), ALU.add, [pst, BB], [q_])
                self.norm_rope(t, q_, q_.ap, 8, qk[:, 0, :], qb_, qb_[:, 0:512], sl_, sl_[:, 0:8], tmpB[1], colT, rowT, qk)
                if lim is not None and len(lim) > 2 and lim[2] <= 6:
                    return
                pst = tm_matmul(t, "AKV", 256)
                k_ = kpriv
                tt(fw, fw.dve, k_[:, 0:128], pst[:, 0:128], bias_of("AKV", 128), ALU.add, [pst, BB], [k_])
                self.norm_rope(t, k_, k_[:, 0:128], 2, qk[:, 1, :], qb_, qb_[:, 512:640], sl_, sl_[:, 8:10], tmpB[1], colT, rowT, qk)
                tt(fw, fw.dve, tmA[:, R_AV:R_AV + 130].rearrange("p (g c) -> p g c", g=2)[:, :, 0:64],
                   pst[:, 128:256].rearrange("p (g c) -> p g c", g=2), bias_of("AKV", 128, 128).rearrange("p (g c) -> p g c", g=2), ALU.add, [pst, BB], [tmA])
                if lim is not None and len(lim) > 2 and lim[2] <= 7:
                    return
                for part in range(2):
                    o = part * 8
                    mm(fw, ps_sm[:, 0:4], self.maskF[:, 0, :], g16[:, o:o + 4], part == 0, part == 1, [g16, self.maskF], [ps_sm], inc=False)
                for part in range(2):
                    o = part * 8
                    mm(fw, ps_sm[:, 4:8], self.maskB[:, 0, :], g16[:, o + 4:o + 8], part == 0, part == 1, [g16, self.maskB], [ps_sm], inc=False)
                for part in range(2):
                    o = part * 8
                    mm(fw, ps_sm[:, 8:16], self.onesb.ap, g16[:, o:o + 8], part == 0, part == 1, [g16, self.onesb], [ps_sm], inc=(part == 1))
                ta = g_[:, 32:40]
                EA = g_[:, 40:48]
                tt(fw, fw.dve, ta, g_[:, 0:8], ps_sm[:, 0:8], ALU.add, [g_, ps_sm], [g_])
                act(fw, EA, ta, AF.Exp, [g_], [g_])
                act(fw, sm_[:, 0:8], ps_sm[:, 0:8], AF.Exp, [ps_sm], [sm_], scale=-1.0)
                ebe = g_[:, 48:56]
                act(fw, ebe, ps_sm[:, 8:16], AF.Exp, [ps_sm], [g_], scale=-1.0)
                for hf in range(2):
                    cp(fw, fw.dve, sm_[hf * 64:(hf + 1) * 64, 8:12].rearrange("p (d j) -> p d j", d=2),
                       ebe[hf * 64:(hf + 1) * 64, :].rearrange("p (d j two) -> p d j two", d=2, j=2)[:, :, :, hf], [g_], [sm_])
                fw.dma(fw.sp, self.SMd[t], sm_.ap, reads=[sm_], ds=sm_.ds)
                pst = tm_matmul(t, "MV", 512)
                v_ = tmpA[0]
                tt(fw, fw.dve, v_.ap, pst.ap, bias_of("MV", 512), ALU.add, [pst, BB], [v_])
                for d in range(2):
                    eng = fw.dve if d == 0 else fw.pool
                    dst = tmA[:, R_MV + d * 516:R_MV + (d + 1) * 516].rearrange("p (h e) -> p h e", h=4)
                    tt(fw, eng, dst[:, :, 0:128], v_.ap.rearrange("p (h e) -> p h e", h=4),
                       EA[:, d * 4:(d + 1) * 4].unsqueeze(2).to_broadcast([128, 4, 128]), ALU.mult, [v_, g_], [tmA])
                    cp(fw, eng, dst[:, :, 128:129], EA[:, d * 4:(d + 1) * 4].unsqueeze(2), [g_], [tmA])
                if lim is not None and len(lim) > 2 and lim[2] <= 9:
                    return
                pst = tm_matmul(t, "MO", 512)
                a_, b_ = tmpA[1], tmpB[0]
                tt(fw, fw.dve, a_.ap, pst.ap, bias_of("MO", 512), ALU.add, [pst, BB], [a_])
                act(fw, b_.ap, a_.ap, AF.Sigmoid, [a_], [b_])
                tt(fw, fw.pool, tmA[:, R_MO:R_MO + 512], b_.ap, gn[:, 1, :], ALU.mult, [b_, gn], [tmA])
                fw.dma(fw.sp, self.TMd[t][:, 0:R_SPLIT], tmA.ap, reads=[tmA], ds=tmA.ds)
                if mid_hook is not None:
                    mid_hook()
                if lim is not None and len(lim) > 2 and lim[2] <= 10:
                    return
                for i in range(6):
                    pst = tm_matmul(t, "G%d" % i, 512)
                    a_ = tmpA[i % 2]
                    tt(fw, fw.dve, a_.ap, pst.ap, bias_of("G%d" % i, 512), ALU.add, [pst, BB], [a_])
                    act(fw, tmB[:, i * 512:(i + 1) * 512], a_.ap, AF.Sigmoid, [a_], [tmB])
                fw.dma(fw.sp, self.TMd[t][:, R_SPLIT:R_COLS], tmB.ap, reads=[tmB], ds=tmB.ds)
                pbq = ps_aq.ap.bitcast(BF16)
                for h in range(8):
                    tr(fw, pbq[0:64, h * 128:(h + 1) * 128], qb_[:, h * 64:(h + 1) * 64], self.identb.ap, [qb_, self.identb], [ps_aq], inc=(h == 7))
                for h in range(2):
                    tr(fw, pbk[0:64, 256 + h * 128:256 + (h + 1) * 128], qb_[:, 512 + h * 64:512 + (h + 1) * 64], self.identb.ap, [qb_, self.identb], [ps_sm], inc=(h == 1))
                cp(fw, fw.act, aq_.ap.rearrange("p a b -> p (a b)"), pbq[0:64, :], [ps_aq], [aq_])
                cp(fw, fw.act, ak_.ap.rearrange("p a b -> p (a b)"), pbk[0:64, 256:512], [ps_sm], [ak_])
                fw.dma(fw.sp, self.AQd[t], aq_.ap.rearrange("p a b -> p (a b)"), reads=[aq_], ds=aq_.ds)
                fw.dma(fw.sp, self.AKd[:, :, t * 128:(t + 1) * 128], ak_.ap, reads=[ak_], ds=ak_.ds)
                if lim is not None and len(lim) > 2 and lim[2] <= 8:
                    return

            lim = getattr(self, "a_lim", None)
            if lim == "pre":
                fw.barrier()
                return
            nt = NT if lim is None else lim[0]
            s1(0)
            s2(0)
            for t in range(nt):
                if t + 1 < nt:
                    s1(t + 1)
                    s3(t, (lambda t=t: s2(t + 1)))
                else:
                    s3(t)
            fw.barrier()

    def phase_b(self, l):
        nc, fw = self.nc, self.fw
        ps = self.ps
        last = (l == DEPTH - 1)
        with ExitStack() as es:
            WB = fw.sb(es, "WBb", [128, 3, 4, D], BF16)
            for b in range(3):
                fw.dma(fw.pool, WB[:, b, :, :], self.w_br[l, b].rearrange("(kc p) c -> p kc c", p=128), writes=[WB], ds=WB.ds, serialize=False)
            WO = fw.sb(es, "WOb", [128, 8, D], BF16)
            wo_src = self.w_out[l].rearrange("(kc p) c -> p kc c", p=128)
            for hh in range(2):
                fw.dma(fw.pool, WO[:, hh * 4:(hh + 1) * 4, :], wo_src[:, hh * 4:(hh + 1) * 4, :], writes=[WO], ds=WO.ds, serialize=False)
            AKT = fw.sb(es, "AKTb", [64, 2, T], BF16)
            fw.dma(fw.sp, AKT.ap, self.AKd, writes=[AKT], ds=AKT.ds)
            AVa = fw.sb(es, "AVab", [128, NT, 130], BF16)
            for q in range(0, NT, 8):
                q1 = min(NT, q + 8)
                fw.dma(fw.sp, AVa[:, q:q1, :], self.TMd[q:q1, :, R_AV:R_AV + 130].rearrange("t p c -> p t c"), writes=[AVa], ds=AVa.ds, serialize=False)
            retc = fw.sb(es, "retcb", [128, 16], F32)
            fw.dma(fw.sp, retc.ap, self.RETCd[l], writes=[retc], ds=retc.ds)
            gms = fw.sb(es, "gmsb", [128, D], F32)
            ln1t = fw.sb(es, "ln1tb", [128, 2, D], F32)
            for i in range(2):
                self.load_bcast(ln1t, ln1t[:, i, :], self.ln1[l, i:i + 1, :])

            def load_gms(j):
                self.load_bcast(gms, gms.ap, self.MOD[l, j:j + 1, 2 * D:3 * D], reads=[self.modbuf])
            load_gms(1)
            orders = {0: list(range(NT)), 1: [1, 0] + list(range(NT - 1, 1, -1))}
            SXd = (self.SFd, self.SBd)

            with ExitStack() as es2:
                S = [[fw.sb(es2, "Sst%d_%d" % (d, k), [128, 258], F32) for k in range(4)] for d in range(2)]
                for d in range(2):
                    for k in range(4):
                        fw.op(fw.dve if k % 2 == 0 else fw.pool, lambda e, d=d, k=k: e.memset(S[d][k].ap, 0.0), [], [S[d][k]])
                Sbf = [fw.ring(es2, "Sbf%d" % d, [128, 4, 258], BF16, 2) for d in range(2)]
                ldr = [fw.ring(es2, "ldr%d" % d, [128, 1540], BF16, 3) for d in range(2)]
                smr = [fw.ring(es2, "smr%d" % d, [128, SM_COLS], F32, 3) for d in range(2)]
                for i in range(NT):
                    for d in range(2):
                        t = orders[d][i]
                        L_ = ldr[d][i % 3]
                        sm_ = smr[d][i % 3]
                        fw.dma(fw.sp, L_[:, 0:512], self.TMd[t][:, R_RK:R_RK + 512], writes=[L_], ds=L_.ds)
                        fw.dma(fw.sp, L_[:, 512:1024], self.TMd[t][:, R_RV + d * 512:R_RV + (d + 1) * 512], writes=[L_], ds=L_.ds, serialize=False)
                        fw.dma(fw.sp, L_[:, 1024:1540], self.TMd[t][:, R_MV + d * 516:R_MV + (d + 1) * 516], writes=[L_], ds=L_.ds, serialize=False)
                        fw.dma(fw.sp, sm_.ap, self.SMd[t], writes=[sm_], ds=sm_.ds)
                        for mxj in range(4):
                            mx, j = mxj // 2, mxj % 2
                            W_ = 128 if mx == 0 else 129
                            pst = ps[d * 4 + mxj]
                            K_ = L_[:, mx * 256 + j * 128:mx * 256 + (j + 1) * 128]
                            for blk in range(2):
                                h = 2 * j + blk
                                V_ = L_[:, 512 + h * 128:512 + (h + 1) * 128] if mx == 0 else L_[:, 1024 + h * 129:1024 + (h + 1) * 129]
                                mm(fw, pst[:, blk * 129:blk * 129 + W_], K_, V_, True, True, [L_], [pst], inc=(blk == 1))
                        for mxj in range(4):
                            mx, j = mxj // 2, mxj % 2
                            W_ = 128 if mx == 0 else 129
                            pst = ps[d * 4 + mxj]
                            St = S[d][mxj]
                            sb_ = Sbf[d][i % 2]
                            cp(fw, fw.act, sb_[:, mxj, :], St.ap, [St], [sb_])
                            if mxj == 3:
                                fw.dma(fw.pool, SXd[d][t], sb_.ap.rearrange("p a b -> p (a b)"), reads=[sb_], ds=sb_.ds)
                            e_ = retc[:, 8 + d * 2 + j:9 + d * 2 + j] if mx == 0 else sm_[:, 8 + d * 2 + j:9 + d * 2 + j]
                            e_src = retc if mx == 0 else sm_
                            Sv = St.ap.rearrange("p (b w) -> p b w", b=2)[:, :, 0:W_]
                            Pv = pst[:, 0:258].rearrange("p (b w) -> p b w", b=2)[:, :, 0:W_]
                            act(fw, Sv, Sv, AF.Identity, [St, e_src], [St], scale=e_)
                            stt(fw, fw.dve, Sv, Pv, e_, Sv, ALU.mult, ALU.add, [pst, e_src, St], [St])
                fw.barrier()
            self.sxbuf = Buf("SX")

            tmA = fw.ring(es, "tmAb", [128, R_SPLIT], BF16, 2)
            tmG = fw.ring(es, "tmGb", [128, R_COLS - R_SPLIT], BF16, 3)
            smr = fw.ring(es, "smrb", [128, SM_COLS], F32, 2)
            fmr = fw.ring(es, "fmrb", [128, 8, 128], BF16, 2)
            aqr = fw.ring(es, "aqrb", [64, 8, 128], BF16, 2)
            sfr = [fw.ring(es, "sxr%d" % d, [128, 4, 258], BF16, 2) for d in range(2)]
            xr = fw.ring(es, "xrb", [128, D], F32, 2)
            PT = fw.ring(es, "PTb", [128, 2, 512], BF16, 2)
            pTr = fw.ring(es, "pTb", [128, 512], BF16, 3)
            yf = fw.ring(es, "yfb", [128, 512], F32, 4)
            sml = fw.ring(es, "smlb", [128, 64], F32, 2)
            st4 = fw.sb(es, "st4b", [128, 4, 6], F32)
            ymix = fw.ring(es, "ymixb", [128, 3, 512], BF16, 2)
            yT = fw.ring(es, "yTb", [128, 12, 128], BF16, 2)
            zt = fw.ring(es, "ztb", [128, D], F32, 2)
            zb = fw.sb(es, "zbb", [128, D], BF16)
            zT = fw.sb(es, "zTb", [128, 8, 128], BF16)
            rr = fw.ring(es, "rrb", [128, D], F32, 2)
            st6 = fw.sb(es, "st6b", [128, 2, 6], F32)
            mv = fw.sb(es, "mvb", [128, 4], F32)
            ps_s, ps_of, ps_ob, ps_sc, ps_acc = ps[0], (ps[1], ps[2]), (ps[3], ps[4]), (ps[5], ps[6]), ps[7]
            self._yk = 0

            def nexty():
                self._yk += 1
                return yf[self._yk % 4]

            def loads(t):
                fw.dma(fw.sp, tmA[t % 2].ap, self.TMd[t][:, 0:R_SPLIT], writes=[tmA[t % 2]], ds=tmA[t % 2].ds)
                fw.dma(fw.sp, tmG[t % 3].ap, self.TMd[t][:, R_SPLIT:R_COLS], writes=[tmG[t % 3]], ds=tmG[t % 3].ds)
                fw.dma(fw.sp, smr[t % 2].ap, self.SMd[t], writes=[smr[t % 2]], ds=smr[t % 2].ds)
                fw.dma(fw.sp, fmr[t % 2].ap.rearrange("p a b -> p (a b)"), self.FMd[t], writes=[fmr[t % 2]], ds=fmr[t % 2].ds)
                fw.dma(fw.sp, aqr[t % 2].ap.rearrange("p a b -> p (a b)"), self.AQd[t], writes=[aqr[t % 2]], ds=aqr[t % 2].ds)
                for d in range(2):
                    fw.dma(fw.sp, sfr[d][t % 2].ap.rearrange("p a b -> p (a b)"), SXd[d][t], writes=[sfr[d][t % 2]], ds=sfr[d][t % 2].ds)

            def par(ap2, hp, n=2):
                return ap2.rearrange("p (j two k) -> p j two k", j=2, two=2)[:, :, hp, :]

            def linattn(t, mx):
                ta_, sm_, fm_ = tmA[t % 2], smr[t % 2], fmr[t % 2]
                W_ = 128 if mx == 0 else 129
                qi, ki = (0, 2) if mx == 0 else (4, 6)
                pt_ = PT[(2 * t + mx) % 2]
                for h in range(4):
                    j, hf = h // 2, h % 2
                    mm(fw, ps_sc[hf][:, j * 128:(j + 1) * 128], fm_[hf * 64:(hf + 1) * 64, ki + j, :], fm_[hf * 64:(hf + 1) * 64, qi + j, :], True, True, [fm_], [ps_sc[hf]], inc=(h >= 2))
                for d, msk in ((0, self.maskF), (1, self.maskB)):
                    for hp in range(2):
                        tt(fw, fw.dve, par(pt_[:, d, :], hp), ps_sc[hp][:, 0:256].rearrange("p (j k) -> p j k", j=2), msk[:, 0:2, :], ALU.mult, [ps_sc[hp], msk], [pt_])
                for d in range(2):
                    banks = ps_of if d == 0 else ps_ob
                    S_ = sfr[d][t % 2]
                    for h in (0, 2, 1, 3):
                        j, hf = h // 2, h % 2
                        bank = banks[hf]
                        if mx == 0:
                            V_ = ta_[:, R_RV + d * 512 + h * 128:R_RV + d * 512 + (h + 1) * 128]
                        else:
                            V_ = ta_[:, R_MV + d * 516 + h * 129:R_MV + d * 516 + (h + 1) * 129]
                        out = bank[:, j * 129:j * 129 + W_]
                        mm(fw, out, pt_[:, d, h * 128:(h + 1) * 128], V_, (j == 0), False, [pt_, ta_], [bank], inc=False, skip_group_check=True)
                        Sv = S_[hf * 64:(hf + 1) * 64, mx * 2 + j, hf * 129:hf * 129 + W_]
                        mm(fw, out, fm_[hf * 64:(hf + 1) * 64, qi + j, :], Sv, False, True, [fm_, S_], [bank], inc=(j == 1), skip_group_check=True)
                y = nexty()
                sl = sml[t % 2]
                if mx == 0:
                    t1 = nexty()
                    for hp in range(2):
                        o_f = ps_of[hp][:, 0:258].rearrange("p (j w) -> p j w", j=2)[:, :, 0:128]
                        o_b = ps_ob[hp][:, 0:258].rearrange("p (j w) -> p j w", j=2)[:, :, 0:128]
                        ebf = par(retc[:, 0:4], hp).to_broadcast([128, 2, 128])
                        ebb = par(retc[:, 4:8], hp).to_broadcast([128, 2, 128])
                        tt(fw, fw.dve, par(t1.ap, hp), o_f, ebf, ALU.mult, [ps_of[hp], retc], [t1])
                        tt(fw, fw.dve, par(y.ap, hp), o_b, ebb, ALU.mult, [ps_ob[hp], retc], [y])
                    tt(fw, fw.pool, y.ap, y.ap, t1.ap, ALU.add, [y, t1], [y])
                else:
                    hd = []
                    for d in range(2):
                        banks = ps_of if d == 0 else ps_ob
                        q1 = sl[:, d * 16:d * 16 + 4]
                        q2 = sl[:, d * 16 + 4:d * 16 + 8]
                        r_ = sl[:, d * 16 + 8:d * 16 + 12]
                        eb = sm_[:, d * 4:(d + 1) * 4]
                        for hp in range(2):
                            den = banks[hp][:, 0:258].rearrange("p (j w) -> p j w", j=2)[:, :, 128:129]
                            tt(fw, fw.dve, par(q1, hp), den, par(eb, hp), ALU.mult, [banks[hp], sm_], [sl])
                        stt(fw, fw.dve, q2, q1, -1.0, q1, ALU.mult, ALU.max, [sl], [sl])
                        ts(fw, fw.dve, q2, q2, 1.0, None, ALU.max, None, [sl], [sl])
                        fw.op(fw.dve, lambda e, q2=q2: e.reciprocal(out=q2, in_=q2), [sl], [sl])
                        tt(fw, fw.dve, r_, q2, eb, ALU.mult, [sl, sm_], [sl])
                        hdt = y if d == 0 else nexty()
                        for hp in range(2):
                            num = banks[hp][:, 0:258].rearrange("p (j w) -> p j w", j=2)[:, :, 0:128]
                            tt(fw, fw.dve, par(hdt.ap, hp), num, par(r_, hp).to_broadcast([128, 2, 128]), ALU.mult, [banks[hp], sl], [hdt])
                        hd.append(hdt)
                    tt(fw, fw.pool, y.ap, hd[0].ap, hd[1].ap, ALU.add, [hd[0], hd[1]], [y])
                y3 = y.ap.rearrange("p (h e) -> p h e", h=4)
                for h in range(4):
                    fw.op(fw.dve, lambda e, h=h: e.bn_stats(out=st4[:, h, :], in_=y3[:, h, :]), [y], [st4])
                mvh = sl[:, 32:40].rearrange("p (h two) -> p h two", h=4)
                for h in range(4):
                    fw.op(fw.dve, lambda e, h=h: e.bn_aggr(out=mvh[:, h, :], in_=st4[:, h, :]), [st4], [sl])
                rs = sl[:, 40:44]
                act(fw, rs, mvh[:, :, 1], AF.Ln, [sl], [sl], bias=self.eps_t[:, 0:1])
                act(fw, rs, rs, AF.Exp, [sl], [sl], scale=-0.5)
                for h in range(4):
                    ts(fw, fw.dve, y3[:, h, :], y3[:, h, :], mvh[:, h, 0:1], rs[:, h:h + 1], ALU.subtract, ALU.mult, [y, sl], [y])
                gcol = R_RG if mx == 0 else R_MO
                ym = ymix[t % 2]
                tt(fw, fw.pool, ym[:, 0 if mx == 0 else 2, :], y.ap, ta_[:, gcol:gcol + 512], ALU.mult, [y, ta_], [ym])

            def attention(t, hooks):
                aq_ = aqr[t % 2]
                ym = ymix[t % 2]
                kts = list(range(NCT)) if t < NCT else list(range(NT))
                its = [(g, n_, kt) for g in range(2) for n_, kt in enumerate(kts)]
                nk = len(kts)
                hk = list(hooks)
                every = max(1, (len(its) - 4) // max(1, len(hk))) if hk else 0

                def score(idx):
                    g, n_, kt = its[idx]
                    psc = ps_sc[idx % 2]
                    mm(fw, psc.ap, AKT[:, g, kt * 128:(kt + 1) * 128], aq_[:, g * 4:(g + 1) * 4, :].rearrange("p a b -> p (a b)"), True, True, [AKT, aq_], [psc])
                score(0)
                for idx, (g, n_, kt) in enumerate(its):
                    if idx + 1 < len(its):
                        score(idx + 1)
                    psc = ps_sc[idx % 2]
                    p_ = pTr[idx % 3]
                    act(fw, p_.ap, psc.ap, AF.Exp, [psc], [p_])
                    for r in range(4):
                        mm(fw, ps_acc[:, r * 65:(r + 1) * 65], p_[:, r * 128:(r + 1) * 128], AVa[:, kt, g * 65:(g + 1) * 65],
                           (n_ == 0 and r == 0), (n_ == nk - 1), [p_, AVa], [ps_acc], inc=(r == 3), skip_group_check=True)
                    if n_ == nk - 1:
                        sl = sml[t % 2]
                        rd = sl[:, 48 + g * 4:52 + g * 4]
                        acc3 = ps_acc[:, 0:260].rearrange("p (r c) -> p r c", r=4)
                        fw.op(fw.dve, lambda e, rd=rd, acc3=acc3: e.reciprocal(out=rd, in_=acc3[:, :, 64]), [ps_acc], [sl])
                        tt(fw, fw.dve, ym[:, 1, g * 256:(g + 1) * 256].rearrange("p (r c) -> p r c", r=4), acc3[:, :, 0:64],
                           rd.unsqueeze(2).to_broadcast([128, 4, 64]), ALU.mult, [ps_acc, sl], [ym])
                    if hk and every and idx >= 2 and (idx - 2) % every == 0:
                        hk.pop(0)()
                for h_ in hk:
                    h_()

            def merge_stages(t):
                ym, yT_, tg_, x_ = ymix[t % 2], yT[t % 2], tmG[t % 3], xr[t % 2]
                pb = ps_s.ap.bitcast(BF16)
                zsum = zt[0]

                def m_tr(grp):
                    def f():
                        if grp == 0:
                            if t == NCT:
                                load_gms(0)
                            if "YDd" in self.debug:
                                fw.dma(fw.sp, self.YDd[t], ym.ap.rearrange("p a b -> p (a b)"), reads=[ym], ds=ym.ds)
                            fw.dma(fw.sp, x_.ap, self.X[t * 128:(t + 1) * 128, :], reads=[self.xbuf], writes=[x_], ds=x_.ds)
                        n = 8 if grp == 0 else 4
                        for i in range(n):
                            ii = grp * 8 + i
                            tr(fw, pb[:, i * 128:(i + 1) * 128], ym[:, ii // 4, (ii % 4) * 128:(ii % 4 + 1) * 128], self.identb.ap, [ym, self.identb], [ps_s], inc=(i == n - 1))
                        cp(fw, fw.act, yT_[:, grp * 8:grp * 8 + n, :].rearrange("p a b -> p (a b)"), pb[:, 0:n * 128], [ps_s], [yT_])
                    return f

                def m_br(b):
                    def f():
                        banks = ps_of if b % 2 == 0 else ps_ob
                        for n in range(2):
                            for kc in range(4):
                                mm(fw, banks[n].ap, yT_[:, b * 4 + kc, :], WB[:, b, kc, n * 512:(n + 1) * 512], kc == 0, kc == 3, [yT_, WB], [banks[n]])
                        dst = zsum if b == 0 else zt[1]
                        for n in range(2):
                            tt(fw, fw.dve, dst[:, n * 512:(n + 1) * 512], banks[n].ap, tg_[:, b * D + n * 512:b * D + (n + 1) * 512], ALU.mult, [banks[n], tg_], [dst])
                        if b == 1:
                            tt(fw, fw.pool, zsum.ap, zsum.ap, dst.ap, ALU.add, [zsum, dst], [zsum])
                        if b == 2:
                            tt(fw, fw.pool, zb.ap, zsum.ap, dst.ap, ALU.add, [zsum, dst], [zb])
                    return f

                def m_zt():
                    pb8 = pb.rearrange("p (a b) -> p a b", a=8)
                    for kc in range(8):
                        tr(fw, pb8[:, kc, :], zb[:, kc * 128:(kc + 1) * 128], self.identb.ap, [zb, self.identb], [ps_s], inc=(kc == 7))
                    cp(fw, fw.act, zT.ap.rearrange("p a b -> p (a b)"), pb, [ps_s], [zT])

                def m_out():
                    for n in range(2):
                        for kc in range(8):
                            mm(fw, ps_ob[n].ap, zT[:, kc, :], WO[:, kc, n * 512:(n + 1) * 512], kc == 0, kc == 7, [zT, WO], [ps_ob[n]])
                    r_ = rr[0]
                    for n in range(2):
                        tt(fw, fw.dve, r_[:, n * 512:(n + 1) * 512], ps_ob[n].ap, gms[:, n * 512:(n + 1) * 512], ALU.mult, [ps_ob[n], gms], [r_])
                    stt(fw, fw.dve, r_.ap, x_.ap, ALPHA, r_.ap, ALU.mult, ALU.add, [x_, r_], [r_])

                def m_ln():
                    self.ln_affine(rr[0], rr[1], st6, mv, ln1t)
                    fw.dma(fw.sp, self.X1[t * 128:(t + 1) * 128, :], rr[1].ap, reads=[rr[1]], ds=rr[1].ds)
                return [m_tr(0), m_tr(1), m_br(0), m_br(1), m_br(2), m_zt, m_out, m_ln]

            t0 = NCT if last else 0
            loads(t0)
            for t in range(t0, NT):
                if t + 1 < NT:
                    loads(t + 1)
                linattn(t, 0)
                linattn(t, 1)
                attention(t, merge_stages(t - 1) if t - 1 >= t0 else [])
            for f_ in merge_stages(NT - 1):
                f_()
            self.x1buf = Buf("X1")
            fw.barrier()

    def phase_d(self, l):
        nc, fw = self.nc, self.fw
        ps = self.ps
        last = (l == DEPTH - 1)
        with ExitStack() as es:
            import os
            dsk = os.environ.get("D_SKIP", "").split(",")
            WU = fw.sb(es, "WUd", [128, 8, 2 * DFF], BF16)
            wu_src = self.w_up[l].rearrange("(kc p) c -> p kc c", p=128)
            for c0 in range(0, 2 * DFF if "wu" not in dsk else 0, 512):
                fw.dma(fw.pool, WU[:, :, c0:c0 + 512], wu_src[:, :, c0:c0 + 512], writes=[WU], ds=WU.ds, serialize=False)
            WD = fw.sb(es, "WDd", [128, NF, D], BF16)
            wd_src = self.w_down[l].rearrange("(f p) c -> p f c", p=128)
            for f0 in range(0, NF if "wd" not in dsk else 0, 4):
                f1 = min(NF, f0 + 4)
                fw.dma(fw.pool, WD[:, f0:f1, :], wd_src[:, f0:f1, :], writes=[WD], ds=WD.ds, serialize=False)
            cvp = fw.sb(es, "cvpd", [128, 4, 44], F32)
            if "cvp" not in dsk:
                fw.dma(fw.sp, cvp.ap, self.convp[l], writes=[cvp], ds=cvp.ds)
            modt = fw.sb(es, "modd", [128, 2, D], F32)
            ln2t = fw.sb(es, "ln2td", [128, 2, D], F32)
            for i in range(2):
                self.load_bcast(ln2t, ln2t[:, i, :], self.ln2[l, i:i + 1, :])

            def load_mod(j):
                for i in range(2):
                    self.load_bcast(modt, modt[:, i, :], self.MOD[l, j:j + 1, (3 + i) * D:(4 + i) * D], reads=[self.modbuf])

            def load_gate(j):
                self.load_bcast(gmlp, gmlp.ap, self.MOD[l, j:j + 1, 5 * D:6 * D], reads=[self.modbuf])
            gmlp = fw.sb(es, "gmlpd", [128, D], F32)
            load_mod(1)
            load_gate(1)
            x1r = fw.ring(es, "x1rd", [128, D], F32, 2)
            xn = fw.sb(es, "xnd", [128, D], F32)
            hb = fw.sb(es, "hbd", [128, D], BF16)
            HTB = fw.ring(es, "HTBd", [128, 8, 132], BF16, 3)
            cA = [fw.ring(es, "cAd%d" % n_, [128, 128], F32, 3) for n_ in range(2)]
            cB = [fw.ring(es, "cBd%d" % n_, [128, 128], F32, 3) for n_ in range(2)]
            sg = fw.ring(es, "sgd", [128, 128], F32, 3)
            actT_t = fw.ring(es, "actTd", [128, NF, 128], BF16, 2)
            actT = [[Tl(fw, a_[:, i, :], "aT%d" % i) for i in range(NF)] for a_ in actT_t]
            rr = [xn, fw.sb(es, "rrd1", [128, D], F32)]
            st6 = fw.sb(es, "st6d", [128, 2, 6], F32)
            mv = fw.sb(es, "mvd", [128, 4], F32)
            ps_tr = ps[0]
            ps_up = ps[1:5]
            ps_dn = (ps[5], ps[6])
            t0 = NCT if last else 0

            def seq_first(t):
                return t == 0 or t == NCT

            def seq_last(t):
                return t == NCT - 1 or t == NT - 1

            def f1(t):
                if t == NCT:
                    load_mod(0)
                x_ = x1r[t % 2]
                fw.dma(fw.sp, x_.ap, self.X1[t * 128:(t + 1) * 128, :], reads=[self.x1buf], writes=[x_], ds=x_.ds)
                for hh in range(2):
                    fw.op(fw.dve, lambda e, hh=hh: e.bn_stats(out=st6[:, hh, :], in_=x_[:, hh * 512:(hh + 1) * 512]), [x_], [st6])
                fw.op(fw.dve, lambda e: e.bn_aggr(out=mv[:, 0:2], in_=st6.ap.rearrange("p a b -> p (a b)")), [st6], [mv])
                rstd_from(fw, mv[:, 2:3], mv[:, 1:2], [mv], [mv])
                ts(fw, fw.dve, xn.ap, x_.ap, mv[:, 0:1], mv[:, 2:3], ALU.subtract, ALU.mult, [x_, mv], [xn])
                tt(fw, fw.pool, xn.ap, xn.ap, modt[:, 1, :], ALU.mult, [xn, modt], [xn])
                tt(fw, fw.dve, hb.ap, xn.ap, modt[:, 0, :], ALU.add, [xn, modt], [hb])
                pb = ps_tr.ap.bitcast(BF16).rearrange("p (a b) -> p a b", a=8)
                for kc in range(8):
                    tr(fw, pb[:, kc, :], hb[:, kc * 128:(kc + 1) * 128], self.identb.ap, [hb, self.identb], [ps_tr], inc=(kc == 7))
                H_ = HTB[t % 3]
                if "cpH" in dsk:
                    return
                cp(fw, fw.act, H_[:, :, 2:130], pb, [ps_tr], [H_])
                if "halo" in dsk:
                    return
                if seq_first(t):
                    fw.op(fw.dve, lambda e: e.memset(H_[:, :, 1:2], 0.0), [], [H_])
                else:
                    Hp = HTB[(t - 1) % 3]
                    cp(fw, fw.dve, Hp[:, :, 130:131], H_[:, :, 2:3], [H_], [Hp])
                if seq_last(t):
                    fw.op(fw.dve, lambda e: e.memset(H_[:, :, 130:131], 0.0), [], [H_])
                elif t + 1 < NT:
                    Hn = HTB[(t + 1) % 3]
                    cp(fw, fw.dve, Hn[:, :, 1:2], H_[:, :, 129:130], [H_], [Hn])

            def f2(t):
                if t == NCT:
                    load_gate(0)
                H_ = HTB[t % 3]
                x_ = x1r[t % 2]
                aT = actT[t % 2]
                SK = 5

                def down(i):
                    for n in range(2):
                        mm(fw, ps_dn[n].ap, aT[i].ap, WD[:, i, n * 512:(n + 1) * 512], i == 0, i == NF - 1, [aT[i], WD], [ps_dn[n]], inc=True)

                def tail(i):
                    s_ = sg[i % 3]
                    act(fw, s_.ap, cA[0][i % 3].ap, AF.Silu, [cA[0][i % 3]], [s_])
                    tt(fw, fw.pool, aT[i].ap, cA[1][i % 3].ap, s_.ap, ALU.mult, [cA[1][i % 3], s_], [aT[i]])

                npair = NF if self.d_lim is None else self.d_lim[1]
                for i in range(npair):
                    pu = ps_up[i % 4]
                    for n_, ch in enumerate((NF + i, i)):
                        for kc in range(8):
                            mm(fw, pu[:, n_ * 130:(n_ + 1) * 130], WU[:, kc, ch * 128:(ch + 1) * 128], H_[:, kc, 1:131], kc == 0, kc == 7, [WU, H_], [pu],
                               inc=(kc == 7 and n_ == 1))
                    if i >= SK and npair == NF:
                        down(i - SK)
                    for n_, ch in enumerate((NF + i, i)):
                        u = pu[:, n_ * 130:(n_ + 1) * 130]
                        a_, b_ = cA[n_][i % 3], cB[n_][i % 3]
                        act(fw, a_.ap, u[:, 1:129], AF.Identity, [pu, cvp], [a_], bias=cvp[:, 3, ch:ch + 1], scale=cvp[:, 1, ch:ch + 1])
                        stt(fw, fw.dve, a_.ap, u[:, 0:128], cvp[:, 0, ch:ch + 1], a_.ap, ALU.mult, ALU.add, [pu, cvp, a_], [a_])
                        stt(fw, fw.dve, a_.ap, u[:, 2:130], cvp[:, 2, ch:ch + 1], a_.ap, ALU.mult, ALU.add, [pu, cvp, a_], [a_])
                    if i >= 1:
                        tail(i - 1)
                if npair < NF:
                    return
                tail(NF - 1)
                for i in range(NF - SK, NF):
                    down(i)
                r_ = rr[0]
                for n in range(2):
                    tt(fw, fw.dve, r_[:, n * 512:(n + 1) * 512], ps_dn[n].ap, gmlp[:, n * 512:(n + 1) * 512], ALU.mult, [ps_dn[n], gmlp], [r_])
                stt(fw, fw.dve, r_.ap, x_.ap, ALPHA, r_.ap, ALU.mult, ALU.add, [x_, r_], [r_])
                self.ln_affine(r_, rr[1], st6, mv, ln2t)
                if last:
                    fw.dma(fw.sp, self.out[(t - NCT) * 128:(t - NCT + 1) * 128, :], rr[1].ap, reads=[rr[1]], ds=rr[1].ds)
                else:
                    fw.dma(fw.sp, self.X[t * 128:(t + 1) * 128, :], rr[1].ap, reads=[rr[1]], writes=[self.xbuf], ds=rr[1].ds)

            dl = self.d_lim
            nt_ = NT if dl is None else t0 + dl[0]
            if "f1" in dsk:
                fw.barrier()
                return
            f1(t0)
            for t in range(t0, nt_):
                if t + 1 < NT:
                    f1(t + 1)
                if dl is None or dl[1] > 0:
                    f2(t)
            fw.barrier()

    def ln_affine(self, src, dst, st6, mv, gbt):
        fw = self.fw
        for hh in range(2):
            fw.op(fw.dve, lambda e, hh=hh: e.bn_stats(out=st6[:, hh, :], in_=src[:, hh * 512:(hh + 1) * 512]), [src], [st6])
        fw.op(fw.dve, lambda e: e.bn_aggr(out=mv[:, 0:2], in_=st6.ap.rearrange("p a b -> p (a b)")), [st6], [mv])
        rstd_from(fw, mv[:, 2:3], mv[:, 1:2], [mv], [mv])
        ts(fw, fw.dve, dst.ap, src.ap, mv[:, 0:1], mv[:, 2:3], ALU.subtract, ALU.mult, [src, mv], [dst])
        tt(fw, fw.pool, dst.ap, dst.ap, gbt[:, 0, :], ALU.mult, [dst, gbt], [dst])
        tt(fw, fw.pool, dst.ap, dst.ap, gbt[:, 1, :], ALU.add, [dst, gbt], [dst])

    def norm_rope(self, t, src_tl, src, nh, gain, dst_tl, dst, sl_tl, ss, tmp_tl, colT, rowT, qk):
        fw = self.fw
        n = nh * 64
        sq = tmp_tl[:, 0:n]
        s3 = src.rearrange("p (h d) -> p h d", h=nh)
        tt(fw, fw.dve, sq, src, src, ALU.mult, [src_tl], [tmp_tl])
        fw.op(fw.dve, lambda e: e.tensor_reduce(out=ss, in_=sq.rearrange("p (h d) -> p h d", h=nh), axis=AX.X, op=ALU.add), [tmp_tl], [sl_tl])
        rstd_from(fw, ss, ss, [sl_tl], [sl_tl], scale=1.0 / 64.0)
        tt(fw, fw.pool, s3, s3, ss.unsqueeze(2).to_broadcast([128, nh, 64]), ALU.mult, [src_tl, sl_tl], [src_tl])
        is_lat = t >= NCT
        gb = gain.unsqueeze(1).to_broadcast([128, nh, 64])
        if not is_lat:
            tt(fw, fw.pool, dst.rearrange("p (h d) -> p h d", h=nh), s3, gb, ALU.mult, [src_tl, qk], [dst_tl])
            return
        tt(fw, fw.pool, s3, s3, gb, ALU.mult, [src_tl, qk], [src_tl])
        j = t - NCT
        s5 = src.rearrange("p (h a two i) -> p h a two i", h=nh, a=2, two=2)
        o5 = tmp_tl[:, 0:n].rearrange("p (h a two i) -> p h a two i", h=nh, a=2, two=2)
        d5 = dst.rearrange("p (h a two i) -> p h a two i", h=nh, a=2, two=2)
        for a, tab in ((0, rowT[:, j, :]), (1, colT.ap)):
            cos_b = tab[:, 0:16].unsqueeze(1).to_broadcast([128, nh, 16])
            sin_b = tab[:, 16:32].unsqueeze(1).to_broadcast([128, nh, 16])
            tabt = rowT if a == 0 else colT
            x1 = s5[:, :, a, 0, :]
            x2 = s5[:, :, a, 1, :]
            tt(fw, fw.pool, o5[:, :, a, 0, :], x2, sin_b, ALU.mult, [src_tl, tabt], [tmp_tl])
            tt(fw, fw.pool, o5[:, :, a, 1, :], x1, sin_b, ALU.mult, [src_tl, tabt], [tmp_tl])
            tt(fw, fw.pool, x1, x1, cos_b, ALU.mult, [src_tl, tabt], [src_tl])
            tt(fw, fw.pool, x2, x2, cos_b, ALU.mult, [src_tl, tabt], [src_tl])
            tt(fw, fw.pool, d5[:, :, a, 0, :], x1, o5[:, :, a, 0, :], ALU.subtract, [src_tl, tmp_tl], [dst_tl])
            tt(fw, fw.pool, d5[:, :, a, 1, :], x2, o5[:, :, a, 1, :], ALU.add, [src_tl, tmp_tl], [dst_tl])


def make_consts():
    c = np.zeros((128, 1024), np.float32)
    p = np.arange(128)
    c[:, 0:128] = np.eye(128, dtype=np.float32)
    c[:, 128:256] = (p[:, None] <= p[None, :])
    c[:, 256:384] = (p[:, None] >= p[None, :])
    c[:, 384:512] = 1.0
    c[:, 512] = p + 1
    c[:, 513] = 128 - p
    c[:, 514] = -(p + 1)
    c[:, 515] = -(128 - p)
    c[:, 516] = p % 64
    c[:, 517:533] = np.arange(16)[None, :]
    return c


def host_inputs(inputs, b, L=DEPTH):
    f = lambda a: np.ascontiguousarray(np.asarray(a, dtype=np.float32))
    inputs = {k: (np.asarray(v)[:L] if k not in ('x', 'c', 'ctx', 'c_ctx') else v) for k, v in inputs.items()}
    m = {}
    m["xin"] = f(np.concatenate([inputs["ctx"][b], inputs["x"][b]], axis=0))
    cc = np.stack([np.asarray(inputs["c"][b]), np.asarray(inputs["c_ctx"])], axis=-1)
    m["ccT"] = f(cc.reshape(8, 128, 2).transpose(1, 0, 2))
    m["w_mod"] = f(inputs["w_mod"])
    m["b_mod"] = f(inputs["b_mod"])
    m["w_in"] = f(inputs["w_in"])
    m["b_in"] = f(inputs["b_in"])
    fmcols = np.concatenate([np.arange(O_RQ, O_RQ + 256), np.arange(O_RK, O_RK + 256), np.arange(O_MQ, O_MQ + 256), np.arange(O_MK, O_MK + 256)])
    m["b_in_fm"] = f(np.asarray(inputs["b_in"])[:, fmcols].reshape(L, 8, 128).transpose(0, 2, 1))
    m["decay"] = f(np.asarray(inputs["ret_decay_logit"]).reshape(L, 8))
    m["ret_gn"] = f(inputs["ret_gn_g"])
    m["qn_g"] = f(inputs["attn_qn_g"])
    m["kn_g"] = f(inputs["attn_kn_g"])
    m["m_gn"] = f(inputs["mlstm_gn_g"])
    m["w_br"] = f(np.stack([inputs["w_br_ret"], inputs["w_br_att"], inputs["w_br_mlstm"]], axis=1))
    m["w_out"] = f(inputs["w_out"])
    m["ln1"] = f(np.stack([inputs["ln1_g"], inputs["ln1_b"]], axis=1))
    m["w_up"] = f(inputs["w_up"])
    cw = np.asarray(inputs["conv_w"])
    cb = np.asarray(inputs["conv_b"])
    cpk = np.concatenate([cw, cb[:, None, :]], axis=1)
    m["convp"] = f(cpk.reshape(L, 4, 44, 128).transpose(0, 3, 1, 2))
    m["w_down"] = f(inputs["w_down"])
    m["ln2"] = f(np.stack([inputs["ln2_g"], inputs["ln2_b"]], axis=1))
    m["consts"] = make_consts()
    return m


_PROG = {}


def kernel(**inputs):
    if "p" not in _PROG:
        _PROG["p"] = Prog()
    prog = _PROG["p"]
    in_maps = [host_inputs(inputs, b) for b in range(NCORES)]
    res = run_bass_kernel_spmd(prog.nc, in_maps, core_ids=list(range(NCORES)))
    out = np.stack([np.asarray(r["out"]).reshape(32 * 128, D) for r in res.results], axis=0)
    return out.astype(np.float32)
```
